# Optimizing a Trainium2 kernel written in Bass

```python
import math
import jax
import jax.numpy as jnp
from jax import lax
import numpy as np

D_MODEL = 1024
BATCH = 2
SEQ = 8192
DEPTH = 2
DEC_BATCH = 4
DEC_SEQ = 4096
PAST_LEN = 128

N_BRANCH = 4
MIX_W = D_MODEL // 4
N_HEADS = 4
FN_GROUP = MIX_W // N_HEADS
RET_DK = MIX_W // N_HEADS
RET_DV = MIX_W // N_HEADS
GLA_DK = MIX_W // N_HEADS // 2
GLA_DV = MIX_W // N_HEADS
GLA_RANK = 16
GLA_TAU = 16.0
HY_W = MIX_W
HY_ORDER = 2
FLT_BANDS = 16
FLT_EMB = 1 + 2 * FLT_BANDS
FLT_HIDDEN = 64
D_FF = 2816
CHUNK = 64
ROPE_BASE = 10000.0
EPS = 1e-6
IN_SIZES = (MIX_W,
            MIX_W, MIX_W, MIX_W, MIX_W,
            3 * HY_W,
            N_HEADS * GLA_DK, N_HEADS * GLA_DK,
            N_HEADS * GLA_DV, MIX_W,
            2 * GLA_RANK,
            N_BRANCH * D_MODEL)
N_IN = sum(IN_SIZES)

kernel_name = 'hybrid_bidir_encoder'


def rms_norm(x, w):
    xf = x.astype(jnp.float32)
    xf = xf * lax.rsqrt(jnp.mean(xf * xf, axis=-1, keepdims=True) + EPS)
    return xf.astype(x.dtype) * w


def conv3(x, w, b):
    xp = jnp.pad(x, ((0, 0), (1, 1), (0, 0)))
    return xp[:, :-2] * w[0] + xp[:, 1:-1] * w[1] + xp[:, 2:] * w[2] + b


def rotary(x):
    L, dh = x.shape[1], x.shape[-1]
    half = dh // 2
    inv = ROPE_BASE ** (-jnp.arange(half, dtype=jnp.float32) / half)
    ang = jnp.arange(L, dtype=jnp.float32)[:, None] * inv[None, :]
    cos = jnp.cos(ang)[None, :, None, :]
    sin = jnp.sin(ang)[None, :, None, :]
    x1, x2 = x[..., :half], x[..., half:]
    return jnp.concatenate([x1 * cos - x2 * sin, x1 * sin + x2 * cos], axis=-1)


def gated_linear_scan(q, k, v, g, strict):
    Bn, L, H, K = q.shape
    V = v.shape[-1]
    N = L // CHUNK
    q, k, g = (a.astype(jnp.float32).reshape(Bn, N, CHUNK, H, K) for a in (q, k, g))
    v = v.astype(jnp.float32).reshape(Bn, N, CHUNK, H, V)
    b = jnp.cumsum(g, axis=2)
    b_mid = b[:, :, CHUNK // 2:CHUNK // 2 + 1]
    b_last = b[:, :, -1:]
    scores = jnp.einsum('bnihk,bnjhk->bnhij', q * jnp.exp(b - b_mid), k * jnp.exp(b_mid - b))
    mask = jnp.tril(jnp.ones((CHUNK, CHUNK), dtype=bool), k=-1 if strict else 0)
    scores = jnp.where(mask, scores, 0.0)
    intra = jnp.einsum('bnhij,bnjhv->bnihv', scores, v)
    chunk_state = jnp.einsum('bnjhk,bnjhv->nbhkv', k * jnp.exp(b_last - b), v)
    chunk_decay = jnp.moveaxis(jnp.exp(b_last[:, :, 0]), 1, 0)

    def step(S, inp):
        dec, U = inp
        return dec[..., None] * S + U, S

    _, S_enter = lax.scan(step, jnp.zeros((Bn, H, K, V), jnp.float32), (chunk_decay, chunk_state))
    inter = jnp.einsum('bnihk,nbhkv->bnihv', q * jnp.exp(b), S_enter)
    return (intra + inter).reshape(Bn, L, H, V)


def bidirectional_gated_linear(q, k, v, g_fwd, g_bwd):
    fwd = gated_linear_scan(q, k, v, g_fwd, False)
    flip = lambda a: a[:, ::-1]
    bwd = flip(gated_linear_scan(flip(q), flip(k), flip(v), flip(g_bwd), True))
    return fwd + bwd


def head_norm(o, gain, center):
    Bn, L, H, dv = o.shape
    if center:
        o = o - jnp.mean(o, axis=-1, keepdims=True)
    o = o * lax.rsqrt(jnp.mean(o * o, axis=-1, keepdims=True) + EPS)
    return o.reshape(Bn, L, H * dv) * gain


def hyena_filters(L, flt_w1, flt_b1, flt_freq, flt_w2, flt_b2, flt_w3, flt_b3):
    t = jnp.linspace(0.0, 1.0, L, dtype=jnp.float32)[:, None]
    w = 2.0 * math.pi * jnp.arange(L, dtype=jnp.float32)[:, None] / L
    f = jnp.linspace(1e-4, FLT_BANDS - 1, FLT_BANDS, dtype=jnp.float32)[None, :]
    feat = jnp.concatenate([t, jnp.cos(f * w), -jnp.sin(f * w)], axis=-1)
    hdn = jnp.sin(flt_freq * (feat @ flt_w1 + flt_b1))
    hdn = jnp.sin(flt_freq * (hdn @ flt_w2 + flt_b2))
    filt = (hdn @ flt_w3 + flt_b3).astype(jnp.float32).reshape(L, HY_ORDER, 2, HY_W)
    deltas = jnp.abs(jnp.linspace(math.log(1e-2) / 0.3, math.log(1e-2) / 1.5, HY_W, dtype=jnp.float32))
    filt = filt * jnp.exp(-t[:, :, None, None] * deltas)
    hf, hb = filt[:, :, 0], filt[:, :, 1]
    g = jnp.concatenate([hf, jnp.zeros((1, HY_ORDER, HY_W), jnp.float32), hb[1:][::-1]], axis=0)
    g = g / (jnp.sum(jnp.abs(g), axis=0, keepdims=True) + EPS)
    return jnp.moveaxis(g, 1, 0)


def long_conv(u, g):
    L = u.shape[1]
    U = jnp.fft.rfft(u, n=2 * L, axis=1)
    G = jnp.fft.rfft(g, n=2 * L, axis=0)
    return jnp.fft.irfft(U * G[None], n=2 * L, axis=1)[:, :L]


def hyena_mixer(u, hy_conv_w, hy_conv_b, flt_w1, flt_b1, flt_freq, flt_w2, flt_b2, flt_w3, flt_b3, hy_skip):
    L = u.shape[1]
    u = conv3(u, hy_conv_w, hy_conv_b).astype(jnp.float32)
    v, x1, x2 = jnp.split(u, 3, axis=-1)
    filt = hyena_filters(L, flt_w1, flt_b1, flt_freq, flt_w2, flt_b2, flt_w3, flt_b3)
    z = x1 * (long_conv(v, filt[0]) + hy_skip[0] * v)
    z = x2 * (long_conv(z, filt[1]) + hy_skip[1] * z)
    return z


def token_mixer(h, w_in, hy_conv_w, hy_conv_b, flt_w1, flt_b1, flt_freq, flt_w2, flt_b2, flt_w3, flt_b3,
                hy_skip, gla_w_decay, gla_b_decay, ret_gn, gla_gn, w_branch, w_out):
    Bn, L, _ = h.shape
    proj = h @ w_in
    offsets = np.cumsum(IN_SIZES)[:-1].tolist()
    (u_fn, q_r, k_r, v_r, g_r, u_hy, q_g, k_g, v_g, g_g, lr_g, gates) = jnp.split(proj, offsets, axis=-1)
    heads = lambda a, d: a.astype(jnp.float32).reshape(Bn, L, N_HEADS, d)

    o_fn = jnp.fft.fftn(heads(u_fn, FN_GROUP), axes=(1, 3), norm='ortho').real.reshape(Bn, L, MIX_W)

    q = rotary(heads(q_r, RET_DK)) * RET_DK ** -0.5
    k = rotary(heads(k_r, RET_DK))
    v = heads(v_r, RET_DV)
    log_gamma = jnp.log(1.0 - 2.0 ** (-5.0 - jnp.arange(N_HEADS, dtype=jnp.float32)))
    g_ret = jnp.broadcast_to(log_gamma[None, None, :, None], q.shape)
    o_ret = bidirectional_gated_linear(q, k, v, g_ret, g_ret)
    o_ret = head_norm(o_ret, ret_gn, True) * jax.nn.silu(g_r)

    o_hy = hyena_mixer(u_hy, hy_conv_w, hy_conv_b, flt_w1, flt_b1, flt_freq, flt_w2, flt_b2, flt_w3, flt_b3, hy_skip)

    qg = heads(q_g, GLA_DK) * GLA_DK ** -0.5
    kg = heads(k_g, GLA_DK)
    vg = heads(v_g, GLA_DV)
    z = jnp.einsum('blrk,rkd->blrd', lr_g.astype(jnp.float32).reshape(Bn, L, 2, GLA_RANK), gla_w_decay) + gla_b_decay
    log_dec = (jax.nn.log_sigmoid(z.astype(jnp.float32)) / GLA_TAU).reshape(Bn, L, 2, N_HEADS, GLA_DK)
    o_gla = bidirectional_gated_linear(qg, kg, vg, log_dec[:, :, 0], log_dec[:, :, 1])
    o_gla = head_norm(o_gla, gla_gn, False) * jax.nn.silu(g_g)

    branches = jnp.stack([o_fn, o_ret, o_hy, o_gla], axis=2)
    proj_b = jnp.einsum('blnm,nmd->blnd', branches, w_branch)
    gate = jax.nn.sigmoid(gates.reshape(Bn, L, N_BRANCH, D_MODEL))
    merged = jnp.sum(gate * proj_b, axis=2)
    return merged @ w_out


def conv_ffn(h, ffn_up, ffn_conv_w, ffn_conv_b, ffn_down):
    a = conv3(h @ ffn_up, ffn_conv_w, ffn_conv_b)
    gate, val = jnp.split(a, 2, axis=-1)
    return (jax.nn.gelu(gate, approximate=True) * val) @ ffn_down


def encoder_layer(x, c, ada_w, ada_b, norm_pre_mix, norm_post_mix, norm_pre_ffn, norm_post_ffn, w_in,
                  hy_conv_w, hy_conv_b, flt_w1, flt_b1, flt_freq, flt_w2, flt_b2, flt_w3, flt_b3, hy_skip,
                  gla_w_decay, gla_b_decay, ret_gn, gla_gn, w_branch, w_out, ffn_up, ffn_conv_w, ffn_conv_b, ffn_down):
    mod = jax.nn.silu(c) @ ada_w + ada_b
    sh_m, sc_m, gt_m, sh_f, sc_f, gt_f = jnp.split(mod[:, None, :], 6, axis=-1)
    h = rms_norm(x, norm_pre_mix) * (1.0 + sc_m) + sh_m
    y = token_mixer(h, w_in, hy_conv_w, hy_conv_b, flt_w1, flt_b1, flt_freq, flt_w2, flt_b2, flt_w3, flt_b3,
                    hy_skip, gla_w_decay, gla_b_decay, ret_gn, gla_gn, w_branch, w_out)
    x = x + gt_m * rms_norm(y, norm_post_mix)
    h = rms_norm(x, norm_pre_ffn) * (1.0 + sc_f) + sh_f
    y = conv_ffn(h, ffn_up, ffn_conv_w, ffn_conv_b, ffn_down)
    return x + gt_f * rms_norm(y, norm_post_ffn)


def setup_inputs(seed: int = 0) -> dict:
    key = jax.random.key(seed)
    ks = jax.random.split(key, 32)
    f32 = jnp.float32

    def nrm(k, shape, scale):
        return scale * jax.random.normal(k, shape, f32)

    def gain(k, shape):
        return 1.0 + 0.1 * jax.random.normal(k, shape, f32)

    return {
        'x_prompt': nrm(ks[0], (BATCH, SEQ, D_MODEL), 1.0),
        'x_sample': nrm(ks[1], (DEC_BATCH, DEC_SEQ, D_MODEL), 1.0),
        'c_prompt': nrm(ks[2], (BATCH, D_MODEL), 1.0),
        'c_sample': nrm(ks[3], (DEC_BATCH, D_MODEL), 1.0),
        'ada_w': nrm(ks[4], (DEPTH, D_MODEL, 6 * D_MODEL), 0.5 * D_MODEL ** -0.5),
        'ada_b': nrm(ks[5], (DEPTH, 6 * D_MODEL), 0.02),
        'norm_pre_mix': gain(ks[6], (DEPTH, D_MODEL)),
        'norm_post_mix': gain(ks[7], (DEPTH, D_MODEL)),
        'norm_pre_ffn': gain(ks[8], (DEPTH, D_MODEL)),
        'norm_post_ffn': gain(ks[9], (DEPTH, D_MODEL)),
        'w_in': nrm(ks[10], (DEPTH, D_MODEL, N_IN), D_MODEL ** -0.5),
        'hy_conv_w': nrm(ks[11], (DEPTH, 3, 3 * HY_W), 3 ** -0.5),
        'hy_conv_b': nrm(ks[12], (DEPTH, 3 * HY_W), 0.02),
        'flt_w1': nrm(ks[13], (DEPTH, FLT_EMB, FLT_HIDDEN), FLT_EMB ** -0.5),
        'flt_b1': nrm(ks[14], (DEPTH, FLT_HIDDEN), 0.02),
        'flt_freq': gain(ks[15], (DEPTH, FLT_HIDDEN)),
        'flt_w2': nrm(ks[16], (DEPTH, FLT_HIDDEN, FLT_HIDDEN), FLT_HIDDEN ** -0.5),
        'flt_b2': nrm(ks[17], (DEPTH, FLT_HIDDEN), 0.02),
        'flt_w3': nrm(ks[18], (DEPTH, FLT_HIDDEN, HY_ORDER * 2 * HY_W), FLT_HIDDEN ** -0.5),
        'flt_b3': nrm(ks[19], (DEPTH, HY_ORDER * 2 * HY_W), 0.02),
        'hy_skip': nrm(ks[20], (DEPTH, HY_ORDER, HY_W), 0.5),
        'gla_w_decay': nrm(ks[21], (DEPTH, 2, GLA_RANK, N_HEADS * GLA_DK), GLA_RANK ** -0.5),
        'gla_b_decay': nrm(ks[22], (DEPTH, 2, N_HEADS * GLA_DK), 0.1),
        'ret_gn': gain(ks[23], (DEPTH, N_HEADS * RET_DV)),
        'gla_gn': gain(ks[24], (DEPTH, N_HEADS * GLA_DV)),
        'w_branch': nrm(ks[25], (DEPTH, N_BRANCH, MIX_W, D_MODEL), MIX_W ** -0.5),
        'w_out': nrm(ks[26], (DEPTH, D_MODEL, D_MODEL), D_MODEL ** -0.5),
        'ffn_up': nrm(ks[27], (DEPTH, D_MODEL, 2 * D_FF), D_MODEL ** -0.5),
        'ffn_conv_w': nrm(ks[28], (DEPTH, 3, 2 * D_FF), 3 ** -0.5),
        'ffn_conv_b': nrm(ks[29], (DEPTH, 2 * D_FF), 0.02),
        'ffn_down': nrm(ks[30], (DEPTH, D_FF, D_MODEL), D_FF ** -0.5),
    }


def reference(x_prompt, x_sample, c_prompt, c_sample, ada_w, ada_b, norm_pre_mix, norm_post_mix, norm_pre_ffn,
              norm_post_ffn, w_in, hy_conv_w, hy_conv_b, flt_w1, flt_b1, flt_freq, flt_w2, flt_b2, flt_w3, flt_b3,
              hy_skip, gla_w_decay, gla_b_decay, ret_gn, gla_gn, w_branch, w_out, ffn_up, ffn_conv_w, ffn_conv_b,
              ffn_down):
    y_prompt = x_prompt
    y_sample = x_sample
    for i in range(DEPTH):
        lp = (ada_w[i], ada_b[i], norm_pre_mix[i], norm_post_mix[i], norm_pre_ffn[i], norm_post_ffn[i], w_in[i],
              hy_conv_w[i], hy_conv_b[i], flt_w1[i], flt_b1[i], flt_freq[i], flt_w2[i], flt_b2[i], flt_w3[i],
              flt_b3[i], hy_skip[i], gla_w_decay[i], gla_b_decay[i], ret_gn[i], gla_gn[i], w_branch[i], w_out[i],
              ffn_up[i], ffn_conv_w[i], ffn_conv_b[i], ffn_down[i])
        y_prompt = encoder_layer(y_prompt, c_prompt, *lp)
        y_sample = encoder_layer(y_sample, c_sample, *lp)
    return (y_prompt, y_sample)
```

```python
import math
from contextlib import ExitStack
import numpy as np
import ml_dtypes
import concourse.bass as bass
import concourse.mybir as mybir
from concourse.bass_utils import run_bass_kernel_spmd

F32 = mybir.dt.float32
BF16 = mybir.dt.bfloat16
AF = mybir.ActivationFunctionType
ALU = mybir.AluOpType
NPBF = ml_dtypes.bfloat16

D = 1024
DEPTH = 2
DFF = 2816
EPS = 1e-6
MAGIC = 12582912.0
import os
PIPE_DEPTH = int(os.environ.get("PIPE_DEPTH", "2"))
PD_POS = int(os.environ.get("PD_POS", "99"))
CONC_FILT = int(os.environ.get("CONC_FILT", "1"))
USE_POOL = int(os.environ.get("USE_POOL", "0"))
PRIM_STEPS = int(os.environ.get("PRIM_STEPS", "1"))


class Stream:
    def __init__(self, P, inc):
        self.P, self.inc = P, inc
        self.sem = P.new_sem()
        self.count = 0

    def bump(self):
        if self.count + self.inc > 30000:
            self.sem = self.P.new_sem()
            self.count = 0
        self.count += self.inc
        return (self.sem, self.count)

    def cur(self):
        return (self.sem, self.count) if self.count else None


class Buf:
    def __init__(self, name, t=None):
        self.name, self.t = name, t
        self.w = None
        self.r = {}

    def __getitem__(self, idx):
        return self.t[idx]


class Prog:
    def __init__(self, nc, es):
        self.nc, self.es = nc, es
        self.nsem = 0
        self.engs = {'pe': nc.tensor, 'act': nc.scalar, 'dve': nc.vector, 'pool': nc.gpsimd, 'sp': nc.sync}
        self.streams = {k: Stream(self, 1) for k in ('pe', 'act', 'dve', 'pool')}
        self.seen = {k: {} for k in self.engs}
        self.dma_pool = {q: [Stream(self, 16) for _ in range(8)] for q in ('sp', 'pool')}
        self.dma_rr = {q: 0 for q in self.dma_pool}
        self.scopes = [es]
        self.nuniq = 0

    def new_sem(self):
        self.nsem += 1
        return self.es.enter_context(self.nc.semaphore(f"s{self.nsem}"))

    def sb(self, name, shape, dt=F32):
        self.nuniq += 1
        return Buf(name, self.scopes[-1].enter_context(self.nc.sbuf_tensor(f"{name}_{self.nuniq}", shape, dt)))

    def ps(self, name, shape, dt=F32):
        self.nuniq += 1
        return Buf(name, self.scopes[-1].enter_context(self.nc.psum_tensor(f"{name}_{self.nuniq}", shape, dt)))

    def push(self):
        st = ExitStack()
        self.scopes.append(st)
        return st

    def pop(self):
        self.barrier()
        self.scopes.pop().close()

    def _wait(self, eng, tok):
        if tok is None:
            return
        sem, val = tok
        seen = self.seen[eng]
        if seen.get(id(sem), 0) >= val:
            return
        self.engs[eng].wait_ge(sem, val)
        seen[id(sem)] = val

    def barrier(self):
        toks = [s.cur() for s in self.streams.values()]
        for pool in self.dma_pool.values():
            toks += [s.cur() for s in pool]
        for eng in self.engs:
            for t in toks:
                self._wait(eng, t)

    def _deps(self, eng, reads, writes, accum):
        for b in reads:
            self._wait(eng, b.w)
        for b in writes:
            if not accum:
                self._wait(eng, b.w)
            for t in b.r.values():
                self._wait(eng, t)

    def _commit(self, tok, reads, writes):
        for b in writes:
            b.w = tok
            b.r = {}
        for b in reads:
            b.r[id(tok[0])] = tok

    def op(self, eng, fn, reads=(), writes=(), accum=False):
        self._deps(eng, reads, writes, accum)
        inst = fn(self.engs[eng])
        tok = self.streams[eng].bump()
        inst.then_inc(tok[0], 1)
        self._commit(tok, reads, writes)
        return tok

    def dma(self, q, out, in_, reads=(), writes=()):
        pool = self.dma_pool[q]
        st = pool[self.dma_rr[q] % len(pool)]
        self.dma_rr[q] += 1
        self._wait(q, st.cur())
        self._deps(q, reads, writes, False)
        inst = self.engs[q].dma_start(out=out, in_=in_)
        tok = st.bump()
        inst.then_inc(tok[0], 16)
        self._commit(tok, reads, writes)
        return tok


def _cplx_pair(M):
    return np.concatenate([M.real, M.imag], 1), np.concatenate([-M.imag, M.real], 1)


def make_tables(T, kind):
    NT = T // 128
    NS = 2 * NT
    H = NT // 2
    isS = (kind == 'S')
    tb = {}
    tb['ident_b'] = np.eye(128).astype(NPBF)
    tb['ident_f'] = np.eye(128, dtype=np.float32)
    cc = np.arange(64)
    ang = 2 * np.pi * np.outer(cc, cc) / 64
    bdc = np.zeros((128, 128)); bds = np.zeros((128, 128))
    for g in range(2):
        bdc[g * 64:(g + 1) * 64, g * 64:(g + 1) * 64] = np.cos(ang)
        bds[g * 64:(g + 1) * 64, g * 64:(g + 1) * 64] = -np.sin(ang)
    tb['bdcs'] = np.stack([bdc, bds], 1).astype(NPBF)
    n1 = np.arange(NT)
    M1 = np.zeros((NT, NS), np.complex128)
    M2 = np.zeros((NS, 128, 128), np.complex128)
    n2 = np.arange(128)[:, None]
    k2 = np.arange(128)[None, :]
    if not isS:
        L = T
        for j in range(NT):
            M1[:, j] = np.exp(-2j * np.pi * n1 * j / NT)
            M2[j] = np.exp(-2j * np.pi * n2 * (j + NT * k2) / T)
    else:
        L = T // 2
        for s in range(2):
            for j in range(NT):
                M1[s * H:(s + 1) * H, s * NT + j] = np.exp(-2j * np.pi * np.arange(H) * j / H)
                m = np.exp(-2j * np.pi * n2 * (NT * (k2 % 64) + j) / L) * ((k2 // 64) == s)
                M2[s * NT + j] = m
    M2 = M2 / math.sqrt(L * 64)
    a, b = _cplx_pair(M1)
    tb['fn_m1'] = np.stack([a, b], 1).astype(NPBF)
    fm2 = np.zeros((NT, 128, 2, 2, 128), np.float64)
    for s in range(2):
        for j in range(NT):
            fm2[j, :, s, 0] = M2[s * NT + j].real
            fm2[j, :, s, 1] = -M2[s * NT + j].imag
    tb['fn_m2'] = fm2.astype(NPBF)
    HF1 = np.zeros((NS, NS), np.complex128)
    H2 = np.zeros((NS, 128, 128), np.complex128)
    HZ = np.zeros((128, NS, NT), np.complex128)
    if not isS:
        N = 2 * T
        for j in range(NS):
            HF1[:, j] = np.exp(-2j * np.pi * np.arange(NS) * j / NS)
            H2[j] = np.exp(-2j * np.pi * n2 * (j + NS * k2) / N)
        for q in range(128):
            HZ[q] = np.exp(2j * np.pi * np.outer(np.arange(NS), 128 * np.arange(NT) + q) / N) / N
    else:
        N = T
        for s in range(2):
            for j in range(NT):
                HF1[s * NT:(s + 1) * NT, s * NT + j] = np.exp(-2j * np.pi * np.arange(NT) * j / NT)
                H2[s * NT + j] = np.exp(-2j * np.pi * n2 * (j + NT * k2) / N)
        for q in range(128):
            for s in range(2):
                HZ[q, s * NT:(s + 1) * NT, s * H:(s + 1) * H] = \
                    np.exp(2j * np.pi * np.outer(np.arange(NT), 128 * np.arange(H) + q) / N) / N
    if not isS:
        H1 = HF1[:NT]
    else:
        H1 = np.concatenate([HF1[0:H], HF1[NT:NT + H]], 0)
    if not isS:
        act = list(range(NS // 2 + 1)) + [None]
        wts = [1.0 if j in (0, NS // 2) else 2.0 for j in range(NS // 2 + 1)] + [0.0]
    else:
        act = [s * NT + j for s in range(2) for j in range(NT // 2 + 1)]
        wts = [1.0 if j in (0, NT // 2) else 2.0 for s in range(2) for j in range(NT // 2 + 1)]
    def sel(M, axis):
        parts = []
        for a in act:
            if a is None:
                parts.append(np.zeros_like(np.take(M, [0], axis=axis)))
            else:
                parts.append(np.take(M, [a], axis=axis))
        return np.concatenate(parts, axis=axis)
    H1 = sel(H1, 1); HF1 = sel(HF1, 1); H2 = sel(H2, 0)
    HZ = sel(HZ, 1) * np.asarray(wts)[None, :, None]
    tb['hy_h1'] = np.concatenate([H1.real, H1.imag], 1).astype(NPBF)
    tb['hy_hf1'] = np.concatenate([HF1.real, HF1.imag], 1).astype(NPBF)
    tb['hy_h2'] = np.stack([H2.real, H2.imag, -H2.imag], 2).astype(NPBF)
    Fi = np.exp(2j * np.pi * np.outer(np.arange(128), np.arange(128)) / 128)
    a, b = _cplx_pair(Fi)
    tb['hy_i1'] = np.stack([a, b], 1).astype(NPBF)
    tb['hy_z'] = np.stack([HZ.real, -HZ.imag], 2).transpose(1, 0, 2, 3).astype(NPBF).copy()
    Lf = L
    mpos = np.arange(2 * T)
    mloc = mpos % (2 * Lf)
    lag = np.where(mloc < Lf, mloc, 2 * Lf - mloc)
    lag = np.where(mloc == Lf, 0, lag)
    mf = (mloc < Lf).astype(np.float32)
    mb = (mloc > Lf).astype(np.float32)
    tl = np.linspace(0.0, 1.0, Lf, dtype=np.float32)
    wl = (2.0 * np.float32(math.pi) * np.arange(Lf, dtype=np.float32) / np.float32(Lf)).astype(np.float32)
    fb = np.linspace(1e-4, 15, 16, dtype=np.float32)[None, :]
    feat = np.concatenate([tl[:, None], np.cos(fb * wl[:, None]), -np.sin(fb * wl[:, None])], -1).astype(np.float32)
    tb['flt_feat'] = np.ascontiguousarray(feat[lag].T).astype(np.float32)
    tb['flt_msk'] = np.stack([mf, mb, -tl[lag]], 0).astype(np.float32)
    deltas = np.abs(np.linspace(math.log(1e-2) / 0.3, math.log(1e-2) / 1.5, 256, dtype=np.float32))
    tb['flt_delta'] = np.ascontiguousarray(deltas.reshape(2, 128).T).astype(np.float32)
    sc = np.zeros((128, 4), np.float32)
    sc[:, 0] = 0.0 if isS else 1.0
    sc[:, 1] = 0.5 if isS else 1.0
    tb['scal'] = sc
    NB = T // 512
    hal = np.ones((NB, 2), np.float32)
    hal[0, 0] = 0.0; hal[NB - 1, 1] = 0.0
    if isS:
        hal[NB // 2, 0] = 0.0; hal[NB // 2 - 1, 1] = 0.0
    tb['hal'] = np.broadcast_to(hal.reshape(1, NB * 2), (128, NB * 2)).astype(np.float32).copy()
    pos = (np.arange(T) % L).astype(np.float32)
    inv = (10000.0 ** (-np.arange(32, dtype=np.float32) / 32)).astype(np.float32)
    angr = pos[:, None] * inv[None, :]
    tb['rot'] = np.stack([np.cos(angr), np.sin(angr)], 1).astype(np.float32)
    lg = np.log(1.0 - 2.0 ** (-5.0 - np.arange(4)))
    i = np.arange(128)
    ret_e = np.zeros((128, 2, 6, 128), np.float64)
    ret_tok = np.zeros((128, 2, 256), np.float64)
    ret_dec = np.zeros((128, 2, 2), np.float64)
    for c in range(2):
        for p in range(128):
            h = 2 * c + p // 64
            bf = (i + 1) * lg[h]; bb = (128 - i) * lg[h]
            ret_e[p, c, 0] = np.exp(bf - bf[64]) / 8.0
            ret_e[p, c, 1] = np.exp(bf[64] - bf)
            ret_e[p, c, 2] = np.exp(bb - bb[64]) / 8.0
            ret_e[p, c, 3] = np.exp(bb[64] - bb)
            ret_e[p, c, 4] = np.exp(bf) / 8.0
            ret_e[p, c, 5] = np.exp(bb) / 8.0
            ret_dec[p, c, :] = np.exp(128 * lg[h])
    for h in range(4):
        ret_tok[:, 0, h * 64:(h + 1) * 64] = np.exp((127 - i) * lg[h])[:, None]
        ret_tok[:, 1, h * 64:(h + 1) * 64] = np.exp(i * lg[h])[:, None]
    tb['ret_e'] = ret_e.astype(np.float32)
    tb['ret_tok'] = ret_tok.astype(np.float32)
    tb['ret_dec'] = ret_dec.astype(np.float32)
    mh_ret = np.zeros((128, 2), np.float32); mh_gla = np.zeros((128, 4), np.float32)
    bd_ret = np.zeros((128, 2, 256), np.float32); bd_gla = np.zeros((128, 1, 256), np.float32)
    for p in range(128):
        mh_ret[p, p // 64] = 1.0; mh_gla[p, p // 32] = 1.0
        for c in range(2):
            h = 2 * c + p // 64
            bd_ret[p, c, h * 64:(h + 1) * 64] = 1.0
        h = p // 32
        bd_gla[p, 0, h * 64:(h + 1) * 64] = 1.0
    tb['mh_ret'] = mh_ret; tb['mh_gla'] = mh_gla; tb['bd_ret'] = bd_ret; tb['bd_gla'] = bd_gla
    jj = np.arange(128)[:, None]; ii = np.arange(128)[None, :]
    tri = np.zeros((128, 6, 128), np.float32)
    tri[:, 0] = (jj <= ii)
    tri[:, 1] = (jj > ii)
    tri[:, 2] = -(jj <= ii).astype(np.float32) / 16.0
    tri[:, 3] = -(jj >= ii).astype(np.float32) / 16.0
    tri[:, 4] = -(jj > ii).astype(np.float32) / 16.0
    tri[:, 5] = -(jj < ii).astype(np.float32) / 16.0
    tb['tri'] = tri
    return tb


OFF_FN, OFF_QR, OFF_HY, OFF_QG, OFF_GATES = 0, 256, 1280, 2048, 2848
NTM = 1824


def build_program(T, debug=()):
    NT, NS, NB, H = T // 128, T // 64, T // 512, T // 256
    NSA = NT + 2
    nc = bass.Bass("TRN2", target_bir_lowering=False)
    I = {}

    def inp(name, shape, dt=F32):
        I[name] = nc.dram_tensor(name, list(shape), dt, kind="ExternalInput").ap()
        return I[name]

    def scratch(name, shape, dt=F32):
        kind = "ExternalOutput" if name in debug else "Internal"
        return nc.dram_tensor(name, list(shape), dt, kind=kind).ap()

    x_in = inp("x", [T, D]); inp("cT", [128, 8, 2])
    inp("ada_w", [DEPTH, D, 6 * D]); inp("ada_b_col", [DEPTH, 128, 48]); inp("ada_b", [DEPTH, 6 * D])
    inp("normw_col", [DEPTH, 128, 4, 8]); inp("norm_post_mix", [DEPTH, D]); inp("norm_post_ffn", [DEPTH, D])
    inp("w_in", [DEPTH, D, 6944])
    inp("hy_cw", [DEPTH, 128, 6, 3]); inp("hy_cb", [DEPTH, 128, 6])
    inp("flt_w1", [DEPTH, 33, 64]); inp("flt_c1", [DEPTH, 64, 2]); inp("flt_w2", [DEPTH, 64, 64]); inp("flt_c2", [DEPTH, 64, 2])
    inp("flt_w3", [DEPTH, 64, 1024]); inp("flt_b3", [DEPTH, 1024]); inp("hy_skip", [DEPTH, 2, 256])
    inp("gla_wd", [DEPTH, 33, 256]); inp("ret_gn", [DEPTH, 256]); inp("gla_gn", [DEPTH, 256])
    inp("w_branch", [DEPTH, 1024, D]); inp("w_out", [DEPTH, D, D])
    inp("ffn_up", [DEPTH, D, 2 * DFF]); inp("ffn_cw", [DEPTH, 128, 44, 3]); inp("ffn_cb", [DEPTH, 128, 44])
    inp("ffn_down", [DEPTH, DFF, D])
    for nm, shp, dt in (("ident_b", [128, 128], BF16), ("ident_f", [128, 128], F32), ("bdcs", [128, 2, 128], BF16),
                        ("fn_m1", [NT, 2, 2 * NS], BF16), ("fn_m2", [NT, 128, 2, 2, 128], BF16),
                        ("hy_h1", [NT, 2 * NSA], BF16), ("hy_hf1", [NS, 2 * NSA], BF16), ("hy_h2", [NSA, 128, 3, 128], BF16),
                        ("hy_i1", [128, 2, 256], BF16), ("hy_z", [NSA, 128, 2, NT], BF16),
                        ("flt_feat", [33, 2 * T], F32), ("flt_msk", [3, 2 * T], F32), ("flt_delta", [128, 2], F32),
                        ("scal", [128, 4], F32), ("hal", [128, NB * 2], F32), ("rot", [T, 2, 32], F32),
                        ("ret_e", [128, 2, 6, 128], F32), ("ret_tok", [128, 2, 256], F32), ("ret_dec", [128, 2, 2], F32),
                        ("mh_ret", [128, 2], F32), ("mh_gla", [128, 4], F32), ("bd_ret", [128, 2, 256], F32),
                        ("bd_gla", [128, 1, 256], F32), ("tri", [128, 6, 128], F32)):
        inp(nm, shp, dt)
    y_out = nc.dram_tensor("y", [T, D], F32, kind="ExternalOutput").ap()
    x1d = scratch("x1d", [T, D]); xmid = scratch("xmid", [T, D])
    projT = scratch("projT", [T, NTM]); zF = scratch("zF", [2, 256, T], BF16); uhF = scratch("uhF", [768, T])
    brF = scratch("brF", [1024, T], BF16); modrow = scratch("modrow", [2, 2048]); hF = scratch("hF", [D, T + 2], BF16); gFF = scratch("gFF", [DFF, T], BF16)
    gF = scratch("gF", [512, 2 * T], BF16); rnD = scratch("rnD", [512]); Gd = scratch("Gd", [16, 128, NSA, 64], BF16)

    es = ExitStack()
    with es:
        es.enter_context(nc.allow_non_contiguous_dma(reason="strided scratch layouts"))
        P = Prog(nc, es)
        dB = {k: Buf(k) for k in ("x1d", "xmid", "projT", "zF", "uhF", "brF", "modrow", "gF", "rnD", "Gd", "y", "in", "hF", "gFF")}
        IN = dB["in"]

        def ld(dst_buf, dst_ap, src_ap, src=IN):
            return P.dma('sp', dst_ap, src_ap, reads=[src], writes=[dst_buf])

        def stq(dst_ap, src_buf, src_ap, dst):
            return P.dma('pool', dst_ap, src_ap, reads=[src_buf], writes=[dst])

        def dve(fn, r, w):
            return P.op('dve', fn, r, w)

        def act(fn, r, w):
            return P.op('act', fn, r, w)

        def gps(fn, r, w):
            return P.op('pool' if USE_POOL else 'dve', fn, r, w)

        def pe(fn, r, w):
            return P.op('pe', fn, r, w, accum=True)

        identb = P.sb("identb", [128, 128], BF16); identf = P.sb("identf", [128, 128], F32)
        scal = P.sb("scal", [128, 4]); hal = P.sb("hal", [128, NB * 2]); epsb = P.sb("epsb", [128, 1])
        modc = P.sb("modc", [128, 48, 2]); Am = P.sb("Am", [128, 8, 2]); Af = P.sb("Af", [128, 8, 2])
        gtb = P.sb("gtb", [128, 2, 2, D])
        ld(identb, identb[:], I["ident_b"][:, :]); ld(identf, identf[:], I["ident_f"][:, :])
        ld(scal, scal[:], I["scal"][:, :]); ld(hal, hal[:], I["hal"][:, :])
        dve(lambda e: e.memset(epsb[:], EPS), [], [epsb])

        def phase_mod(l):
            P.push()
            cT = P.sb("cT", [128, 8, 2]); scT = P.sb("scT", [128, 8, 2])
            ld(cT, cT[:], I["cT"][:, :, :])
            act(lambda e: e.activation(out=scT[:], in_=cT[:], func=AF.Silu), [cT], [scT])
            psc = P.ps("psc", [128, 96]); psr = P.ps("psr", [2, 2048]); macc = P.sb("macc", [128, 96])
            wts = [P.sb(f"adaw{i}", [128, 6 * D]) for i in range(2)]
            rowcols = (2048, 2560, 5120, 5632)
            for kc in range(8):
                wt = wts[kc % 2]
                ld(wt, wt[:], I["ada_w"][l, kc * 128:(kc + 1) * 128, :])
                for q in range(48):
                    pe(lambda e, q=q: e.matmul(psc[:, 2 * q:2 * q + 2], lhsT=wt[:, q * 128:(q + 1) * 128], rhs=scT[:, kc, :],
                                               start=True, stop=True), [wt, scT], [psc])
                if kc == 0:
                    dve(lambda e: e.tensor_copy(out=macc[:], in_=psc[:]), [psc], [macc])
                else:
                    dve(lambda e: e.tensor_tensor(out=macc[:], in0=macc[:], in1=psc[:], op=ALU.add), [psc, macc], [macc])
                for bi, c0 in enumerate(rowcols):
                    pe(lambda e, bi=bi, c0=c0: e.matmul(psr[:, bi * 512:(bi + 1) * 512], lhsT=scT[:, kc, :], rhs=wt[:, c0:c0 + 512],
                                                        start=(kc == 0), stop=(kc == 7)), [wt, scT], [psr])
            abc = P.sb("abc", [128, 48]); nwc = P.sb("nwc", [128, 4, 8])
            ld(abc, abc[:], I["ada_b_col"][l]); ld(nwc, nwc[:], I["normw_col"][l])
            dve(lambda e: e.tensor_tensor(out=modc[:], in0=macc[:].rearrange("p (q s) -> p q s", s=2),
                                          in1=abc[:].unsqueeze(2).broadcast_to([128, 48, 2]), op=ALU.add), [macc, abc], [modc])
            dve(lambda e: e.scalar_tensor_tensor(out=Am[:], in0=modc[:, 8:16, :], scalar=1.0,
                                                 in1=nwc[:, 0, :].unsqueeze(2).broadcast_to([128, 8, 2]), op0=ALU.add, op1=ALU.mult),
                [modc, nwc], [Am])
            dve(lambda e: e.scalar_tensor_tensor(out=Af[:], in0=modc[:, 32:40, :], scalar=1.0,
                                                 in1=nwc[:, 2, :].unsqueeze(2).broadcast_to([128, 8, 2]), op0=ALU.add, op1=ALU.mult),
                [modc, nwc], [Af])
            abr = P.sb("abr", [2, 2048]); nwr = P.sb("nwr", [2, 2048]); gr = P.sb("gr", [2, 2048])
            ld(abr, abr[:, 0:1024], I["ada_b"][l, 2048:3072].partition_broadcast(2))
            ld(abr, abr[:, 1024:2048], I["ada_b"][l, 5120:6144].partition_broadcast(2))
            ld(nwr, nwr[:, 0:1024], I["norm_post_mix"][l].partition_broadcast(2))
            ld(nwr, nwr[:, 1024:2048], I["norm_post_ffn"][l].partition_broadcast(2))
            dve(lambda e: e.tensor_tensor(out=gr[:], in0=psr[:], in1=abr[:], op=ALU.add), [psr, abr], [gr])
            dve(lambda e: e.tensor_tensor(out=gr[:], in0=gr[:], in1=nwr[:], op=ALU.mult), [gr, nwr], [gr])
            stq(modrow[:, :], gr, gr[:], dB["modrow"])
            for sg in range(2):
                ld(gtb, gtb[:, sg, :, :].rearrange("p a d -> p (a d)"), modrow[sg].partition_broadcast(128), src=dB["modrow"])
            P.pop()

        def phase_norm(src_ap2d, src_buf, A, bq0):
            P.push()
            zt = P.sb("zt", [128, 8, 1], BF16)
            dve(lambda e: e.memset(zt[:], 0.0), [], [zt])
            hFv = hF.rearrange("(c p) t -> p c t", p=128)
            stq(hFv[:, :, 0:1], zt, zt[:], dB["hF"]); stq(hFv[:, :, T + 1:T + 2], zt, zt[:], dB["hF"])
            xts = [P.sb(f"xt{i}", [128, D]) for i in range(2)]
            sq = P.sb("sq", [128, D]); ss = P.sb("ss", [128, 1]); rs = P.sb("rs", [128, 1])
            xn = P.sb("xn", [128, D], BF16); ptr = P.ps("ptr", [128, 8, 128], BF16); tmp = P.sb("tmpT", [128, 8, 128])
            hts = [P.sb(f"ht{i}", [128, 8, 128], BF16) for i in range(2)]
            xns = [P.sb(f"xnN{i}", [128, D], BF16) for i in range(2)]

            def gen(it):
                xt, ht, xn = xts[it % 2], hts[it % 2], xns[it % 2]
                seg = 0 if it < NT // 2 else 1
                ld(xt, xt[:], src_ap2d[it * 128:(it + 1) * 128, :], src=src_buf)
                act(lambda e: e.activation(out=sq[:], in_=xt[:], func=AF.Square, accum_out=ss[:, 0:1]), [xt], [sq, ss])
                act(lambda e: e.activation(out=rs[:], in_=ss[:], func=AF.Sqrt, scale=1.0 / D, bias=epsb[:, 0:1]), [ss, epsb], [rs])
                yield
                dve(lambda e: e.reciprocal(out=rs[:], in_=rs[:]), [rs], [rs])
                dve(lambda e: e.tensor_scalar(out=xn[:], in0=xt[:], scalar1=rs[:, 0:1], scalar2=None, op0=ALU.mult), [xt, rs], [xn])
                yield 'prev_done'
                for c8 in range(8):
                    pe(lambda e, c8=c8: e.transpose(ptr[:, c8, :], xn[:, c8 * 128:(c8 + 1) * 128], identb[:]), [xn, identb], [ptr])
                yield
                dve(lambda e: e.tensor_tensor(out=tmp[:], in0=ptr[:], in1=A[:, :, seg:seg + 1].broadcast_to([128, 8, 128]), op=ALU.mult),
                    [ptr, A], [tmp])
                dve(lambda e: e.tensor_tensor(out=ht[:], in0=tmp[:], in1=modc[:, bq0:bq0 + 8, seg:seg + 1].broadcast_to([128, 8, 128]), op=ALU.add),
                    [tmp, modc], [ht])
                stq(hFv[:, :, 1 + it * 128:1 + (it + 1) * 128], ht, ht[:], dB["hF"])

            run_pipelined(gen, range(NT))
            P.pop()

        def load_w_bf16(dst, dst_ap_fn, src_ap_fn, nk, width, stg):
            for k in range(nk):
                st = stg[k % 2]
                ld(st, st[:, 0:width], src_ap_fn(k))
                if k % 2 == 0:
                    dve(lambda e, k=k, st=st: e.tensor_copy(out=dst_ap_fn(k), in_=st[:, 0:width]), [st], [dst])
                else:
                    act(lambda e, k=k, st=st: e.activation(out=dst_ap_fn(k), in_=st[:, 0:width], func=AF.Copy), [st], [dst])

        def load_window(hw, b):
            hFv = hF.rearrange("(c p) t -> p c t", p=128)
            ld(hw, hw[:], hFv[:, :, b * 512:b * 512 + 514], src=dB["hF"])
            for side, col in ((0, 0), (1, 513)):
                dve(lambda e, side=side, col=col: e.tensor_tensor(out=hw[:, :, col:col + 1], in0=hw[:, :, col:col + 1],
                                                                  in1=hal[:, 2 * b + side:2 * b + side + 1].unsqueeze(1).broadcast_to([128, 8, 1]),
                                                                  op=ALU.mult), [hw, hal], [hw])

        def conv3_fm(out_ap, pm, ph, cw, cb, ci, rbufs, wbuf):
            act(lambda e: e.activation(out=out_ap, in_=pm[:, 0:512], func=AF.Identity, scale=cw[:, ci, 1:2], bias=cb[:, ci:ci + 1]), rbufs, [wbuf])
            dve(lambda e: e.scalar_tensor_tensor(out=out_ap[:, 1:512], in0=pm[:, 0:511], scalar=cw[:, ci, 0:1], in1=out_ap[:, 1:512],
                                                 op0=ALU.mult, op1=ALU.add), rbufs + [wbuf], [wbuf])
            dve(lambda e: e.scalar_tensor_tensor(out=out_ap[:, 0:511], in0=pm[:, 1:512], scalar=cw[:, ci, 2:3], in1=out_ap[:, 0:511],
                                                 op0=ALU.mult, op1=ALU.add), rbufs + [wbuf], [wbuf])
            dve(lambda e: e.scalar_tensor_tensor(out=out_ap[:, 0:1], in0=ph[:, 0:1], scalar=cw[:, ci, 0:1], in1=out_ap[:, 0:1],
                                                 op0=ALU.mult, op1=ALU.add), rbufs + [wbuf], [wbuf])
            dve(lambda e: e.scalar_tensor_tensor(out=out_ap[:, 511:512], in0=ph[:, 1:2], scalar=cw[:, ci, 2:3], in1=out_ap[:, 511:512],
                                                 op0=ALU.mult, op1=ALU.add), rbufs + [wbuf], [wbuf])

        def phase_A(l):
            P.push()
            wA = P.sb("wA", [128, 8, OFF_GATES], BF16)
            stg = [P.sb(f"stgA{i}", [128, OFF_GATES]) for i in range(2)]
            load_w_bf16(wA, lambda k: wA[:, k, :], lambda k: I["w_in"][l, k * 128:(k + 1) * 128, 0:OFF_GATES], 8, OFF_GATES, stg)
            bdcs = P.sb("bdcs", [128, 2, 128], BF16); ld(bdcs, bdcs[:], I["bdcs"][:, :, :])
            cw = P.sb("hcw", [128, 6, 3]); cb = P.sb("hcb", [128, 6])
            ld(cw, cw[:], I["hy_cw"][l]); ld(cb, cb[:], I["hy_cb"][l])
            hws = [P.sb(f"hw{i}", [128, 8, 514], BF16) for i in range(2)]
            pms = [P.ps(f"pmA{i}", [128, 512]) for i in range(2)]
            ph = P.ps("phA", [128, 2])
            pj = P.sb("pj", [128, NTM]); uT = P.sb("uT", [128, 2, 512], BF16)
            zts = [P.sb(f"ztA{i}", [128, 512], BF16) for i in range(2)]
            cvs = [P.sb(f"cvA{i}", [128, 512]) for i in range(2)]
            tmcols = ((256, 512), (768, 512), (2048, 512), (2560, 288))
            zFv = zF
            npm = 0
            for b in range(NB):
                hw = hws[b % 2]
                load_window(hw, b)
                t0 = b * 512
                for s in range(4):
                    o = 0
                    for (c0, wd) in tmcols:
                        pm = pms[npm % 2]; npm += 1
                        for kc in range(8):
                            pe(lambda e, kc=kc, pm=pm, c0=c0, wd=wd: e.matmul(pm[:, 0:wd], lhsT=hw[:, kc, 1 + s * 128:1 + (s + 1) * 128],
                                                                                rhs=wA[:, kc, c0:c0 + wd], start=(kc == 0), stop=(kc == 7)),
                               [hw, wA], [pm])
                        if (npm % 2) == 0:
                            dve(lambda e, pm=pm, o=o, wd=wd: e.tensor_copy(out=pj[:, o:o + wd], in_=pm[:, 0:wd]), [pm], [pj])
                        else:
                            act(lambda e, pm=pm, o=o, wd=wd: e.activation(out=pj[:, o:o + wd], in_=pm[:, 0:wd], func=AF.Copy), [pm], [pj])
                        o += wd
                        yield
                    stq(projT[t0 + s * 128:t0 + (s + 1) * 128, :], pj, pj[:], dB["projT"])
                for ch in range(2):
                    pm = pms[npm % 2]; npm += 1
                    for kc in range(8):
                        pe(lambda e, kc=kc, pm=pm, ch=ch: e.matmul(pm[:], lhsT=wA[:, kc, ch * 128:(ch + 1) * 128], rhs=hw[:, kc, 1:513],
                                                                     start=(kc == 0), stop=(kc == 7)), [hw, wA], [pm])
                    act(lambda e, pm=pm, ch=ch: e.activation(out=uT[:, ch, :], in_=pm[:], func=AF.Copy), [pm], [uT])
                    yield
                for ri in range(2):
                    for ch in range(2):
                        pm = pms[npm % 2]; zt = zts[npm % 2]; npm += 1
                        pe(lambda e, pm=pm, ri=ri, ch=ch: e.matmul(pm[:], lhsT=bdcs[:, ri, :], rhs=uT[:, ch, :], start=True, stop=True), [bdcs, uT], [pm])
                        act(lambda e, pm=pm, zt=zt: e.activation(out=zt[:], in_=pm[:], func=AF.Copy), [pm], [zt])
                        stq(zFv[ri, ch * 128:(ch + 1) * 128, t0:t0 + 512], zt, zt[:], dB["zF"])
                        yield
                for ch in range(6):
                    pm = pms[npm % 2]; cv = cvs[npm % 2]; npm += 1
                    c0 = OFF_HY + ch * 128
                    for kc in range(8):
                        pe(lambda e, kc=kc, pm=pm, c0=c0: e.matmul(pm[:], lhsT=wA[:, kc, c0:c0 + 128], rhs=hw[:, kc, 1:513],
                                                                     start=(kc == 0), stop=(kc == 7)), [hw, wA], [pm])
                    for kc in range(8):
                        pe(lambda e, kc=kc, c0=c0: e.matmul(ph[:], lhsT=wA[:, kc, c0:c0 + 128], rhs=hw[:, kc, 0:514:513],
                                                              start=(kc == 0), stop=(kc == 7)), [hw, wA], [ph])
                    conv3_fm(cv[:], pm, ph, cw, cb, ch, [pm, ph, cw, cb], cv)
                    stq(uhF[ch * 128:(ch + 1) * 128, t0:t0 + 512], cv, cv[:], dB["uhF"])
                    yield
            yield 'finished'
            P.pop()

        def rms_residual(py, xt, sq, ss, rs, tmpo, outt, seg, which, dst_ap, dst_buf):
            act(lambda e: e.activation(out=sq[:], in_=py[:], func=AF.Square, accum_out=ss[:, 0:1]), [py], [sq, ss])
            act(lambda e: e.activation(out=rs[:], in_=ss[:], func=AF.Sqrt, scale=1.0 / D, bias=epsb[:, 0:1]), [ss, epsb], [rs])
            dve(lambda e: e.reciprocal(out=rs[:], in_=rs[:]), [rs], [rs])
            dve(lambda e: e.scalar_tensor_tensor(out=tmpo[:], in0=py[:], scalar=rs[:, 0:1], in1=gtb[:, seg, which, :], op0=ALU.mult, op1=ALU.mult),
                [py, rs, gtb], [tmpo])
            dve(lambda e: e.tensor_tensor(out=outt[:], in0=tmpo[:], in1=xt[:], op=ALU.add), [tmpo, xt], [outt])
            stq(dst_ap, outt, outt[:], dst_buf)

        def phase_C(l, src_ap2d, src_buf):
            P.push()
            wbr = P.sb("wbr", [128, 8, D], BF16); wout = P.sb("wout", [128, 8, D], BF16); wg = P.sb("wg", [128, 8, 4 * D], BF16)
            stg = [P.sb(f"stgC{i}", [128, 4 * D]) for i in range(2)]
            load_w_bf16(wbr, lambda k: wbr[:, k, :], lambda k: I["w_branch"][l, k * 128:(k + 1) * 128, :], 8, D, stg)
            load_w_bf16(wout, lambda k: wout[:, k, :], lambda k: I["w_out"][l, k * 128:(k + 1) * 128, :], 8, D, stg)
            load_w_bf16(wg, lambda k: wg[:, k, :], lambda k: I["w_in"][l, k * 128:(k + 1) * 128, OFF_GATES:OFF_GATES + 4 * D], 8, 4 * D, stg)
            xts = [P.sb(f"xtC{i}", [128, D]) for i in range(2)]
            hts = [P.sb(f"htC{i}", [128, 8, 128], BF16) for i in range(2)]
            brs = [P.sb(f"brC{i}", [128, 8, 128], BF16) for i in range(2)]
            pgs = [P.ps(f"pg{i}", [128, 512]) for i in range(2)]; pbs = [P.ps(f"pb{i}", [128, 512]) for i in range(2)]
            py = P.ps("py", [128, D]); ptr = P.ps("ptrC", [128, 8, 128], BF16)
            sigs = [P.sb(f"sig{i}", [128, 512]) for i in range(2)]; tms = [P.sb(f"tmc{i}", [128, 512]) for i in range(2)]
            merged = P.sb("merged", [128, D]); tmpm = P.sb("tmpm", [128, D]); mb = P.sb("mb", [128, D], BF16)
            mT = P.sb("mT", [128, 8, 128], BF16); sq = P.sb("sqC", [128, D]); ss = P.sb("ssC", [128, 1]); rs = P.sb("rsC", [128, 1])
            outt = P.sb("outC", [128, D])
            hFv = hF.rearrange("(c p) t -> p c t", p=128); brv = brF.rearrange("(k p) t -> p k t", p=128)
            for it in range(NT):
                xt, ht, brt = xts[it % 2], hts[it % 2], brs[it % 2]
                seg = 0 if it < NT // 2 else 1
                ld(xt, xt[:], src_ap2d[it * 128:(it + 1) * 128, :], src=src_buf)
                ld(ht, ht[:], hFv[:, :, 1 + it * 128:1 + (it + 1) * 128], src=dB["hF"])
                ld(brt, brt[:], brv[:, :, it * 128:(it + 1) * 128], src=dB["brF"])
                for br in range(4):
                    for cb in range(2):
                        u = br * 2 + cb
                        pgu, pbu, sgu, tmu = pgs[u % 2], pbs[u % 2], sigs[u % 2], tms[u % 2]
                        for kc in range(8):
                            pe(lambda e, kc=kc, cb=cb, pgu=pgu, br=br: e.matmul(pgu[:], lhsT=ht[:, kc, :],
                                                                               rhs=wg[:, kc, br * D + cb * 512:br * D + (cb + 1) * 512], start=(kc == 0), stop=(kc == 7)),
                               [ht, wg], [pgu])
                        for k2 in range(2):
                            pe(lambda e, k2=k2, cb=cb, pbu=pbu, br=br: e.matmul(pbu[:], lhsT=brt[:, br * 2 + k2, :],
                                                                               rhs=wbr[:, br * 2 + k2, cb * 512:(cb + 1) * 512], start=(k2 == 0), stop=(k2 == 1)),
                               [brt, wbr], [pbu])
                        act(lambda e, pgu=pgu, sgu=sgu: e.activation(out=sgu[:], in_=pgu[:], func=AF.Sigmoid), [pgu], [sgu])
                        mslice = merged[:, cb * 512:(cb + 1) * 512]
                        if br == 0:
                            dve(lambda e, sgu=sgu, pbu=pbu, mslice=mslice: e.tensor_tensor(out=mslice, in0=sgu[:], in1=pbu[:], op=ALU.mult), [sgu, pbu], [merged])
                        else:
                            dve(lambda e, sgu=sgu, pbu=pbu, tmu=tmu: e.tensor_tensor(out=tmu[:], in0=sgu[:], in1=pbu[:], op=ALU.mult), [sgu, pbu], [tmu])
                            dve(lambda e, tmu=tmu, mslice=mslice: e.tensor_tensor(out=mslice, in0=mslice, in1=tmu[:], op=ALU.add), [merged, tmu], [merged])
                act(lambda e: e.activation(out=mb[:], in_=merged[:], func=AF.Copy), [merged], [mb])
                for c8 in range(8):
                    pe(lambda e, c8=c8: e.transpose(ptr[:, c8, :], mb[:, c8 * 128:(c8 + 1) * 128], identb[:]), [mb, identb], [ptr])
                dve(lambda e: e.tensor_copy(out=mT[:], in_=ptr[:]), [ptr], [mT])
                for cb in range(2):
                    for kc in range(8):
                        pe(lambda e, kc=kc, cb=cb: e.matmul(py[:, cb * 512:(cb + 1) * 512], lhsT=mT[:, kc, :], rhs=wout[:, kc, cb * 512:(cb + 1) * 512],
                                                           start=(kc == 0), stop=(kc == 7)), [mT, wout], [py])
                rms_residual(py, xt, sq, ss, rs, tmpm, outt, seg, 0, xmid[it * 128:(it + 1) * 128, :], dB["xmid"])
            P.pop()

        def phase_D(l, dst_ap2d, dst_buf):
            P.push()
            wup = P.sb("wup", [128, 8, 2 * DFF], BF16)
            stg = [P.sb(f"stgD{i}", [128, 1408]) for i in range(2)]
            for part in range(4):
                c0 = part * 1408
                load_w_bf16(wup, lambda k, c0=c0: wup[:, k, c0:c0 + 1408], lambda k, c0=c0: I["ffn_up"][l, k * 128:(k + 1) * 128, c0:c0 + 1408], 8, 1408, stg)
            cw = P.sb("fcw", [128, 44, 3]); cb_ = P.sb("fcb", [128, 44])
            ld(cw, cw[:], I["ffn_cw"][l]); ld(cb_, cb_[:], I["ffn_cb"][l])
            hws = [P.sb(f"hwD{i}", [128, 8, 514], BF16) for i in range(2)]
            pms = [P.ps(f"pmD{i}", [128, 512]) for i in range(4)]; phs = [P.ps(f"phD{i}", [128, 2]) for i in range(4)]
            cvs = [P.sb(f"cvD{i}", [128, 512]) for i in range(4)]; gls = [P.sb(f"glD{i}", [128, 512]) for i in range(2)]
            gts = [P.sb(f"gtD{i}", [128, 512], BF16) for i in range(2)]
            for b in range(NB):
                hw = hws[b % 2]
                load_window(hw, b)
                for pc in range(22):
                    for wi in range(2):
                        ci = pc + 22 * wi
                        bi = 2 * (pc % 2) + wi
                        pm, ph, cv = pms[bi], phs[bi], cvs[bi]
                        for kc in range(8):
                            pe(lambda e, kc=kc, pm=pm, ci=ci: e.matmul(pm[:], lhsT=wup[:, kc, ci * 128:(ci + 1) * 128], rhs=hw[:, kc, 1:513],
                                                                         start=(kc == 0), stop=(kc == 7)), [hw, wup], [pm])
                        for kc in range(8):
                            pe(lambda e, kc=kc, ph=ph, ci=ci: e.matmul(ph[:], lhsT=wup[:, kc, ci * 128:(ci + 1) * 128], rhs=hw[:, kc, 0:514:513],
                                                                         start=(kc == 0), stop=(kc == 7)), [hw, wup], [ph])
                        conv3_fm(cv[:], pm, ph, cw, cb_, ci, [pm, ph, cw, cb_], cv)
                    gt = gts[pc % 2]; gl = gls[pc % 2]; cg_, cv_ = cvs[2 * (pc % 2)], cvs[2 * (pc % 2) + 1]
                    act(lambda e, gl=gl, cg_=cg_: e.activation(out=gl[:], in_=cg_[:], func=AF.Gelu_apprx_tanh), [cg_], [gl])
                    dve(lambda e, gt=gt, gl=gl, cv_=cv_: e.tensor_tensor(out=gt[:], in0=gl[:], in1=cv_[:], op=ALU.mult), [gl, cv_], [gt])
                    stq(gFF[pc * 128:(pc + 1) * 128, b * 512:(b + 1) * 512], gt, gt[:], dB["gFF"])
            P.pop()
            P.push()
            wdn = P.sb("wdn", [128, 22, D], BF16)
            stg = [P.sb(f"stgE{i}", [128, D]) for i in range(2)]
            load_w_bf16(wdn, lambda k: wdn[:, k, :], lambda k: I["ffn_down"][l, k * 128:(k + 1) * 128, :], 22, D, stg)
            xts = [P.sb(f"xtE{i}", [128, D]) for i in range(2)]
            ggs = [P.sb(f"ggE{i}", [128, 22, 128], BF16) for i in range(2)]
            py = P.ps("pyE", [128, D]); sq = P.sb("sqE", [128, D]); ss = P.sb("ssE", [128, 1]); rs = P.sb("rsE", [128, 1])
            tmpo = P.sb("tmpE", [128, D]); outt = P.sb("outE", [128, D])
            gv = gFF.rearrange("(k p) t -> p k t", p=128)
            for it in range(NT):
                xt, gg = xts[it % 2], ggs[it % 2]
                seg = 0 if it < NT // 2 else 1
                ld(xt, xt[:], xmid[it * 128:(it + 1) * 128, :], src=dB["xmid"])
                ld(gg, gg[:], gv[:, :, it * 128:(it + 1) * 128], src=dB["gFF"])
                for cb in range(2):
                    for pc in range(22):
                        pe(lambda e, pc=pc, cb=cb: e.matmul(py[:, cb * 512:(cb + 1) * 512], lhsT=gg[:, pc, :], rhs=wdn[:, pc, cb * 512:(cb + 1) * 512],
                                                           start=(pc == 0), stop=(pc == 21)), [gg, wdn], [py])
                rms_residual(py, xt, sq, ss, rs, tmpo, outt, seg, 1, dst_ap2d[it * 128:(it + 1) * 128, :], dst_buf)
            P.pop()

        def phase_fnet(l):
            P.push()
            Cg = 64
            m1 = P.sb("fm1", [NT, 2, 2 * NS], BF16); ld(m1, m1[:], I["fn_m1"][:, :, :])
            zin = P.sb("zin", [NT, 2, Cg, 128], BF16); A = P.sb("fA", [128, NS, 2, Cg], BF16)
            osb = P.sb("osb", [Cg, T], BF16); osbv = osb[:].rearrange("c (k j) -> c k j", j=NT)
            ps1s = [P.ps(f"fps1{i}", [128, 2, 2 * NS]) for i in range(2)]
            ps2s = [P.ps(f"fps2{i}", [Cg, 4, 128]) for i in range(2)]
            m2s = [P.sb(f"fm2{i}", [128, 4, 2, 2, 128], BF16) for i in range(2)]
            n1 = 0
            for g in range(256 // Cg):
                c0 = g * Cg
                for ri in range(2):
                    ld(zin, zin[:, ri, :, :], zF[ri, c0:c0 + Cg, :].rearrange("c (g n) -> g c n", n=128), src=dB["zF"])
                for c in range(0, Cg, 2):
                    ps1 = ps1s[n1 % 2]; n1 += 1
                    for cc in range(2):
                        pe(lambda e, cc=cc, ps1=ps1: e.matmul(ps1[:, cc, :], lhsT=zin[:, 0, c + cc, :], rhs=m1[:, 0, :], start=True, stop=False), [zin, m1], [ps1])
                        pe(lambda e, cc=cc, ps1=ps1: e.matmul(ps1[:, cc, :], lhsT=zin[:, 1, c + cc, :], rhs=m1[:, 1, :], start=False, stop=True), [zin, m1], [ps1])
                    for ri in range(2):
                        src_ap = ps1[:, :, ri * NS:(ri + 1) * NS].rearrange("p c s -> p s c")
                        if ri == 0:
                            dve(lambda e, src_ap=src_ap: e.tensor_copy(out=A[:, :, 0, c:c + 2], in_=src_ap), [ps1], [A])
                        else:
                            act(lambda e, src_ap=src_ap: e.activation(out=A[:, :, 1, c:c + 2], in_=src_ap, func=AF.Copy), [ps1], [A])
                m2v = I["fn_m2"].rearrange("j p s r k -> p j s r k")
                for j in range(NT):
                    m2t = m2s[(j // 4) % 2]
                    if j % 4 == 0:
                        ld(m2t, m2t[:], m2v[:, j:j + 4, :, :, :])
                    ps2 = ps2s[(j // 4) % 2]
                    k = 0
                    for s_ in range(2):
                        for ri in range(2):
                            pe(lambda e, s_=s_, ri=ri, k=k, ps2=ps2, m2t=m2t: e.matmul(ps2[:, j % 4, :], lhsT=A[:, s_ * NT + j, ri, :], rhs=m2t[:, j % 4, s_, ri, :],
                                                                                      start=(k == 0), stop=(k == 3)), [A, m2t], [ps2])
                            k += 1
                    if j % 4 == 3:
                        j0 = j - 3
                        src_ap = ps2[:, :, :].rearrange("c j k -> c k j")
                        if (j // 4) % 2 == 0:
                            dve(lambda e, src_ap=src_ap, j0=j0: e.tensor_copy(out=osbv[:, :, j0:j0 + 4], in_=src_ap), [ps2], [osb])
                        else:
                            act(lambda e, src_ap=src_ap, j0=j0: e.activation(out=osbv[:, :, j0:j0 + 4], in_=src_ap, func=AF.Copy), [ps2], [osb])
                stq(brF[c0:c0 + Cg, :], osb, osb[:], dB["brF"])
            P.pop()

        def drain(g):
            for _ in g:
                pass

        def run_concurrent(primary, secondary, ratio=int(os.environ.get("CONC_RATIO", "1"))):
            p_alive, s_alive, p_fin = True, True, False
            while p_alive or s_alive:
                for _ in range(PRIM_STEPS):
                    if p_alive and not (p_fin and s_alive):
                        try:
                            if next(primary) == 'finished':
                                p_fin = True
                        except StopIteration:
                            p_alive = False
                for _ in range(ratio):
                    if s_alive:
                        try:
                            next(secondary)
                        except StopIteration:
                            s_alive = False

        def run_pipelined(make_gen, order, depth=PIPE_DEPTH):
            active = []
            order = list(order)
            pos = 0
            while pos < len(order) or active:
                if pos < len(order) and len(active) < depth and all(e[1] == 'second' for e in active):
                    active.append([make_gen(order[pos]), 'first']); pos += 1
                for ent in list(active):
                    if ent[1] == 'waiting':
                        if active[0] is ent:
                            ent[1] = 'second'
                        else:
                            continue
                    try:
                        v = next(ent[0])
                        if v == 'prev_done' and ent[1] == 'first':
                            ent[1] = 'second' if active[0] is ent else 'waiting'
                    except StopIteration:
                        active.remove(ent)

        def phase_scan(l, ret):
            P.push()
            nkc, hpc = (2, 2) if ret else (1, 4)
            KW = nkc * 128
            col0, width = (0, 1024) if ret else (1024, 800)
            qo, ko, vo, go = (0, 256, 512, 768) if ret else (0, 128, 256, 512)
            lro = 768
            kbr = 1 if ret else 3
            tri = P.sb("tri", [128, 6, 128]); ld(tri, tri[:], I["tri"][:, :, :])
            mh = P.sb("mh", [128, hpc]); ld(mh, mh[:], I["mh_ret" if ret else "mh_gla"][:, :])
            bd = P.sb("bd", [128, nkc, 256]); ld(bd, bd[:], I["bd_ret" if ret else "bd_gla"][:, :, :])
            gn = P.sb("gn", [128, 256]); ld(gn, gn[:], I["ret_gn" if ret else "gla_gn"][l].partition_broadcast(128))
            lns = math.log(32.0 ** -0.5)
            if ret:
                Ec = P.sb("E", [128, nkc, 6, 128]); Epc = P.sb("Epad", [128, nkc, 2, hpc, 128])
                dtokc = P.sb("dtok", [128, 2, KW]); decc = P.sb("dec", [128, nkc, 2])
                ld(Ec, Ec[:], I["ret_e"][:, :, :, :]); ld(dtokc, dtokc[:], I["ret_tok"][:, :, :]); ld(decc, decc[:], I["ret_dec"][:, :, :])
                for c in range(nkc):
                    for d_ in range(2):
                        dve(lambda e, c=c, d_=d_: e.tensor_tensor(out=Epc[:, c, d_, :, :], in0=Ec[:, c, 1 + 2 * d_, :].unsqueeze(1).broadcast_to([128, hpc, 128]),
                                                                  in1=mh[:].unsqueeze(2).broadcast_to([128, hpc, 128]), op=ALU.mult), [Ec, mh], [Epc])
            else:
                wd = P.sb("wd", [33, 256]); ld(wd, wd[:], I["gla_wd"][l])
                lnsb = P.sb("lnsb", [128, 1])
                dve(lambda e: e.memset(lnsb[:], lns), [], [lnsb])
                psZ = P.ps("psZ", [128, 512]); psB = P.ps("psB", [128, 4, 128])

            class TS:
                pass

            def mk_set(i):
                S = TS()
                S.pt = P.sb(f"pt{i}", [128, width])
                S.Qt = P.sb(f"Qt{i}", [128, nkc, 4, 128], BF16); S.Kp = P.sb(f"Kp{i}", [128, nkc, 2, hpc, 128], BF16)
                S.khat = P.sb(f"khat{i}", [128, 2, KW], BF16); S.Vb = P.sb(f"Vb{i}", [128, 256], BF16)
                S.st1 = P.sb(f"st1{i}", [128, 4, 128]); S.st2 = P.sb(f"st2{i}", [128, 4, 128]); S.PT = P.sb(f"PT{i}", [128, 4, 128], BF16)
                S.hn1 = P.sb(f"hn1{i}", [128, 4]); S.hn2 = P.sb(f"hn2{i}", [128, 4]); S.oc = P.sb(f"oc{i}", [128, 4, 64]); S.osq = P.sb(f"osq{i}", [128, 4, 64])
                S.sg = P.sb(f"sg{i}", [128, 256]); S.resb = P.sb(f"resb{i}", [128, 256], BF16); S.resT = P.sb(f"resT{i}", [128, 2, 128], BF16)
                S.tU = P.sb(f"tU{i}", [128, nkc, 256])
                if ret:
                    S.rot = P.sb(f"rot{i}", [128, 2, 32]); S.qkr = P.sb(f"qkr{i}", [128, 8, 64])
                    S.rt1 = P.sb(f"rt1{i}", [128, 8, 32]); S.rt2 = P.sb(f"rt2{i}", [128, 8, 32])
                    S.E, S.Epad, S.dtok, S.dec = Ec, Epc, dtokc, decc
                else:
                    S.lrT = P.sb(f"lrT{i}", [33, 128]); dve(lambda e: e.memset(S.lrT[:], 1.0), [], [S.lrT])
                    S.et = P.sb(f"et{i}", [128, 256]); S.lt = P.sb(f"lt{i}", [128, 256]); S.bsb = P.sb(f"bsb{i}", [128, 4, 128])
                    S.mids = P.sb(f"mids{i}", [128, 4])
                    S.E = P.sb(f"E{i}", [128, nkc, 6, 128]); S.Epad = P.sb(f"Epad{i}", [128, nkc, 2, hpc, 128])
                    S.dtok = P.sb(f"dtok{i}", [128, 2, KW]); S.dec = P.sb(f"dec{i}", [128, nkc, 2])
                return S

            sets = [mk_set(0), mk_set(1)]
            psT = P.ps("psT", [128, 2 * nkc, 128])
            psS = [P.ps(f"psS{i}", [128, 4, 128]) for i in range(2)]
            psO = P.ps("psO", [128, 256]); psU = P.ps("psU", [128, nkc, 256]); psR = P.ps("psR", [128, 2, 128], BF16)
            Sm = P.sb("Sm", [128, nkc, 256]); Sbf = P.sb("Sbf", [128, nkc, 256], BF16); Sball = P.sb("Sball", [128, NT, nkc, 256], BF16)
            brv = brF.rearrange("(k p) t -> p k t", p=128)

            def prep(n, S, full=True):
                pt = S.pt
                ld(pt, pt[:], projT[n * 128:(n + 1) * 128, col0:col0 + width], src=dB["projT"])
                if ret:
                    rot, qkr, rt1, rt2 = S.rot, S.qkr, S.rt1, S.rt2
                    ld(rot, rot[:], I["rot"][n * 128:(n + 1) * 128, :, :])
                    h0 = 0 if full else 4
                    nh_ = 8 - h0
                    src = pt[:, 0:512].rearrange("p (h d) -> p h d", d=64)[:, h0:8, :]
                    cosb = rot[:, 0, :].unsqueeze(1).broadcast_to([128, nh_, 32]); sinb = rot[:, 1, :].unsqueeze(1).broadcast_to([128, nh_, 32])
                    qkr_full = qkr
                    qkr = qkr[:, h0:8, :]; rt1 = rt1[:, h0:8, :]; rt2 = rt2[:, h0:8, :]
                    gps(lambda e: e.tensor_tensor(out=rt1, in0=src[:, :, 0:32], in1=cosb, op=ALU.mult), [pt, rot], [S.rt1])
                    gps(lambda e: e.tensor_tensor(out=rt2, in0=src[:, :, 32:64], in1=sinb, op=ALU.mult), [pt, rot], [S.rt2])
                    gps(lambda e: e.tensor_tensor(out=qkr[:, :, 0:32], in0=rt1, in1=rt2, op=ALU.subtract), [S.rt1, S.rt2], [S.qkr])
                    gps(lambda e: e.tensor_tensor(out=rt1, in0=src[:, :, 0:32], in1=sinb, op=ALU.mult), [pt, rot, S.qkr], [S.rt1])
                    gps(lambda e: e.tensor_tensor(out=rt2, in0=src[:, :, 32:64], in1=cosb, op=ALU.mult), [pt, rot, S.qkr], [S.rt2])
                    gps(lambda e: e.tensor_tensor(out=qkr[:, :, 32:64], in0=rt1, in1=rt2, op=ALU.add), [S.rt1, S.rt2], [S.qkr])
                    qk = qkr_full[:].rearrange("p h d -> p (h d)")
                    S.q_tok, S.k_tok, S.qkb = qk[:, 0:256], qk[:, 256:512], qkr_full
                else:
                    lrT, et, lt, bsb, mids, E, Epad, dtok, dec = S.lrT, S.et, S.lt, S.bsb, S.mids, S.E, S.Epad, S.dtok, S.dec
                    S.q_tok, S.k_tok, S.qkb = pt[:, qo:qo + 128], pt[:, ko:ko + 128], pt
                    pe(lambda e: e.transpose(psZ[0:32, 256:384], pt[:, lro:lro + 32], identf[:]), [pt, identf], [psZ])
                    yield
                    dve(lambda e: e.tensor_copy(out=lrT[0:32, :], in_=psZ[0:32, 256:384]), [psZ], [lrT])
                    yield
                    pe(lambda e: e.matmul(psZ[:, 0:256], lhsT=lrT[:], rhs=wd[:], start=True, stop=True), [lrT, wd], [psZ])
                    yield
                    act(lambda e: e.activation(out=et[:], in_=psZ[:, 0:256], func=AF.Exp, scale=-1.0), [psZ], [et])
                    act(lambda e: e.activation(out=lt[:], in_=et[:], func=AF.Ln, bias=1.0), [et], [lt])
                    yield
                    pe(lambda e: e.matmul(psB[:, 0, :], lhsT=lt[:, 0:128], rhs=tri[:, 2, :], start=True, stop=True), [lt, tri], [psB])
                    pe(lambda e: e.matmul(psB[:, 1, :], lhsT=lt[:, 128:256], rhs=tri[:, 3, :], start=True, stop=True), [lt, tri], [psB])
                    pe(lambda e: e.matmul(psB[:, 2, :], lhsT=tri[:, 4, :], rhs=lt[:, 0:128], start=True, stop=True), [lt, tri], [psB])
                    pe(lambda e: e.matmul(psB[:, 3, :], lhsT=tri[:, 5, :], rhs=lt[:, 128:256], start=True, stop=True), [lt, tri], [psB])
                    yield
                    dve(lambda e: e.tensor_copy(out=bsb[:], in_=psB[:]), [psB], [bsb])
                    dve(lambda e: e.tensor_scalar(out=mids[:, 0:2], in0=bsb[:, 0:2, 64], scalar1=-1.0, scalar2=lns, op0=ALU.mult, op1=ALU.add), [bsb], [mids])
                    dve(lambda e: e.tensor_copy(out=mids[:, 2:4], in_=bsb[:, 0:2, 64]), [bsb], [mids])
                    yield
                    for d_ in (range(2) if full else (1,)):
                        if full:
                            act(lambda e, d_=d_: e.activation(out=E[:, 0, 2 * d_, :], in_=bsb[:, d_, :], func=AF.Exp, bias=mids[:, d_:d_ + 1]), [bsb, mids], [E])
                            act(lambda e, d_=d_: e.activation(out=E[:, 0, 2 * d_ + 1, :], in_=bsb[:, d_, :], func=AF.Exp, scale=-1.0, bias=mids[:, 2 + d_:3 + d_]), [bsb, mids], [E])
                            act(lambda e, d_=d_: e.activation(out=E[:, 0, 4 + d_, :], in_=bsb[:, d_, :], func=AF.Exp, bias=lnsb[:, 0:1]), [bsb, lnsb], [E])
                        act(lambda e, d_=d_: e.activation(out=dtok[:, d_, :], in_=bsb[:, 2 + d_, :], func=AF.Exp), [bsb], [dtok])
                    if full:
                        act(lambda e: e.activation(out=dec[:, 0, 0:1], in_=bsb[:, 0, 127:128], func=AF.Exp), [bsb], [dec])
                    act(lambda e: e.activation(out=dec[:, 0, 1:2], in_=bsb[:, 1, 0:1], func=AF.Exp), [bsb], [dec])
                    yield
                    if full:
                        for d_ in range(2):
                            dve(lambda e, d_=d_: e.tensor_tensor(out=Epad[:, 0, d_, :, :], in0=E[:, 0, 1 + 2 * d_, :].unsqueeze(1).broadcast_to([128, hpc, 128]),
                                                                 in1=mh[:].unsqueeze(2).broadcast_to([128, hpc, 128]), op=ALU.mult), [E, mh], [Epad])
                act(lambda e: e.activation(out=S.Vb[:], in_=pt[:, vo:vo + 256], func=AF.Copy), [pt], [S.Vb])
                for d_ in (range(2) if full else (1,)):
                    gps(lambda e, d_=d_: e.tensor_tensor(out=S.khat[:, d_, :], in0=S.k_tok, in1=S.dtok[:, d_, :], op=ALU.mult), [S.qkb, S.dtok], [S.khat])
                yield

            def state_update(d_, S):
                for c in range(nkc):
                    pe(lambda e, c=c: e.matmul(psU[:, c, :], lhsT=S.khat[:, d_, c * 128:(c + 1) * 128], rhs=S.Vb[:], start=True, stop=True), [S.khat, S.Vb], [psU])
                yield
                dve(lambda e: e.tensor_tensor(out=S.tU[:], in0=psU[:], in1=bd[:], op=ALU.mult), [psU, bd], [S.tU])
                for c in range(nkc):
                    dve(lambda e, c=c: e.scalar_tensor_tensor(out=Sm[:, c, :], in0=Sm[:, c, :], scalar=S.dec[:, c, d_:d_ + 1], in1=S.tU[:, c, :],
                                                              op0=ALU.mult, op1=ALU.add), [Sm, S.dec, S.tU], [Sm])

            def keep_mul():
                dve(lambda e: e.tensor_scalar(out=Sm[:], in0=Sm[:], scalar1=scal[:, 0:1], scalar2=None, op0=ALU.mult), [Sm, scal], [Sm])

            def gen1(n):
                S = sets[n % 2]
                yield from prep(n, S, full=False)
                yield 'prev_done'
                act(lambda e: e.activation(out=Sball[:, n, :, :], in_=Sm[:], func=AF.Copy), [Sm], [Sball])
                yield from state_update(1, S)
                if n == NT // 2:
                    keep_mul()

            dve(lambda e: e.memset(Sm[:], 0.0), [], [Sm])
            run_pipelined(gen1, reversed(range(NT)), depth=int(os.environ.get('PIPE1', '2')))

            def gen2(n):
                S = sets[n % 2]
                if PD_POS == 0:
                    yield 'prev_done'
                yield from prep(n, S)
                if PD_POS == 1:
                    yield 'prev_done'
                Qt, Kp, PT, Vb, E, Epad = S.Qt, S.Kp, S.PT, S.Vb, S.E, S.Epad
                for c in range(nkc):
                    pe(lambda e, c=c: e.transpose(psT[:, c, :], S.q_tok[:, c * 128:(c + 1) * 128], identf[:]), [S.qkb, identf], [psT])
                    pe(lambda e, c=c: e.transpose(psT[:, nkc + c, :], S.k_tok[:, c * 128:(c + 1) * 128], identf[:]), [S.qkb, identf], [psT])
                yield
                if PD_POS == 2:
                    yield 'prev_done'
                for c in range(nkc):
                    for vi, ei in enumerate((0, 2, 4, 5)):
                        dve(lambda e, c=c, vi=vi, ei=ei: e.tensor_tensor(out=Qt[:, c, vi, :], in0=psT[:, c, :], in1=E[:, c, ei, :], op=ALU.mult), [psT, E], [Qt])
                    for d_ in range(2):
                        dve(lambda e, c=c, d_=d_: e.tensor_tensor(out=Kp[:, c, d_, :, :], in0=psT[:, nkc + c, :].unsqueeze(1).broadcast_to([128, hpc, 128]),
                                                                  in1=Epad[:, c, d_, :, :], op=ALU.mult), [psT, Epad], [Kp])
                yield
                if PD_POS == 3:
                    yield 'prev_done'
                for d_ in range(2):
                    for c in range(nkc):
                        for hh in range(hpc):
                            pe(lambda e, d_=d_, c=c, hh=hh: e.matmul(psS[d_][:, c * hpc + hh, :], lhsT=Kp[:, c, d_, hh, :], rhs=Qt[:, c, d_, :], start=True, stop=True),
                               [Kp, Qt], [psS[d_]])
                yield
                if PD_POS == 4:
                    yield 'prev_done'
                dve(lambda e: e.tensor_tensor(out=S.st1[:], in0=psS[0][:], in1=tri[:, 0, :].unsqueeze(1).broadcast_to([128, 4, 128]), op=ALU.mult), [psS[0], tri], [S.st1])
                dve(lambda e: e.tensor_tensor(out=S.st2[:], in0=psS[1][:], in1=tri[:, 1, :].unsqueeze(1).broadcast_to([128, 4, 128]), op=ALU.mult), [psS[1], tri], [S.st2])
                gps(lambda e: e.tensor_tensor(out=PT[:], in0=S.st1[:], in1=S.st2[:], op=ALU.add), [S.st1, S.st2], [PT])
                yield 'prev_done'
                if n == NT // 2:
                    keep_mul()
                act(lambda e: e.activation(out=Sbf[:], in_=Sm[:], func=AF.Copy), [Sm], [Sbf])
                yield
                for h_ in range(4):
                    c = h_ // hpc
                    hs = slice(h_ * 64, (h_ + 1) * 64)
                    pe(lambda e, c=c, hs=hs: e.matmul(psO[:, hs], lhsT=Qt[:, c, 2, :], rhs=Sbf[:, c, hs], start=True, stop=False), [Qt, Sbf], [psO])
                    pe(lambda e, c=c, hs=hs: e.matmul(psO[:, hs], lhsT=Qt[:, c, 3, :], rhs=Sball[:, n, c, hs], start=False, stop=False), [Qt, Sball], [psO])
                    pe(lambda e, h_=h_, hs=hs: e.matmul(psO[:, hs], lhsT=PT[:, h_, :], rhs=Vb[:, hs], start=False, stop=True), [PT, Vb], [psO])
                yield
                hn1, hn2, oc, osq, sg, resb, resT, pt = S.hn1, S.hn2, S.oc, S.osq, S.sg, S.resb, S.resT, S.pt
                O3 = psO[:].rearrange("p (h d) -> p h d", d=64)
                if ret:
                    dve(lambda e: e.tensor_reduce(out=hn1[:], in_=O3, axis=mybir.AxisListType.X, op=ALU.add), [psO], [hn1])
                    dve(lambda e: e.tensor_scalar(out=hn1[:], in0=hn1[:], scalar1=-1.0 / 64, scalar2=None, op0=ALU.mult), [hn1], [hn1])
                    dve(lambda e: e.tensor_tensor(out=oc[:], in0=O3, in1=hn1[:].unsqueeze(2).broadcast_to([128, 4, 64]), op=ALU.add), [psO, hn1], [oc])
                else:
                    dve(lambda e: e.tensor_copy(out=oc[:], in_=O3), [psO], [oc])
                gps(lambda e: e.tensor_tensor(out=osq[:], in0=oc[:], in1=oc[:], op=ALU.mult), [oc], [osq])
                dve(lambda e: e.tensor_reduce(out=hn2[:], in_=osq[:], axis=mybir.AxisListType.X, op=ALU.add), [osq], [hn2])
                act(lambda e: e.activation(out=sg[:], in_=pt[:, go:go + 256], func=AF.Silu), [pt], [sg])
                act(lambda e: e.activation(out=hn2[:], in_=hn2[:], func=AF.Sqrt, scale=1.0 / 64, bias=epsb[:, 0:1]), [hn2, epsb], [hn2])
                yield
                dve(lambda e: e.reciprocal(out=hn2[:], in_=hn2[:]), [hn2], [hn2])
                gps(lambda e: e.tensor_tensor(out=oc[:], in0=oc[:], in1=hn2[:].unsqueeze(2).broadcast_to([128, 4, 64]), op=ALU.mult), [oc, hn2], [oc])
                gps(lambda e: e.tensor_tensor(out=sg[:], in0=sg[:], in1=gn[:], op=ALU.mult), [sg, gn], [sg])
                gps(lambda e: e.tensor_tensor(out=resb[:], in0=oc[:].rearrange("p h d -> p (h d)"), in1=sg[:], op=ALU.mult), [oc, sg], [resb])
                yield
                for c2_ in range(2):
                    pe(lambda e, c2_=c2_: e.transpose(psR[:, c2_, :], resb[:, c2_ * 128:(c2_ + 1) * 128], identb[:]), [resb, identb], [psR])
                yield from state_update(0, S)
                dve(lambda e: e.tensor_copy(out=resT[:], in_=psR[:]), [psR], [resT])
                stq(brv[:, 2 * kbr:2 * kbr + 2, n * 128:(n + 1) * 128], resT, resT[:], dB["brF"])

            dve(lambda e: e.memset(Sm[:], 0.0), [], [Sm])
            run_pipelined(gen2, range(NT), depth=int(os.environ.get('PIPE2', '2')))
            P.pop()

        def phase_hyena(l, mode='all'):
            NBLK = 2 * T // 512
            Cg = 32
            TWO_PI = 2.0 * math.pi
            def part1():
                P.push()
                w1 = P.sb("w1", [33, 64]); w2 = P.sb("w2", [64, 64]); w3a = P.sb("w3a", [65, 1024])
                c1 = P.sb("c1", [64, 2]); c2_ = P.sb("c2", [64, 2]); fb = P.sb("fb", [64, 2]); delta = P.sb("delta", [128, 2])
                ld(w1, w1[:], I["flt_w1"][l]); ld(w2, w2[:], I["flt_w2"][l]); ld(w3a, w3a[0:64, :], I["flt_w3"][l])
                ld(w3a, w3a[64:65, :], I["flt_b3"][l:l + 1, :])
                ld(c1, c1[:], I["flt_c1"][l]); ld(c2_, c2_[:], I["flt_c2"][l]); ld(delta, delta[:], I["flt_delta"][:, :])
                dve(lambda e: e.tensor_tensor(out=fb[:, 0:1], in0=c1[:, 0:1], in1=c1[:, 1:2], op=ALU.mult), [c1], [fb])
                dve(lambda e: e.tensor_tensor(out=fb[:, 1:2], in0=c2_[:, 0:1], in1=c2_[:, 1:2], op=ALU.mult), [c2_, fb], [fb])
                h2a = P.sb("h2a", [65, 512], BF16); dve(lambda e: e.memset(h2a[:], 1.0), [], [h2a])
                w3b = P.sb("w3b", [65, 1024], BF16); dve(lambda e: e.tensor_copy(out=w3b[:], in_=w3a[:]), [w3a], [w3b])
                h1 = P.sb("h1", [64, 512]); a1 = P.sb("a1", [64, 512]); kk = P.sb("kk", [64, 512])
                nrm = P.sb("nrm", [128, 4, NBLK]); rn = P.sb("rn", [128, 4])
                fts = [P.sb(f"ft{i}", [33, 512]) for i in range(2)]; msks = [P.sb(f"msk{i}", [128, 3, 512]) for i in range(2)]
                win = P.sb("win", [128, 2, 512]); t1 = P.sb("ft1", [128, 512]); t2 = P.sb("ft2", [128, 512]); ab = P.sb("fab", [128, 512])
                gbs = [P.sb(f"gb{i}", [128, 512], BF16) for i in range(2)]
                psh = P.ps("psh", [64, 512]); psf = [P.ps(f"psf{i}", [128, 512]) for i in range(2)]

                def sin_layer(cc, col, dst):
                    dve(lambda e: e.tensor_scalar(out=a1[:], in0=psh[:], scalar1=cc[:, 0:1], scalar2=fb[:, col:col + 1], op0=ALU.mult, op1=ALU.add), [psh, cc, fb], [a1])
                    dve(lambda e: e.tensor_scalar(out=kk[:], in0=a1[:], scalar1=1.0 / TWO_PI, scalar2=MAGIC, op0=ALU.mult, op1=ALU.add), [a1], [kk])
                    dve(lambda e: e.tensor_scalar(out=kk[:], in0=kk[:], scalar1=-MAGIC, scalar2=None, op0=ALU.add), [kk], [kk])
                    dve(lambda e: e.scalar_tensor_tensor(out=a1[:], in0=kk[:], scalar=-TWO_PI, in1=a1[:], op0=ALU.mult, op1=ALU.add), [kk, a1], [a1])
                    act(lambda e: e.activation(out=dst, in_=a1[:], func=AF.Sin), [a1], [h1 if dst is not None and cc is c1 else h2a])

                ng = 0
                for blk in range(NBLK):
                    m0 = blk * 512
                    ft, msk = fts[blk % 2], msks[blk % 2]
                    ld(ft, ft[:], I["flt_feat"][:, m0:m0 + 512])
                    for r_ in range(3):
                        ld(msk, msk[:, r_, :], I["flt_msk"][r_, m0:m0 + 512].partition_broadcast(128))
                    pe(lambda e: e.matmul(psh[:], lhsT=w1[:], rhs=ft[:], start=True, stop=True), [w1, ft], [psh])
                    sin_layer(c1, 0, h1[:])
                    yield
                    pe(lambda e: e.matmul(psh[:], lhsT=w2[:], rhs=h1[:], start=True, stop=True), [w2, h1], [psh])
                    sin_layer(c2_, 1, h2a[0:64, :])
                    yield
                    for ch in range(2):
                        act(lambda e, ch=ch: e.activation(out=win[:, ch, :], in_=msk[:, 2, :], func=AF.Exp, scale=delta[:, ch:ch + 1]), [msk, delta], [win])
                    for o in range(2):
                        for ch in range(2):
                            for dr in range(2):
                                q = o * 4 + dr * 2 + ch
                                pe(lambda e, dr=dr, q=q: e.matmul(psf[dr][:], lhsT=w3b[:, q * 128:(q + 1) * 128], rhs=h2a[:], start=True, stop=True), [w3b, h2a], [psf[dr]])
                            dve(lambda e: e.tensor_tensor(out=t1[:], in0=psf[0][:], in1=msk[:, 0, :], op=ALU.mult), [psf[0], msk], [t1])
                            dve(lambda e: e.tensor_tensor(out=t2[:], in0=psf[1][:], in1=msk[:, 1, :], op=ALU.mult), [psf[1], msk], [t2])
                            dve(lambda e: e.tensor_tensor(out=t1[:], in0=t1[:], in1=t2[:], op=ALU.add), [t1, t2], [t1])
                            dve(lambda e, ch=ch: e.tensor_tensor(out=t1[:], in0=t1[:], in1=win[:, ch, :], op=ALU.mult), [t1, win], [t1])
                            gb = gbs[ng % 2]; ng += 1
                            idx = o * 2 + ch
                            act(lambda e, gb=gb: e.activation(out=gb[:], in_=t1[:], func=AF.Copy), [t1], [gb])
                            act(lambda e, idx=idx, blk=blk: e.activation(out=ab[:], in_=t1[:], func=AF.Abs, accum_out=nrm[:, idx, blk:blk + 1]), [t1], [ab, nrm])
                            stq(gF[idx * 128:(idx + 1) * 128, m0:m0 + 512], gb, gb[:], dB["gF"])
                            yield
                dve(lambda e: e.tensor_reduce(out=rn[:], in_=nrm[:], axis=mybir.AxisListType.X, op=ALU.add), [nrm], [rn])
                dve(lambda e: e.tensor_scalar(out=rn[:], in0=rn[:], scalar1=scal[:, 1:2], scalar2=EPS, op0=ALU.mult, op1=ALU.add), [rn, scal], [rn])
                dve(lambda e: e.reciprocal(out=rn[:], in_=rn[:]), [rn], [rn])
                stq(rnD.rearrange("(q p) -> p q", p=128), rn, rn[:], dB["rnD"])
                P.pop()

            def stage1(din, m1, AA, ps1s, cnt, Cg=Cg):
                for c in range(0, Cg, 2):
                    ps1 = ps1s[cnt[0] % 2]; cnt[0] += 1
                    for cc in range(2):
                        pe(lambda e, cc=cc, ps1=ps1, c=c: e.matmul(ps1[:, cc, :], lhsT=din[:, c + cc, :], rhs=m1[:], start=True, stop=True), [din, m1], [ps1])
                    for ri in range(2):
                        src_ap = ps1[:, :, ri * NSA:(ri + 1) * NSA].rearrange("p c s -> p s c")
                        if ri == 0:
                            dve(lambda e, src_ap=src_ap, c=c: e.tensor_copy(out=AA[:, :, 0, c:c + 2], in_=src_ap), [ps1], [AA])
                        else:
                            act(lambda e, src_ap=src_ap, c=c: e.activation(out=AA[:, :, 1, c:c + 2], in_=src_ap, func=AF.Copy), [ps1], [AA])
                    yield

            def stage2(AA, h2ts, psXs, evac, spb=8):
                h2v = I["hy_h2"].rearrange("s p a k -> p s a k")
                for j in range(NSA):
                    h2t = h2ts[(j // 8) % 2]; jj = j % spb
                    if j % 8 == 0:
                        nj_ = min(8, NSA - j)
                        ld(h2t, h2t[:, 0:nj_, :, :], h2v[:, j:j + nj_, :, :])
                    psX = psXs[(j // spb) % 2]
                    j8 = j % 8
                    pe(lambda e, psX=psX, jj=jj, h2t=h2t, j=j, j8=j8: e.matmul(psX[:, jj, 0, :], lhsT=h2t[:, j8, 0, :], rhs=AA[:, j, 0, :], start=True, stop=False), [h2t, AA], [psX])
                    pe(lambda e, psX=psX, jj=jj, h2t=h2t, j=j, j8=j8: e.matmul(psX[:, jj, 0, :], lhsT=h2t[:, j8, 2, :], rhs=AA[:, j, 1, :], start=False, stop=True), [h2t, AA], [psX])
                    pe(lambda e, psX=psX, jj=jj, h2t=h2t, j=j, j8=j8: e.matmul(psX[:, jj, 1, :], lhsT=h2t[:, j8, 0, :], rhs=AA[:, j, 1, :], start=True, stop=False), [h2t, AA], [psX])
                    pe(lambda e, psX=psX, jj=jj, h2t=h2t, j=j, j8=j8: e.matmul(psX[:, jj, 1, :], lhsT=h2t[:, j8, 1, :], rhs=AA[:, j, 0, :], start=False, stop=True), [h2t, AA], [psX])
                    if jj == spb - 1 or j == NSA - 1:
                        evac(psX, j - jj, jj + 1)
                        yield

            def part2():
                P.push()
                rnb = P.sb("rnb", [128, 512]); ld(rnb, rnb[:], rnD.partition_broadcast(128), src=dB["rnD"])
                hf1 = P.sb("hf1", [NS, 2 * NSA], BF16); ld(hf1, hf1[:], I["hy_hf1"][:, :])
                Cf = 64
                gin = P.sb("gin", [NS, Cf, 128], BF16); AA = P.sb("AAf", [128, NSA, 2, Cf], BF16)
                Gsb = P.sb("Gsbf", [128, NSA, 2, Cf], BF16)
                ps1s = [P.ps(f"hps1f{i}", [128, 2, 2 * NSA]) for i in range(2)]; psXs = [P.ps(f"hpsXf{i}", [128, 4, 2, Cf]) for i in range(2)]
                h2ts = [P.sb(f"h2tf{i}", [128, 8, 3, 128], BF16) for i in range(2)]
                cnt = [0]
                for gi in range(512 // Cf):
                    ld(gin, gin[:], gF[gi * Cf:(gi + 1) * Cf, :].rearrange("c (g n) -> g c n", n=128), src=dB["gF"])
                    yield from stage1(gin, hf1, AA, ps1s, cnt, Cg=Cf)

                    def evacG(psX, j0, nj, gi=gi):
                        dve(lambda e: e.tensor_tensor(out=Gsb[:, j0:j0 + nj, :, :], in0=psX[:, 0:nj, :, :],
                                                      in1=rnb[:, gi * Cf:(gi + 1) * Cf].unsqueeze(1).unsqueeze(1).broadcast_to([128, nj, 2, Cf]), op=ALU.mult),
                            [psX, rnb], [Gsb])
                    yield from stage2(AA, h2ts, psXs, evacG, spb=4)
                    for hh_ in range(2):
                        for s0_ in range(0, NSA, 32):
                            s1_ = min(NSA, s0_ + 32)
                            stq(Gd[2 * gi + hh_].rearrange("p s (r c) -> p s r c", r=2)[:, s0_:s1_], Gsb, Gsb[:, s0_:s1_, :, hh_ * 32:(hh_ + 1) * 32], dB["Gd"])
                P.pop()

            def part3():
                P.push()
                hz = P.sb("hz", [NSA, 128, 2, NT], BF16); ld(hz, hz[:], I["hy_z"][:, :, :, :])
                h1t = P.sb("hh1", [NT, 2 * NSA], BF16); ld(h1t, h1t[:], I["hy_h1"][:, :])
                i1 = P.sb("hi1", [128, 2, 256], BF16); ld(i1, i1[:], I["hy_i1"][:, :, :])
                skb = P.sb("skb", [128, 2, 256])
                for o in range(2):
                    ld(skb, skb[:, o, :], I["hy_skip"][l, o].partition_broadcast(128))
                AA = P.sb("AAd", [128, NSA, 2, Cg], BF16); Ysb = P.sb("Ysb", [128, 2, Cg, NSA], BF16); Bsb = P.sb("Bsb", [NSA, 128, 2, Cg], BF16)
                Gsb = P.sb("Gsbd", [128, NSA, 2, Cg], BF16)
                din = P.sb("din", [NT, Cg, 128], BF16); vt = P.sb("vt", [NT, Cg, 128]); x1t = P.sb("x1t", [NT, Cg, 128]); x2t = P.sb("x2t", [NT, Cg, 128])
                ob = P.sb("ob", [NT, Cg, 128], BF16)
                pw = [P.sb(f"pw{i}", [128, 8, Cg]) for i in range(4)]
                tcv = P.sb("tcv", [NT, Cg, 16])
                ps1s = [P.ps(f"hps1d{i}", [128, 2, 2 * NSA]) for i in range(2)]; psXs = [P.ps(f"hpsXd{i}", [128, 8, 2, Cg]) for i in range(2)]
                psIs = [P.ps(f"hpsI{i}", [NSA, 2, 256]) for i in range(2)]; psYs = [P.ps(f"hpsY{i}", [NT, 16, Cg]) for i in range(2)]
                h2ts = [P.sb(f"h2td{i}", [128, 8, 3, 128], BF16) for i in range(2)]
                cnt = [0]

                def evacY(psX, j0, nj):
                    Xre, Xim = psX[:, 0:nj, 0, :], psX[:, 0:nj, 1, :]
                    Gre, Gim = Gsb[:, j0:j0 + nj, 0, :], Gsb[:, j0:j0 + nj, 1, :]
                    dve(lambda e: e.tensor_tensor(out=pw[0][:, 0:nj, :], in0=Xre, in1=Gre, op=ALU.mult), [psX, Gsb], [pw[0]])
                    dve(lambda e: e.tensor_tensor(out=pw[1][:, 0:nj, :], in0=Xim, in1=Gim, op=ALU.mult), [psX, Gsb], [pw[1]])
                    dve(lambda e: e.tensor_tensor(out=Ysb[:, 0, :, j0:j0 + nj].rearrange("p c j -> p j c"), in0=pw[0][:, 0:nj, :], in1=pw[1][:, 0:nj, :], op=ALU.subtract),
                        [pw[0], pw[1]], [Ysb])
                    dve(lambda e: e.tensor_tensor(out=pw[2][:, 0:nj, :], in0=Xre, in1=Gim, op=ALU.mult), [psX, Gsb], [pw[2]])
                    dve(lambda e: e.tensor_tensor(out=pw[3][:, 0:nj, :], in0=Xim, in1=Gre, op=ALU.mult), [psX, Gsb], [pw[3]])
                    dve(lambda e: e.tensor_tensor(out=Ysb[:, 1, :, j0:j0 + nj].rearrange("p c j -> p j c"), in0=pw[2][:, 0:nj, :], in1=pw[3][:, 0:nj, :], op=ALU.add),
                        [pw[2], pw[3]], [Ysb])

                def long_conv(o, g, xg, svt):
                    ld(Gsb, Gsb[:].rearrange("p s r c -> p s (r c)"), Gd[o * (256 // Cg) + g], src=dB["Gd"])
                    drain(stage1(din, h1t, AA, ps1s, cnt))
                    drain(stage2(AA, h2ts, psXs, evacY))
                    for c in range(0, Cg, 2):
                        psI = psIs[(c // 2) % 2]
                        for cc in range(2):
                            pe(lambda e, cc=cc, psI=psI, c=c: e.matmul(psI[:, cc, :], lhsT=Ysb[:, 0, c + cc, :], rhs=i1[:, 0, :], start=True, stop=False), [Ysb, i1], [psI])
                            pe(lambda e, cc=cc, psI=psI, c=c: e.matmul(psI[:, cc, :], lhsT=Ysb[:, 1, c + cc, :], rhs=i1[:, 1, :], start=False, stop=True), [Ysb, i1], [psI])
                        for ri in range(2):
                            src_ap = psI[:, :, ri * 128:(ri + 1) * 128].rearrange("p c n -> p n c")
                            if ri == 0:
                                dve(lambda e, src_ap=src_ap, c=c: e.tensor_copy(out=Bsb[:, :, 0, c:c + 2], in_=src_ap), [psI], [Bsb])
                            else:
                                act(lambda e, src_ap=src_ap, c=c: e.activation(out=Bsb[:, :, 1, c:c + 2], in_=src_ap, func=AF.Copy), [psI], [Bsb])
                    for nb in range(8):
                        psY = psYs[nb % 2]
                        for q in range(16):
                            n2 = nb * 16 + q
                            pe(lambda e, psY=psY, q=q, n2=n2: e.matmul(psY[:, q, :], lhsT=hz[:, n2, 0, :], rhs=Bsb[:, n2, 0, :], start=True, stop=False), [hz, Bsb], [psY])
                            pe(lambda e, psY=psY, q=q, n2=n2: e.matmul(psY[:, q, :], lhsT=hz[:, n2, 1, :], rhs=Bsb[:, n2, 1, :], start=False, stop=True), [hz, Bsb], [psY])
                        sl = slice(nb * 16, (nb + 1) * 16)
                        dve(lambda e, psY=psY, sl=sl: e.tensor_tensor(out=tcv[:], in0=psY[:].rearrange("p n c -> p c n"), in1=svt[:, :, sl], op=ALU.add), [psY, svt], [tcv])
                        dve(lambda e, sl=sl: e.tensor_tensor(out=xg[:, :, sl], in0=tcv[:], in1=xg[:, :, sl], op=ALU.mult), [tcv, xg], [xg])

                uv = lambda r0: uhF[r0:r0 + Cg, :].rearrange("c (g n) -> g c n", n=128)
                for g in range(256 // Cg):
                    c0 = g * Cg
                    ld(vt, vt[:], uv(c0), src=dB["uhF"]); ld(x1t, x1t[:], uv(256 + c0), src=dB["uhF"]); ld(x2t, x2t[:], uv(512 + c0), src=dB["uhF"])
                    act(lambda e: e.activation(out=din[:], in_=vt[:], func=AF.Copy), [vt], [din])
                    dve(lambda e, c0=c0: e.tensor_tensor(out=vt[:], in0=vt[:], in1=skb[0:NT, 0, c0:c0 + Cg].unsqueeze(2).broadcast_to([NT, Cg, 128]), op=ALU.mult), [vt, skb], [vt])
                    long_conv(0, g, x1t, vt)
                    act(lambda e: e.activation(out=din[:], in_=x1t[:], func=AF.Copy), [x1t], [din])
                    dve(lambda e, c0=c0: e.tensor_tensor(out=vt[:], in0=x1t[:], in1=skb[0:NT, 1, c0:c0 + Cg].unsqueeze(2).broadcast_to([NT, Cg, 128]), op=ALU.mult), [x1t, skb], [vt])
                    long_conv(1, g, x2t, vt)
                    act(lambda e: e.activation(out=ob[:], in_=x2t[:], func=AF.Copy), [x2t], [ob])
                    stq(brF[512 + c0:512 + c0 + Cg, :].rearrange("c (g n) -> g c n", n=128), ob, ob[:], dB["brF"])
                P.pop()
            if mode == 'filtgen':
                def both():
                    yield from part1()
                    yield from part2()
                return both()
            if mode in ('all', 'filt'):
                drain(part1())
                drain(part2())
            if mode in ('all', 'conv'):
                part3()

        def phase_zero_br(l):
            P.push()
            zt = P.sb("zbr", [128, 2048], BF16)
            dve(lambda e: e.memset(zt[:], 0.0), [], [zt])
            for k in range(8):
                for t0 in range(0, T, 2048):
                    w_ = min(2048, T - t0)
                    stq(brF[k * 128:(k + 1) * 128, t0:t0 + w_], zt, zt[:, 0:w_], dB["brF"])
            P.pop()

        PHASES = dict(mod=phase_mod, norm=phase_norm, A=lambda l: drain(phase_A(l)), Afilt=lambda l: run_concurrent(phase_A(l), phase_hyena(l, 'filtgen')), C=phase_C, D=phase_D, zero=phase_zero_br, fnet=phase_fnet, ret=lambda l: phase_scan(l, True), gla=lambda l: phase_scan(l, False), hyena=phase_hyena, hyfilt=lambda l: phase_hyena(l, 'filt'), hyconv=lambda l: phase_hyena(l, 'conv'))
        nc._I = I
        return_hook(P, PHASES, locals())
    return nc


def return_hook(P, PHASES, env):
    sched = env.get('debug') or ()
    I, dB = env['I'], env['dB']
    stop = None
    for d in sched:
        if isinstance(d, str) and d.startswith("stop:"):
            stop = d[5:]
    x_in, x1d, xmid, y_out = env['x_in'], env['x1d'], env['xmid'], env['y_out']
    Am, Af = env['Am'], env['Af']
    only = [d[5:] for d in sched if isinstance(d, str) and d.startswith("only:")]
    if only:
        for nm in only:
            if nm == 'norm':
                PHASES['norm'](x_in, dB["in"], Am, 0)
            elif nm == 'C':
                PHASES['C'](0, x_in, dB["in"])
            elif nm == 'D':
                PHASES['D'](0, x1d, dB["x1d"])
            else:
                PHASES[nm](0)
        P.barrier()
        return
    for l in range(DEPTH):
        src, sbuf = (x_in, dB["in"]) if l == 0 else (x1d, dB["x1d"])
        dst, dbuf = (x1d, dB["x1d"]) if l == 0 else (y_out, dB["y"])
        if stop == "none":
            break
        PHASES['mod'](l)
        if stop == "mod":
            break
        PHASES['norm'](src, sbuf, Am, 0)
        if stop == "norm":
            break
        PHASES['Afilt' if CONC_FILT else 'A'](l)
        if stop == "A":
            break
        PHASES['zero'](l)
        for nm in ('fnet', 'ret', 'hyconv' if CONC_FILT else 'hyena', 'gla'):
            if nm in PHASES:
                PHASES[nm](l)
        if stop == "mix":
            break
        PHASES['C'](l, src, sbuf)
        PHASES['norm'](xmid, dB["xmid"], Af, 24)
        PHASES['D'](l, dst, dbuf)
        if stop == "L0":
            break
    P.barrier()


def prep_core_inputs(x, c2, W, tb):
    m = {"x": np.ascontiguousarray(x, np.float32)}
    m["cT"] = np.ascontiguousarray(c2.reshape(2, 8, 128).transpose(2, 1, 0), np.float32)
    m.update(W)
    m.update(tb)
    return m


def prep_weights(inp):
    f = lambda a: np.ascontiguousarray(a, np.float32)
    W = {}
    W["ada_w"] = f(inp["ada_w"]); W["ada_b"] = f(inp["ada_b"])
    W["ada_b_col"] = f(inp["ada_b"].reshape(DEPTH, 48, 128).transpose(0, 2, 1))
    nw = np.stack([inp["norm_pre_mix"], inp["norm_post_mix"], inp["norm_pre_ffn"], inp["norm_post_ffn"]], 1)
    W["normw_col"] = f(nw.reshape(DEPTH, 4, 8, 128).transpose(0, 3, 1, 2))
    W["norm_post_mix"] = f(inp["norm_post_mix"]); W["norm_post_ffn"] = f(inp["norm_post_ffn"])
    W["w_in"] = f(inp["w_in"])
    W["hy_cw"] = f(inp["hy_conv_w"].reshape(DEPTH, 3, 6, 128).transpose(0, 3, 2, 1))
    W["hy_cb"] = f(inp["hy_conv_b"].reshape(DEPTH, 6, 128).transpose(0, 2, 1))
    W["flt_w1"] = f(inp["flt_w1"]); W["flt_w2"] = f(inp["flt_w2"]); W["flt_w3"] = f(inp["flt_w3"]); W["flt_b3"] = f(inp["flt_b3"])
    W["flt_c1"] = f(np.stack([inp["flt_freq"], inp["flt_b1"]], -1)); W["flt_c2"] = f(np.stack([inp["flt_freq"], inp["flt_b2"]], -1))
    W["hy_skip"] = f(inp["hy_skip"])
    wd = np.zeros((DEPTH, 33, 256), np.float32)
    wd[:, 0:16, 0:128] = inp["gla_w_decay"][:, 0]; wd[:, 16:32, 128:256] = inp["gla_w_decay"][:, 1]
    wd[:, 32, 0:128] = inp["gla_b_decay"][:, 0]; wd[:, 32, 128:256] = inp["gla_b_decay"][:, 1]
    W["gla_wd"] = wd
    W["ret_gn"] = f(inp["ret_gn"]); W["gla_gn"] = f(inp["gla_gn"])
    W["w_branch"] = f(inp["w_branch"].reshape(DEPTH, 1024, D)); W["w_out"] = f(inp["w_out"])
    W["ffn_up"] = f(inp["ffn_up"])
    W["ffn_cw"] = f(inp["ffn_conv_w"].reshape(DEPTH, 3, 44, 128).transpose(0, 3, 2, 1))
    W["ffn_cb"] = f(inp["ffn_conv_b"].reshape(DEPTH, 44, 128).transpose(0, 2, 1))
    W["ffn_down"] = f(inp["ffn_down"])
    return W


_T = 8192


def kernel(**inp):
    inp = {k: np.asarray(v) for k, v in inp.items()}
    T = _T
    W = prep_weights(inp)
    tbP, tbS = make_tables(T, 'P'), make_tables(T, 'S')
    xp, xs, cp, cs = inp["x_prompt"], inp["x_sample"], inp["c_prompt"], inp["c_sample"]
    in_maps = []
    for b in range(2):
        in_maps.append(prep_core_inputs(xp[b], np.stack([cp[b], cp[b]]), W, tbP))
    for b in range(2):
        in_maps.append(prep_core_inputs(xs[2 * b:2 * b + 2].reshape(T, D), cs[2 * b:2 * b + 2], W, tbS))
    nc = build_program(T)
    res = run_bass_kernel_spmd(nc, in_maps, core_ids=list(range(4)))
    outs = [np.asarray(r["y"], np.float32) for r in res.results]
    y_prompt = np.stack([outs[0], outs[1]], 0)
    y_sample = np.concatenate([outs[2].reshape(2, T // 2, D), outs[3].reshape(2, T // 2, D)], 0)
    return (y_prompt, y_sample)
```

```python
import math
from contextlib import ExitStack
import numpy as np
import ml_dtypes
import concourse.bass as bass
import concourse.mybir as mybir
from concourse.bass_utils import run_bass_kernel_spmd

F32 = mybir.dt.float32
BF16 = mybir.dt.bfloat16
AF = mybir.ActivationFunctionType
ALU = mybir.AluOpType
NPBF = ml_dtypes.bfloat16

D = 1024
DEPTH = 2
DFF = 2816
EPS = 1e-6
MAGIC = 12582912.0
import os
PIPE_DEPTH = int(os.environ.get("PIPE_DEPTH", "2"))
PD_POS = int(os.environ.get("PD_POS", "99"))
CONC_FILT = int(os.environ.get("CONC_FILT", "1"))
USE_POOL = int(os.environ.get("USE_POOL", "0"))
PRIM_STEPS = int(os.environ.get("PRIM_STEPS", "1"))


class Stream:
    def __init__(self, P, inc):
        self.P, self.inc = P, inc
        self.sem = P.new_sem()
        self.count = 0

    def bump(self):
        if self.count + self.inc > 30000:
            self.sem = self.P.new_sem()
            self.count = 0
        self.count += self.inc
        return (self.sem, self.count)

    def cur(self):
        return (self.sem, self.count) if self.count else None


class Buf:
    def __init__(self, name, t=None):
        self.name, self.t = name, t
        self.w = None
        self.r = {}

    def __getitem__(self, idx):
        return self.t[idx]


class Prog:
    def __init__(self, nc, es):
        self.nc, self.es = nc, es
        self.nsem = 0
        self.engs = {'pe': nc.tensor, 'act': nc.scalar, 'dve': nc.vector, 'pool': nc.gpsimd, 'sp': nc.sync}
        self.streams = {k: Stream(self, 1) for k in ('pe', 'act', 'dve', 'pool')}
        self.seen = {k: {} for k in self.engs}
        self.dma_pool = {q: [Stream(self, 16) for _ in range(8)] for q in ('sp', 'pool')}
        self.dma_rr = {q: 0 for q in self.dma_pool}
        self.scopes = [es]
        self.nuniq = 0

    def new_sem(self):
        self.nsem += 1
        return self.es.enter_context(self.nc.semaphore(f"s{self.nsem}"))

    def sb(self, name, shape, dt=F32):
        self.nuniq += 1
        return Buf(name, self.scopes[-1].enter_context(self.nc.sbuf_tensor(f"{name}_{self.nuniq}", shape, dt)))

    def ps(self, name, shape, dt=F32):
        self.nuniq += 1
        return Buf(name, self.scopes[-1].enter_context(self.nc.psum_tensor(f"{name}_{self.nuniq}", shape, dt)))

    def push(self):
        st = ExitStack()
        self.scopes.append(st)
        return st

    def pop(self):
        self.barrier()
        self.scopes.pop().close()

    def _wait(self, eng, tok):
        if tok is None:
            return
        sem, val = tok
        seen = self.seen[eng]
        if seen.get(id(sem), 0) >= val:
            return
        self.engs[eng].wait_ge(sem, val)
        seen[id(sem)] = val

    def barrier(self):
        toks = [s.cur() for s in self.streams.values()]
        for pool in self.dma_pool.values():
            toks += [s.cur() for s in pool]
        for eng in self.engs:
            for t in toks:
                self._wait(eng, t)

    def _deps(self, eng, reads, writes, accum):
        for b in reads:
            self._wait(eng, b.w)
        for b in writes:
            if not accum:
                self._wait(eng, b.w)
            for t in b.r.values():
                self._wait(eng, t)

    def _commit(self, tok, reads, writes):
        for b in writes:
            b.w = tok
            b.r = {}
        for b in reads:
            b.r[id(tok[0])] = tok

    def op(self, eng, fn, reads=(), writes=(), accum=False):
        self._deps(eng, reads, writes, accum)
        inst = fn(self.engs[eng])
        tok = self.streams[eng].bump()
        inst.then_inc(tok[0], 1)
        self._commit(tok, reads, writes)
        return tok

    def dma(self, q, out, in_, reads=(), writes=()):
        pool = self.dma_pool[q]
        st = pool[self.dma_rr[q] % len(pool)]
        self.dma_rr[q] += 1
        self._wait(q, st.cur())
        self._deps(q, reads, writes, False)
        inst = self.engs[q].dma_start(out=out, in_=in_)
        tok = st.bump()
        inst.then_inc(tok[0], 16)
        self._commit(tok, reads, writes)
        return tok


def _cplx_pair(M):
    return np.concatenate([M.real, M.imag], 1), np.concatenate([-M.imag, M.real], 1)


def make_tables(T, kind):
    NT = T // 128
    NS = 2 * NT
    H = NT // 2
    isS = (kind == 'S')
    tb = {}
    tb['ident_b'] = np.eye(128).astype(NPBF)
    tb['ident_f'] = np.eye(128, dtype=np.float32)
    cc = np.arange(64)
    ang = 2 * np.pi * np.outer(cc, cc) / 64
    bdc = np.zeros((128, 128)); bds = np.zeros((128, 128))
    for g in range(2):
        bdc[g * 64:(g + 1) * 64, g * 64:(g + 1) * 64] = np.cos(ang)
        bds[g * 64:(g + 1) * 64, g * 64:(g + 1) * 64] = -np.sin(ang)
    tb['bdcs'] = np.stack([bdc, bds], 1).astype(NPBF)
    n1 = np.arange(NT)
    M1 = np.zeros((NT, NS), np.complex128)
    M2 = np.zeros((NS, 128, 128), np.complex128)
    n2 = np.arange(128)[:, None]
    k2 = np.arange(128)[None, :]
    if not isS:
        L = T
        for j in range(NT):
            M1[:, j] = np.exp(-2j * np.pi * n1 * j / NT)
            M2[j] = np.exp(-2j * np.pi * n2 * (j + NT * k2) / T)
    else:
        L = T // 2
        for s in range(2):
            for j in range(NT):
                M1[s * H:(s + 1) * H, s * NT + j] = np.exp(-2j * np.pi * np.arange(H) * j / H)
                m = np.exp(-2j * np.pi * n2 * (NT * (k2 % 64) + j) / L) * ((k2 // 64) == s)
                M2[s * NT + j] = m
    M2 = M2 / math.sqrt(L * 64)
    a, b = _cplx_pair(M1)
    tb['fn_m1'] = np.stack([a, b], 1).astype(NPBF)
    fm2 = np.zeros((NT, 128, 2, 2, 128), np.float64)
    for s in range(2):
        for j in range(NT):
            fm2[j, :, s, 0] = M2[s * NT + j].real
            fm2[j, :, s, 1] = -M2[s * NT + j].imag
    tb['fn_m2'] = fm2.astype(NPBF)
    HF1 = np.zeros((NS, NS), np.complex128)
    H2 = np.zeros((NS, 128, 128), np.complex128)
    HZ = np.zeros((128, NS, NT), np.complex128)
    if not isS:
        N = 2 * T
        for j in range(NS):
            HF1[:, j] = np.exp(-2j * np.pi * np.arange(NS) * j / NS)
            H2[j] = np.exp(-2j * np.pi * n2 * (j + NS * k2) / N)
        for q in range(128):
            HZ[q] = np.exp(2j * np.pi * np.outer(np.arange(NS), 128 * np.arange(NT) + q) / N) / N
    else:
        N = T
        for s in range(2):
            for j in range(NT):
                HF1[s * NT:(s + 1) * NT, s * NT + j] = np.exp(-2j * np.pi * np.arange(NT) * j / NT)
                H2[s * NT + j] = np.exp(-2j * np.pi * n2 * (j + NT * k2) / N)
        for q in range(128):
            for s in range(2):
                HZ[q, s * NT:(s + 1) * NT, s * H:(s + 1) * H] = \
                    np.exp(2j * np.pi * np.outer(np.arange(NT), 128 * np.arange(H) + q) / N) / N
    if not isS:
        H1 = HF1[:NT]
    else:
        H1 = np.concatenate([HF1[0:H], HF1[NT:NT + H]], 0)
    if not isS:
        act = list(range(NS // 2 + 1)) + [None]
        wts = [1.0 if j in (0, NS // 2) else 2.0 for j in range(NS // 2 + 1)] + [0.0]
    else:
        act = [s * NT + j for s in range(2) for j in range(NT // 2 + 1)]
        wts = [1.0 if j in (0, NT // 2) else 2.0 for s in range(2) for j in range(NT // 2 + 1)]
    def sel(M, axis):
        parts = []
        for a in act:
            if a is None:
                parts.append(np.zeros_like(np.take(M, [0], axis=axis)))
            else:
                parts.append(np.take(M, [a], axis=axis))
        return np.concatenate(parts, axis=axis)
    H1 = sel(H1, 1); HF1 = sel(HF1, 1); H2 = sel(H2, 0)
    HZ = sel(HZ, 1) * np.asarray(wts)[None, :, None]
    tb['hy_h1'] = np.concatenate([H1.real, H1.imag], 1).astype(NPBF)
    tb['hy_hf1'] = np.concatenate([HF1.real, HF1.imag], 1).astype(NPBF)
    tb['hy_h2'] = np.stack([H2.real, H2.imag, -H2.imag], 2).astype(NPBF)
    Fi = np.exp(2j * np.pi * np.outer(np.arange(128), np.arange(128)) / 128)
    a, b = _cplx_pair(Fi)
    tb['hy_i1'] = np.stack([a, b], 1).astype(NPBF)
    tb['hy_z'] = np.stack([HZ.real, -HZ.imag], 2).transpose(1, 0, 2, 3).astype(NPBF).copy()
    Lf = L
    mpos = np.arange(2 * T)
    mloc = mpos % (2 * Lf)
    lag = np.where(mloc < Lf, mloc, 2 * Lf - mloc)
    lag = np.where(mloc == Lf, 0, lag)
    mf = (mloc < Lf).astype(np.float32)
    mb = (mloc > Lf).astype(np.float32)
    tl = np.linspace(0.0, 1.0, Lf, dtype=np.float32)
    wl = (2.0 * np.float32(math.pi) * np.arange(Lf, dtype=np.float32) / np.float32(Lf)).astype(np.float32)
    fb = np.linspace(1e-4, 15, 16, dtype=np.float32)[None, :]
    feat = np.concatenate([tl[:, None], np.cos(fb * wl[:, None]), -np.sin(fb * wl[:, None])], -1).astype(np.float32)
    tb['flt_feat'] = np.ascontiguousarray(feat[lag].T).astype(np.float32)
    tb['flt_msk'] = np.stack([mf, mb, -tl[lag]], 0).astype(np.float32)
    deltas = np.abs(np.linspace(math.log(1e-2) / 0.3, math.log(1e-2) / 1.5, 256, dtype=np.float32))
    tb['flt_delta'] = np.ascontiguousarray(deltas.reshape(2, 128).T).astype(np.float32)
    sc = np.zeros((128, 4), np.float32)
    sc[:, 0] = 0.0 if isS else 1.0
    sc[:, 1] = 0.5 if isS else 1.0
    tb['scal'] = sc
    NB = T // 512
    hal = np.ones((NB, 2), np.float32)
    hal[0, 0] = 0.0; hal[NB - 1, 1] = 0.0
    if isS:
        hal[NB // 2, 0] = 0.0; hal[NB // 2 - 1, 1] = 0.0
    tb['hal'] = np.broadcast_to(hal.reshape(1, NB * 2), (128, NB * 2)).astype(np.float32).copy()
    pos = (np.arange(T) % L).astype(np.float32)
    inv = (10000.0 ** (-np.arange(32, dtype=np.float32) / 32)).astype(np.float32)
    angr = pos[:, None] * inv[None, :]
    tb['rot'] = np.stack([np.cos(angr), np.sin(angr)], 1).astype(np.float32)
    lg = np.log(1.0 - 2.0 ** (-5.0 - np.arange(4)))
    i = np.arange(128)
    ret_e = np.zeros((128, 2, 6, 128), np.float64)
    ret_tok = np.zeros((128, 2, 256), np.float64)
    ret_dec = np.zeros((128, 2, 2), np.float64)
    for c in range(2):
        for p in range(128):
            h = 2 * c + p // 64
            bf = (i + 1) * lg[h]; bb = (128 - i) * lg[h]
            ret_e[p, c, 0] = np.exp(bf - bf[64]) / 8.0
            ret_e[p, c, 1] = np.exp(bf[64] - bf)
            ret_e[p, c, 2] = np.exp(bb - bb[64]) / 8.0
            ret_e[p, c, 3] = np.exp(bb[64] - bb)
            ret_e[p, c, 4] = np.exp(bf) / 8.0
            ret_e[p, c, 5] = np.exp(bb) / 8.0
            ret_dec[p, c, :] = np.exp(128 * lg[h])
    for h in range(4):
        ret_tok[:, 0, h * 64:(h + 1) * 64] = np.exp((127 - i) * lg[h])[:, None]
        ret_tok[:, 1, h * 64:(h + 1) * 64] = np.exp(i * lg[h])[:, None]
    tb['ret_e'] = ret_e.astype(np.float32)
    tb['ret_tok'] = ret_tok.astype(np.float32)
    tb['ret_dec'] = ret_dec.astype(np.float32)
    mh_ret = np.zeros((128, 2), np.float32); mh_gla = np.zeros((128, 4), np.float32)
    bd_ret = np.zeros((128, 2, 256), np.float32); bd_gla = np.zeros((128, 1, 256), np.float32)
    for p in range(128):
        mh_ret[p, p // 64] = 1.0; mh_gla[p, p // 32] = 1.0
        for c in range(2):
            h = 2 * c + p // 64
            bd_ret[p, c, h * 64:(h + 1) * 64] = 1.0
        h = p // 32
        bd_gla[p, 0, h * 64:(h + 1) * 64] = 1.0
    tb['mh_ret'] = mh_ret; tb['mh_gla'] = mh_gla; tb['bd_ret'] = bd_ret; tb['bd_gla'] = bd_gla
    jj = np.arange(128)[:, None]; ii = np.arange(128)[None, :]
    tri = np.zeros((128, 6, 128), np.float32)
    tri[:, 0] = (jj <= ii)
    tri[:, 1] = (jj > ii)
    tri[:, 2] = -(jj <= ii).astype(np.float32) / 16.0
    tri[:, 3] = -(jj >= ii).astype(np.float32) / 16.0
    tri[:, 4] = -(jj > ii).astype(np.float32) / 16.0
    tri[:, 5] = -(jj < ii).astype(np.float32) / 16.0
    tb['tri'] = tri
    return tb


OFF_FN, OFF_QR, OFF_HY, OFF_QG, OFF_GATES = 0, 256, 1280, 2048, 2848
NTM = 1824


def build_program(T, debug=()):
    NT, NS, NB, H = T // 128, T // 64, T // 512, T // 256
    NSA = NT + 2
    nc = bass.Bass("TRN2", target_bir_lowering=False)
    I = {}

    def inp(name, shape, dt=F32):
        I[name] = nc.dram_tensor(name, list(shape), dt, kind="ExternalInput").ap()
        return I[name]

    def scratch(name, shape, dt=F32):
        kind = "ExternalOutput" if name in debug else "Internal"
        return nc.dram_tensor(name, list(shape), dt, kind=kind).ap()

    x_in = inp("x", [T, D]); inp("cT", [128, 8, 2])
    inp("ada_w", [DEPTH, D, 6 * D]); inp("ada_b_col", [DEPTH, 128, 48]); inp("ada_b", [DEPTH, 6 * D])
    inp("normw_col", [DEPTH, 128, 4, 8]); inp("norm_post_mix", [DEPTH, D]); inp("norm_post_ffn", [DEPTH, D])
    inp("w_in", [DEPTH, D, 6944])
    inp("hy_cw", [DEPTH, 128, 6, 3]); inp("hy_cb", [DEPTH, 128, 6])
    inp("flt_w1", [DEPTH, 33, 64]); inp("flt_c1", [DEPTH, 64, 2]); inp("flt_w2", [DEPTH, 64, 64]); inp("flt_c2", [DEPTH, 64, 2])
    inp("flt_w3", [DEPTH, 64, 1024]); inp("flt_b3", [DEPTH, 1024]); inp("hy_skip", [DEPTH, 2, 256])
    inp("gla_wd", [DEPTH, 33, 256]); inp("ret_gn", [DEPTH, 256]); inp("gla_gn", [DEPTH, 256])
    inp("w_branch", [DEPTH, 1024, D]); inp("w_out", [DEPTH, D, D])
    inp("ffn_up", [DEPTH, D, 2 * DFF]); inp("ffn_cw", [DEPTH, 128, 44, 3]); inp("ffn_cb", [DEPTH, 128, 44])
    inp("ffn_down", [DEPTH, DFF, D])
    for nm, shp, dt in (("ident_b", [128, 128], BF16), ("ident_f", [128, 128], F32), ("bdcs", [128, 2, 128], BF16),
                        ("fn_m1", [NT, 2, 2 * NS], BF16), ("fn_m2", [NT, 128, 2, 2, 128], BF16),
                        ("hy_h1", [NT, 2 * NSA], BF16), ("hy_hf1", [NS, 2 * NSA], BF16), ("hy_h2", [NSA, 128, 3, 128], BF16),
                        ("hy_i1", [128, 2, 256], BF16), ("hy_z", [NSA, 128, 2, NT], BF16),
                        ("flt_feat", [33, 2 * T], F32), ("flt_msk", [3, 2 * T], F32), ("flt_delta", [128, 2], F32),
                        ("scal", [128, 4], F32), ("hal", [128, NB * 2], F32), ("rot", [T, 2, 32], F32),
                        ("ret_e", [128, 2, 6, 128], F32), ("ret_tok", [128, 2, 256], F32), ("ret_dec", [128, 2, 2], F32),
                        ("mh_ret", [128, 2], F32), ("mh_gla", [128, 4], F32), ("bd_ret", [128, 2, 256], F32),
                        ("bd_gla", [128, 1, 256], F32), ("tri", [128, 6, 128], F32)):
        inp(nm, shp, dt)
    y_out = nc.dram_tensor("y", [T, D], F32, kind="ExternalOutput").ap()
    x1d = scratch("x1d", [T, D]); xmid = scratch("xmid", [T, D])
    projT = scratch("projT", [T, NTM]); zF = scratch("zF", [2, 256, T], BF16); uhF = scratch("uhF", [768, T])
    brF = scratch("brF", [1024, T], BF16); modrow = scratch("modrow", [2, 2048]); hF = scratch("hF", [D, T + 2], BF16); gFF = scratch("gFF", [DFF, T], BF16)
    gF = scratch("gF", [512, 2 * T], BF16); rnD = scratch("rnD", [512]); Gd = scratch("Gd", [16, 128, NSA, 64], BF16)

    es = ExitStack()
    with es:
        es.enter_context(nc.allow_non_contiguous_dma(reason="strided scratch layouts"))
        P = Prog(nc, es)
        dB = {k: Buf(k) for k in ("x1d", "xmid", "projT", "zF", "uhF", "brF", "modrow", "gF", "rnD", "Gd", "y", "in", "hF", "gFF")}
        IN = dB["in"]

        def ld(dst_buf, dst_ap, src_ap, src=IN):
            return P.dma('sp', dst_ap, src_ap, reads=[src], writes=[dst_buf])

        def stq(dst_ap, src_buf, src_ap, dst):
            return P.dma('pool', dst_ap, src_ap, reads=[src_buf], writes=[dst])

        def dve(fn, r, w):
            return P.op('dve', fn, r, w)

        def act(fn, r, w):
            return P.op('act', fn, r, w)

        def gps(fn, r, w):
            return P.op('pool' if USE_POOL else 'dve', fn, r, w)

        def pe(fn, r, w):
            return P.op('pe', fn, r, w, accum=True)

        identb = P.sb("identb", [128, 128], BF16); identf = P.sb("identf", [128, 128], F32)
        scal = P.sb("scal", [128, 4]); hal = P.sb("hal", [128, NB * 2]); epsb = P.sb("epsb", [128, 1])
        modc = P.sb("modc", [128, 48, 2]); Am = P.sb("Am", [128, 8, 2]); Af = P.sb("Af", [128, 8, 2])
        gtb = P.sb("gtb", [128, 2, 2, D])
        ld(identb, identb[:], I["ident_b"][:, :]); ld(identf, identf[:], I["ident_f"][:, :])
        ld(scal, scal[:], I["scal"][:, :]); ld(hal, hal[:], I["hal"][:, :])
        dve(lambda e: e.memset(epsb[:], EPS), [], [epsb])

        def phase_mod(l):
            P.push()
            cT = P.sb("cT", [128, 8, 2]); scT = P.sb("scT", [128, 8, 2])
            ld(cT, cT[:], I["cT"][:, :, :])
            act(lambda e: e.activation(out=scT[:], in_=cT[:], func=AF.Silu), [cT], [scT])
            psc = P.ps("psc", [128, 96]); psr = P.ps("psr", [2, 2048]); macc = P.sb("macc", [128, 96])
            wts = [P.sb(f"adaw{i}", [128, 6 * D]) for i in range(2)]
            rowcols = (2048, 2560, 5120, 5632)
            for kc in range(8):
                wt = wts[kc % 2]
                ld(wt, wt[:], I["ada_w"][l, kc * 128:(kc + 1) * 128, :])
                for q in range(48):
                    pe(lambda e, q=q: e.matmul(psc[:, 2 * q:2 * q + 2], lhsT=wt[:, q * 128:(q + 1) * 128], rhs=scT[:, kc, :],
                                               start=True, stop=True), [wt, scT], [psc])
                if kc == 0:
                    dve(lambda e: e.tensor_copy(out=macc[:], in_=psc[:]), [psc], [macc])
                else:
                    dve(lambda e: e.tensor_tensor(out=macc[:], in0=macc[:], in1=psc[:], op=ALU.add), [psc, macc], [macc])
                for bi, c0 in enumerate(rowcols):
                    pe(lambda e, bi=bi, c0=c0: e.matmul(psr[:, bi * 512:(bi + 1) * 512], lhsT=scT[:, kc, :], rhs=wt[:, c0:c0 + 512],
                                                        start=(kc == 0), stop=(kc == 7)), [wt, scT], [psr])
            abc = P.sb("abc", [128, 48]); nwc = P.sb("nwc", [128, 4, 8])
            ld(abc, abc[:], I["ada_b_col"][l]); ld(nwc, nwc[:], I["normw_col"][l])
            dve(lambda e: e.tensor_tensor(out=modc[:], in0=macc[:].rearrange("p (q s) -> p q s", s=2),
                                          in1=abc[:].unsqueeze(2).broadcast_to([128, 48, 2]), op=ALU.add), [macc, abc], [modc])
            dve(lambda e: e.scalar_tensor_tensor(out=Am[:], in0=modc[:, 8:16, :], scalar=1.0,
                                                 in1=nwc[:, 0, :].unsqueeze(2).broadcast_to([128, 8, 2]), op0=ALU.add, op1=ALU.mult),
                [modc, nwc], [Am])
            dve(lambda e: e.scalar_tensor_tensor(out=Af[:], in0=modc[:, 32:40, :], scalar=1.0,
                                                 in1=nwc[:, 2, :].unsqueeze(2).broadcast_to([128, 8, 2]), op0=ALU.add, op1=ALU.mult),
                [modc, nwc], [Af])
            abr = P.sb("abr", [2, 2048]); nwr = P.sb("nwr", [2, 2048]); gr = P.sb("gr", [2, 2048])
            ld(abr, abr[:, 0:1024], I["ada_b"][l, 2048:3072].partition_broadcast(2))
            ld(abr, abr[:, 1024:2048], I["ada_b"][l, 5120:6144].partition_broadcast(2))
            ld(nwr, nwr[:, 0:1024], I["norm_post_mix"][l].partition_broadcast(2))
            ld(nwr, nwr[:, 1024:2048], I["norm_post_ffn"][l].partition_broadcast(2))
            dve(lambda e: e.tensor_tensor(out=gr[:], in0=psr[:], in1=abr[:], op=ALU.add), [psr, abr], [gr])
            dve(lambda e: e.tensor_tensor(out=gr[:], in0=gr[:], in1=nwr[:], op=ALU.mult), [gr, nwr], [gr])
            stq(modrow[:, :], gr, gr[:], dB["modrow"])
            for sg in range(2):
                ld(gtb, gtb[:, sg, :, :].rearrange("p a d -> p (a d)"), modrow[sg].partition_broadcast(128), src=dB["modrow"])
            P.pop()

        def phase_norm(src_ap2d, src_buf, A, bq0):
            P.push()
            zt = P.sb("zt", [128, 8, 1], BF16)
            dve(lambda e: e.memset(zt[:], 0.0), [], [zt])
            hFv = hF.rearrange("(c p) t -> p c t", p=128)
            stq(hFv[:, :, 0:1], zt, zt[:], dB["hF"]); stq(hFv[:, :, T + 1:T + 2], zt, zt[:], dB["hF"])
            xts = [P.sb(f"xt{i}", [128, D]) for i in range(2)]
            sq = P.sb("sq", [128, D]); ss = P.sb("ss", [128, 1]); rs = P.sb("rs", [128, 1])
            xn = P.sb("xn", [128, D], BF16); ptr = P.ps("ptr", [128, 8, 128], BF16); tmp = P.sb("tmpT", [128, 8, 128])
            hts = [P.sb(f"ht{i}", [128, 8, 512], BF16) for i in range(2)]
            xns = [P.sb(f"xnN{i}", [128, D], BF16) for i in range(2)]

            def gen(it):
                xt, ht, xn = xts[it % 2], hts[(it // 4) % 2], xns[it % 2]
                q4 = it % 4
                seg = 0 if it < NT // 2 else 1
                ld(xt, xt[:], src_ap2d[it * 128:(it + 1) * 128, :], src=src_buf)
                act(lambda e: e.activation(out=sq[:], in_=xt[:], func=AF.Square, accum_out=ss[:, 0:1]), [xt], [sq, ss])
                act(lambda e: e.activation(out=rs[:], in_=ss[:], func=AF.Sqrt, scale=1.0 / D, bias=epsb[:, 0:1]), [ss, epsb], [rs])
                yield
                dve(lambda e: e.reciprocal(out=rs[:], in_=rs[:]), [rs], [rs])
                dve(lambda e: e.tensor_scalar(out=xn[:], in0=xt[:], scalar1=rs[:, 0:1], scalar2=None, op0=ALU.mult), [xt, rs], [xn])
                yield 'prev_done'
                for c8 in range(8):
                    pe(lambda e, c8=c8: e.transpose(ptr[:, c8, :], xn[:, c8 * 128:(c8 + 1) * 128], identb[:]), [xn, identb], [ptr])
                yield
                dve(lambda e: e.tensor_tensor(out=tmp[:], in0=ptr[:], in1=A[:, :, seg:seg + 1].broadcast_to([128, 8, 128]), op=ALU.mult),
                    [ptr, A], [tmp])
                dve(lambda e: e.tensor_tensor(out=ht[:, :, q4 * 128:(q4 + 1) * 128], in0=tmp[:], in1=modc[:, bq0:bq0 + 8, seg:seg + 1].broadcast_to([128, 8, 128]), op=ALU.add),
                    [tmp, modc], [ht])
                if q4 == 3:
                    stq(hFv[:, :, 1 + (it - 3) * 128:1 + (it + 1) * 128], ht, ht[:], dB["hF"])

            run_pipelined(gen, range(NT))
            P.pop()

        def load_w_bf16(dst, dst_ap_fn, src_ap_fn, nk, width, stg):
            for k in range(nk):
                st = stg[k % 2]
                ld(st, st[:, 0:width], src_ap_fn(k))
                if k % 2 == 0:
                    dve(lambda e, k=k, st=st: e.tensor_copy(out=dst_ap_fn(k), in_=st[:, 0:width]), [st], [dst])
                else:
                    act(lambda e, k=k, st=st: e.activation(out=dst_ap_fn(k), in_=st[:, 0:width], func=AF.Copy), [st], [dst])

        def load_window(hw, b):
            hFv = hF.rearrange("(c p) t -> p c t", p=128)
            ld(hw, hw[:], hFv[:, :, b * 512:b * 512 + 514], src=dB["hF"])
            for side, col in ((0, 0), (1, 513)):
                dve(lambda e, side=side, col=col: e.tensor_tensor(out=hw[:, :, col:col + 1], in0=hw[:, :, col:col + 1],
                                                                  in1=hal[:, 2 * b + side:2 * b + side + 1].unsqueeze(1).broadcast_to([128, 8, 1]),
                                                                  op=ALU.mult), [hw, hal], [hw])

        def conv3_fm(out_ap, pm, ph, cw, cb, ci, rbufs, wbuf):
            act(lambda e: e.activation(out=out_ap, in_=pm[:, 0:512], func=AF.Identity, scale=cw[:, ci, 1:2], bias=cb[:, ci:ci + 1]), rbufs, [wbuf])
            dve(lambda e: e.scalar_tensor_tensor(out=out_ap[:, 1:512], in0=pm[:, 0:511], scalar=cw[:, ci, 0:1], in1=out_ap[:, 1:512],
                                                 op0=ALU.mult, op1=ALU.add), rbufs + [wbuf], [wbuf])
            dve(lambda e: e.scalar_tensor_tensor(out=out_ap[:, 0:511], in0=pm[:, 1:512], scalar=cw[:, ci, 2:3], in1=out_ap[:, 0:511],
                                                 op0=ALU.mult, op1=ALU.add), rbufs + [wbuf], [wbuf])
            dve(lambda e: e.scalar_tensor_tensor(out=out_ap[:, 0:1], in0=ph[:, 0:1], scalar=cw[:, ci, 0:1], in1=out_ap[:, 0:1],
                                                 op0=ALU.mult, op1=ALU.add), rbufs + [wbuf], [wbuf])
            dve(lambda e: e.scalar_tensor_tensor(out=out_ap[:, 511:512], in0=ph[:, 1:2], scalar=cw[:, ci, 2:3], in1=out_ap[:, 511:512],
                                                 op0=ALU.mult, op1=ALU.add), rbufs + [wbuf], [wbuf])

        def phase_A(l):
            P.push()
            wA = P.sb("wA", [128, 8, OFF_GATES], BF16)
            stg = [P.sb(f"stgA{i}", [128, OFF_GATES]) for i in range(2)]
            load_w_bf16(wA, lambda k: wA[:, k, :], lambda k: I["w_in"][l, k * 128:(k + 1) * 128, 0:OFF_GATES], 8, OFF_GATES, stg)
            bdcs = P.sb("bdcs", [128, 2, 128], BF16); ld(bdcs, bdcs[:], I["bdcs"][:, :, :])
            cw = P.sb("hcw", [128, 6, 3]); cb = P.sb("hcb", [128, 6])
            ld(cw, cw[:], I["hy_cw"][l]); ld(cb, cb[:], I["hy_cb"][l])
            hws = [P.sb(f"hw{i}", [128, 8, 514], BF16) for i in range(2)]
            pms = [P.ps(f"pmA{i}", [128, 512]) for i in range(2)]
            ph = P.ps("phA", [128, 2])
            pj = P.sb("pj", [128, NTM]); uT = P.sb("uT", [128, 2, 512], BF16)
            zts = [P.sb(f"ztA{i}", [128, 512], BF16) for i in range(2)]
            cvs = [P.sb(f"cvA{i}", [128, 512]) for i in range(2)]
            tmcols = ((256, 512), (768, 512), (2048, 512), (2560, 288))
            zFv = zF
            npm = 0
            for b in range(NB):
                hw = hws[b % 2]
                load_window(hw, b)
                t0 = b * 512
                for s in range(4):
                    o = 0
                    for (c0, wd) in tmcols:
                        pm = pms[npm % 2]; npm += 1
                        for kc in range(8):
                            pe(lambda e, kc=kc, pm=pm, c0=c0, wd=wd: e.matmul(pm[:, 0:wd], lhsT=hw[:, kc, 1 + s * 128:1 + (s + 1) * 128],
                                                                                rhs=wA[:, kc, c0:c0 + wd], start=(kc == 0), stop=(kc == 7)),
                               [hw, wA], [pm])
                        if (npm % 2) == 0:
                            dve(lambda e, pm=pm, o=o, wd=wd: e.tensor_copy(out=pj[:, o:o + wd], in_=pm[:, 0:wd]), [pm], [pj])
                        else:
                            act(lambda e, pm=pm, o=o, wd=wd: e.activation(out=pj[:, o:o + wd], in_=pm[:, 0:wd], func=AF.Copy), [pm], [pj])
                        o += wd
                        yield
                    stq(projT[t0 + s * 128:t0 + (s + 1) * 128, :], pj, pj[:], dB["projT"])
                for ch in range(2):
                    pm = pms[npm % 2]; npm += 1
                    for kc in range(8):
                        pe(lambda e, kc=kc, pm=pm, ch=ch: e.matmul(pm[:], lhsT=wA[:, kc, ch * 128:(ch + 1) * 128], rhs=hw[:, kc, 1:513],
                                                                     start=(kc == 0), stop=(kc == 7)), [hw, wA], [pm])
                    act(lambda e, pm=pm, ch=ch: e.activation(out=uT[:, ch, :], in_=pm[:], func=AF.Copy), [pm], [uT])
                    yield
                for ri in range(2):
                    for ch in range(2):
                        pm = pms[npm % 2]; zt = zts[npm % 2]; npm += 1
                        pe(lambda e, pm=pm, ri=ri, ch=ch: e.matmul(pm[:], lhsT=bdcs[:, ri, :], rhs=uT[:, ch, :], start=True, stop=True), [bdcs, uT], [pm])
                        act(lambda e, pm=pm, zt=zt: e.activation(out=zt[:], in_=pm[:], func=AF.Copy), [pm], [zt])
                        stq(zFv[ri, ch * 128:(ch + 1) * 128, t0:t0 + 512], zt, zt[:], dB["zF"])
                        yield
                for ch in range(6):
                    pm = pms[npm % 2]; cv = cvs[npm % 2]; npm += 1
                    c0 = OFF_HY + ch * 128
                    for kc in range(8):
                        pe(lambda e, kc=kc, pm=pm, c0=c0: e.matmul(pm[:], lhsT=wA[:, kc, c0:c0 + 128], rhs=hw[:, kc, 1:513],
                                                                     start=(kc == 0), stop=(kc == 7)), [hw, wA], [pm])
                    for kc in range(8):
                        pe(lambda e, kc=kc, c0=c0: e.matmul(ph[:], lhsT=wA[:, kc, c0:c0 + 128], rhs=hw[:, kc, 0:514:513],
                                                              start=(kc == 0), stop=(kc == 7)), [hw, wA], [ph])
                    conv3_fm(cv[:], pm, ph, cw, cb, ch, [pm, ph, cw, cb], cv)
                    stq(uhF[ch * 128:(ch + 1) * 128, t0:t0 + 512], cv, cv[:], dB["uhF"])
                    yield
            yield 'finished'
            P.pop()

        def rms_residual(py, xt, sq, ss, rs, tmpo, outt, seg, which, dst_ap, dst_buf):
            act(lambda e: e.activation(out=sq[:], in_=py[:], func=AF.Square, accum_out=ss[:, 0:1]), [py], [sq, ss])
            act(lambda e: e.activation(out=rs[:], in_=ss[:], func=AF.Sqrt, scale=1.0 / D, bias=epsb[:, 0:1]), [ss, epsb], [rs])
            dve(lambda e: e.reciprocal(out=rs[:], in_=rs[:]), [rs], [rs])
            dve(lambda e: e.scalar_tensor_tensor(out=tmpo[:], in0=py[:], scalar=rs[:, 0:1], in1=gtb[:, seg, which, :], op0=ALU.mult, op1=ALU.mult),
                [py, rs, gtb], [tmpo])
            dve(lambda e: e.tensor_tensor(out=outt[:], in0=tmpo[:], in1=xt[:], op=ALU.add), [tmpo, xt], [outt])
            stq(dst_ap, outt, outt[:], dst_buf)

        def phase_C(l, src_ap2d, src_buf):
            P.push()
            wbr = P.sb("wbr", [128, 8, D], BF16); wout = P.sb("wout", [128, 8, D], BF16); wg = P.sb("wg", [128, 8, 4 * D], BF16)
            stg = [P.sb(f"stgC{i}", [128, 4 * D]) for i in range(2)]
            load_w_bf16(wbr, lambda k: wbr[:, k, :], lambda k: I["w_branch"][l, k * 128:(k + 1) * 128, :], 8, D, stg)
            load_w_bf16(wout, lambda k: wout[:, k, :], lambda k: I["w_out"][l, k * 128:(k + 1) * 128, :], 8, D, stg)
            load_w_bf16(wg, lambda k: wg[:, k, :], lambda k: I["w_in"][l, k * 128:(k + 1) * 128, OFF_GATES:OFF_GATES + 4 * D], 8, 4 * D, stg)
            xts = [P.sb(f"xtC{i}", [128, D]) for i in range(2)]
            hts = [P.sb(f"htC{i}", [128, 8, 128], BF16) for i in range(2)]
            brs = [P.sb(f"brC{i}", [128, 8, 128], BF16) for i in range(2)]
            pgs = [P.ps(f"pg{i}", [128, 512]) for i in range(2)]; pbs = [P.ps(f"pb{i}", [128, 512]) for i in range(2)]
            py = P.ps("py", [128, D]); ptr = P.ps("ptrC", [128, 8, 128], BF16)
            sigs = [P.sb(f"sig{i}", [128, 512]) for i in range(2)]; tms = [P.sb(f"tmc{i}", [128, 512]) for i in range(2)]
            merged = P.sb("merged", [128, D]); tmpm = P.sb("tmpm", [128, D]); mb = P.sb("mb", [128, D], BF16)
            mT = P.sb("mT", [128, 8, 128], BF16); sq = P.sb("sqC", [128, D]); ss = P.sb("ssC", [128, 1]); rs = P.sb("rsC", [128, 1])
            outt = P.sb("outC", [128, D])
            hFv = hF.rearrange("(c p) t -> p c t", p=128); brv = brF.rearrange("(k p) t -> p k t", p=128)
            for it in range(NT):
                xt, ht, brt = xts[it % 2], hts[it % 2], brs[it % 2]
                seg = 0 if it < NT // 2 else 1
                ld(xt, xt[:], src_ap2d[it * 128:(it + 1) * 128, :], src=src_buf)
                ld(ht, ht[:], hFv[:, :, 1 + it * 128:1 + (it + 1) * 128], src=dB["hF"])
                ld(brt, brt[:], brv[:, :, it * 128:(it + 1) * 128], src=dB["brF"])
                for br in range(4):
                    for cb in range(2):
                        u = br * 2 + cb
                        pgu, pbu, sgu, tmu = pgs[u % 2], pbs[u % 2], sigs[u % 2], tms[u % 2]
                        for kc in range(8):
                            pe(lambda e, kc=kc, cb=cb, pgu=pgu, br=br: e.matmul(pgu[:], lhsT=ht[:, kc, :],
                                                                               rhs=wg[:, kc, br * D + cb * 512:br * D + (cb + 1) * 512], start=(kc == 0), stop=(kc == 7)),
                               [ht, wg], [pgu])
                        for k2 in range(2):
                            pe(lambda e, k2=k2, cb=cb, pbu=pbu, br=br: e.matmul(pbu[:], lhsT=brt[:, br * 2 + k2, :],
                                                                               rhs=wbr[:, br * 2 + k2, cb * 512:(cb + 1) * 512], start=(k2 == 0), stop=(k2 == 1)),
                               [brt, wbr], [pbu])
                        act(lambda e, pgu=pgu, sgu=sgu: e.activation(out=sgu[:], in_=pgu[:], func=AF.Sigmoid), [pgu], [sgu])
                        mslice = merged[:, cb * 512:(cb + 1) * 512]
                        if br == 0:
                            dve(lambda e, sgu=sgu, pbu=pbu, mslice=mslice: e.tensor_tensor(out=mslice, in0=sgu[:], in1=pbu[:], op=ALU.mult), [sgu, pbu], [merged])
                        else:
                            dve(lambda e, sgu=sgu, pbu=pbu, tmu=tmu: e.tensor_tensor(out=tmu[:], in0=sgu[:], in1=pbu[:], op=ALU.mult), [sgu, pbu], [tmu])
                            dve(lambda e, tmu=tmu, mslice=mslice: e.tensor_tensor(out=mslice, in0=mslice, in1=tmu[:], op=ALU.add), [merged, tmu], [merged])
                act(lambda e: e.activation(out=mb[:], in_=merged[:], func=AF.Copy), [merged], [mb])
                for c8 in range(8):
                    pe(lambda e, c8=c8: e.transpose(ptr[:, c8, :], mb[:, c8 * 128:(c8 + 1) * 128], identb[:]), [mb, identb], [ptr])
                dve(lambda e: e.tensor_copy(out=mT[:], in_=ptr[:]), [ptr], [mT])
                for cb in range(2):
                    for kc in range(8):
                        pe(lambda e, kc=kc, cb=cb: e.matmul(py[:, cb * 512:(cb + 1) * 512], lhsT=mT[:, kc, :], rhs=wout[:, kc, cb * 512:(cb + 1) * 512],
                                                           start=(kc == 0), stop=(kc == 7)), [mT, wout], [py])
                rms_residual(py, xt, sq, ss, rs, tmpm, outt, seg, 0, xmid[it * 128:(it + 1) * 128, :], dB["xmid"])
            P.pop()

        def phase_D(l, dst_ap2d, dst_buf):
            P.push()
            wup = P.sb("wup", [128, 8, 2 * DFF], BF16)
            stg = [P.sb(f"stgD{i}", [128, 1408]) for i in range(2)]
            for part in range(4):
                c0 = part * 1408
                load_w_bf16(wup, lambda k, c0=c0: wup[:, k, c0:c0 + 1408], lambda k, c0=c0: I["ffn_up"][l, k * 128:(k + 1) * 128, c0:c0 + 1408], 8, 1408, stg)
            cw = P.sb("fcw", [128, 44, 3]); cb_ = P.sb("fcb", [128, 44])
            ld(cw, cw[:], I["ffn_cw"][l]); ld(cb_, cb_[:], I["ffn_cb"][l])
            hws = [P.sb(f"hwD{i}", [128, 8, 514], BF16) for i in range(2)]
            pms = [P.ps(f"pmD{i}", [128, 512]) for i in range(4)]; phs = [P.ps(f"phD{i}", [128, 2]) for i in range(4)]
            cvs = [P.sb(f"cvD{i}", [128, 512]) for i in range(4)]; gls = [P.sb(f"glD{i}", [128, 512]) for i in range(2)]
            gts = [P.sb(f"gtD{i}", [128, 512], BF16) for i in range(2)]
            for b in range(NB):
                hw = hws[b % 2]
                load_window(hw, b)
                for pc in range(22):
                    for wi in range(2):
                        ci = pc + 22 * wi
                        bi = 2 * (pc % 2) + wi
                        pm, ph, cv = pms[bi], phs[bi], cvs[bi]
                        for kc in range(8):
                            pe(lambda e, kc=kc, pm=pm, ci=ci: e.matmul(pm[:], lhsT=wup[:, kc, ci * 128:(ci + 1) * 128], rhs=hw[:, kc, 1:513],
                                                                         start=(kc == 0), stop=(kc == 7)), [hw, wup], [pm])
                        for kc in range(8):
                            pe(lambda e, kc=kc, ph=ph, ci=ci: e.matmul(ph[:], lhsT=wup[:, kc, ci * 128:(ci + 1) * 128], rhs=hw[:, kc, 0:514:513],
                                                                         start=(kc == 0), stop=(kc == 7)), [hw, wup], [ph])
                        conv3_fm(cv[:], pm, ph, cw, cb_, ci, [pm, ph, cw, cb_], cv)
                    gt = gts[pc % 2]; gl = gls[pc % 2]; cg_, cv_ = cvs[2 * (pc % 2)], cvs[2 * (pc % 2) + 1]
                    act(lambda e, gl=gl, cg_=cg_: e.activation(out=gl[:], in_=cg_[:], func=AF.Gelu_apprx_tanh), [cg_], [gl])
                    dve(lambda e, gt=gt, gl=gl, cv_=cv_: e.tensor_tensor(out=gt[:], in0=gl[:], in1=cv_[:], op=ALU.mult), [gl, cv_], [gt])
                    stq(gFF[pc * 128:(pc + 1) * 128, b * 512:(b + 1) * 512], gt, gt[:], dB["gFF"])
            P.pop()
            P.push()
            wdn = P.sb("wdn", [128, 22, D], BF16)
            stg = [P.sb(f"stgE{i}", [128, D]) for i in range(2)]
            load_w_bf16(wdn, lambda k: wdn[:, k, :], lambda k: I["ffn_down"][l, k * 128:(k + 1) * 128, :], 22, D, stg)
            xts = [P.sb(f"xtE{i}", [128, D]) for i in range(2)]
            ggs = [P.sb(f"ggE{i}", [128, 22, 512], BF16) for i in range(2)]
            py = P.ps("pyE", [128, D]); sq = P.sb("sqE", [128, D]); ss = P.sb("ssE", [128, 1]); rs = P.sb("rsE", [128, 1])
            tmpo = P.sb("tmpE", [128, D]); outt = P.sb("outE", [128, D])
            gv = gFF.rearrange("(k p) t -> p k t", p=128)
            for it in range(NT):
                xt, gg = xts[it % 2], ggs[(it // 4) % 2]
                q4 = it % 4
                seg = 0 if it < NT // 2 else 1
                ld(xt, xt[:], xmid[it * 128:(it + 1) * 128, :], src=dB["xmid"])
                if q4 == 0:
                    ld(gg, gg[:], gv[:, :, it * 128:(it + 4) * 128], src=dB["gFF"])
                for cb in range(2):
                    for pc in range(22):
                        pe(lambda e, pc=pc, cb=cb: e.matmul(py[:, cb * 512:(cb + 1) * 512], lhsT=gg[:, pc, q4 * 128:(q4 + 1) * 128], rhs=wdn[:, pc, cb * 512:(cb + 1) * 512],
                                                           start=(pc == 0), stop=(pc == 21)), [gg, wdn], [py])
                rms_residual(py, xt, sq, ss, rs, tmpo, outt, seg, 1, dst_ap2d[it * 128:(it + 1) * 128, :], dst_buf)
            P.pop()

        def phase_fnet(l):
            P.push()
            Cg = 64
            m1 = P.sb("fm1", [NT, 2, 2 * NS], BF16); ld(m1, m1[:], I["fn_m1"][:, :, :])
            zin = P.sb("zin", [NT, 2, Cg, 128], BF16); A = P.sb("fA", [128, NS, 2, Cg], BF16)
            osb = P.sb("osb", [Cg, T], BF16); osbv = osb[:].rearrange("c (k j) -> c k j", j=NT)
            ps1s = [P.ps(f"fps1{i}", [128, 2, 2 * NS]) for i in range(2)]
            ps2s = [P.ps(f"fps2{i}", [Cg, 4, 128]) for i in range(2)]
            m2s = [P.sb(f"fm2{i}", [128, 4, 2, 2, 128], BF16) for i in range(2)]
            n1 = 0
            for g in range(256 // Cg):
                c0 = g * Cg
                for ri in range(2):
                    ld(zin, zin[:, ri, :, :], zF[ri, c0:c0 + Cg, :].rearrange("c (g n) -> g c n", n=128), src=dB["zF"])
                for c in range(0, Cg, 2):
                    ps1 = ps1s[n1 % 2]; n1 += 1
                    for cc in range(2):
                        pe(lambda e, cc=cc, ps1=ps1: e.matmul(ps1[:, cc, :], lhsT=zin[:, 0, c + cc, :], rhs=m1[:, 0, :], start=True, stop=False), [zin, m1], [ps1])
                        pe(lambda e, cc=cc, ps1=ps1: e.matmul(ps1[:, cc, :], lhsT=zin[:, 1, c + cc, :], rhs=m1[:, 1, :], start=False, stop=True), [zin, m1], [ps1])
                    for ri in range(2):
                        src_ap = ps1[:, :, ri * NS:(ri + 1) * NS].rearrange("p c s -> p s c")
                        if ri == 0:
                            dve(lambda e, src_ap=src_ap: e.tensor_copy(out=A[:, :, 0, c:c + 2], in_=src_ap), [ps1], [A])
                        else:
                            act(lambda e, src_ap=src_ap: e.activation(out=A[:, :, 1, c:c + 2], in_=src_ap, func=AF.Copy), [ps1], [A])
                m2v = I["fn_m2"].rearrange("j p s r k -> p j s r k")
                for j in range(NT):
                    m2t = m2s[(j // 4) % 2]
                    if j % 4 == 0:
                        ld(m2t, m2t[:], m2v[:, j:j + 4, :, :, :])
                    ps2 = ps2s[(j // 4) % 2]
                    k = 0
                    for s_ in range(2):
                        for ri in range(2):
                            pe(lambda e, s_=s_, ri=ri, k=k, ps2=ps2, m2t=m2t: e.matmul(ps2[:, j % 4, :], lhsT=A[:, s_ * NT + j, ri, :], rhs=m2t[:, j % 4, s_, ri, :],
                                                                                      start=(k == 0), stop=(k == 3)), [A, m2t], [ps2])
                            k += 1
                    if j % 4 == 3:
                        j0 = j - 3
                        src_ap = ps2[:, :, :].rearrange("c j k -> c k j")
                        if (j // 4) % 2 == 0:
                            dve(lambda e, src_ap=src_ap, j0=j0: e.tensor_copy(out=osbv[:, :, j0:j0 + 4], in_=src_ap), [ps2], [osb])
                        else:
                            act(lambda e, src_ap=src_ap, j0=j0: e.activation(out=osbv[:, :, j0:j0 + 4], in_=src_ap, func=AF.Copy), [ps2], [osb])
                stq(brF[c0:c0 + Cg, :], osb, osb[:], dB["brF"])
            P.pop()

        def drain(g):
            for _ in g:
                pass

        def run_concurrent(primary, secondary, ratio=int(os.environ.get("CONC_RATIO", "1"))):
            p_alive, s_alive, p_fin = True, True, False
            while p_alive or s_alive:
                for _ in range(PRIM_STEPS):
                    if p_alive and not (p_fin and s_alive):
                        try:
                            if next(primary) == 'finished':
                                p_fin = True
                        except StopIteration:
                            p_alive = False
                for _ in range(ratio):
                    if s_alive:
                        try:
                            next(secondary)
                        except StopIteration:
                            s_alive = False

        def run_pipelined(make_gen, order, depth=PIPE_DEPTH):
            active = []
            order = list(order)
            pos = 0
            while pos < len(order) or active:
                if pos < len(order) and len(active) < depth and all(e[1] == 'second' for e in active):
                    active.append([make_gen(order[pos]), 'first']); pos += 1
                for ent in list(active):
                    if ent[1] == 'waiting':
                        if active[0] is ent:
                            ent[1] = 'second'
                        else:
                            continue
                    try:
                        v = next(ent[0])
                        if v == 'prev_done' and ent[1] == 'first':
                            ent[1] = 'second' if active[0] is ent else 'waiting'
                    except StopIteration:
                        active.remove(ent)

        def phase_scan(l, ret):
            P.push()
            nkc, hpc = (2, 2) if ret else (1, 4)
            KW = nkc * 128
            col0, width = (0, 1024) if ret else (1024, 800)
            qo, ko, vo, go = (0, 256, 512, 768) if ret else (0, 128, 256, 512)
            lro = 768
            kbr = 1 if ret else 3
            tri = P.sb("tri", [128, 6, 128]); ld(tri, tri[:], I["tri"][:, :, :])
            mh = P.sb("mh", [128, hpc]); ld(mh, mh[:], I["mh_ret" if ret else "mh_gla"][:, :])
            bd = P.sb("bd", [128, nkc, 256]); ld(bd, bd[:], I["bd_ret" if ret else "bd_gla"][:, :, :])
            gn = P.sb("gn", [128, 256]); ld(gn, gn[:], I["ret_gn" if ret else "gla_gn"][l].partition_broadcast(128))
            lns = math.log(32.0 ** -0.5)
            if ret:
                Ec = P.sb("E", [128, nkc, 6, 128]); Epc = P.sb("Epad", [128, nkc, 2, hpc, 128])
                dtokc = P.sb("dtok", [128, 2, KW]); decc = P.sb("dec", [128, nkc, 2])
                ld(Ec, Ec[:], I["ret_e"][:, :, :, :]); ld(dtokc, dtokc[:], I["ret_tok"][:, :, :]); ld(decc, decc[:], I["ret_dec"][:, :, :])
                for c in range(nkc):
                    for d_ in range(2):
                        dve(lambda e, c=c, d_=d_: e.tensor_tensor(out=Epc[:, c, d_, :, :], in0=Ec[:, c, 1 + 2 * d_, :].unsqueeze(1).broadcast_to([128, hpc, 128]),
                                                                  in1=mh[:].unsqueeze(2).broadcast_to([128, hpc, 128]), op=ALU.mult), [Ec, mh], [Epc])
            else:
                wd = P.sb("wd", [33, 256]); ld(wd, wd[:], I["gla_wd"][l])
                lnsb = P.sb("lnsb", [128, 1])
                dve(lambda e: e.memset(lnsb[:], lns), [], [lnsb])
                psZ = P.ps("psZ", [128, 512]); psB = P.ps("psB", [128, 4, 128])

            class TS:
                pass

            def mk_set(i):
                S = TS()
                S.pt = P.sb(f"pt{i}", [128, width])
                S.Qt = P.sb(f"Qt{i}", [128, nkc, 4, 128], BF16); S.Kp = P.sb(f"Kp{i}", [128, nkc, 2, hpc, 128], BF16)
                S.khat = P.sb(f"khat{i}", [128, 2, KW], BF16); S.Vb = P.sb(f"Vb{i}", [128, 256], BF16)
                S.st1 = P.sb(f"st1{i}", [128, 4, 128]); S.st2 = P.sb(f"st2{i}", [128, 4, 128]); S.PT = P.sb(f"PT{i}", [128, 4, 128], BF16)
                S.hn1 = P.sb(f"hn1{i}", [128, 4]); S.hn2 = P.sb(f"hn2{i}", [128, 4]); S.oc = P.sb(f"oc{i}", [128, 4, 64]); S.osq = P.sb(f"osq{i}", [128, 4, 64])
                S.sg = P.sb(f"sg{i}", [128, 256]); S.resb = P.sb(f"resb{i}", [128, 256], BF16); S.resT = P.sb(f"resT{i}", [128, 2, 128], BF16)
                S.tU = P.sb(f"tU{i}", [128, nkc, 256])
                if ret:
                    S.rot = P.sb(f"rot{i}", [128, 2, 32]); S.qkr = P.sb(f"qkr{i}", [128, 8, 64])
                    S.rt1 = P.sb(f"rt1{i}", [128, 8, 32]); S.rt2 = P.sb(f"rt2{i}", [128, 8, 32])
                    S.E, S.Epad, S.dtok, S.dec = Ec, Epc, dtokc, decc
                else:
                    S.lrT = P.sb(f"lrT{i}", [33, 128]); dve(lambda e: e.memset(S.lrT[:], 1.0), [], [S.lrT])
                    S.et = P.sb(f"et{i}", [128, 256]); S.lt = P.sb(f"lt{i}", [128, 256]); S.bsb = P.sb(f"bsb{i}", [128, 4, 128])
                    S.mids = P.sb(f"mids{i}", [128, 4])
                    S.E = P.sb(f"E{i}", [128, nkc, 6, 128]); S.Epad = P.sb(f"Epad{i}", [128, nkc, 2, hpc, 128])
                    S.dtok = P.sb(f"dtok{i}", [128, 2, KW]); S.dec = P.sb(f"dec{i}", [128, nkc, 2])
                return S

            sets = [mk_set(0), mk_set(1)]
            psT = P.ps("psT", [128, 2 * nkc, 128])
            psS = [P.ps(f"psS{i}", [128, 4, 128]) for i in range(2)]
            psO = P.ps("psO", [128, 256]); psU = P.ps("psU", [128, nkc, 256]); psR = P.ps("psR", [128, 2, 128], BF16)
            Sm = P.sb("Sm", [128, nkc, 256]); Sbf = P.sb("Sbf", [128, nkc, 256], BF16); Sball = P.sb("Sball", [128, NT, nkc, 256], BF16)
            brv = brF.rearrange("(k p) t -> p k t", p=128)

            def prep(n, S, full=True):
                pt = S.pt
                ld(pt, pt[:], projT[n * 128:(n + 1) * 128, col0:col0 + width], src=dB["projT"])
                if ret:
                    rot, qkr, rt1, rt2 = S.rot, S.qkr, S.rt1, S.rt2
                    ld(rot, rot[:], I["rot"][n * 128:(n + 1) * 128, :, :])
                    h0 = 0 if full else 4
                    nh_ = 8 - h0
                    src = pt[:, 0:512].rearrange("p (h d) -> p h d", d=64)[:, h0:8, :]
                    cosb = rot[:, 0, :].unsqueeze(1).broadcast_to([128, nh_, 32]); sinb = rot[:, 1, :].unsqueeze(1).broadcast_to([128, nh_, 32])
                    qkr_full = qkr
                    qkr = qkr[:, h0:8, :]; rt1 = rt1[:, h0:8, :]; rt2 = rt2[:, h0:8, :]
                    gps(lambda e: e.tensor_tensor(out=rt1, in0=src[:, :, 0:32], in1=cosb, op=ALU.mult), [pt, rot], [S.rt1])
                    gps(lambda e: e.tensor_tensor(out=rt2, in0=src[:, :, 32:64], in1=sinb, op=ALU.mult), [pt, rot], [S.rt2])
                    gps(lambda e: e.tensor_tensor(out=qkr[:, :, 0:32], in0=rt1, in1=rt2, op=ALU.subtract), [S.rt1, S.rt2], [S.qkr])
                    gps(lambda e: e.tensor_tensor(out=rt1, in0=src[:, :, 0:32], in1=sinb, op=ALU.mult), [pt, rot, S.qkr], [S.rt1])
                    gps(lambda e: e.tensor_tensor(out=rt2, in0=src[:, :, 32:64], in1=cosb, op=ALU.mult), [pt, rot, S.qkr], [S.rt2])
                    gps(lambda e: e.tensor_tensor(out=qkr[:, :, 32:64], in0=rt1, in1=rt2, op=ALU.add), [S.rt1, S.rt2], [S.qkr])
                    qk = qkr_full[:].rearrange("p h d -> p (h d)")
                    S.q_tok, S.k_tok, S.qkb = qk[:, 0:256], qk[:, 256:512], qkr_full
                else:
                    lrT, et, lt, bsb, mids, E, Epad, dtok, dec = S.lrT, S.et, S.lt, S.bsb, S.mids, S.E, S.Epad, S.dtok, S.dec
                    S.q_tok, S.k_tok, S.qkb = pt[:, qo:qo + 128], pt[:, ko:ko + 128], pt
                    pe(lambda e: e.transpose(psZ[0:32, 256:384], pt[:, lro:lro + 32], identf[:]), [pt, identf], [psZ])
                    yield
                    dve(lambda e: e.tensor_copy(out=lrT[0:32, :], in_=psZ[0:32, 256:384]), [psZ], [lrT])
                    yield
                    pe(lambda e: e.matmul(psZ[:, 0:256], lhsT=lrT[:], rhs=wd[:], start=True, stop=True), [lrT, wd], [psZ])
                    yield
                    act(lambda e: e.activation(out=et[:], in_=psZ[:, 0:256], func=AF.Exp, scale=-1.0), [psZ], [et])
                    act(lambda e: e.activation(out=lt[:], in_=et[:], func=AF.Ln, bias=1.0), [et], [lt])
                    yield
                    pe(lambda e: e.matmul(psB[:, 0, :], lhsT=lt[:, 0:128], rhs=tri[:, 2, :], start=True, stop=True), [lt, tri], [psB])
                    pe(lambda e: e.matmul(psB[:, 1, :], lhsT=lt[:, 128:256], rhs=tri[:, 3, :], start=True, stop=True), [lt, tri], [psB])
                    pe(lambda e: e.matmul(psB[:, 2, :], lhsT=tri[:, 4, :], rhs=lt[:, 0:128], start=True, stop=True), [lt, tri], [psB])
                    pe(lambda e: e.matmul(psB[:, 3, :], lhsT=tri[:, 5, :], rhs=lt[:, 128:256], start=True, stop=True), [lt, tri], [psB])
                    yield
                    dve(lambda e: e.tensor_copy(out=bsb[:], in_=psB[:]), [psB], [bsb])
                    dve(lambda e: e.tensor_scalar(out=mids[:, 0:2], in0=bsb[:, 0:2, 64], scalar1=-1.0, scalar2=lns, op0=ALU.mult, op1=ALU.add), [bsb], [mids])
                    dve(lambda e: e.tensor_copy(out=mids[:, 2:4], in_=bsb[:, 0:2, 64]), [bsb], [mids])
                    yield
                    for d_ in (range(2) if full else (1,)):
                        if full:
                            act(lambda e, d_=d_: e.activation(out=E[:, 0, 2 * d_, :], in_=bsb[:, d_, :], func=AF.Exp, bias=mids[:, d_:d_ + 1]), [bsb, mids], [E])
                            act(lambda e, d_=d_: e.activation(out=E[:, 0, 2 * d_ + 1, :], in_=bsb[:, d_, :], func=AF.Exp, scale=-1.0, bias=mids[:, 2 + d_:3 + d_]), [bsb, mids], [E])
                            act(lambda e, d_=d_: e.activation(out=E[:, 0, 4 + d_, :], in_=bsb[:, d_, :], func=AF.Exp, bias=lnsb[:, 0:1]), [bsb, lnsb], [E])
                        act(lambda e, d_=d_: e.activation(out=dtok[:, d_, :], in_=bsb[:, 2 + d_, :], func=AF.Exp), [bsb], [dtok])
                    if full:
                        act(lambda e: e.activation(out=dec[:, 0, 0:1], in_=bsb[:, 0, 127:128], func=AF.Exp), [bsb], [dec])
                    act(lambda e: e.activation(out=dec[:, 0, 1:2], in_=bsb[:, 1, 0:1], func=AF.Exp), [bsb], [dec])
                    yield
                    if full:
                        for d_ in range(2):
                            dve(lambda e, d_=d_: e.tensor_tensor(out=Epad[:, 0, d_, :, :], in0=E[:, 0, 1 + 2 * d_, :].unsqueeze(1).broadcast_to([128, hpc, 128]),
                                                                 in1=mh[:].unsqueeze(2).broadcast_to([128, hpc, 128]), op=ALU.mult), [E, mh], [Epad])
                act(lambda e: e.activation(out=S.Vb[:], in_=pt[:, vo:vo + 256], func=AF.Copy), [pt], [S.Vb])
                for d_ in (range(2) if full else (1,)):
                    gps(lambda e, d_=d_: e.tensor_tensor(out=S.khat[:, d_, :], in0=S.k_tok, in1=S.dtok[:, d_, :], op=ALU.mult), [S.qkb, S.dtok], [S.khat])
                yield

            def state_update(d_, S):
                for c in range(nkc):
                    pe(lambda e, c=c: e.matmul(psU[:, c, :], lhsT=S.khat[:, d_, c * 128:(c + 1) * 128], rhs=S.Vb[:], start=True, stop=True), [S.khat, S.Vb], [psU])
                yield
                dve(lambda e: e.tensor_tensor(out=S.tU[:], in0=psU[:], in1=bd[:], op=ALU.mult), [psU, bd], [S.tU])
                for c in range(nkc):
                    dve(lambda e, c=c: e.scalar_tensor_tensor(out=Sm[:, c, :], in0=Sm[:, c, :], scalar=S.dec[:, c, d_:d_ + 1], in1=S.tU[:, c, :],
                                                              op0=ALU.mult, op1=ALU.add), [Sm, S.dec, S.tU], [Sm])

            def keep_mul():
                dve(lambda e: e.tensor_scalar(out=Sm[:], in0=Sm[:], scalar1=scal[:, 0:1], scalar2=None, op0=ALU.mult), [Sm, scal], [Sm])

            def gen1(n):
                S = sets[n % 2]
                yield from prep(n, S, full=False)
                yield 'prev_done'
                act(lambda e: e.activation(out=Sball[:, n, :, :], in_=Sm[:], func=AF.Copy), [Sm], [Sball])
                yield from state_update(1, S)
                if n == NT // 2:
                    keep_mul()

            dve(lambda e: e.memset(Sm[:], 0.0), [], [Sm])
            run_pipelined(gen1, reversed(range(NT)), depth=int(os.environ.get('PIPE1', '2')))

            def gen2(n):
                S = sets[n % 2]
                if PD_POS == 0:
                    yield 'prev_done'
                yield from prep(n, S)
                if PD_POS == 1:
                    yield 'prev_done'
                Qt, Kp, PT, Vb, E, Epad = S.Qt, S.Kp, S.PT, S.Vb, S.E, S.Epad
                for c in range(nkc):
                    pe(lambda e, c=c: e.transpose(psT[:, c, :], S.q_tok[:, c * 128:(c + 1) * 128], identf[:]), [S.qkb, identf], [psT])
                    pe(lambda e, c=c: e.transpose(psT[:, nkc + c, :], S.k_tok[:, c * 128:(c + 1) * 128], identf[:]), [S.qkb, identf], [psT])
                yield
                if PD_POS == 2:
                    yield 'prev_done'
                for c in range(nkc):
                    for vi, ei in enumerate((0, 2, 4, 5)):
                        dve(lambda e, c=c, vi=vi, ei=ei: e.tensor_tensor(out=Qt[:, c, vi, :], in0=psT[:, c, :], in1=E[:, c, ei, :], op=ALU.mult), [psT, E], [Qt])
                    for d_ in range(2):
                        dve(lambda e, c=c, d_=d_: e.tensor_tensor(out=Kp[:, c, d_, :, :], in0=psT[:, nkc + c, :].unsqueeze(1).broadcast_to([128, hpc, 128]),
                                                                  in1=Epad[:, c, d_, :, :], op=ALU.mult), [psT, Epad], [Kp])
                yield
                if PD_POS == 3:
                    yield 'prev_done'
                for d_ in range(2):
                    for c in range(nkc):
                        for hh in range(hpc):
                            pe(lambda e, d_=d_, c=c, hh=hh: e.matmul(psS[d_][:, c * hpc + hh, :], lhsT=Kp[:, c, d_, hh, :], rhs=Qt[:, c, d_, :], start=True, stop=True),
                               [Kp, Qt], [psS[d_]])
                yield
                if PD_POS == 4:
                    yield 'prev_done'
                dve(lambda e: e.tensor_tensor(out=S.st1[:], in0=psS[0][:], in1=tri[:, 0, :].unsqueeze(1).broadcast_to([128, 4, 128]), op=ALU.mult), [psS[0], tri], [S.st1])
                dve(lambda e: e.tensor_tensor(out=S.st2[:], in0=psS[1][:], in1=tri[:, 1, :].unsqueeze(1).broadcast_to([128, 4, 128]), op=ALU.mult), [psS[1], tri], [S.st2])
                gps(lambda e: e.tensor_tensor(out=PT[:], in0=S.st1[:], in1=S.st2[:], op=ALU.add), [S.st1, S.st2], [PT])
                yield 'prev_done'
                if n == NT // 2:
                    keep_mul()
                act(lambda e: e.activation(out=Sbf[:], in_=Sm[:], func=AF.Copy), [Sm], [Sbf])
                yield
                for h_ in range(4):
                    c = h_ // hpc
                    hs = slice(h_ * 64, (h_ + 1) * 64)
                    pe(lambda e, c=c, hs=hs: e.matmul(psO[:, hs], lhsT=Qt[:, c, 2, :], rhs=Sbf[:, c, hs], start=True, stop=False), [Qt, Sbf], [psO])
                    pe(lambda e, c=c, hs=hs: e.matmul(psO[:, hs], lhsT=Qt[:, c, 3, :], rhs=Sball[:, n, c, hs], start=False, stop=False), [Qt, Sball], [psO])
                    pe(lambda e, h_=h_, hs=hs: e.matmul(psO[:, hs], lhsT=PT[:, h_, :], rhs=Vb[:, hs], start=False, stop=True), [PT, Vb], [psO])
                yield
                hn1, hn2, oc, osq, sg, resb, resT, pt = S.hn1, S.hn2, S.oc, S.osq, S.sg, S.resb, S.resT, S.pt
                O3 = psO[:].rearrange("p (h d) -> p h d", d=64)
                if ret:
                    dve(lambda e: e.tensor_reduce(out=hn1[:], in_=O3, axis=mybir.AxisListType.X, op=ALU.add), [psO], [hn1])
                    dve(lambda e: e.tensor_scalar(out=hn1[:], in0=hn1[:], scalar1=-1.0 / 64, scalar2=None, op0=ALU.mult), [hn1], [hn1])
                    dve(lambda e: e.tensor_tensor(out=oc[:], in0=O3, in1=hn1[:].unsqueeze(2).broadcast_to([128, 4, 64]), op=ALU.add), [psO, hn1], [oc])
                else:
                    dve(lambda e: e.tensor_copy(out=oc[:], in_=O3), [psO], [oc])
                gps(lambda e: e.tensor_tensor(out=osq[:], in0=oc[:], in1=oc[:], op=ALU.mult), [oc], [osq])
                dve(lambda e: e.tensor_reduce(out=hn2[:], in_=osq[:], axis=mybir.AxisListType.X, op=ALU.add), [osq], [hn2])
                act(lambda e: e.activation(out=sg[:], in_=pt[:, go:go + 256], func=AF.Silu), [pt], [sg])
                act(lambda e: e.activation(out=hn2[:], in_=hn2[:], func=AF.Sqrt, scale=1.0 / 64, bias=epsb[:, 0:1]), [hn2, epsb], [hn2])
                yield
                dve(lambda e: e.reciprocal(out=hn2[:], in_=hn2[:]), [hn2], [hn2])
                gps(lambda e: e.tensor_tensor(out=oc[:], in0=oc[:], in1=hn2[:].unsqueeze(2).broadcast_to([128, 4, 64]), op=ALU.mult), [oc, hn2], [oc])
                gps(lambda e: e.tensor_tensor(out=sg[:], in0=sg[:], in1=gn[:], op=ALU.mult), [sg, gn], [sg])
                gps(lambda e: e.tensor_tensor(out=resb[:], in0=oc[:].rearrange("p h d -> p (h d)"), in1=sg[:], op=ALU.mult), [oc, sg], [resb])
                yield
                for c2_ in range(2):
                    pe(lambda e, c2_=c2_: e.transpose(psR[:, c2_, :], resb[:, c2_ * 128:(c2_ + 1) * 128], identb[:]), [resb, identb], [psR])
                yield from state_update(0, S)
                dve(lambda e: e.tensor_copy(out=resT[:], in_=psR[:]), [psR], [resT])
                stq(brv[:, 2 * kbr:2 * kbr + 2, n * 128:(n + 1) * 128], resT, resT[:], dB["brF"])

            dve(lambda e: e.memset(Sm[:], 0.0), [], [Sm])
            run_pipelined(gen2, range(NT), depth=int(os.environ.get('PIPE2', '2')))
            P.pop()

        def phase_hyena(l, mode='all'):
            NBLK = 2 * T // 512
            Cg = 32
            TWO_PI = 2.0 * math.pi
            def part1():
                P.push()
                w1 = P.sb("w1", [33, 64]); w2 = P.sb("w2", [64, 64]); w3a = P.sb("w3a", [65, 1024])
                c1 = P.sb("c1", [64, 2]); c2_ = P.sb("c2", [64, 2]); fb = P.sb("fb", [64, 2]); delta = P.sb("delta", [128, 2])
                ld(w1, w1[:], I["flt_w1"][l]); ld(w2, w2[:], I["flt_w2"][l]); ld(w3a, w3a[0:64, :], I["flt_w3"][l])
                ld(w3a, w3a[64:65, :], I["flt_b3"][l:l + 1, :])
                ld(c1, c1[:], I["flt_c1"][l]); ld(c2_, c2_[:], I["flt_c2"][l]); ld(delta, delta[:], I["flt_delta"][:, :])
                dve(lambda e: e.tensor_tensor(out=fb[:, 0:1], in0=c1[:, 0:1], in1=c1[:, 1:2], op=ALU.mult), [c1], [fb])
                dve(lambda e: e.tensor_tensor(out=fb[:, 1:2], in0=c2_[:, 0:1], in1=c2_[:, 1:2], op=ALU.mult), [c2_, fb], [fb])
                h2a = P.sb("h2a", [65, 512], BF16); dve(lambda e: e.memset(h2a[:], 1.0), [], [h2a])
                w3b = P.sb("w3b", [65, 1024], BF16); dve(lambda e: e.tensor_copy(out=w3b[:], in_=w3a[:]), [w3a], [w3b])
                h1 = P.sb("h1", [64, 512]); a1 = P.sb("a1", [64, 512]); kk = P.sb("kk", [64, 512])
                nrm = P.sb("nrm", [128, 4, NBLK]); rn = P.sb("rn", [128, 4])
                fts = [P.sb(f"ft{i}", [33, 512]) for i in range(2)]; msks = [P.sb(f"msk{i}", [128, 3, 512]) for i in range(2)]
                win = P.sb("win", [128, 2, 512]); t1 = P.sb("ft1", [128, 512]); t2 = P.sb("ft2", [128, 512]); ab = P.sb("fab", [128, 512])
                gbs = [P.sb(f"gb{i}", [128, 512], BF16) for i in range(2)]
                psh = P.ps("psh", [64, 512]); psf = [P.ps(f"psf{i}", [128, 512]) for i in range(2)]

                def sin_layer(cc, col, dst):
                    dve(lambda e: e.tensor_scalar(out=a1[:], in0=psh[:], scalar1=cc[:, 0:1], scalar2=fb[:, col:col + 1], op0=ALU.mult, op1=ALU.add), [psh, cc, fb], [a1])
                    dve(lambda e: e.tensor_scalar(out=kk[:], in0=a1[:], scalar1=1.0 / TWO_PI, scalar2=MAGIC, op0=ALU.mult, op1=ALU.add), [a1], [kk])
                    dve(lambda e: e.tensor_scalar(out=kk[:], in0=kk[:], scalar1=-MAGIC, scalar2=None, op0=ALU.add), [kk], [kk])
                    dve(lambda e: e.scalar_tensor_tensor(out=a1[:], in0=kk[:], scalar=-TWO_PI, in1=a1[:], op0=ALU.mult, op1=ALU.add), [kk, a1], [a1])
                    act(lambda e: e.activation(out=dst, in_=a1[:], func=AF.Sin), [a1], [h1 if dst is not None and cc is c1 else h2a])

                ng = 0
                for blk in range(NBLK):
                    m0 = blk * 512
                    ft, msk = fts[blk % 2], msks[blk % 2]
                    ld(ft, ft[:], I["flt_feat"][:, m0:m0 + 512])
                    for r_ in range(3):
                        ld(msk, msk[:, r_, :], I["flt_msk"][r_, m0:m0 + 512].partition_broadcast(128))
                    pe(lambda e: e.matmul(psh[:], lhsT=w1[:], rhs=ft[:], start=True, stop=True), [w1, ft], [psh])
                    sin_layer(c1, 0, h1[:])
                    yield
                    pe(lambda e: e.matmul(psh[:], lhsT=w2[:], rhs=h1[:], start=True, stop=True), [w2, h1], [psh])
                    sin_layer(c2_, 1, h2a[0:64, :])
                    yield
                    for ch in range(2):
                        act(lambda e, ch=ch: e.activation(out=win[:, ch, :], in_=msk[:, 2, :], func=AF.Exp, scale=delta[:, ch:ch + 1]), [msk, delta], [win])
                    for o in range(2):
                        for ch in range(2):
                            for dr in range(2):
                                q = o * 4 + dr * 2 + ch
                                pe(lambda e, dr=dr, q=q: e.matmul(psf[dr][:], lhsT=w3b[:, q * 128:(q + 1) * 128], rhs=h2a[:], start=True, stop=True), [w3b, h2a], [psf[dr]])
                            dve(lambda e: e.tensor_tensor(out=t1[:], in0=psf[0][:], in1=msk[:, 0, :], op=ALU.mult), [psf[0], msk], [t1])
                            dve(lambda e: e.tensor_tensor(out=t2[:], in0=psf[1][:], in1=msk[:, 1, :], op=ALU.mult), [psf[1], msk], [t2])
                            dve(lambda e: e.tensor_tensor(out=t1[:], in0=t1[:], in1=t2[:], op=ALU.add), [t1, t2], [t1])
                            dve(lambda e, ch=ch: e.tensor_tensor(out=t1[:], in0=t1[:], in1=win[:, ch, :], op=ALU.mult), [t1, win], [t1])
                            gb = gbs[ng % 2]; ng += 1
                            idx = o * 2 + ch
                            act(lambda e, gb=gb: e.activation(out=gb[:], in_=t1[:], func=AF.Copy), [t1], [gb])
                            act(lambda e, idx=idx, blk=blk: e.activation(out=ab[:], in_=t1[:], func=AF.Abs, accum_out=nrm[:, idx, blk:blk + 1]), [t1], [ab, nrm])
                            stq(gF[idx * 128:(idx + 1) * 128, m0:m0 + 512], gb, gb[:], dB["gF"])
                            yield
                dve(lambda e: e.tensor_reduce(out=rn[:], in_=nrm[:], axis=mybir.AxisListType.X, op=ALU.add), [nrm], [rn])
                dve(lambda e: e.tensor_scalar(out=rn[:], in0=rn[:], scalar1=scal[:, 1:2], scalar2=EPS, op0=ALU.mult, op1=ALU.add), [rn, scal], [rn])
                dve(lambda e: e.reciprocal(out=rn[:], in_=rn[:]), [rn], [rn])
                stq(rnD.rearrange("(q p) -> p q", p=128), rn, rn[:], dB["rnD"])
                P.pop()

            def stage1(din, m1, AA, ps1s, cnt, Cg=Cg):
                for c in range(0, Cg, 2):
                    ps1 = ps1s[cnt[0] % 2]; cnt[0] += 1
                    for cc in range(2):
                        pe(lambda e, cc=cc, ps1=ps1, c=c: e.matmul(ps1[:, cc, :], lhsT=din[:, c + cc, :], rhs=m1[:], start=True, stop=True), [din, m1], [ps1])
                    for ri in range(2):
                        src_ap = ps1[:, :, ri * NSA:(ri + 1) * NSA].rearrange("p c s -> p s c")
                        if ri == 0:
                            dve(lambda e, src_ap=src_ap, c=c: e.tensor_copy(out=AA[:, :, 0, c:c + 2], in_=src_ap), [ps1], [AA])
                        else:
                            act(lambda e, src_ap=src_ap, c=c: e.activation(out=AA[:, :, 1, c:c + 2], in_=src_ap, func=AF.Copy), [ps1], [AA])
                    yield

            def stage2(AA, h2ts, psXs, evac, spb=8):
                h2v = I["hy_h2"].rearrange("s p a k -> p s a k")
                for j in range(NSA):
                    h2t = h2ts[(j // 8) % 2]; jj = j % spb
                    if j % 8 == 0:
                        nj_ = min(8, NSA - j)
                        ld(h2t, h2t[:, 0:nj_, :, :], h2v[:, j:j + nj_, :, :])
                    psX = psXs[(j // spb) % 2]
                    j8 = j % 8
                    pe(lambda e, psX=psX, jj=jj, h2t=h2t, j=j, j8=j8: e.matmul(psX[:, jj, 0, :], lhsT=h2t[:, j8, 0, :], rhs=AA[:, j, 0, :], start=True, stop=False), [h2t, AA], [psX])
                    pe(lambda e, psX=psX, jj=jj, h2t=h2t, j=j, j8=j8: e.matmul(psX[:, jj, 0, :], lhsT=h2t[:, j8, 2, :], rhs=AA[:, j, 1, :], start=False, stop=True), [h2t, AA], [psX])
                    pe(lambda e, psX=psX, jj=jj, h2t=h2t, j=j, j8=j8: e.matmul(psX[:, jj, 1, :], lhsT=h2t[:, j8, 0, :], rhs=AA[:, j, 1, :], start=True, stop=False), [h2t, AA], [psX])
                    pe(lambda e, psX=psX, jj=jj, h2t=h2t, j=j, j8=j8: e.matmul(psX[:, jj, 1, :], lhsT=h2t[:, j8, 1, :], rhs=AA[:, j, 0, :], start=False, stop=True), [h2t, AA], [psX])
                    if jj == spb - 1 or j == NSA - 1:
                        evac(psX, j - jj, jj + 1)
                        yield

            def part2():
                P.push()
                rnb = P.sb("rnb", [128, 512]); ld(rnb, rnb[:], rnD.partition_broadcast(128), src=dB["rnD"])
                hf1 = P.sb("hf1", [NS, 2 * NSA], BF16); ld(hf1, hf1[:], I["hy_hf1"][:, :])
                Cf = 64
                gin = P.sb("gin", [NS, Cf, 128], BF16); AA = P.sb("AAf", [128, NSA, 2, Cf], BF16)
                Gsb = P.sb("Gsbf", [128, NSA, 2, Cf], BF16)
                ps1s = [P.ps(f"hps1f{i}", [128, 2, 2 * NSA]) for i in range(2)]; psXs = [P.ps(f"hpsXf{i}", [128, 4, 2, Cf]) for i in range(2)]
                h2ts = [P.sb(f"h2tf{i}", [128, 8, 3, 128], BF16) for i in range(2)]
                cnt = [0]
                for gi in range(512 // Cf):
                    ld(gin, gin[:], gF[gi * Cf:(gi + 1) * Cf, :].rearrange("c (g n) -> g c n", n=128), src=dB["gF"])
                    yield from stage1(gin, hf1, AA, ps1s, cnt, Cg=Cf)

                    def evacG(psX, j0, nj, gi=gi):
                        dve(lambda e: e.tensor_tensor(out=Gsb[:, j0:j0 + nj, :, :], in0=psX[:, 0:nj, :, :],
                                                      in1=rnb[:, gi * Cf:(gi + 1) * Cf].unsqueeze(1).unsqueeze(1).broadcast_to([128, nj, 2, Cf]), op=ALU.mult),
                            [psX, rnb], [Gsb])
                    yield from stage2(AA, h2ts, psXs, evacG, spb=4)
                    for hh_ in range(2):
                        for s0_ in range(0, NSA, 32):
                            s1_ = min(NSA, s0_ + 32)
                            stq(Gd[2 * gi + hh_].rearrange("p s (r c) -> p s r c", r=2)[:, s0_:s1_], Gsb, Gsb[:, s0_:s1_, :, hh_ * 32:(hh_ + 1) * 32], dB["Gd"])
                P.pop()

            def part3():
                P.push()
                hz = P.sb("hz", [NSA, 128, 2, NT], BF16); ld(hz, hz[:], I["hy_z"][:, :, :, :])
                h1t = P.sb("hh1", [NT, 2 * NSA], BF16); ld(h1t, h1t[:], I["hy_h1"][:, :])
                i1 = P.sb("hi1", [128, 2, 256], BF16); ld(i1, i1[:], I["hy_i1"][:, :, :])
                skb = P.sb("skb", [128, 2, 256])
                for o in range(2):
                    ld(skb, skb[:, o, :], I["hy_skip"][l, o].partition_broadcast(128))
                AA = P.sb("AAd", [128, NSA, 2, Cg], BF16); Ysb = P.sb("Ysb", [128, 2, Cg, NSA], BF16); Bsb = P.sb("Bsb", [NSA, 128, 2, Cg], BF16)
                Gsb = P.sb("Gsbd", [128, NSA, 2, Cg], BF16)
                din = P.sb("din", [NT, Cg, 128], BF16); vt = P.sb("vt", [NT, Cg, 128]); x1t = P.sb("x1t", [NT, Cg, 128]); x2t = P.sb("x2t", [NT, Cg, 128])
                ob = P.sb("ob", [NT, Cg, 128], BF16)
                pw = [P.sb(f"pw{i}", [128, 8, Cg]) for i in range(4)]
                tcv = P.sb("tcv", [NT, Cg, 16])
                ps1s = [P.ps(f"hps1d{i}", [128, 2, 2 * NSA]) for i in range(2)]; psXs = [P.ps(f"hpsXd{i}", [128, 8, 2, Cg]) for i in range(2)]
                psIs = [P.ps(f"hpsI{i}", [NSA, 2, 256]) for i in range(2)]; psYs = [P.ps(f"hpsY{i}", [NT, 16, Cg]) for i in range(2)]
                h2ts = [P.sb(f"h2td{i}", [128, 8, 3, 128], BF16) for i in range(2)]
                cnt = [0]

                def evacY(psX, j0, nj):
                    Xre, Xim = psX[:, 0:nj, 0, :], psX[:, 0:nj, 1, :]
                    Gre, Gim = Gsb[:, j0:j0 + nj, 0, :], Gsb[:, j0:j0 + nj, 1, :]
                    dve(lambda e: e.tensor_tensor(out=pw[0][:, 0:nj, :], in0=Xre, in1=Gre, op=ALU.mult), [psX, Gsb], [pw[0]])
                    dve(lambda e: e.tensor_tensor(out=pw[1][:, 0:nj, :], in0=Xim, in1=Gim, op=ALU.mult), [psX, Gsb], [pw[1]])
                    dve(lambda e: e.tensor_tensor(out=Ysb[:, 0, :, j0:j0 + nj].rearrange("p c j -> p j c"), in0=pw[0][:, 0:nj, :], in1=pw[1][:, 0:nj, :], op=ALU.subtract),
                        [pw[0], pw[1]], [Ysb])
                    dve(lambda e: e.tensor_tensor(out=pw[2][:, 0:nj, :], in0=Xre, in1=Gim, op=ALU.mult), [psX, Gsb], [pw[2]])
                    dve(lambda e: e.tensor_tensor(out=pw[3][:, 0:nj, :], in0=Xim, in1=Gre, op=ALU.mult), [psX, Gsb], [pw[3]])
                    dve(lambda e: e.tensor_tensor(out=Ysb[:, 1, :, j0:j0 + nj].rearrange("p c j -> p j c"), in0=pw[2][:, 0:nj, :], in1=pw[3][:, 0:nj, :], op=ALU.add),
                        [pw[2], pw[3]], [Ysb])

                def long_conv(o, g, xg, svt):
                    ld(Gsb, Gsb[:].rearrange("p s r c -> p s (r c)"), Gd[o * (256 // Cg) + g], src=dB["Gd"])
                    drain(stage1(din, h1t, AA, ps1s, cnt))
                    drain(stage2(AA, h2ts, psXs, evacY))
                    for c in range(0, Cg, 2):
                        psI = psIs[(c // 2) % 2]
                        for cc in range(2):
                            pe(lambda e, cc=cc, psI=psI, c=c: e.matmul(psI[:, cc, :], lhsT=Ysb[:, 0, c + cc, :], rhs=i1[:, 0, :], start=True, stop=False), [Ysb, i1], [psI])
                            pe(lambda e, cc=cc, psI=psI, c=c: e.matmul(psI[:, cc, :], lhsT=Ysb[:, 1, c + cc, :], rhs=i1[:, 1, :], start=False, stop=True), [Ysb, i1], [psI])
                        for ri in range(2):
                            src_ap = psI[:, :, ri * 128:(ri + 1) * 128].rearrange("p c n -> p n c")
                            if ri == 0:
                                dve(lambda e, src_ap=src_ap, c=c: e.tensor_copy(out=Bsb[:, :, 0, c:c + 2], in_=src_ap), [psI], [Bsb])
                            else:
                                act(lambda e, src_ap=src_ap, c=c: e.activation(out=Bsb[:, :, 1, c:c + 2], in_=src_ap, func=AF.Copy), [psI], [Bsb])
                    for nb in range(8):
                        psY = psYs[nb % 2]
                        for q in range(16):
                            n2 = nb * 16 + q
                            pe(lambda e, psY=psY, q=q, n2=n2: e.matmul(psY[:, q, :], lhsT=hz[:, n2, 0, :], rhs=Bsb[:, n2, 0, :], start=True, stop=False), [hz, Bsb], [psY])
                            pe(lambda e, psY=psY, q=q, n2=n2: e.matmul(psY[:, q, :], lhsT=hz[:, n2, 1, :], rhs=Bsb[:, n2, 1, :], start=False, stop=True), [hz, Bsb], [psY])
                        sl = slice(nb * 16, (nb + 1) * 16)
                        dve(lambda e, psY=psY, sl=sl: e.tensor_tensor(out=tcv[:], in0=psY[:].rearrange("p n c -> p c n"), in1=svt[:, :, sl], op=ALU.add), [psY, svt], [tcv])
                        dve(lambda e, sl=sl: e.tensor_tensor(out=xg[:, :, sl], in0=tcv[:], in1=xg[:, :, sl], op=ALU.mult), [tcv, xg], [xg])

                uv = lambda r0: uhF[r0:r0 + Cg, :].rearrange("c (g n) -> g c n", n=128)
                for g in range(256 // Cg):
                    c0 = g * Cg
                    ld(vt, vt[:], uv(c0), src=dB["uhF"]); ld(x1t, x1t[:], uv(256 + c0), src=dB["uhF"]); ld(x2t, x2t[:], uv(512 + c0), src=dB["uhF"])
                    act(lambda e: e.activation(out=din[:], in_=vt[:], func=AF.Copy), [vt], [din])
                    dve(lambda e, c0=c0: e.tensor_tensor(out=vt[:], in0=vt[:], in1=skb[0:NT, 0, c0:c0 + Cg].unsqueeze(2).broadcast_to([NT, Cg, 128]), op=ALU.mult), [vt, skb], [vt])
                    long_conv(0, g, x1t, vt)
                    act(lambda e: e.activation(out=din[:], in_=x1t[:], func=AF.Copy), [x1t], [din])
                    dve(lambda e, c0=c0: e.tensor_tensor(out=vt[:], in0=x1t[:], in1=skb[0:NT, 1, c0:c0 + Cg].unsqueeze(2).broadcast_to([NT, Cg, 128]), op=ALU.mult), [x1t, skb], [vt])
                    long_conv(1, g, x2t, vt)
                    act(lambda e: e.activation(out=ob[:], in_=x2t[:], func=AF.Copy), [x2t], [ob])
                    stq(brF[512 + c0:512 + c0 + Cg, :].rearrange("c (g n) -> g c n", n=128), ob, ob[:], dB["brF"])
                P.pop()
            if mode == 'filtgen':
                def both():
                    yield from part1()
                    yield from part2()
                return both()
            if mode in ('all', 'filt'):
                drain(part1())
                drain(part2())
            if mode in ('all', 'conv'):
                part3()

        def phase_zero_br(l):
            P.push()
            zt = P.sb("zbr", [128, 2048], BF16)
            dve(lambda e: e.memset(zt[:], 0.0), [], [zt])
            for k in range(8):
                for t0 in range(0, T, 2048):
                    w_ = min(2048, T - t0)
                    stq(brF[k * 128:(k + 1) * 128, t0:t0 + w_], zt, zt[:, 0:w_], dB["brF"])
            P.pop()

        PHASES = dict(mod=phase_mod, norm=phase_norm, A=lambda l: drain(phase_A(l)), Afilt=lambda l: run_concurrent(phase_A(l), phase_hyena(l, 'filtgen')), C=phase_C, D=phase_D, zero=phase_zero_br, fnet=phase_fnet, ret=lambda l: phase_scan(l, True), gla=lambda l: phase_scan(l, False), hyena=phase_hyena, hyfilt=lambda l: phase_hyena(l, 'filt'), hyconv=lambda l: phase_hyena(l, 'conv'))
        nc._I = I
        return_hook(P, PHASES, locals())
    return nc


def return_hook(P, PHASES, env):
    sched = env.get('debug') or ()
    I, dB = env['I'], env['dB']
    stop = None
    for d in sched:
        if isinstance(d, str) and d.startswith("stop:"):
            stop = d[5:]
    x_in, x1d, xmid, y_out = env['x_in'], env['x1d'], env['xmid'], env['y_out']
    Am, Af = env['Am'], env['Af']
    only = [d[5:] for d in sched if isinstance(d, str) and d.startswith("only:")]
    if only:
        for nm in only:
            if nm == 'norm':
                PHASES['norm'](x_in, dB["in"], Am, 0)
            elif nm == 'C':
                PHASES['C'](0, x_in, dB["in"])
            elif nm == 'D':
                PHASES['D'](0, x1d, dB["x1d"])
            else:
                PHASES[nm](0)
        P.barrier()
        return
    for l in range(DEPTH):
        src, sbuf = (x_in, dB["in"]) if l == 0 else (x1d, dB["x1d"])
        dst, dbuf = (x1d, dB["x1d"]) if l == 0 else (y_out, dB["y"])
        if stop == "none":
            break
        PHASES['mod'](l)
        if stop == "mod":
            break
        PHASES['norm'](src, sbuf, Am, 0)
        if stop == "norm":
            break
        PHASES['Afilt' if CONC_FILT else 'A'](l)
        if stop == "A":
            break
        PHASES['zero'](l)
        for nm in ('fnet', 'ret', 'hyconv' if CONC_FILT else 'hyena', 'gla'):
            if nm in PHASES:
                PHASES[nm](l)
        if stop == "mix":
            break
        PHASES['C'](l, src, sbuf)
        PHASES['norm'](xmid, dB["xmid"], Af, 24)
        PHASES['D'](l, dst, dbuf)
        if stop == "L0":
            break
    P.barrier()


def prep_core_inputs(x, c2, W, tb):
    m = {"x": np.ascontiguousarray(x, np.float32)}
    m["cT"] = np.ascontiguousarray(c2.reshape(2, 8, 128).transpose(2, 1, 0), np.float32)
    m.update(W)
    m.update(tb)
    return m


def prep_weights(inp):
    f = lambda a: np.ascontiguousarray(a, np.float32)
    W = {}
    W["ada_w"] = f(inp["ada_w"]); W["ada_b"] = f(inp["ada_b"])
    W["ada_b_col"] = f(inp["ada_b"].reshape(DEPTH, 48, 128).transpose(0, 2, 1))
    nw = np.stack([inp["norm_pre_mix"], inp["norm_post_mix"], inp["norm_pre_ffn"], inp["norm_post_ffn"]], 1)
    W["normw_col"] = f(nw.reshape(DEPTH, 4, 8, 128).transpose(0, 3, 1, 2))
    W["norm_post_mix"] = f(inp["norm_post_mix"]); W["norm_post_ffn"] = f(inp["norm_post_ffn"])
    W["w_in"] = f(inp["w_in"])
    W["hy_cw"] = f(inp["hy_conv_w"].reshape(DEPTH, 3, 6, 128).transpose(0, 3, 2, 1))
    W["hy_cb"] = f(inp["hy_conv_b"].reshape(DEPTH, 6, 128).transpose(0, 2, 1))
    W["flt_w1"] = f(inp["flt_w1"]); W["flt_w2"] = f(inp["flt_w2"]); W["flt_w3"] = f(inp["flt_w3"]); W["flt_b3"] = f(inp["flt_b3"])
    W["flt_c1"] = f(np.stack([inp["flt_freq"], inp["flt_b1"]], -1)); W["flt_c2"] = f(np.stack([inp["flt_freq"], inp["flt_b2"]], -1))
    W["hy_skip"] = f(inp["hy_skip"])
    wd = np.zeros((DEPTH, 33, 256), np.float32)
    wd[:, 0:16, 0:128] = inp["gla_w_decay"][:, 0]; wd[:, 16:32, 128:256] = inp["gla_w_decay"][:, 1]
    wd[:, 32, 0:128] = inp["gla_b_decay"][:, 0]; wd[:, 32, 128:256] = inp["gla_b_decay"][:, 1]
    W["gla_wd"] = wd
    W["ret_gn"] = f(inp["ret_gn"]); W["gla_gn"] = f(inp["gla_gn"])
    W["w_branch"] = f(inp["w_branch"].reshape(DEPTH, 1024, D)); W["w_out"] = f(inp["w_out"])
    W["ffn_up"] = f(inp["ffn_up"])
    W["ffn_cw"] = f(inp["ffn_conv_w"].reshape(DEPTH, 3, 44, 128).transpose(0, 3, 2, 1))
    W["ffn_cb"] = f(inp["ffn_conv_b"].reshape(DEPTH, 44, 128).transpose(0, 2, 1))
    W["ffn_down"] = f(inp["ffn_down"])
    return W


_T = 8192


def kernel(**inp):
    inp = {k: np.asarray(v) for k, v in inp.items()}
    T = _T
    W = prep_weights(inp)
    tbP, tbS = make_tables(T, 'P'), make_tables(T, 'S')
    xp, xs, cp, cs = inp["x_prompt"], inp["x_sample"], inp["c_prompt"], inp["c_sample"]
    in_maps = []
    for b in range(2):
        in_maps.append(prep_core_inputs(xp[b], np.stack([cp[b], cp[b]]), W, tbP))
    for b in range(2):
        in_maps.append(prep_core_inputs(xs[2 * b:2 * b + 2].reshape(T, D), cs[2 * b:2 * b + 2], W, tbS))
    nc = build_program(T)
    res = run_bass_kernel_spmd(nc, in_maps, core_ids=list(range(4)))
    outs = [np.asarray(r["y"], np.float32) for r in res.results]
    y_prompt = np.stack([outs[0], outs[1]], 0)
    y_sample = np.concatenate([outs[2].reshape(2, T // 2, D), outs[3].reshape(2, T // 2, D)], 0)
    return (y_prompt, y_sample)
```

```python
import math
from contextlib import ExitStack
import numpy as np
import ml_dtypes
import concourse.bass as bass
import concourse.mybir as mybir
from concourse.bass_utils import run_bass_kernel_spmd

F32 = mybir.dt.float32
BF16 = mybir.dt.bfloat16
AF = mybir.ActivationFunctionType
ALU = mybir.AluOpType
NPBF = ml_dtypes.bfloat16

D = 1024
DEPTH = 2
DFF = 2816
EPS = 1e-6
MAGIC = 12582912.0
import os
PIPE_DEPTH = int(os.environ.get("PIPE_DEPTH", "2"))
PD_POS = int(os.environ.get("PD_POS", "99"))
CONC_FILT = int(os.environ.get("CONC_FILT", "1"))
USE_POOL = int(os.environ.get("USE_POOL", "0"))
PRIM_STEPS = int(os.environ.get("PRIM_STEPS", "1"))


class Stream:
    def __init__(self, P, inc):
        self.P, self.inc = P, inc
        self.sem = P.new_sem()
        self.count = 0

    def bump(self):
        if self.count + self.inc > 30000:
            self.sem = self.P.new_sem()
            self.count = 0
        self.count += self.inc
        return (self.sem, self.count)

    def cur(self):
        return (self.sem, self.count) if self.count else None


class Buf:
    def __init__(self, name, t=None):
        self.name, self.t = name, t
        self.w = None
        self.r = {}

    def __getitem__(self, idx):
        return self.t[idx]


class Prog:
    def __init__(self, nc, es):
        self.nc, self.es = nc, es
        self.nsem = 0
        self.engs = {'pe': nc.tensor, 'act': nc.scalar, 'dve': nc.vector, 'pool': nc.gpsimd, 'sp': nc.sync}
        self.streams = {k: Stream(self, 1) for k in ('pe', 'act', 'dve', 'pool')}
        self.seen = {k: {} for k in self.engs}
        self.dma_pool = {q: [Stream(self, 16) for _ in range(8)] for q in ('sp', 'pool')}
        self.dma_rr = {q: 0 for q in self.dma_pool}
        self.scopes = [es]
        self.nuniq = 0

    def new_sem(self):
        self.nsem += 1
        return self.es.enter_context(self.nc.semaphore(f"s{self.nsem}"))

    def sb(self, name, shape, dt=F32):
        self.nuniq += 1
        return Buf(name, self.scopes[-1].enter_context(self.nc.sbuf_tensor(f"{name}_{self.nuniq}", shape, dt)))

    def ps(self, name, shape, dt=F32):
        self.nuniq += 1
        return Buf(name, self.scopes[-1].enter_context(self.nc.psum_tensor(f"{name}_{self.nuniq}", shape, dt)))

    def push(self):
        st = ExitStack()
        self.scopes.append(st)
        return st

    def pop(self):
        self.barrier()
        self.scopes.pop().close()

    def _wait(self, eng, tok):
        if tok is None:
            return
        sem, val = tok
        seen = self.seen[eng]
        if seen.get(id(sem), 0) >= val:
            return
        self.engs[eng].wait_ge(sem, val)
        seen[id(sem)] = val

    def barrier(self):
        toks = [s.cur() for s in self.streams.values()]
        for pool in self.dma_pool.values():
            toks += [s.cur() for s in pool]
        for eng in self.engs:
            for t in toks:
                self._wait(eng, t)

    def _deps(self, eng, reads, writes, accum):
        for b in reads:
            self._wait(eng, b.w)
        for b in writes:
            if not accum:
                self._wait(eng, b.w)
            for t in b.r.values():
                self._wait(eng, t)

    def _commit(self, tok, reads, writes):
        for b in writes:
            b.w = tok
            b.r = {}
        for b in reads:
            b.r[id(tok[0])] = tok

    def op(self, eng, fn, reads=(), writes=(), accum=False):
        self._deps(eng, reads, writes, accum)
        inst = fn(self.engs[eng])
        tok = self.streams[eng].bump()
        inst.then_inc(tok[0], 1)
        self._commit(tok, reads, writes)
        return tok

    def dma(self, q, out, in_, reads=(), writes=()):
        pool = self.dma_pool[q]
        st = pool[self.dma_rr[q] % len(pool)]
        self.dma_rr[q] += 1
        self._wait(q, st.cur())
        self._deps(q, reads, writes, False)
        inst = self.engs[q].dma_start(out=out, in_=in_)
        tok = st.bump()
        inst.then_inc(tok[0], 16)
        self._commit(tok, reads, writes)
        return tok


def _cplx_pair(M):
    return np.concatenate([M.real, M.imag], 1), np.concatenate([-M.imag, M.real], 1)


def make_tables(T, kind):
    NT = T // 128
    NS = 2 * NT
    H = NT // 2
    isS = (kind == 'S')
    tb = {}
    tb['ident_b'] = np.eye(128).astype(NPBF)
    tb['ident_f'] = np.eye(128, dtype=np.float32)
    cc = np.arange(64)
    ang = 2 * np.pi * np.outer(cc, cc) / 64
    bdc = np.zeros((128, 128)); bds = np.zeros((128, 128))
    for g in range(2):
        bdc[g * 64:(g + 1) * 64, g * 64:(g + 1) * 64] = np.cos(ang)
        bds[g * 64:(g + 1) * 64, g * 64:(g + 1) * 64] = -np.sin(ang)
    tb['bdcs'] = np.stack([bdc, bds], 1).astype(NPBF)
    n1 = np.arange(NT)
    M1 = np.zeros((NT, NS), np.complex128)
    M2 = np.zeros((NS, 128, 128), np.complex128)
    n2 = np.arange(128)[:, None]
    k2 = np.arange(128)[None, :]
    if not isS:
        L = T
        for j in range(NT):
            M1[:, j] = np.exp(-2j * np.pi * n1 * j / NT)
            M2[j] = np.exp(-2j * np.pi * n2 * (j + NT * k2) / T)
    else:
        L = T // 2
        for s in range(2):
            for j in range(NT):
                M1[s * H:(s + 1) * H, s * NT + j] = np.exp(-2j * np.pi * np.arange(H) * j / H)
                m = np.exp(-2j * np.pi * n2 * (NT * (k2 % 64) + j) / L) * ((k2 // 64) == s)
                M2[s * NT + j] = m
    M2 = M2 / math.sqrt(L * 64)
    a, b = _cplx_pair(M1)
    tb['fn_m1'] = np.stack([a, b], 1).astype(NPBF)
    fm2 = np.zeros((NT, 128, 2, 2, 128), np.float64)
    for s in range(2):
        for j in range(NT):
            fm2[j, :, s, 0] = M2[s * NT + j].real
            fm2[j, :, s, 1] = -M2[s * NT + j].imag
    tb['fn_m2'] = fm2.astype(NPBF)
    HF1 = np.zeros((NS, NS), np.complex128)
    H2 = np.zeros((NS, 128, 128), np.complex128)
    HZ = np.zeros((128, NS, NT), np.complex128)
    if not isS:
        N = 2 * T
        for j in range(NS):
            HF1[:, j] = np.exp(-2j * np.pi * np.arange(NS) * j / NS)
            H2[j] = np.exp(-2j * np.pi * n2 * (j + NS * k2) / N)
        for q in range(128):
            HZ[q] = np.exp(2j * np.pi * np.outer(np.arange(NS), 128 * np.arange(NT) + q) / N) / N
    else:
        N = T
        for s in range(2):
            for j in range(NT):
                HF1[s * NT:(s + 1) * NT, s * NT + j] = np.exp(-2j * np.pi * np.arange(NT) * j / NT)
                H2[s * NT + j] = np.exp(-2j * np.pi * n2 * (j + NT * k2) / N)
        for q in range(128):
            for s in range(2):
                HZ[q, s * NT:(s + 1) * NT, s * H:(s + 1) * H] = \
                    np.exp(2j * np.pi * np.outer(np.arange(NT), 128 * np.arange(H) + q) / N) / N
    if not isS:
        H1 = HF1[:NT]
    else:
        H1 = np.concatenate([HF1[0:H], HF1[NT:NT + H]], 0)
    if not isS:
        act = list(range(NS // 2 + 1)) + [None]
        wts = [1.0 if j in (0, NS // 2) else 2.0 for j in range(NS // 2 + 1)] + [0.0]
    else:
        act = [s * NT + j for s in range(2) for j in range(NT // 2 + 1)]
        wts = [1.0 if j in (0, NT // 2) else 2.0 for s in range(2) for j in range(NT // 2 + 1)]
    def sel(M, axis):
        parts = []
        for a in act:
            if a is None:
                parts.append(np.zeros_like(np.take(M, [0], axis=axis)))
            else:
                parts.append(np.take(M, [a], axis=axis))
        return np.concatenate(parts, axis=axis)
    H1 = sel(H1, 1); HF1 = sel(HF1, 1); H2 = sel(H2, 0)
    HZ = sel(HZ, 1) * np.asarray(wts)[None, :, None]
    tb['hy_h1'] = np.concatenate([H1.real, H1.imag], 1).astype(NPBF)
    tb['hy_hf1'] = np.concatenate([HF1.real, HF1.imag], 1).astype(NPBF)
    tb['hy_h2'] = np.stack([H2.real, H2.imag, -H2.imag], 2).astype(NPBF)
    Fi = np.exp(2j * np.pi * np.outer(np.arange(128), np.arange(128)) / 128)
    a, b = _cplx_pair(Fi)
    tb['hy_i1'] = np.stack([a, b], 1).astype(NPBF)
    tb['hy_z'] = np.stack([HZ.real, -HZ.imag], 2).transpose(1, 0, 2, 3).astype(NPBF).copy()
    Lf = L
    mpos = np.arange(2 * T)
    mloc = mpos % (2 * Lf)
    lag = np.where(mloc < Lf, mloc, 2 * Lf - mloc)
    lag = np.where(mloc == Lf, 0, lag)
    mf = (mloc < Lf).astype(np.float32)
    mb = (mloc > Lf).astype(np.float32)
    tl = np.linspace(0.0, 1.0, Lf, dtype=np.float32)
    wl = (2.0 * np.float32(math.pi) * np.arange(Lf, dtype=np.float32) / np.float32(Lf)).astype(np.float32)
    fb = np.linspace(1e-4, 15, 16, dtype=np.float32)[None, :]
    feat = np.concatenate([tl[:, None], np.cos(fb * wl[:, None]), -np.sin(fb * wl[:, None])], -1).astype(np.float32)
    tb['flt_feat'] = np.ascontiguousarray(feat[lag].T).astype(np.float32)
    tb['flt_msk'] = np.stack([mf, mb, -tl[lag]], 0).astype(np.float32)
    deltas = np.abs(np.linspace(math.log(1e-2) / 0.3, math.log(1e-2) / 1.5, 256, dtype=np.float32))
    tb['flt_delta'] = np.ascontiguousarray(deltas.reshape(2, 128).T).astype(np.float32)
    sc = np.zeros((128, 4), np.float32)
    sc[:, 0] = 0.0 if isS else 1.0
    sc[:, 1] = 0.5 if isS else 1.0
    tb['scal'] = sc
    NB = T // 512
    hal = np.ones((NB, 2), np.float32)
    hal[0, 0] = 0.0; hal[NB - 1, 1] = 0.0
    if isS:
        hal[NB // 2, 0] = 0.0; hal[NB // 2 - 1, 1] = 0.0
    tb['hal'] = np.broadcast_to(hal.reshape(1, NB * 2), (128, NB * 2)).astype(np.float32).copy()
    pos = (np.arange(T) % L).astype(np.float32)
    inv = (10000.0 ** (-np.arange(32, dtype=np.float32) / 32)).astype(np.float32)
    angr = pos[:, None] * inv[None, :]
    tb['rot'] = np.stack([np.cos(angr), np.sin(angr)], 1).astype(np.float32)
    lg = np.log(1.0 - 2.0 ** (-5.0 - np.arange(4)))
    i = np.arange(128)
    ret_e = np.zeros((128, 2, 6, 128), np.float64)
    ret_tok = np.zeros((128, 2, 256), np.float64)
    ret_dec = np.zeros((128, 2, 2), np.float64)
    for c in range(2):
        for p in range(128):
            h = 2 * c + p // 64
            bf = (i + 1) * lg[h]; bb = (128 - i) * lg[h]
            ret_e[p, c, 0] = np.exp(bf - bf[64]) / 8.0
            ret_e[p, c, 1] = np.exp(bf[64] - bf)
            ret_e[p, c, 2] = np.exp(bb - bb[64]) / 8.0
            ret_e[p, c, 3] = np.exp(bb[64] - bb)
            ret_e[p, c, 4] = np.exp(bf) / 8.0
            ret_e[p, c, 5] = np.exp(bb) / 8.0
            ret_dec[p, c, :] = np.exp(128 * lg[h])
    for h in range(4):
        ret_tok[:, 0, h * 64:(h + 1) * 64] = np.exp((127 - i) * lg[h])[:, None]
        ret_tok[:, 1, h * 64:(h + 1) * 64] = np.exp(i * lg[h])[:, None]
    tb['ret_e'] = ret_e.astype(np.float32)
    tb['ret_tok'] = ret_tok.astype(np.float32)
    tb['ret_dec'] = ret_dec.astype(np.float32)
    mh_ret = np.zeros((128, 2), np.float32); mh_gla = np.zeros((128, 4), np.float32)
    bd_ret = np.zeros((128, 2, 256), np.float32); bd_gla = np.zeros((128, 1, 256), np.float32)
    for p in range(128):
        mh_ret[p, p // 64] = 1.0; mh_gla[p, p // 32] = 1.0
        for c in range(2):
            h = 2 * c + p // 64
            bd_ret[p, c, h * 64:(h + 1) * 64] = 1.0
        h = p // 32
        bd_gla[p, 0, h * 64:(h + 1) * 64] = 1.0
    tb['mh_ret'] = mh_ret; tb['mh_gla'] = mh_gla; tb['bd_ret'] = bd_ret; tb['bd_gla'] = bd_gla
    jj = np.arange(128)[:, None]; ii = np.arange(128)[None, :]
    tri = np.zeros((128, 6, 128), np.float32)
    tri[:, 0] = (jj <= ii)
    tri[:, 1] = (jj > ii)
    tri[:, 2] = -(jj <= ii).astype(np.float32) / 16.0
    tri[:, 3] = -(jj >= ii).astype(np.float32) / 16.0
    tri[:, 4] = -(jj > ii).astype(np.float32) / 16.0
    tri[:, 5] = -(jj < ii).astype(np.float32) / 16.0
    tb['tri'] = tri
    return tb


OFF_FN, OFF_QR, OFF_HY, OFF_QG, OFF_GATES = 0, 256, 1280, 2048, 2848
NTM = 1824


def build_program(T, debug=()):
    NT, NS, NB, H = T // 128, T // 64, T // 512, T // 256
    NSA = NT + 2
    nc = bass.Bass("TRN2", target_bir_lowering=False)
    I = {}

    def inp(name, shape, dt=F32):
        I[name] = nc.dram_tensor(name, list(shape), dt, kind="ExternalInput").ap()
        return I[name]

    def scratch(name, shape, dt=F32):
        kind = "ExternalOutput" if name in debug else "Internal"
        return nc.dram_tensor(name, list(shape), dt, kind=kind).ap()

    x_in = inp("x", [T, D]); inp("cT", [128, 8, 2])
    inp("ada_w", [DEPTH, D, 6 * D]); inp("ada_b_col", [DEPTH, 128, 48]); inp("ada_b", [DEPTH, 6 * D])
    inp("normw_col", [DEPTH, 128, 4, 8]); inp("norm_post_mix", [DEPTH, D]); inp("norm_post_ffn", [DEPTH, D])
    inp("w_in", [DEPTH, D, 6944])
    inp("hy_cw", [DEPTH, 128, 6, 3]); inp("hy_cb", [DEPTH, 128, 6])
    inp("flt_w1", [DEPTH, 33, 64]); inp("flt_c1", [DEPTH, 64, 2]); inp("flt_w2", [DEPTH, 64, 64]); inp("flt_c2", [DEPTH, 64, 2])
    inp("flt_w3", [DEPTH, 64, 1024]); inp("flt_b3", [DEPTH, 1024]); inp("hy_skip", [DEPTH, 2, 256])
    inp("gla_wd", [DEPTH, 33, 256]); inp("ret_gn", [DEPTH, 256]); inp("gla_gn", [DEPTH, 256])
    inp("w_branch", [DEPTH, 1024, D]); inp("w_out", [DEPTH, D, D])
    inp("ffn_up", [DEPTH, D, 2 * DFF]); inp("ffn_cw", [DEPTH, 128, 44, 3]); inp("ffn_cb", [DEPTH, 128, 44])
    inp("ffn_down", [DEPTH, DFF, D])
    for nm, shp, dt in (("ident_b", [128, 128], BF16), ("ident_f", [128, 128], F32), ("bdcs", [128, 2, 128], BF16),
                        ("fn_m1", [NT, 2, 2 * NS], BF16), ("fn_m2", [NT, 128, 2, 2, 128], BF16),
                        ("hy_h1", [NT, 2 * NSA], BF16), ("hy_hf1", [NS, 2 * NSA], BF16), ("hy_h2", [NSA, 128, 3, 128], BF16),
                        ("hy_i1", [128, 2, 256], BF16), ("hy_z", [NSA, 128, 2, NT], BF16),
                        ("flt_feat", [33, 2 * T], F32), ("flt_msk", [3, 2 * T], F32), ("flt_delta", [128, 2], F32),
                        ("scal", [128, 4], F32), ("hal", [128, NB * 2], F32), ("rot", [T, 2, 32], F32),
                        ("ret_e", [128, 2, 6, 128], F32), ("ret_tok", [128, 2, 256], F32), ("ret_dec", [128, 2, 2], F32),
                        ("mh_ret", [128, 2], F32), ("mh_gla", [128, 4], F32), ("bd_ret", [128, 2, 256], F32),
                        ("bd_gla", [128, 1, 256], F32), ("tri", [128, 6, 128], F32)):
        inp(nm, shp, dt)
    y_out = nc.dram_tensor("y", [T, D], F32, kind="ExternalOutput").ap()
    x1d = scratch("x1d", [T, D]); xmid = scratch("xmid", [T, D])
    projT = scratch("projT", [T, NTM]); zF = scratch("zF", [2, 256, T], BF16); uhF = scratch("uhF", [768, T])
    brF = scratch("brF", [1024, T], BF16); modrow = scratch("modrow", [2, 2048]); hF = scratch("hF", [D, T + 2], BF16); gFF = scratch("gFF", [DFF, T], BF16)
    gF = scratch("gF", [512, 2 * T], BF16); rnD = scratch("rnD", [512]); Gd = scratch("Gd", [16, 128, NSA, 64], BF16)

    es = ExitStack()
    with es:
        es.enter_context(nc.allow_non_contiguous_dma(reason="strided scratch layouts"))
        P = Prog(nc, es)
        dB = {k: Buf(k) for k in ("x1d", "xmid", "projT", "zF", "uhF", "brF", "modrow", "gF", "rnD", "Gd", "y", "in", "hF", "gFF")}
        IN = dB["in"]

        def ld(dst_buf, dst_ap, src_ap, src=IN):
            return P.dma('sp', dst_ap, src_ap, reads=[src], writes=[dst_buf])

        def stq(dst_ap, src_buf, src_ap, dst):
            return P.dma('pool', dst_ap, src_ap, reads=[src_buf], writes=[dst])

        def dve(fn, r, w):
            return P.op('dve', fn, r, w)

        def act(fn, r, w):
            return P.op('act', fn, r, w)

        def gps(fn, r, w):
            return P.op('pool' if USE_POOL else 'dve', fn, r, w)

        def pe(fn, r, w):
            return P.op('pe', fn, r, w, accum=True)

        identb = P.sb("identb", [128, 128], BF16); identf = P.sb("identf", [128, 128], F32)
        scal = P.sb("scal", [128, 4]); hal = P.sb("hal", [128, NB * 2]); epsb = P.sb("epsb", [128, 1])
        modc = P.sb("modc", [128, 48, 2]); Am = P.sb("Am", [128, 8, 2]); Af = P.sb("Af", [128, 8, 2])
        gtb = P.sb("gtb", [128, 2, 2, D])
        ld(identb, identb[:], I["ident_b"][:, :]); ld(identf, identf[:], I["ident_f"][:, :])
        ld(scal, scal[:], I["scal"][:, :]); ld(hal, hal[:], I["hal"][:, :])
        dve(lambda e: e.memset(epsb[:], EPS), [], [epsb])

        def phase_mod(l):
            P.push()
            cT = P.sb("cT", [128, 8, 2]); scT = P.sb("scT", [128, 8, 2])
            ld(cT, cT[:], I["cT"][:, :, :])
            act(lambda e: e.activation(out=scT[:], in_=cT[:], func=AF.Silu), [cT], [scT])
            psc = P.ps("psc", [128, 96]); psr = P.ps("psr", [2, 2048]); macc = P.sb("macc", [128, 96])
            wts = [P.sb(f"adaw{i}", [128, 6 * D]) for i in range(2)]
            rowcols = (2048, 2560, 5120, 5632)
            for kc in range(8):
                wt = wts[kc % 2]
                ld(wt, wt[:], I["ada_w"][l, kc * 128:(kc + 1) * 128, :])
                for q in range(48):
                    pe(lambda e, q=q: e.matmul(psc[:, 2 * q:2 * q + 2], lhsT=wt[:, q * 128:(q + 1) * 128], rhs=scT[:, kc, :],
                                               start=True, stop=True), [wt, scT], [psc])
                if kc == 0:
                    dve(lambda e: e.tensor_copy(out=macc[:], in_=psc[:]), [psc], [macc])
                else:
                    dve(lambda e: e.tensor_tensor(out=macc[:], in0=macc[:], in1=psc[:], op=ALU.add), [psc, macc], [macc])
                for bi, c0 in enumerate(rowcols):
                    pe(lambda e, bi=bi, c0=c0: e.matmul(psr[:, bi * 512:(bi + 1) * 512], lhsT=scT[:, kc, :], rhs=wt[:, c0:c0 + 512],
                                                        start=(kc == 0), stop=(kc == 7)), [wt, scT], [psr])
            abc = P.sb("abc", [128, 48]); nwc = P.sb("nwc", [128, 4, 8])
            ld(abc, abc[:], I["ada_b_col"][l]); ld(nwc, nwc[:], I["normw_col"][l])
            dve(lambda e: e.tensor_tensor(out=modc[:], in0=macc[:].rearrange("p (q s) -> p q s", s=2),
                                          in1=abc[:].unsqueeze(2).broadcast_to([128, 48, 2]), op=ALU.add), [macc, abc], [modc])
            dve(lambda e: e.scalar_tensor_tensor(out=Am[:], in0=modc[:, 8:16, :], scalar=1.0,
                                                 in1=nwc[:, 0, :].unsqueeze(2).broadcast_to([128, 8, 2]), op0=ALU.add, op1=ALU.mult),
                [modc, nwc], [Am])
            dve(lambda e: e.scalar_tensor_tensor(out=Af[:], in0=modc[:, 32:40, :], scalar=1.0,
                                                 in1=nwc[:, 2, :].unsqueeze(2).broadcast_to([128, 8, 2]), op0=ALU.add, op1=ALU.mult),
                [modc, nwc], [Af])
            abr = P.sb("abr", [2, 2048]); nwr = P.sb("nwr", [2, 2048]); gr = P.sb("gr", [2, 2048])
            ld(abr, abr[:, 0:1024], I["ada_b"][l, 2048:3072].partition_broadcast(2))
            ld(abr, abr[:, 1024:2048], I["ada_b"][l, 5120:6144].partition_broadcast(2))
            ld(nwr, nwr[:, 0:1024], I["norm_post_mix"][l].partition_broadcast(2))
            ld(nwr, nwr[:, 1024:2048], I["norm_post_ffn"][l].partition_broadcast(2))
            dve(lambda e: e.tensor_tensor(out=gr[:], in0=psr[:], in1=abr[:], op=ALU.add), [psr, abr], [gr])
            dve(lambda e: e.tensor_tensor(out=gr[:], in0=gr[:], in1=nwr[:], op=ALU.mult), [gr, nwr], [gr])
            stq(modrow[:, :], gr, gr[:], dB["modrow"])
            for sg in range(2):
                ld(gtb, gtb[:, sg, :, :].rearrange("p a d -> p (a d)"), modrow[sg].partition_broadcast(128), src=dB["modrow"])
            P.pop()

        def phase_norm(src_ap2d, src_buf, A, bq0):
            P.push()
            zt = P.sb("zt", [128, 8, 1], BF16)
            dve(lambda e: e.memset(zt[:], 0.0), [], [zt])
            hFv = hF.rearrange("(c p) t -> p c t", p=128)
            stq(hFv[:, :, 0:1], zt, zt[:], dB["hF"]); stq(hFv[:, :, T + 1:T + 2], zt, zt[:], dB["hF"])
            xts = [P.sb(f"xt{i}", [128, D]) for i in range(2)]
            sq = P.sb("sq", [128, D]); ss = P.sb("ss", [128, 1]); rs = P.sb("rs", [128, 1])
            xn = P.sb("xn", [128, D], BF16); ptr = P.ps("ptr", [128, 8, 128], BF16); tmp = P.sb("tmpT", [128, 8, 128])
            hts = [P.sb(f"ht{i}", [128, 8, 512], BF16) for i in range(2)]
            xns = [P.sb(f"xnN{i}", [128, D], BF16) for i in range(2)]

            def gen(it):
                xt, ht, xn = xts[it % 2], hts[(it // 4) % 2], xns[it % 2]
                q4 = it % 4
                seg = 0 if it < NT // 2 else 1
                ld(xt, xt[:], src_ap2d[it * 128:(it + 1) * 128, :], src=src_buf)
                act(lambda e: e.activation(out=sq[:], in_=xt[:], func=AF.Square, accum_out=ss[:, 0:1]), [xt], [sq, ss])
                act(lambda e: e.activation(out=rs[:], in_=ss[:], func=AF.Sqrt, scale=1.0 / D, bias=epsb[:, 0:1]), [ss, epsb], [rs])
                yield
                dve(lambda e: e.reciprocal(out=rs[:], in_=rs[:]), [rs], [rs])
                dve(lambda e: e.tensor_scalar(out=xn[:], in0=xt[:], scalar1=rs[:, 0:1], scalar2=None, op0=ALU.mult), [xt, rs], [xn])
                yield 'prev_done'
                for c8 in range(8):
                    pe(lambda e, c8=c8: e.transpose(ptr[:, c8, :], xn[:, c8 * 128:(c8 + 1) * 128], identb[:]), [xn, identb], [ptr])
                yield
                dve(lambda e: e.tensor_tensor(out=tmp[:], in0=ptr[:], in1=A[:, :, seg:seg + 1].broadcast_to([128, 8, 128]), op=ALU.mult),
                    [ptr, A], [tmp])
                dve(lambda e: e.tensor_tensor(out=ht[:, :, q4 * 128:(q4 + 1) * 128], in0=tmp[:], in1=modc[:, bq0:bq0 + 8, seg:seg + 1].broadcast_to([128, 8, 128]), op=ALU.add),
                    [tmp, modc], [ht])
                if q4 == 3:
                    stq(hFv[:, :, 1 + (it - 3) * 128:1 + (it + 1) * 128], ht, ht[:], dB["hF"])

            run_pipelined(gen, range(NT))
            P.pop()

        def load_w_bf16(dst, dst_ap_fn, src_ap_fn, nk, width, stg):
            for k in range(nk):
                st = stg[k % 2]
                ld(st, st[:, 0:width], src_ap_fn(k))
                if k % 2 == 0:
                    dve(lambda e, k=k, st=st: e.tensor_copy(out=dst_ap_fn(k), in_=st[:, 0:width]), [st], [dst])
                else:
                    act(lambda e, k=k, st=st: e.activation(out=dst_ap_fn(k), in_=st[:, 0:width], func=AF.Copy), [st], [dst])

        def load_window(hw, b):
            hFv = hF.rearrange("(c p) t -> p c t", p=128)
            ld(hw, hw[:], hFv[:, :, b * 512:b * 512 + 514], src=dB["hF"])
            for side, col in ((0, 0), (1, 513)):
                dve(lambda e, side=side, col=col: e.tensor_tensor(out=hw[:, :, col:col + 1], in0=hw[:, :, col:col + 1],
                                                                  in1=hal[:, 2 * b + side:2 * b + side + 1].unsqueeze(1).broadcast_to([128, 8, 1]),
                                                                  op=ALU.mult), [hw, hal], [hw])

        def conv3_fm(out_ap, pm, ph, cw, cb, ci, rbufs, wbuf):
            act(lambda e: e.activation(out=out_ap, in_=pm[:, 0:512], func=AF.Identity, scale=cw[:, ci, 1:2], bias=cb[:, ci:ci + 1]), rbufs, [wbuf])
            dve(lambda e: e.scalar_tensor_tensor(out=out_ap[:, 1:512], in0=pm[:, 0:511], scalar=cw[:, ci, 0:1], in1=out_ap[:, 1:512],
                                                 op0=ALU.mult, op1=ALU.add), rbufs + [wbuf], [wbuf])
            dve(lambda e: e.scalar_tensor_tensor(out=out_ap[:, 0:511], in0=pm[:, 1:512], scalar=cw[:, ci, 2:3], in1=out_ap[:, 0:511],
                                                 op0=ALU.mult, op1=ALU.add), rbufs + [wbuf], [wbuf])
            dve(lambda e: e.scalar_tensor_tensor(out=out_ap[:, 0:1], in0=ph[:, 0:1], scalar=cw[:, ci, 0:1], in1=out_ap[:, 0:1],
                                                 op0=ALU.mult, op1=ALU.add), rbufs + [wbuf], [wbuf])
            dve(lambda e: e.scalar_tensor_tensor(out=out_ap[:, 511:512], in0=ph[:, 1:2], scalar=cw[:, ci, 2:3], in1=out_ap[:, 511:512],
                                                 op0=ALU.mult, op1=ALU.add), rbufs + [wbuf], [wbuf])

        def phase_A(l):
            P.push()
            wA = P.sb("wA", [128, 8, OFF_GATES], BF16)
            stg = [P.sb(f"stgA{i}", [128, OFF_GATES]) for i in range(2)]
            load_w_bf16(wA, lambda k: wA[:, k, :], lambda k: I["w_in"][l, k * 128:(k + 1) * 128, 0:OFF_GATES], 8, OFF_GATES, stg)
            bdcs = P.sb("bdcs", [128, 2, 128], BF16); ld(bdcs, bdcs[:], I["bdcs"][:, :, :])
            cw = P.sb("hcw", [128, 6, 3]); cb = P.sb("hcb", [128, 6])
            ld(cw, cw[:], I["hy_cw"][l]); ld(cb, cb[:], I["hy_cb"][l])
            hws = [P.sb(f"hw{i}", [128, 8, 514], BF16) for i in range(2)]
            pms = [P.ps(f"pmA{i}", [128, 512]) for i in range(2)]
            ph = P.ps("phA", [128, 2])
            pjs = [P.sb(f"pj{i}", [128, NTM]) for i in range(2)]; uT = P.sb("uT", [128, 2, 512], BF16)
            zts = [P.sb(f"ztA{i}", [128, 512], BF16) for i in range(2)]
            cvs = [P.sb(f"cvA{i}", [128, 512]) for i in range(2)]
            tmcols = ((256, 512), (768, 512), (2048, 512), (2560, 288))
            zFv = zF
            npm = 0
            for b in range(NB):
                hw = hws[b % 2]
                load_window(hw, b)
                t0 = b * 512
                for s in range(4):
                    o = 0
                    pj = pjs[s % 2]
                    for (c0, wd) in tmcols:
                        pm = pms[npm % 2]; npm += 1
                        for kc in range(8):
                            pe(lambda e, kc=kc, pm=pm, c0=c0, wd=wd: e.matmul(pm[:, 0:wd], lhsT=hw[:, kc, 1 + s * 128:1 + (s + 1) * 128],
                                                                                rhs=wA[:, kc, c0:c0 + wd], start=(kc == 0), stop=(kc == 7)),
                               [hw, wA], [pm])
                        if (npm % 2) == 0:
                            dve(lambda e, pm=pm, o=o, wd=wd, pj=pj: e.tensor_copy(out=pj[:, o:o + wd], in_=pm[:, 0:wd]), [pm], [pj])
                        else:
                            act(lambda e, pm=pm, o=o, wd=wd, pj=pj: e.activation(out=pj[:, o:o + wd], in_=pm[:, 0:wd], func=AF.Copy), [pm], [pj])
                        o += wd
                        yield
                    stq(projT[t0 + s * 128:t0 + (s + 1) * 128, :], pj, pj[:], dB["projT"])
                for ch in range(2):
                    pm = pms[npm % 2]; npm += 1
                    for kc in range(8):
                        pe(lambda e, kc=kc, pm=pm, ch=ch: e.matmul(pm[:], lhsT=wA[:, kc, ch * 128:(ch + 1) * 128], rhs=hw[:, kc, 1:513],
                                                                     start=(kc == 0), stop=(kc == 7)), [hw, wA], [pm])
                    act(lambda e, pm=pm, ch=ch: e.activation(out=uT[:, ch, :], in_=pm[:], func=AF.Copy), [pm], [uT])
                    yield
                for ri in range(2):
                    for ch in range(2):
                        pm = pms[npm % 2]; zt = zts[npm % 2]; npm += 1
                        pe(lambda e, pm=pm, ri=ri, ch=ch: e.matmul(pm[:], lhsT=bdcs[:, ri, :], rhs=uT[:, ch, :], start=True, stop=True), [bdcs, uT], [pm])
                        act(lambda e, pm=pm, zt=zt: e.activation(out=zt[:], in_=pm[:], func=AF.Copy), [pm], [zt])
                        stq(zFv[ri, ch * 128:(ch + 1) * 128, t0:t0 + 512], zt, zt[:], dB["zF"])
                        yield
                for ch in range(6):
                    pm = pms[npm % 2]; cv = cvs[npm % 2]; npm += 1
                    c0 = OFF_HY + ch * 128
                    for kc in range(8):
                        pe(lambda e, kc=kc, pm=pm, c0=c0: e.matmul(pm[:], lhsT=wA[:, kc, c0:c0 + 128], rhs=hw[:, kc, 1:513],
                                                                     start=(kc == 0), stop=(kc == 7)), [hw, wA], [pm])
                    for kc in range(8):
                        pe(lambda e, kc=kc, c0=c0: e.matmul(ph[:], lhsT=wA[:, kc, c0:c0 + 128], rhs=hw[:, kc, 0:514:513],
                                                              start=(kc == 0), stop=(kc == 7)), [hw, wA], [ph])
                    conv3_fm(cv[:], pm, ph, cw, cb, ch, [pm, ph, cw, cb], cv)
                    stq(uhF[ch * 128:(ch + 1) * 128, t0:t0 + 512], cv, cv[:], dB["uhF"])
                    yield
            yield 'finished'
            P.pop()

        def rms_residual(py, xt, sq, ss, rs, tmpo, outt, seg, which, dst_ap, dst_buf):
            act(lambda e: e.activation(out=sq[:], in_=py[:], func=AF.Square, accum_out=ss[:, 0:1]), [py], [sq, ss])
            act(lambda e: e.activation(out=rs[:], in_=ss[:], func=AF.Sqrt, scale=1.0 / D, bias=epsb[:, 0:1]), [ss, epsb], [rs])
            dve(lambda e: e.reciprocal(out=rs[:], in_=rs[:]), [rs], [rs])
            dve(lambda e: e.scalar_tensor_tensor(out=tmpo[:], in0=py[:], scalar=rs[:, 0:1], in1=gtb[:, seg, which, :], op0=ALU.mult, op1=ALU.mult),
                [py, rs, gtb], [tmpo])
            dve(lambda e: e.tensor_tensor(out=outt[:], in0=tmpo[:], in1=xt[:], op=ALU.add), [tmpo, xt], [outt])
            stq(dst_ap, outt, outt[:], dst_buf)

        def phase_C(l, src_ap2d, src_buf):
            P.push()
            wbr = P.sb("wbr", [128, 8, D], BF16); wout = P.sb("wout", [128, 8, D], BF16); wg = P.sb("wg", [128, 8, 4 * D], BF16)
            stg = [P.sb(f"stgC{i}", [128, 4 * D]) for i in range(2)]
            load_w_bf16(wbr, lambda k: wbr[:, k, :], lambda k: I["w_branch"][l, k * 128:(k + 1) * 128, :], 8, D, stg)
            load_w_bf16(wout, lambda k: wout[:, k, :], lambda k: I["w_out"][l, k * 128:(k + 1) * 128, :], 8, D, stg)
            load_w_bf16(wg, lambda k: wg[:, k, :], lambda k: I["w_in"][l, k * 128:(k + 1) * 128, OFF_GATES:OFF_GATES + 4 * D], 8, 4 * D, stg)
            xts = [P.sb(f"xtC{i}", [128, D]) for i in range(2)]
            hts = [P.sb(f"htC{i}", [128, 8, 128], BF16) for i in range(2)]
            brs = [P.sb(f"brC{i}", [128, 8, 128], BF16) for i in range(2)]
            pgs = [P.ps(f"pg{i}", [128, 512]) for i in range(2)]; pbs = [P.ps(f"pb{i}", [128, 512]) for i in range(2)]
            py = P.ps("py", [128, D]); ptr = P.ps("ptrC", [128, 8, 128], BF16)
            sigs = [P.sb(f"sig{i}", [128, 512]) for i in range(2)]; tms = [P.sb(f"tmc{i}", [128, 512]) for i in range(2)]
            merged = P.sb("merged", [128, D]); tmpm = P.sb("tmpm", [128, D]); mb = P.sb("mb", [128, D], BF16)
            mT = P.sb("mT", [128, 8, 128], BF16); sq = P.sb("sqC", [128, D]); ss = P.sb("ssC", [128, 1]); rs = P.sb("rsC", [128, 1])
            outt = P.sb("outC", [128, D])
            hFv = hF.rearrange("(c p) t -> p c t", p=128); brv = brF.rearrange("(k p) t -> p k t", p=128)
            for it in range(NT):
                xt, ht, brt = xts[it % 2], hts[it % 2], brs[it % 2]
                seg = 0 if it < NT // 2 else 1
                ld(xt, xt[:], src_ap2d[it * 128:(it + 1) * 128, :], src=src_buf)
                ld(ht, ht[:], hFv[:, :, 1 + it * 128:1 + (it + 1) * 128], src=dB["hF"])
                ld(brt, brt[:], brv[:, :, it * 128:(it + 1) * 128], src=dB["brF"])
                for br in range(4):
                    for cb in range(2):
                        u = br * 2 + cb
                        pgu, pbu, sgu, tmu = pgs[u % 2], pbs[u % 2], sigs[u % 2], tms[u % 2]
                        for kc in range(8):
                            pe(lambda e, kc=kc, cb=cb, pgu=pgu, br=br: e.matmul(pgu[:], lhsT=ht[:, kc, :],
                                                                               rhs=wg[:, kc, br * D + cb * 512:br * D + (cb + 1) * 512], start=(kc == 0), stop=(kc == 7)),
                               [ht, wg], [pgu])
                        for k2 in range(2):
                            pe(lambda e, k2=k2, cb=cb, pbu=pbu, br=br: e.matmul(pbu[:], lhsT=brt[:, br * 2 + k2, :],
                                                                               rhs=wbr[:, br * 2 + k2, cb * 512:(cb + 1) * 512], start=(k2 == 0), stop=(k2 == 1)),
                               [brt, wbr], [pbu])
                        act(lambda e, pgu=pgu, sgu=sgu: e.activation(out=sgu[:], in_=pgu[:], func=AF.Sigmoid), [pgu], [sgu])
                        mslice = merged[:, cb * 512:(cb + 1) * 512]
                        if br == 0:
                            dve(lambda e, sgu=sgu, pbu=pbu, mslice=mslice: e.tensor_tensor(out=mslice, in0=sgu[:], in1=pbu[:], op=ALU.mult), [sgu, pbu], [merged])
                        else:
                            dve(lambda e, sgu=sgu, pbu=pbu, tmu=tmu: e.tensor_tensor(out=tmu[:], in0=sgu[:], in1=pbu[:], op=ALU.mult), [sgu, pbu], [tmu])
                            dve(lambda e, tmu=tmu, mslice=mslice: e.tensor_tensor(out=mslice, in0=mslice, in1=tmu[:], op=ALU.add), [merged, tmu], [merged])
                act(lambda e: e.activation(out=mb[:], in_=merged[:], func=AF.Copy), [merged], [mb])
                for c8 in range(8):
                    pe(lambda e, c8=c8: e.transpose(ptr[:, c8, :], mb[:, c8 * 128:(c8 + 1) * 128], identb[:]), [mb, identb], [ptr])
                dve(lambda e: e.tensor_copy(out=mT[:], in_=ptr[:]), [ptr], [mT])
                for cb in range(2):
                    for kc in range(8):
                        pe(lambda e, kc=kc, cb=cb: e.matmul(py[:, cb * 512:(cb + 1) * 512], lhsT=mT[:, kc, :], rhs=wout[:, kc, cb * 512:(cb + 1) * 512],
                                                           start=(kc == 0), stop=(kc == 7)), [mT, wout], [py])
                rms_residual(py, xt, sq, ss, rs, tmpm, outt, seg, 0, xmid[it * 128:(it + 1) * 128, :], dB["xmid"])
            P.pop()

        def phase_D(l, dst_ap2d, dst_buf):
            P.push()
            wup = P.sb("wup", [128, 8, 2 * DFF], BF16)
            stg = [P.sb(f"stgD{i}", [128, 1408]) for i in range(2)]
            for part in range(4):
                c0 = part * 1408
                load_w_bf16(wup, lambda k, c0=c0: wup[:, k, c0:c0 + 1408], lambda k, c0=c0: I["ffn_up"][l, k * 128:(k + 1) * 128, c0:c0 + 1408], 8, 1408, stg)
            cw = P.sb("fcw", [128, 44, 3]); cb_ = P.sb("fcb", [128, 44])
            ld(cw, cw[:], I["ffn_cw"][l]); ld(cb_, cb_[:], I["ffn_cb"][l])
            hws = [P.sb(f"hwD{i}", [128, 8, 514], BF16) for i in range(2)]
            pms = [P.ps(f"pmD{i}", [128, 512]) for i in range(4)]; phs = [P.ps(f"phD{i}", [128, 2]) for i in range(4)]
            cvs = [P.sb(f"cvD{i}", [128, 512]) for i in range(4)]; gls = [P.sb(f"glD{i}", [128, 512]) for i in range(2)]
            gts = [P.sb(f"gtD{i}", [128, 512], BF16) for i in range(2)]
            for b in range(NB):
                hw = hws[b % 2]
                load_window(hw, b)
                for pc in range(22):
                    for wi in range(2):
                        ci = pc + 22 * wi
                        bi = 2 * (pc % 2) + wi
                        pm, ph, cv = pms[bi], phs[bi], cvs[bi]
                        for kc in range(8):
                            pe(lambda e, kc=kc, pm=pm, ci=ci: e.matmul(pm[:], lhsT=wup[:, kc, ci * 128:(ci + 1) * 128], rhs=hw[:, kc, 1:513],
                                                                         start=(kc == 0), stop=(kc == 7)), [hw, wup], [pm])
                        for kc in range(8):
                            pe(lambda e, kc=kc, ph=ph, ci=ci: e.matmul(ph[:], lhsT=wup[:, kc, ci * 128:(ci + 1) * 128], rhs=hw[:, kc, 0:514:513],
                                                                         start=(kc == 0), stop=(kc == 7)), [hw, wup], [ph])
                        conv3_fm(cv[:], pm, ph, cw, cb_, ci, [pm, ph, cw, cb_], cv)
                    gt = gts[pc % 2]; gl = gls[pc % 2]; cg_, cv_ = cvs[2 * (pc % 2)], cvs[2 * (pc % 2) + 1]
                    act(lambda e, gl=gl, cg_=cg_: e.activation(out=gl[:], in_=cg_[:], func=AF.Gelu_apprx_tanh), [cg_], [gl])
                    dve(lambda e, gt=gt, gl=gl, cv_=cv_: e.tensor_tensor(out=gt[:], in0=gl[:], in1=cv_[:], op=ALU.mult), [gl, cv_], [gt])
                    stq(gFF[pc * 128:(pc + 1) * 128, b * 512:(b + 1) * 512], gt, gt[:], dB["gFF"])
            P.pop()
            P.push()
            wdn = P.sb("wdn", [128, 22, D], BF16)
            stg = [P.sb(f"stgE{i}", [128, D]) for i in range(2)]
            load_w_bf16(wdn, lambda k: wdn[:, k, :], lambda k: I["ffn_down"][l, k * 128:(k + 1) * 128, :], 22, D, stg)
            xts = [P.sb(f"xtE{i}", [128, D]) for i in range(2)]
            ggs = [P.sb(f"ggE{i}", [128, 22, 512], BF16) for i in range(2)]
            pys = [P.ps(f"pyE{i}", [128, D]) for i in range(2)]; sq = P.sb("sqE", [128, D])
            sss = [P.sb(f"ssE{i}", [128, 1]) for i in range(2)]; rss = [P.sb(f"rsE{i}", [128, 1]) for i in range(2)]
            tmpos = [P.sb(f"tmpE{i}", [128, D]) for i in range(2)]; outts = [P.sb(f"outE{i}", [128, D]) for i in range(2)]
            gv = gFF.rearrange("(k p) t -> p k t", p=128)
            for it in range(NT):
                xt, gg = xts[it % 2], ggs[(it // 4) % 2]
                py, ss, rs, tmpo, outt = pys[it % 2], sss[it % 2], rss[it % 2], tmpos[it % 2], outts[it % 2]
                q4 = it % 4
                seg = 0 if it < NT // 2 else 1
                ld(xt, xt[:], xmid[it * 128:(it + 1) * 128, :], src=dB["xmid"])
                if q4 == 0:
                    ld(gg, gg[:], gv[:, :, it * 128:(it + 4) * 128], src=dB["gFF"])
                for cb in range(2):
                    for pc in range(22):
                        pe(lambda e, pc=pc, cb=cb: e.matmul(py[:, cb * 512:(cb + 1) * 512], lhsT=gg[:, pc, q4 * 128:(q4 + 1) * 128], rhs=wdn[:, pc, cb * 512:(cb + 1) * 512],
                                                           start=(pc == 0), stop=(pc == 21)), [gg, wdn], [py])
                rms_residual(py, xt, sq, ss, rs, tmpo, outt, seg, 1, dst_ap2d[it * 128:(it + 1) * 128, :], dst_buf)
            P.pop()

        def phase_fnet(l):
            P.push()
            Cg = 64
            m1 = P.sb("fm1", [NT, 2, 2 * NS], BF16); ld(m1, m1[:], I["fn_m1"][:, :, :])
            zin = P.sb("zin", [NT, 2, Cg, 128], BF16); A = P.sb("fA", [128, NS, 2, Cg], BF16)
            osb = P.sb("osb", [Cg, T], BF16); osbv = osb[:].rearrange("c (k j) -> c k j", j=NT)
            ps1s = [P.ps(f"fps1{i}", [128, 2, 2 * NS]) for i in range(2)]
            ps2s = [P.ps(f"fps2{i}", [Cg, 4, 128]) for i in range(2)]
            m2s = [P.sb(f"fm2{i}", [128, 4, 2, 2, 128], BF16) for i in range(2)]
            n1 = 0
            for g in range(256 // Cg):
                c0 = g * Cg
                for ri in range(2):
                    ld(zin, zin[:, ri, :, :], zF[ri, c0:c0 + Cg, :].rearrange("c (g n) -> g c n", n=128), src=dB["zF"])
                for c in range(0, Cg, 2):
                    ps1 = ps1s[n1 % 2]; n1 += 1
                    for cc in range(2):
                        pe(lambda e, cc=cc, ps1=ps1: e.matmul(ps1[:, cc, :], lhsT=zin[:, 0, c + cc, :], rhs=m1[:, 0, :], start=True, stop=False), [zin, m1], [ps1])
                        pe(lambda e, cc=cc, ps1=ps1: e.matmul(ps1[:, cc, :], lhsT=zin[:, 1, c + cc, :], rhs=m1[:, 1, :], start=False, stop=True), [zin, m1], [ps1])
                    for ri in range(2):
                        src_ap = ps1[:, :, ri * NS:(ri + 1) * NS].rearrange("p c s -> p s c")
                        if ri == 0:
                            dve(lambda e, src_ap=src_ap: e.tensor_copy(out=A[:, :, 0, c:c + 2], in_=src_ap), [ps1], [A])
                        else:
                            act(lambda e, src_ap=src_ap: e.activation(out=A[:, :, 1, c:c + 2], in_=src_ap, func=AF.Copy), [ps1], [A])
                m2v = I["fn_m2"].rearrange("j p s r k -> p j s r k")
                for j in range(NT):
                    m2t = m2s[(j // 4) % 2]
                    if j % 4 == 0:
                        ld(m2t, m2t[:], m2v[:, j:j + 4, :, :, :])
                    ps2 = ps2s[(j // 4) % 2]
                    k = 0
                    for s_ in range(2):
                        for ri in range(2):
                            pe(lambda e, s_=s_, ri=ri, k=k, ps2=ps2, m2t=m2t: e.matmul(ps2[:, j % 4, :], lhsT=A[:, s_ * NT + j, ri, :], rhs=m2t[:, j % 4, s_, ri, :],
                                                                                      start=(k == 0), stop=(k == 3)), [A, m2t], [ps2])
                            k += 1
                    if j % 4 == 3:
                        j0 = j - 3
                        src_ap = ps2[:, :, :].rearrange("c j k -> c k j")
                        if (j // 4) % 2 == 0:
                            dve(lambda e, src_ap=src_ap, j0=j0: e.tensor_copy(out=osbv[:, :, j0:j0 + 4], in_=src_ap), [ps2], [osb])
                        else:
                            act(lambda e, src_ap=src_ap, j0=j0: e.activation(out=osbv[:, :, j0:j0 + 4], in_=src_ap, func=AF.Copy), [ps2], [osb])
                stq(brF[c0:c0 + Cg, :], osb, osb[:], dB["brF"])
            P.pop()

        def drain(g):
            for _ in g:
                pass

        def run_concurrent(primary, secondary, ratio=int(os.environ.get("CONC_RATIO", "1"))):
            p_alive, s_alive, p_fin = True, True, False
            while p_alive or s_alive:
                for _ in range(PRIM_STEPS):
                    if p_alive and not (p_fin and s_alive):
                        try:
                            if next(primary) == 'finished':
                                p_fin = True
                        except StopIteration:
                            p_alive = False
                for _ in range(ratio):
                    if s_alive:
                        try:
                            next(secondary)
                        except StopIteration:
                            s_alive = False

        def run_pipelined(make_gen, order, depth=PIPE_DEPTH):
            active = []
            order = list(order)
            pos = 0
            while pos < len(order) or active:
                if pos < len(order) and len(active) < depth and all(e[1] == 'second' for e in active):
                    active.append([make_gen(order[pos]), 'first']); pos += 1
                for ent in list(active):
                    if ent[1] == 'waiting':
                        if active[0] is ent:
                            ent[1] = 'second'
                        else:
                            continue
                    try:
                        v = next(ent[0])
                        if v == 'prev_done' and ent[1] == 'first':
                            ent[1] = 'second' if active[0] is ent else 'waiting'
                    except StopIteration:
                        active.remove(ent)

        def phase_scan(l, ret):
            P.push()
            nkc, hpc = (2, 2) if ret else (1, 4)
            KW = nkc * 128
            col0, width = (0, 1024) if ret else (1024, 800)
            qo, ko, vo, go = (0, 256, 512, 768) if ret else (0, 128, 256, 512)
            lro = 768
            kbr = 1 if ret else 3
            tri = P.sb("tri", [128, 6, 128]); ld(tri, tri[:], I["tri"][:, :, :])
            mh = P.sb("mh", [128, hpc]); ld(mh, mh[:], I["mh_ret" if ret else "mh_gla"][:, :])
            bd = P.sb("bd", [128, nkc, 256]); ld(bd, bd[:], I["bd_ret" if ret else "bd_gla"][:, :, :])
            gn = P.sb("gn", [128, 256]); ld(gn, gn[:], I["ret_gn" if ret else "gla_gn"][l].partition_broadcast(128))
            lns = math.log(32.0 ** -0.5)
            if ret:
                Ec = P.sb("E", [128, nkc, 6, 128]); Epc = P.sb("Epad", [128, nkc, 2, hpc, 128])
                dtokc = P.sb("dtok", [128, 2, KW]); decc = P.sb("dec", [128, nkc, 2])
                ld(Ec, Ec[:], I["ret_e"][:, :, :, :]); ld(dtokc, dtokc[:], I["ret_tok"][:, :, :]); ld(decc, decc[:], I["ret_dec"][:, :, :])
                for c in range(nkc):
                    for d_ in range(2):
                        dve(lambda e, c=c, d_=d_: e.tensor_tensor(out=Epc[:, c, d_, :, :], in0=Ec[:, c, 1 + 2 * d_, :].unsqueeze(1).broadcast_to([128, hpc, 128]),
                                                                  in1=mh[:].unsqueeze(2).broadcast_to([128, hpc, 128]), op=ALU.mult), [Ec, mh], [Epc])
            else:
                wd = P.sb("wd", [33, 256]); ld(wd, wd[:], I["gla_wd"][l])
                lnsb = P.sb("lnsb", [128, 1])
                dve(lambda e: e.memset(lnsb[:], lns), [], [lnsb])
                psZ = P.ps("psZ", [128, 512]); psB = P.ps("psB", [128, 4, 128])

            class TS:
                pass

            def mk_set(i):
                S = TS()
                S.pt = P.sb(f"pt{i}", [128, width])
                S.Qt = P.sb(f"Qt{i}", [128, nkc, 4, 128], BF16); S.Kp = P.sb(f"Kp{i}", [128, nkc, 2, hpc, 128], BF16)
                S.khat = P.sb(f"khat{i}", [128, 2, KW], BF16); S.Vb = P.sb(f"Vb{i}", [128, 256], BF16)
                S.st1 = P.sb(f"st1{i}", [128, 4, 128]); S.st2 = P.sb(f"st2{i}", [128, 4, 128]); S.PT = P.sb(f"PT{i}", [128, 4, 128], BF16)
                S.hn1 = P.sb(f"hn1{i}", [128, 4]); S.hn2 = P.sb(f"hn2{i}", [128, 4]); S.oc = P.sb(f"oc{i}", [128, 4, 64]); S.osq = P.sb(f"osq{i}", [128, 4, 64])
                S.sg = P.sb(f"sg{i}", [128, 256]); S.resb = P.sb(f"resb{i}", [128, 256], BF16); S.resT = P.sb(f"resT{i}", [128, 2, 128], BF16)
                S.tU = P.sb(f"tU{i}", [128, nkc, 256])
                if ret:
                    S.rot = P.sb(f"rot{i}", [128, 2, 32]); S.qkr = P.sb(f"qkr{i}", [128, 8, 64])
                    S.rt1 = P.sb(f"rt1{i}", [128, 8, 32]); S.rt2 = P.sb(f"rt2{i}", [128, 8, 32])
                    S.E, S.Epad, S.dtok, S.dec = Ec, Epc, dtokc, decc
                else:
                    S.lrT = P.sb(f"lrT{i}", [33, 128]); dve(lambda e: e.memset(S.lrT[:], 1.0), [], [S.lrT])
                    S.et = P.sb(f"et{i}", [128, 256]); S.lt = P.sb(f"lt{i}", [128, 256]); S.bsb = P.sb(f"bsb{i}", [128, 4, 128])
                    S.mids = P.sb(f"mids{i}", [128, 4])
                    S.E = P.sb(f"E{i}", [128, nkc, 6, 128]); S.Epad = P.sb(f"Epad{i}", [128, nkc, 2, hpc, 128])
                    S.dtok = P.sb(f"dtok{i}", [128, 2, KW]); S.dec = P.sb(f"dec{i}", [128, nkc, 2])
                return S

            sets = [mk_set(0), mk_set(1)]
            psT = P.ps("psT", [128, 2 * nkc, 128])
            psS = [P.ps(f"psS{i}", [128, 4, 128]) for i in range(2)]
            psO = P.ps("psO", [128, 256]); psU = P.ps("psU", [128, nkc, 256]); psR = P.ps("psR", [128, 2, 128], BF16)
            Sm = P.sb("Sm", [128, nkc, 256]); Sbf = P.sb("Sbf", [128, nkc, 256], BF16); Sball = P.sb("Sball", [128, NT, nkc, 256], BF16)
            brv = brF.rearrange("(k p) t -> p k t", p=128)

            def prep(n, S, full=True):
                pt = S.pt
                ld(pt, pt[:], projT[n * 128:(n + 1) * 128, col0:col0 + width], src=dB["projT"])
                if ret:
                    rot, qkr, rt1, rt2 = S.rot, S.qkr, S.rt1, S.rt2
                    ld(rot, rot[:], I["rot"][n * 128:(n + 1) * 128, :, :])
                    h0 = 0 if full else 4
                    nh_ = 8 - h0
                    src = pt[:, 0:512].rearrange("p (h d) -> p h d", d=64)[:, h0:8, :]
                    cosb = rot[:, 0, :].unsqueeze(1).broadcast_to([128, nh_, 32]); sinb = rot[:, 1, :].unsqueeze(1).broadcast_to([128, nh_, 32])
                    qkr_full = qkr
                    qkr = qkr[:, h0:8, :]; rt1 = rt1[:, h0:8, :]; rt2 = rt2[:, h0:8, :]
                    gps(lambda e: e.tensor_tensor(out=rt1, in0=src[:, :, 0:32], in1=cosb, op=ALU.mult), [pt, rot], [S.rt1])
                    gps(lambda e: e.tensor_tensor(out=rt2, in0=src[:, :, 32:64], in1=sinb, op=ALU.mult), [pt, rot], [S.rt2])
                    gps(lambda e: e.tensor_tensor(out=qkr[:, :, 0:32], in0=rt1, in1=rt2, op=ALU.subtract), [S.rt1, S.rt2], [S.qkr])
                    gps(lambda e: e.tensor_tensor(out=rt1, in0=src[:, :, 0:32], in1=sinb, op=ALU.mult), [pt, rot, S.qkr], [S.rt1])
                    gps(lambda e: e.tensor_tensor(out=rt2, in0=src[:, :, 32:64], in1=cosb, op=ALU.mult), [pt, rot, S.qkr], [S.rt2])
                    gps(lambda e: e.tensor_tensor(out=qkr[:, :, 32:64], in0=rt1, in1=rt2, op=ALU.add), [S.rt1, S.rt2], [S.qkr])
                    qk = qkr_full[:].rearrange("p h d -> p (h d)")
                    S.q_tok, S.k_tok, S.qkb = qk[:, 0:256], qk[:, 256:512], qkr_full
                else:
                    lrT, et, lt, bsb, mids, E, Epad, dtok, dec = S.lrT, S.et, S.lt, S.bsb, S.mids, S.E, S.Epad, S.dtok, S.dec
                    S.q_tok, S.k_tok, S.qkb = pt[:, qo:qo + 128], pt[:, ko:ko + 128], pt
                    pe(lambda e: e.transpose(psZ[0:32, 256:384], pt[:, lro:lro + 32], identf[:]), [pt, identf], [psZ])
                    yield
                    dve(lambda e: e.tensor_copy(out=lrT[0:32, :], in_=psZ[0:32, 256:384]), [psZ], [lrT])
                    yield
                    pe(lambda e: e.matmul(psZ[:, 0:256], lhsT=lrT[:], rhs=wd[:], start=True, stop=True), [lrT, wd], [psZ])
                    yield
                    act(lambda e: e.activation(out=et[:], in_=psZ[:, 0:256], func=AF.Exp, scale=-1.0), [psZ], [et])
                    act(lambda e: e.activation(out=lt[:], in_=et[:], func=AF.Ln, bias=1.0), [et], [lt])
                    yield
                    pe(lambda e: e.matmul(psB[:, 0, :], lhsT=lt[:, 0:128], rhs=tri[:, 2, :], start=True, stop=True), [lt, tri], [psB])
                    pe(lambda e: e.matmul(psB[:, 1, :], lhsT=lt[:, 128:256], rhs=tri[:, 3, :], start=True, stop=True), [lt, tri], [psB])
                    pe(lambda e: e.matmul(psB[:, 2, :], lhsT=tri[:, 4, :], rhs=lt[:, 0:128], start=True, stop=True), [lt, tri], [psB])
                    pe(lambda e: e.matmul(psB[:, 3, :], lhsT=tri[:, 5, :], rhs=lt[:, 128:256], start=True, stop=True), [lt, tri], [psB])
                    yield
                    dve(lambda e: e.tensor_copy(out=bsb[:], in_=psB[:]), [psB], [bsb])
                    dve(lambda e: e.tensor_scalar(out=mids[:, 0:2], in0=bsb[:, 0:2, 64], scalar1=-1.0, scalar2=lns, op0=ALU.mult, op1=ALU.add), [bsb], [mids])
                    dve(lambda e: e.tensor_copy(out=mids[:, 2:4], in_=bsb[:, 0:2, 64]), [bsb], [mids])
                    yield
                    for d_ in (range(2) if full else (1,)):
                        if full:
                            act(lambda e, d_=d_: e.activation(out=E[:, 0, 2 * d_, :], in_=bsb[:, d_, :], func=AF.Exp, bias=mids[:, d_:d_ + 1]), [bsb, mids], [E])
                            act(lambda e, d_=d_: e.activation(out=E[:, 0, 2 * d_ + 1, :], in_=bsb[:, d_, :], func=AF.Exp, scale=-1.0, bias=mids[:, 2 + d_:3 + d_]), [bsb, mids], [E])
                            act(lambda e, d_=d_: e.activation(out=E[:, 0, 4 + d_, :], in_=bsb[:, d_, :], func=AF.Exp, bias=lnsb[:, 0:1]), [bsb, lnsb], [E])
                        act(lambda e, d_=d_: e.activation(out=dtok[:, d_, :], in_=bsb[:, 2 + d_, :], func=AF.Exp), [bsb], [dtok])
                    if full:
                        act(lambda e: e.activation(out=dec[:, 0, 0:1], in_=bsb[:, 0, 127:128], func=AF.Exp), [bsb], [dec])
                    act(lambda e: e.activation(out=dec[:, 0, 1:2], in_=bsb[:, 1, 0:1], func=AF.Exp), [bsb], [dec])
                    yield
                    if full:
                        for d_ in range(2):
                            dve(lambda e, d_=d_: e.tensor_tensor(out=Epad[:, 0, d_, :, :], in0=E[:, 0, 1 + 2 * d_, :].unsqueeze(1).broadcast_to([128, hpc, 128]),
                                                                 in1=mh[:].unsqueeze(2).broadcast_to([128, hpc, 128]), op=ALU.mult), [E, mh], [Epad])
                act(lambda e: e.activation(out=S.Vb[:], in_=pt[:, vo:vo + 256], func=AF.Copy), [pt], [S.Vb])
                for d_ in (range(2) if full else (1,)):
                    gps(lambda e, d_=d_: e.tensor_tensor(out=S.khat[:, d_, :], in0=S.k_tok, in1=S.dtok[:, d_, :], op=ALU.mult), [S.qkb, S.dtok], [S.khat])
                yield

            def state_update(d_, S):
                for c in range(nkc):
                    pe(lambda e, c=c: e.matmul(psU[:, c, :], lhsT=S.khat[:, d_, c * 128:(c + 1) * 128], rhs=S.Vb[:], start=True, stop=True), [S.khat, S.Vb], [psU])
                yield
                dve(lambda e: e.tensor_tensor(out=S.tU[:], in0=psU[:], in1=bd[:], op=ALU.mult), [psU, bd], [S.tU])
                for c in range(nkc):
                    dve(lambda e, c=c: e.scalar_tensor_tensor(out=Sm[:, c, :], in0=Sm[:, c, :], scalar=S.dec[:, c, d_:d_ + 1], in1=S.tU[:, c, :],
                                                              op0=ALU.mult, op1=ALU.add), [Sm, S.dec, S.tU], [Sm])

            def keep_mul():
                dve(lambda e: e.tensor_scalar(out=Sm[:], in0=Sm[:], scalar1=scal[:, 0:1], scalar2=None, op0=ALU.mult), [Sm, scal], [Sm])

            def gen1(n):
                S = sets[n % 2]
                yield from prep(n, S, full=False)
                yield 'prev_done'
                act(lambda e: e.activation(out=Sball[:, n, :, :], in_=Sm[:], func=AF.Copy), [Sm], [Sball])
                yield from state_update(1, S)
                if n == NT // 2:
                    keep_mul()

            dve(lambda e: e.memset(Sm[:], 0.0), [], [Sm])
            run_pipelined(gen1, reversed(range(NT)), depth=int(os.environ.get('PIPE1', '2')))

            def gen2(n):
                S = sets[n % 2]
                if PD_POS == 0:
                    yield 'prev_done'
                yield from prep(n, S)
                if PD_POS == 1:
                    yield 'prev_done'
                Qt, Kp, PT, Vb, E, Epad = S.Qt, S.Kp, S.PT, S.Vb, S.E, S.Epad
                for c in range(nkc):
                    pe(lambda e, c=c: e.transpose(psT[:, c, :], S.q_tok[:, c * 128:(c + 1) * 128], identf[:]), [S.qkb, identf], [psT])
                    pe(lambda e, c=c: e.transpose(psT[:, nkc + c, :], S.k_tok[:, c * 128:(c + 1) * 128], identf[:]), [S.qkb, identf], [psT])
                yield
                if PD_POS == 2:
                    yield 'prev_done'
                for c in range(nkc):
                    for vi, ei in enumerate((0, 2, 4, 5)):
                        dve(lambda e, c=c, vi=vi, ei=ei: e.tensor_tensor(out=Qt[:, c, vi, :], in0=psT[:, c, :], in1=E[:, c, ei, :], op=ALU.mult), [psT, E], [Qt])
                    for d_ in range(2):
                        dve(lambda e, c=c, d_=d_: e.tensor_tensor(out=Kp[:, c, d_, :, :], in0=psT[:, nkc + c, :].unsqueeze(1).broadcast_to([128, hpc, 128]),
                                                                  in1=Epad[:, c, d_, :, :], op=ALU.mult), [psT, Epad], [Kp])
                yield
                if PD_POS == 3:
                    yield 'prev_done'
                for d_ in range(2):
                    for c in range(nkc):
                        for hh in range(hpc):
                            pe(lambda e, d_=d_, c=c, hh=hh: e.matmul(psS[d_][:, c * hpc + hh, :], lhsT=Kp[:, c, d_, hh, :], rhs=Qt[:, c, d_, :], start=True, stop=True),
                               [Kp, Qt], [psS[d_]])
                yield
                if PD_POS == 4:
                    yield 'prev_done'
                dve(lambda e: e.tensor_tensor(out=S.st1[:], in0=psS[0][:], in1=tri[:, 0, :].unsqueeze(1).broadcast_to([128, 4, 128]), op=ALU.mult), [psS[0], tri], [S.st1])
                dve(lambda e: e.tensor_tensor(out=S.st2[:], in0=psS[1][:], in1=tri[:, 1, :].unsqueeze(1).broadcast_to([128, 4, 128]), op=ALU.mult), [psS[1], tri], [S.st2])
                gps(lambda e: e.tensor_tensor(out=PT[:], in0=S.st1[:], in1=S.st2[:], op=ALU.add), [S.st1, S.st2], [PT])
                yield 'prev_done'
                if n == NT // 2:
                    keep_mul()
                act(lambda e: e.activation(out=Sbf[:], in_=Sm[:], func=AF.Copy), [Sm], [Sbf])
                yield
                for h_ in range(4):
                    c = h_ // hpc
                    hs = slice(h_ * 64, (h_ + 1) * 64)
                    pe(lambda e, c=c, hs=hs: e.matmul(psO[:, hs], lhsT=Qt[:, c, 2, :], rhs=Sbf[:, c, hs], start=True, stop=False), [Qt, Sbf], [psO])
                    pe(lambda e, c=c, hs=hs: e.matmul(psO[:, hs], lhsT=Qt[:, c, 3, :], rhs=Sball[:, n, c, hs], start=False, stop=False), [Qt, Sball], [psO])
                    pe(lambda e, h_=h_, hs=hs: e.matmul(psO[:, hs], lhsT=PT[:, h_, :], rhs=Vb[:, hs], start=False, stop=True), [PT, Vb], [psO])
                yield
                hn1, hn2, oc, osq, sg, resb, resT, pt = S.hn1, S.hn2, S.oc, S.osq, S.sg, S.resb, S.resT, S.pt
                O3 = psO[:].rearrange("p (h d) -> p h d", d=64)
                if ret:
                    dve(lambda e: e.tensor_reduce(out=hn1[:], in_=O3, axis=mybir.AxisListType.X, op=ALU.add), [psO], [hn1])
                    dve(lambda e: e.tensor_scalar(out=hn1[:], in0=hn1[:], scalar1=-1.0 / 64, scalar2=None, op0=ALU.mult), [hn1], [hn1])
                    dve(lambda e: e.tensor_tensor(out=oc[:], in0=O3, in1=hn1[:].unsqueeze(2).broadcast_to([128, 4, 64]), op=ALU.add), [psO, hn1], [oc])
                else:
                    dve(lambda e: e.tensor_copy(out=oc[:], in_=O3), [psO], [oc])
                gps(lambda e: e.tensor_tensor(out=osq[:], in0=oc[:], in1=oc[:], op=ALU.mult), [oc], [osq])
                dve(lambda e: e.tensor_reduce(out=hn2[:], in_=osq[:], axis=mybir.AxisListType.X, op=ALU.add), [osq], [hn2])
                act(lambda e: e.activation(out=sg[:], in_=pt[:, go:go + 256], func=AF.Silu), [pt], [sg])
                act(lambda e: e.activation(out=hn2[:], in_=hn2[:], func=AF.Sqrt, scale=1.0 / 64, bias=epsb[:, 0:1]), [hn2, epsb], [hn2])
                yield
                dve(lambda e: e.reciprocal(out=hn2[:], in_=hn2[:]), [hn2], [hn2])
                gps(lambda e: e.tensor_tensor(out=oc[:], in0=oc[:], in1=hn2[:].unsqueeze(2).broadcast_to([128, 4, 64]), op=ALU.mult), [oc, hn2], [oc])
                gps(lambda e: e.tensor_tensor(out=sg[:], in0=sg[:], in1=gn[:], op=ALU.mult), [sg, gn], [sg])
                gps(lambda e: e.tensor_tensor(out=resb[:], in0=oc[:].rearrange("p h d -> p (h d)"), in1=sg[:], op=ALU.mult), [oc, sg], [resb])
                yield
                for c2_ in range(2):
                    pe(lambda e, c2_=c2_: e.transpose(psR[:, c2_, :], resb[:, c2_ * 128:(c2_ + 1) * 128], identb[:]), [resb, identb], [psR])
                yield from state_update(0, S)
                dve(lambda e: e.tensor_copy(out=resT[:], in_=psR[:]), [psR], [resT])
                stq(brv[:, 2 * kbr:2 * kbr + 2, n * 128:(n + 1) * 128], resT, resT[:], dB["brF"])

            dve(lambda e: e.memset(Sm[:], 0.0), [], [Sm])
            run_pipelined(gen2, range(NT), depth=int(os.environ.get('PIPE2', '2')))
            P.pop()

        def phase_hyena(l, mode='all'):
            NBLK = 2 * T // 512
            Cg = 32
            TWO_PI = 2.0 * math.pi
            def part1():
                P.push()
                w1 = P.sb("w1", [33, 64]); w2 = P.sb("w2", [64, 64]); w3a = P.sb("w3a", [65, 1024])
                c1 = P.sb("c1", [64, 2]); c2_ = P.sb("c2", [64, 2]); fb = P.sb("fb", [64, 2]); delta = P.sb("delta", [128, 2])
                ld(w1, w1[:], I["flt_w1"][l]); ld(w2, w2[:], I["flt_w2"][l]); ld(w3a, w3a[0:64, :], I["flt_w3"][l])
                ld(w3a, w3a[64:65, :], I["flt_b3"][l:l + 1, :])
                ld(c1, c1[:], I["flt_c1"][l]); ld(c2_, c2_[:], I["flt_c2"][l]); ld(delta, delta[:], I["flt_delta"][:, :])
                dve(lambda e: e.tensor_tensor(out=fb[:, 0:1], in0=c1[:, 0:1], in1=c1[:, 1:2], op=ALU.mult), [c1], [fb])
                dve(lambda e: e.tensor_tensor(out=fb[:, 1:2], in0=c2_[:, 0:1], in1=c2_[:, 1:2], op=ALU.mult), [c2_, fb], [fb])
                h2a = P.sb("h2a", [65, 512], BF16); dve(lambda e: e.memset(h2a[:], 1.0), [], [h2a])
                w3b = P.sb("w3b", [65, 1024], BF16); dve(lambda e: e.tensor_copy(out=w3b[:], in_=w3a[:]), [w3a], [w3b])
                h1 = P.sb("h1", [64, 512]); a1 = P.sb("a1", [64, 512]); kk = P.sb("kk", [64, 512])
                nrm = P.sb("nrm", [128, 4, NBLK]); rn = P.sb("rn", [128, 4])
                fts = [P.sb(f"ft{i}", [33, 512]) for i in range(2)]; msks = [P.sb(f"msk{i}", [128, 3, 512]) for i in range(2)]
                win = P.sb("win", [128, 2, 512]); t1 = P.sb("ft1", [128, 512]); t2 = P.sb("ft2", [128, 512]); ab = P.sb("fab", [128, 512])
                gbs = [P.sb(f"gb{i}", [128, 512], BF16) for i in range(2)]
                psh = P.ps("psh", [64, 512]); psf = [P.ps(f"psf{i}", [128, 512]) for i in range(2)]

                def sin_layer(cc, col, dst):
                    dve(lambda e: e.tensor_scalar(out=a1[:], in0=psh[:], scalar1=cc[:, 0:1], scalar2=fb[:, col:col + 1], op0=ALU.mult, op1=ALU.add), [psh, cc, fb], [a1])
                    dve(lambda e: e.tensor_scalar(out=kk[:], in0=a1[:], scalar1=1.0 / TWO_PI, scalar2=MAGIC, op0=ALU.mult, op1=ALU.add), [a1], [kk])
                    dve(lambda e: e.tensor_scalar(out=kk[:], in0=kk[:], scalar1=-MAGIC, scalar2=None, op0=ALU.add), [kk], [kk])
                    dve(lambda e: e.scalar_tensor_tensor(out=a1[:], in0=kk[:], scalar=-TWO_PI, in1=a1[:], op0=ALU.mult, op1=ALU.add), [kk, a1], [a1])
                    act(lambda e: e.activation(out=dst, in_=a1[:], func=AF.Sin), [a1], [h1 if dst is not None and cc is c1 else h2a])

                ng = 0
                for blk in range(NBLK):
                    m0 = blk * 512
                    ft, msk = fts[blk % 2], msks[blk % 2]
                    ld(ft, ft[:], I["flt_feat"][:, m0:m0 + 512])
                    for r_ in range(3):
                        ld(msk, msk[:, r_, :], I["flt_msk"][r_, m0:m0 + 512].partition_broadcast(128))
                    pe(lambda e: e.matmul(psh[:], lhsT=w1[:], rhs=ft[:], start=True, stop=True), [w1, ft], [psh])
                    sin_layer(c1, 0, h1[:])
                    yield
                    pe(lambda e: e.matmul(psh[:], lhsT=w2[:], rhs=h1[:], start=True, stop=True), [w2, h1], [psh])
                    sin_layer(c2_, 1, h2a[0:64, :])
                    yield
                    for ch in range(2):
                        act(lambda e, ch=ch: e.activation(out=win[:, ch, :], in_=msk[:, 2, :], func=AF.Exp, scale=delta[:, ch:ch + 1]), [msk, delta], [win])
                    for o in range(2):
                        for ch in range(2):
                            for dr in range(2):
                                q = o * 4 + dr * 2 + ch
                                pe(lambda e, dr=dr, q=q: e.matmul(psf[dr][:], lhsT=w3b[:, q * 128:(q + 1) * 128], rhs=h2a[:], start=True, stop=True), [w3b, h2a], [psf[dr]])
                            dve(lambda e: e.tensor_tensor(out=t1[:], in0=psf[0][:], in1=msk[:, 0, :], op=ALU.mult), [psf[0], msk], [t1])
                            dve(lambda e: e.tensor_tensor(out=t2[:], in0=psf[1][:], in1=msk[:, 1, :], op=ALU.mult), [psf[1], msk], [t2])
                            dve(lambda e: e.tensor_tensor(out=t1[:], in0=t1[:], in1=t2[:], op=ALU.add), [t1, t2], [t1])
                            dve(lambda e, ch=ch: e.tensor_tensor(out=t1[:], in0=t1[:], in1=win[:, ch, :], op=ALU.mult), [t1, win], [t1])
                            gb = gbs[ng % 2]; ng += 1
                            idx = o * 2 + ch
                            act(lambda e, gb=gb: e.activation(out=gb[:], in_=t1[:], func=AF.Copy), [t1], [gb])
                            act(lambda e, idx=idx, blk=blk: e.activation(out=ab[:], in_=t1[:], func=AF.Abs, accum_out=nrm[:, idx, blk:blk + 1]), [t1], [ab, nrm])
                            stq(gF[idx * 128:(idx + 1) * 128, m0:m0 + 512], gb, gb[:], dB["gF"])
                            yield
                dve(lambda e: e.tensor_reduce(out=rn[:], in_=nrm[:], axis=mybir.AxisListType.X, op=ALU.add), [nrm], [rn])
                dve(lambda e: e.tensor_scalar(out=rn[:], in0=rn[:], scalar1=scal[:, 1:2], scalar2=EPS, op0=ALU.mult, op1=ALU.add), [rn, scal], [rn])
                dve(lambda e: e.reciprocal(out=rn[:], in_=rn[:]), [rn], [rn])
                stq(rnD.rearrange("(q p) -> p q", p=128), rn, rn[:], dB["rnD"])
                P.pop()

            def stage1(din, m1, AA, ps1s, cnt, Cg=Cg):
                for c in range(0, Cg, 2):
                    ps1 = ps1s[cnt[0] % 2]; cnt[0] += 1
                    for cc in range(2):
                        pe(lambda e, cc=cc, ps1=ps1, c=c: e.matmul(ps1[:, cc, :], lhsT=din[:, c + cc, :], rhs=m1[:], start=True, stop=True), [din, m1], [ps1])
                    for ri in range(2):
                        src_ap = ps1[:, :, ri * NSA:(ri + 1) * NSA].rearrange("p c s -> p s c")
                        if ri == 0:
                            dve(lambda e, src_ap=src_ap, c=c: e.tensor_copy(out=AA[:, :, 0, c:c + 2], in_=src_ap), [ps1], [AA])
                        else:
                            act(lambda e, src_ap=src_ap, c=c: e.activation(out=AA[:, :, 1, c:c + 2], in_=src_ap, func=AF.Copy), [ps1], [AA])
                    yield

            def stage2(AA, h2ts, psXs, evac, spb=8):
                h2v = I["hy_h2"].rearrange("s p a k -> p s a k")
                for j in range(NSA):
                    h2t = h2ts[(j // 8) % 2]; jj = j % spb
                    if j % 8 == 0:
                        nj_ = min(8, NSA - j)
                        ld(h2t, h2t[:, 0:nj_, :, :], h2v[:, j:j + nj_, :, :])
                    psX = psXs[(j // spb) % 2]
                    j8 = j % 8
                    pe(lambda e, psX=psX, jj=jj, h2t=h2t, j=j, j8=j8: e.matmul(psX[:, jj, 0, :], lhsT=h2t[:, j8, 0, :], rhs=AA[:, j, 0, :], start=True, stop=False), [h2t, AA], [psX])
                    pe(lambda e, psX=psX, jj=jj, h2t=h2t, j=j, j8=j8: e.matmul(psX[:, jj, 0, :], lhsT=h2t[:, j8, 2, :], rhs=AA[:, j, 1, :], start=False, stop=True), [h2t, AA], [psX])
                    pe(lambda e, psX=psX, jj=jj, h2t=h2t, j=j, j8=j8: e.matmul(psX[:, jj, 1, :], lhsT=h2t[:, j8, 0, :], rhs=AA[:, j, 1, :], start=True, stop=False), [h2t, AA], [psX])
                    pe(lambda e, psX=psX, jj=jj, h2t=h2t, j=j, j8=j8: e.matmul(psX[:, jj, 1, :], lhsT=h2t[:, j8, 1, :], rhs=AA[:, j, 0, :], start=False, stop=True), [h2t, AA], [psX])
                    if jj == spb - 1 or j == NSA - 1:
                        evac(psX, j - jj, jj + 1)
                        yield

            def part2():
                P.push()
                rnb = P.sb("rnb", [128, 512]); ld(rnb, rnb[:], rnD.partition_broadcast(128), src=dB["rnD"])
                hf1 = P.sb("hf1", [NS, 2 * NSA], BF16); ld(hf1, hf1[:], I["hy_hf1"][:, :])
                Cf = 64
                gin = P.sb("gin", [NS, Cf, 128], BF16); AA = P.sb("AAf", [128, NSA, 2, Cf], BF16)
                Gsb = P.sb("Gsbf", [128, NSA, 2, Cf], BF16)
                ps1s = [P.ps(f"hps1f{i}", [128, 2, 2 * NSA]) for i in range(2)]; psXs = [P.ps(f"hpsXf{i}", [128, 4, 2, Cf]) for i in range(2)]
                h2ts = [P.sb(f"h2tf{i}", [128, 8, 3, 128], BF16) for i in range(2)]
                cnt = [0]
                for gi in range(512 // Cf):
                    ld(gin, gin[:], gF[gi * Cf:(gi + 1) * Cf, :].rearrange("c (g n) -> g c n", n=128), src=dB["gF"])
                    yield from stage1(gin, hf1, AA, ps1s, cnt, Cg=Cf)

                    def evacG(psX, j0, nj, gi=gi):
                        dve(lambda e: e.tensor_tensor(out=Gsb[:, j0:j0 + nj, :, :], in0=psX[:, 0:nj, :, :],
                                                      in1=rnb[:, gi * Cf:(gi + 1) * Cf].unsqueeze(1).unsqueeze(1).broadcast_to([128, nj, 2, Cf]), op=ALU.mult),
                            [psX, rnb], [Gsb])
                    yield from stage2(AA, h2ts, psXs, evacG, spb=4)
                    for hh_ in range(2):
                        for s0_ in range(0, NSA, 32):
                            s1_ = min(NSA, s0_ + 32)
                            stq(Gd[2 * gi + hh_].rearrange("p s (r c) -> p s r c", r=2)[:, s0_:s1_], Gsb, Gsb[:, s0_:s1_, :, hh_ * 32:(hh_ + 1) * 32], dB["Gd"])
                P.pop()

            def part3():
                P.push()
                hz = P.sb("hz", [NSA, 128, 2, NT], BF16); ld(hz, hz[:], I["hy_z"][:, :, :, :])
                h1t = P.sb("hh1", [NT, 2 * NSA], BF16); ld(h1t, h1t[:], I["hy_h1"][:, :])
                i1 = P.sb("hi1", [128, 2, 256], BF16); ld(i1, i1[:], I["hy_i1"][:, :, :])
                skb = P.sb("skb", [128, 2, 256])
                for o in range(2):
                    ld(skb, skb[:, o, :], I["hy_skip"][l, o].partition_broadcast(128))
                AA = P.sb("AAd", [128, NSA, 2, Cg], BF16); Ysb = P.sb("Ysb", [128, 2, Cg, NSA], BF16); Bsb = P.sb("Bsb", [NSA, 128, 2, Cg], BF16)
                Gsb = P.sb("Gsbd", [128, NSA, 2, Cg], BF16)
                din = P.sb("din", [NT, Cg, 128], BF16); vt = P.sb("vt", [NT, Cg, 128]); x1t = P.sb("x1t", [NT, Cg, 128]); x2t = P.sb("x2t", [NT, Cg, 128])
                ob = P.sb("ob", [NT, Cg, 128], BF16)
                pw = [P.sb(f"pw{i}", [128, 8, Cg]) for i in range(4)]
                tcv = P.sb("tcv", [NT, Cg, 16])
                ps1s = [P.ps(f"hps1d{i}", [128, 2, 2 * NSA]) for i in range(2)]; psXs = [P.ps(f"hpsXd{i}", [128, 8, 2, Cg]) for i in range(2)]
                psIs = [P.ps(f"hpsI{i}", [NSA, 2, 256]) for i in range(2)]; psYs = [P.ps(f"hpsY{i}", [NT, 16, Cg]) for i in range(2)]
                h2ts = [P.sb(f"h2td{i}", [128, 8, 3, 128], BF16) for i in range(2)]
                cnt = [0]

                def evacY(psX, j0, nj):
                    Xre, Xim = psX[:, 0:nj, 0, :], psX[:, 0:nj, 1, :]
                    Gre, Gim = Gsb[:, j0:j0 + nj, 0, :], Gsb[:, j0:j0 + nj, 1, :]
                    dve(lambda e: e.tensor_tensor(out=pw[0][:, 0:nj, :], in0=Xre, in1=Gre, op=ALU.mult), [psX, Gsb], [pw[0]])
                    dve(lambda e: e.tensor_tensor(out=pw[1][:, 0:nj, :], in0=Xim, in1=Gim, op=ALU.mult), [psX, Gsb], [pw[1]])
                    dve(lambda e: e.tensor_tensor(out=Ysb[:, 0, :, j0:j0 + nj].rearrange("p c j -> p j c"), in0=pw[0][:, 0:nj, :], in1=pw[1][:, 0:nj, :], op=ALU.subtract),
                        [pw[0], pw[1]], [Ysb])
                    dve(lambda e: e.tensor_tensor(out=pw[2][:, 0:nj, :], in0=Xre, in1=Gim, op=ALU.mult), [psX, Gsb], [pw[2]])
                    dve(lambda e: e.tensor_tensor(out=pw[3][:, 0:nj, :], in0=Xim, in1=Gre, op=ALU.mult), [psX, Gsb], [pw[3]])
                    dve(lambda e: e.tensor_tensor(out=Ysb[:, 1, :, j0:j0 + nj].rearrange("p c j -> p j c"), in0=pw[2][:, 0:nj, :], in1=pw[3][:, 0:nj, :], op=ALU.add),
                        [pw[2], pw[3]], [Ysb])

                def long_conv(o, g, xg, svt):
                    ld(Gsb, Gsb[:].rearrange("p s r c -> p s (r c)"), Gd[o * (256 // Cg) + g], src=dB["Gd"])
                    drain(stage1(din, h1t, AA, ps1s, cnt))
                    drain(stage2(AA, h2ts, psXs, evacY))
                    for c in range(0, Cg, 2):
                        psI = psIs[(c // 2) % 2]
                        for cc in range(2):
                            pe(lambda e, cc=cc, psI=psI, c=c: e.matmul(psI[:, cc, :], lhsT=Ysb[:, 0, c + cc, :], rhs=i1[:, 0, :], start=True, stop=False), [Ysb, i1], [psI])
                            pe(lambda e, cc=cc, psI=psI, c=c: e.matmul(psI[:, cc, :], lhsT=Ysb[:, 1, c + cc, :], rhs=i1[:, 1, :], start=False, stop=True), [Ysb, i1], [psI])
                        for ri in range(2):
                            src_ap = psI[:, :, ri * 128:(ri + 1) * 128].rearrange("p c n -> p n c")
                            if ri == 0:
                                dve(lambda e, src_ap=src_ap, c=c: e.tensor_copy(out=Bsb[:, :, 0, c:c + 2], in_=src_ap), [psI], [Bsb])
                            else:
                                act(lambda e, src_ap=src_ap, c=c: e.activation(out=Bsb[:, :, 1, c:c + 2], in_=src_ap, func=AF.Copy), [psI], [Bsb])
                    for nb in range(8):
                        psY = psYs[nb % 2]
                        for q in range(16):
                            n2 = nb * 16 + q
                            pe(lambda e, psY=psY, q=q, n2=n2: e.matmul(psY[:, q, :], lhsT=hz[:, n2, 0, :], rhs=Bsb[:, n2, 0, :], start=True, stop=False), [hz, Bsb], [psY])
                            pe(lambda e, psY=psY, q=q, n2=n2: e.matmul(psY[:, q, :], lhsT=hz[:, n2, 1, :], rhs=Bsb[:, n2, 1, :], start=False, stop=True), [hz, Bsb], [psY])
                        sl = slice(nb * 16, (nb + 1) * 16)
                        dve(lambda e, psY=psY, sl=sl: e.tensor_tensor(out=tcv[:], in0=psY[:].rearrange("p n c -> p c n"), in1=svt[:, :, sl], op=ALU.add), [psY, svt], [tcv])
                        dve(lambda e, sl=sl: e.tensor_tensor(out=xg[:, :, sl], in0=tcv[:], in1=xg[:, :, sl], op=ALU.mult), [tcv, xg], [xg])

                uv = lambda r0: uhF[r0:r0 + Cg, :].rearrange("c (g n) -> g c n", n=128)
                for g in range(256 // Cg):
                    c0 = g * Cg
                    ld(vt, vt[:], uv(c0), src=dB["uhF"]); ld(x1t, x1t[:], uv(256 + c0), src=dB["uhF"]); ld(x2t, x2t[:], uv(512 + c0), src=dB["uhF"])
                    act(lambda e: e.activation(out=din[:], in_=vt[:], func=AF.Copy), [vt], [din])
                    dve(lambda e, c0=c0: e.tensor_tensor(out=vt[:], in0=vt[:], in1=skb[0:NT, 0, c0:c0 + Cg].unsqueeze(2).broadcast_to([NT, Cg, 128]), op=ALU.mult), [vt, skb], [vt])
                    long_conv(0, g, x1t, vt)
                    act(lambda e: e.activation(out=din[:], in_=x1t[:], func=AF.Copy), [x1t], [din])
                    dve(lambda e, c0=c0: e.tensor_tensor(out=vt[:], in0=x1t[:], in1=skb[0:NT, 1, c0:c0 + Cg].unsqueeze(2).broadcast_to([NT, Cg, 128]), op=ALU.mult), [x1t, skb], [vt])
                    long_conv(1, g, x2t, vt)
                    act(lambda e: e.activation(out=ob[:], in_=x2t[:], func=AF.Copy), [x2t], [ob])
                    stq(brF[512 + c0:512 + c0 + Cg, :].rearrange("c (g n) -> g c n", n=128), ob, ob[:], dB["brF"])
                P.pop()
            if mode == 'filtgen':
                def both():
                    yield from part1()
                    yield from part2()
                return both()
            if mode in ('all', 'filt'):
                drain(part1())
                drain(part2())
            if mode in ('all', 'conv'):
                part3()

        def phase_zero_br(l):
            P.push()
            zt = P.sb("zbr", [128, 2048], BF16)
            dve(lambda e: e.memset(zt[:], 0.0), [], [zt])
            for k in range(8):
                for t0 in range(0, T, 2048):
                    w_ = min(2048, T - t0)
                    stq(brF[k * 128:(k + 1) * 128, t0:t0 + w_], zt, zt[:, 0:w_], dB["brF"])
            P.pop()

        PHASES = dict(mod=phase_mod, norm=phase_norm, A=lambda l: drain(phase_A(l)), Afilt=lambda l: run_concurrent(phase_A(l), phase_hyena(l, 'filtgen')), C=phase_C, D=phase_D, zero=phase_zero_br, fnet=phase_fnet, ret=lambda l: phase_scan(l, True), gla=lambda l: phase_scan(l, False), hyena=phase_hyena, hyfilt=lambda l: phase_hyena(l, 'filt'), hyconv=lambda l: phase_hyena(l, 'conv'))
        nc._I = I
        return_hook(P, PHASES, locals())
    return nc


def return_hook(P, PHASES, env):
    sched = env.get('debug') or ()
    I, dB = env['I'], env['dB']
    stop = None
    for d in sched:
        if isinstance(d, str) and d.startswith("stop:"):
            stop = d[5:]
    x_in, x1d, xmid, y_out = env['x_in'], env['x1d'], env['xmid'], env['y_out']
    Am, Af = env['Am'], env['Af']
    only = [d[5:] for d in sched if isinstance(d, str) and d.startswith("only:")]
    if only:
        for nm in only:
            if nm == 'norm':
                PHASES['norm'](x_in, dB["in"], Am, 0)
            elif nm == 'C':
                PHASES['C'](0, x_in, dB["in"])
            elif nm == 'D':
                PHASES['D'](0, x1d, dB["x1d"])
            else:
                PHASES[nm](0)
        P.barrier()
        return
    for l in range(DEPTH):
        src, sbuf = (x_in, dB["in"]) if l == 0 else (x1d, dB["x1d"])
        dst, dbuf = (x1d, dB["x1d"]) if l == 0 else (y_out, dB["y"])
        if stop == "none":
            break
        PHASES['mod'](l)
        if stop == "mod":
            break
        PHASES['norm'](src, sbuf, Am, 0)
        if stop == "norm":
            break
        PHASES['Afilt' if CONC_FILT else 'A'](l)
        if stop == "A":
            break
        PHASES['zero'](l)
        for nm in ('fnet', 'ret', 'hyconv' if CONC_FILT else 'hyena', 'gla'):
            if nm in PHASES:
                PHASES[nm](l)
        if stop == "mix":
            break
        PHASES['C'](l, src, sbuf)
        PHASES['norm'](xmid, dB["xmid"], Af, 24)
        PHASES['D'](l, dst, dbuf)
        if stop == "L0":
            break
    P.barrier()


def prep_core_inputs(x, c2, W, tb):
    m = {"x": np.ascontiguousarray(x, np.float32)}
    m["cT"] = np.ascontiguousarray(c2.reshape(2, 8, 128).transpose(2, 1, 0), np.float32)
    m.update(W)
    m.update(tb)
    return m


def prep_weights(inp):
    f = lambda a: np.ascontiguousarray(a, np.float32)
    W = {}
    W["ada_w"] = f(inp["ada_w"]); W["ada_b"] = f(inp["ada_b"])
    W["ada_b_col"] = f(inp["ada_b"].reshape(DEPTH, 48, 128).transpose(0, 2, 1))
    nw = np.stack([inp["norm_pre_mix"], inp["norm_post_mix"], inp["norm_pre_ffn"], inp["norm_post_ffn"]], 1)
    W["normw_col"] = f(nw.reshape(DEPTH, 4, 8, 128).transpose(0, 3, 1, 2))
    W["norm_post_mix"] = f(inp["norm_post_mix"]); W["norm_post_ffn"] = f(inp["norm_post_ffn"])
    W["w_in"] = f(inp["w_in"])
    W["hy_cw"] = f(inp["hy_conv_w"].reshape(DEPTH, 3, 6, 128).transpose(0, 3, 2, 1))
    W["hy_cb"] = f(inp["hy_conv_b"].reshape(DEPTH, 6, 128).transpose(0, 2, 1))
    W["flt_w1"] = f(inp["flt_w1"]); W["flt_w2"] = f(inp["flt_w2"]); W["flt_w3"] = f(inp["flt_w3"]); W["flt_b3"] = f(inp["flt_b3"])
    W["flt_c1"] = f(np.stack([inp["flt_freq"], inp["flt_b1"]], -1)); W["flt_c2"] = f(np.stack([inp["flt_freq"], inp["flt_b2"]], -1))
    W["hy_skip"] = f(inp["hy_skip"])
    wd = np.zeros((DEPTH, 33, 256), np.float32)
    wd[:, 0:16, 0:128] = inp["gla_w_decay"][:, 0]; wd[:, 16:32, 128:256] = inp["gla_w_decay"][:, 1]
    wd[:, 32, 0:128] = inp["gla_b_decay"][:, 0]; wd[:, 32, 128:256] = inp["gla_b_decay"][:, 1]
    W["gla_wd"] = wd
    W["ret_gn"] = f(inp["ret_gn"]); W["gla_gn"] = f(inp["gla_gn"])
    W["w_branch"] = f(inp["w_branch"].reshape(DEPTH, 1024, D)); W["w_out"] = f(inp["w_out"])
    W["ffn_up"] = f(inp["ffn_up"])
    W["ffn_cw"] = f(inp["ffn_conv_w"].reshape(DEPTH, 3, 44, 128).transpose(0, 3, 2, 1))
    W["ffn_cb"] = f(inp["ffn_conv_b"].reshape(DEPTH, 44, 128).transpose(0, 2, 1))
    W["ffn_down"] = f(inp["ffn_down"])
    return W


_T = 8192


def kernel(**inp):
    inp = {k: np.asarray(v) for k, v in inp.items()}
    T = _T
    W = prep_weights(inp)
    tbP, tbS = make_tables(T, 'P'), make_tables(T, 'S')
    xp, xs, cp, cs = inp["x_prompt"], inp["x_sample"], inp["c_prompt"], inp["c_sample"]
    in_maps = []
    for b in range(2):
        in_maps.append(prep_core_inputs(xp[b], np.stack([cp[b], cp[b]]), W, tbP))
    for b in range(2):
        in_maps.append(prep_core_inputs(xs[2 * b:2 * b + 2].reshape(T, D), cs[2 * b:2 * b + 2], W, tbS))
    nc = build_program(T)
    res = run_bass_kernel_spmd(nc, in_maps, core_ids=list(range(4)))
    outs = [np.asarray(r["y"], np.float32) for r in res.results]
    y_prompt = np.stack([outs[0], outs[1]], 0)
    y_sample = np.concatenate([outs[2].reshape(2, T // 2, D), outs[3].reshape(2, T // 2, D)], 0)
    return (y_prompt, y_sample)
```

```python
import math
from contextlib import ExitStack
import numpy as np
import ml_dtypes
import concourse.bass as bass
import concourse.mybir as mybir
from concourse.bass_utils import run_bass_kernel_spmd

F32 = mybir.dt.float32
BF16 = mybir.dt.bfloat16
AF = mybir.ActivationFunctionType
ALU = mybir.AluOpType
NPBF = ml_dtypes.bfloat16

D = 1024
DEPTH = 2
DFF = 2816
EPS = 1e-6
MAGIC = 12582912.0
import os
PIPE_DEPTH = int(os.environ.get("PIPE_DEPTH", "2"))
PD_POS = int(os.environ.get("PD_POS", "99"))
CONC_FILT = int(os.environ.get("CONC_FILT", "1"))
USE_POOL = int(os.environ.get("USE_POOL", "0"))
PRIM_STEPS = int(os.environ.get("PRIM_STEPS", "1"))


class Stream:
    def __init__(self, P, inc):
        self.P, self.inc = P, inc
        self.sem = P.new_sem()
        self.count = 0

    def bump(self):
        if self.count + self.inc > 30000:
            self.sem = self.P.new_sem()
            self.count = 0
        self.count += self.inc
        return (self.sem, self.count)

    def cur(self):
        return (self.sem, self.count) if self.count else None


class Buf:
    def __init__(self, name, t=None):
        self.name, self.t = name, t
        self.w = None
        self.r = {}

    def __getitem__(self, idx):
        return self.t[idx]


class Prog:
    def __init__(self, nc, es):
        self.nc, self.es = nc, es
        self.nsem = 0
        self.engs = {'pe': nc.tensor, 'act': nc.scalar, 'dve': nc.vector, 'pool': nc.gpsimd, 'sp': nc.sync}
        self.streams = {k: Stream(self, 1) for k in ('pe', 'act', 'dve', 'pool')}
        self.seen = {k: {} for k in self.engs}
        self.dma_pool = {q: [Stream(self, 16) for _ in range(8)] for q in ('sp', 'pool')}
        self.dma_rr = {q: 0 for q in self.dma_pool}
        self.scopes = [es]
        self.nuniq = 0

    def new_sem(self):
        self.nsem += 1
        return self.es.enter_context(self.nc.semaphore(f"s{self.nsem}"))

    def sb(self, name, shape, dt=F32):
        self.nuniq += 1
        return Buf(name, self.scopes[-1].enter_context(self.nc.sbuf_tensor(f"{name}_{self.nuniq}", shape, dt)))

    def ps(self, name, shape, dt=F32):
        self.nuniq += 1
        return Buf(name, self.scopes[-1].enter_context(self.nc.psum_tensor(f"{name}_{self.nuniq}", shape, dt)))

    def push(self):
        st = ExitStack()
        self.scopes.append(st)
        return st

    def pop(self):
        self.barrier()
        self.scopes.pop().close()

    def _wait(self, eng, tok):
        if tok is None:
            return
        sem, val = tok
        seen = self.seen[eng]
        if seen.get(id(sem), 0) >= val:
            return
        self.engs[eng].wait_ge(sem, val)
        seen[id(sem)] = val

    def barrier(self):
        toks = [s.cur() for s in self.streams.values()]
        for pool in self.dma_pool.values():
            toks += [s.cur() for s in pool]
        for eng in self.engs:
            for t in toks:
                self._wait(eng, t)

    def _deps(self, eng, reads, writes, accum):
        for b in reads:
            self._wait(eng, b.w)
        for b in writes:
            if not accum:
                self._wait(eng, b.w)
            for t in b.r.values():
                self._wait(eng, t)

    def _commit(self, tok, reads, writes):
        for b in writes:
            b.w = tok
            b.r = {}
        for b in reads:
            b.r[id(tok[0])] = tok

    def op(self, eng, fn, reads=(), writes=(), accum=False):
        self._deps(eng, reads, writes, accum)
        inst = fn(self.engs[eng])
        tok = self.streams[eng].bump()
        inst.then_inc(tok[0], 1)
        self._commit(tok, reads, writes)
        return tok

    def dma(self, q, out, in_, reads=(), writes=()):
        pool = self.dma_pool[q]
        st = pool[self.dma_rr[q] % len(pool)]
        self.dma_rr[q] += 1
        self._wait(q, st.cur())
        self._deps(q, reads, writes, False)
        inst = self.engs[q].dma_start(out=out, in_=in_)
        tok = st.bump()
        inst.then_inc(tok[0], 16)
        self._commit(tok, reads, writes)
        return tok


def _cplx_pair(M):
    return np.concatenate([M.real, M.imag], 1), np.concatenate([-M.imag, M.real], 1)


def make_tables(T, kind):
    NT = T // 128
    NS = 2 * NT
    H = NT // 2
    isS = (kind == 'S')
    tb = {}
    tb['ident_b'] = np.eye(128).astype(NPBF)
    tb['ident_f'] = np.eye(128, dtype=np.float32)
    cc = np.arange(64)
    ang = 2 * np.pi * np.outer(cc, cc) / 64
    bdc = np.zeros((128, 128)); bds = np.zeros((128, 128))
    for g in range(2):
        bdc[g * 64:(g + 1) * 64, g * 64:(g + 1) * 64] = np.cos(ang)
        bds[g * 64:(g + 1) * 64, g * 64:(g + 1) * 64] = -np.sin(ang)
    tb['bdcs'] = np.stack([bdc, bds], 1).astype(NPBF)
    n1 = np.arange(NT)
    M1 = np.zeros((NT, NS), np.complex128)
    M2 = np.zeros((NS, 128, 128), np.complex128)
    n2 = np.arange(128)[:, None]
    k2 = np.arange(128)[None, :]
    if not isS:
        L = T
        for j in range(NT):
            M1[:, j] = np.exp(-2j * np.pi * n1 * j / NT)
            M2[j] = np.exp(-2j * np.pi * n2 * (j + NT * k2) / T)
    else:
        L = T // 2
        for s in range(2):
            for j in range(NT):
                M1[s * H:(s + 1) * H, s * NT + j] = np.exp(-2j * np.pi * np.arange(H) * j / H)
                m = np.exp(-2j * np.pi * n2 * (NT * (k2 % 64) + j) / L) * ((k2 // 64) == s)
                M2[s * NT + j] = m
    M2 = M2 / math.sqrt(L * 64)
    a, b = _cplx_pair(M1)
    tb['fn_m1'] = np.stack([a, b], 1).astype(NPBF)
    fm2 = np.zeros((NT, 128, 2, 2, 128), np.float64)
    for s in range(2):
        for j in range(NT):
            fm2[j, :, s, 0] = M2[s * NT + j].real
            fm2[j, :, s, 1] = -M2[s * NT + j].imag
    tb['fn_m2'] = fm2.astype(NPBF)
    HF1 = np.zeros((NS, NS), np.complex128)
    H2 = np.zeros((NS, 128, 128), np.complex128)
    HZ = np.zeros((128, NS, NT), np.complex128)
    if not isS:
        N = 2 * T
        for j in range(NS):
            HF1[:, j] = np.exp(-2j * np.pi * np.arange(NS) * j / NS)
            H2[j] = np.exp(-2j * np.pi * n2 * (j + NS * k2) / N)
        for q in range(128):
            HZ[q] = np.exp(2j * np.pi * np.outer(np.arange(NS), 128 * np.arange(NT) + q) / N) / N
    else:
        N = T
        for s in range(2):
            for j in range(NT):
                HF1[s * NT:(s + 1) * NT, s * NT + j] = np.exp(-2j * np.pi * np.arange(NT) * j / NT)
                H2[s * NT + j] = np.exp(-2j * np.pi * n2 * (j + NT * k2) / N)
        for q in range(128):
            for s in range(2):
                HZ[q, s * NT:(s + 1) * NT, s * H:(s + 1) * H] = \
                    np.exp(2j * np.pi * np.outer(np.arange(NT), 128 * np.arange(H) + q) / N) / N
    if not isS:
        H1 = HF1[:NT]
    else:
        H1 = np.concatenate([HF1[0:H], HF1[NT:NT + H]], 0)
    if not isS:
        act = list(range(NS // 2 + 1)) + [None]
        wts = [1.0 if j in (0, NS // 2) else 2.0 for j in range(NS // 2 + 1)] + [0.0]
    else:
        act = [s * NT + j for s in range(2) for j in range(NT // 2 + 1)]
        wts = [1.0 if j in (0, NT // 2) else 2.0 for s in range(2) for j in range(NT // 2 + 1)]
    def sel(M, axis):
        parts = []
        for a in act:
            if a is None:
                parts.append(np.zeros_like(np.take(M, [0], axis=axis)))
            else:
                parts.append(np.take(M, [a], axis=axis))
        return np.concatenate(parts, axis=axis)
    H1 = sel(H1, 1); HF1 = sel(HF1, 1); H2 = sel(H2, 0)
    HZ = sel(HZ, 1) * np.asarray(wts)[None, :, None]
    tb['hy_h1'] = np.concatenate([H1.real, H1.imag], 1).astype(NPBF)
    tb['hy_hf1'] = np.concatenate([HF1.real, HF1.imag], 1).astype(NPBF)
    tb['hy_h2'] = np.stack([H2.real, H2.imag, -H2.imag], 2).astype(NPBF)
    Fi = np.exp(2j * np.pi * np.outer(np.arange(128), np.arange(128)) / 128)
    a, b = _cplx_pair(Fi)
    tb['hy_i1'] = np.stack([a, b], 1).astype(NPBF)
    tb['hy_z'] = np.stack([HZ.real, -HZ.imag], 2).transpose(1, 0, 2, 3).astype(NPBF).copy()
    Lf = L
    mpos = np.arange(2 * T)
    mloc = mpos % (2 * Lf)
    lag = np.where(mloc < Lf, mloc, 2 * Lf - mloc)
    lag = np.where(mloc == Lf, 0, lag)
    mf = (mloc < Lf).astype(np.float32)
    mb = (mloc > Lf).astype(np.float32)
    tl = np.linspace(0.0, 1.0, Lf, dtype=np.float32)
    wl = (2.0 * np.float32(math.pi) * np.arange(Lf, dtype=np.float32) / np.float32(Lf)).astype(np.float32)
    fb = np.linspace(1e-4, 15, 16, dtype=np.float32)[None, :]
    feat = np.concatenate([tl[:, None], np.cos(fb * wl[:, None]), -np.sin(fb * wl[:, None])], -1).astype(np.float32)
    tb['flt_feat'] = np.ascontiguousarray(feat[lag].T).astype(np.float32)
    tb['flt_msk'] = np.stack([mf, mb, -tl[lag]], 0).astype(np.float32)
    deltas = np.abs(np.linspace(math.log(1e-2) / 0.3, math.log(1e-2) / 1.5, 256, dtype=np.float32))
    tb['flt_delta'] = np.ascontiguousarray(deltas.reshape(2, 128).T).astype(np.float32)
    sc = np.zeros((128, 4), np.float32)
    sc[:, 0] = 0.0 if isS else 1.0
    sc[:, 1] = 0.5 if isS else 1.0
    tb['scal'] = sc
    NB = T // 512
    hal = np.ones((NB, 2), np.float32)
    hal[0, 0] = 0.0; hal[NB - 1, 1] = 0.0
    if isS:
        hal[NB // 2, 0] = 0.0; hal[NB // 2 - 1, 1] = 0.0
    tb['hal'] = np.broadcast_to(hal.reshape(1, NB * 2), (128, NB * 2)).astype(np.float32).copy()
    pos = (np.arange(T) % L).astype(np.float32)
    inv = (10000.0 ** (-np.arange(32, dtype=np.float32) / 32)).astype(np.float32)
    angr = pos[:, None] * inv[None, :]
    tb['rot'] = np.stack([np.cos(angr), np.sin(angr)], 1).astype(np.float32)
    lg = np.log(1.0 - 2.0 ** (-5.0 - np.arange(4)))
    i = np.arange(128)
    ret_e = np.zeros((128, 2, 6, 128), np.float64)
    ret_tok = np.zeros((128, 2, 256), np.float64)
    ret_dec = np.zeros((128, 2, 2), np.float64)
    for c in range(2):
        for p in range(128):
            h = 2 * c + p // 64
            bf = (i + 1) * lg[h]; bb = (128 - i) * lg[h]
            ret_e[p, c, 0] = np.exp(bf - bf[64]) / 8.0
            ret_e[p, c, 1] = np.exp(bf[64] - bf)
            ret_e[p, c, 2] = np.exp(bb - bb[64]) / 8.0
            ret_e[p, c, 3] = np.exp(bb[64] - bb)
            ret_e[p, c, 4] = np.exp(bf) / 8.0
            ret_e[p, c, 5] = np.exp(bb) / 8.0
            ret_dec[p, c, :] = np.exp(128 * lg[h])
    for h in range(4):
        ret_tok[:, 0, h * 64:(h + 1) * 64] = np.exp((127 - i) * lg[h])[:, None]
        ret_tok[:, 1, h * 64:(h + 1) * 64] = np.exp(i * lg[h])[:, None]
    tb['ret_e'] = ret_e.astype(np.float32)
    tb['ret_tok'] = ret_tok.astype(np.float32)
    tb['ret_dec'] = ret_dec.astype(np.float32)
    mh_ret = np.zeros((128, 2), np.float32); mh_gla = np.zeros((128, 4), np.float32)
    bd_ret = np.zeros((128, 2, 256), np.float32); bd_gla = np.zeros((128, 1, 256), np.float32)
    for p in range(128):
        mh_ret[p, p // 64] = 1.0; mh_gla[p, p // 32] = 1.0
        for c in range(2):
            h = 2 * c + p // 64
            bd_ret[p, c, h * 64:(h + 1) * 64] = 1.0
        h = p // 32
        bd_gla[p, 0, h * 64:(h + 1) * 64] = 1.0
    tb['mh_ret'] = mh_ret; tb['mh_gla'] = mh_gla; tb['bd_ret'] = bd_ret; tb['bd_gla'] = bd_gla
    jj = np.arange(128)[:, None]; ii = np.arange(128)[None, :]
    tri = np.zeros((128, 6, 128), np.float32)
    tri[:, 0] = (jj <= ii)
    tri[:, 1] = (jj > ii)
    tri[:, 2] = -(jj <= ii).astype(np.float32) / 16.0
    tri[:, 3] = -(jj >= ii).astype(np.float32) / 16.0
    tri[:, 4] = -(jj > ii).astype(np.float32) / 16.0
    tri[:, 5] = -(jj < ii).astype(np.float32) / 16.0
    tb['tri'] = tri
    return tb


OFF_FN, OFF_QR, OFF_HY, OFF_QG, OFF_GATES = 0, 256, 1280, 2048, 2848
NTM = 1824


def build_program(T, debug=()):
    NT, NS, NB, H = T // 128, T // 64, T // 512, T // 256
    NSA = NT + 2
    nc = bass.Bass("TRN2", target_bir_lowering=False)
    I = {}

    def inp(name, shape, dt=F32):
        I[name] = nc.dram_tensor(name, list(shape), dt, kind="ExternalInput").ap()
        return I[name]

    def scratch(name, shape, dt=F32):
        kind = "ExternalOutput" if name in debug else "Internal"
        return nc.dram_tensor(name, list(shape), dt, kind=kind).ap()

    x_in = inp("x", [T, D]); inp("cT", [128, 8, 2])
    inp("ada_w", [DEPTH, D, 6 * D]); inp("ada_b_col", [DEPTH, 128, 48]); inp("ada_b", [DEPTH, 6 * D])
    inp("normw_col", [DEPTH, 128, 4, 8]); inp("norm_post_mix", [DEPTH, D]); inp("norm_post_ffn", [DEPTH, D])
    inp("w_in", [DEPTH, D, 6944])
    inp("hy_cw", [DEPTH, 128, 6, 3]); inp("hy_cb", [DEPTH, 128, 6])
    inp("flt_w1", [DEPTH, 33, 64]); inp("flt_c1", [DEPTH, 64, 2]); inp("flt_w2", [DEPTH, 64, 64]); inp("flt_c2", [DEPTH, 64, 2])
    inp("flt_w3", [DEPTH, 64, 1024]); inp("flt_b3", [DEPTH, 1024]); inp("hy_skip", [DEPTH, 2, 256])
    inp("gla_wd", [DEPTH, 33, 256]); inp("ret_gn", [DEPTH, 256]); inp("gla_gn", [DEPTH, 256])
    inp("w_branch", [DEPTH, 1024, D]); inp("w_out", [DEPTH, D, D])
    inp("ffn_up", [DEPTH, D, 2 * DFF]); inp("ffn_cw", [DEPTH, 128, 44, 3]); inp("ffn_cb", [DEPTH, 128, 44])
    inp("ffn_down", [DEPTH, DFF, D])
    for nm, shp, dt in (("ident_b", [128, 128], BF16), ("ident_f", [128, 128], F32), ("bdcs", [128, 2, 128], BF16),
                        ("fn_m1", [NT, 2, 2 * NS], BF16), ("fn_m2", [NT, 128, 2, 2, 128], BF16),
                        ("hy_h1", [NT, 2 * NSA], BF16), ("hy_hf1", [NS, 2 * NSA], BF16), ("hy_h2", [NSA, 128, 3, 128], BF16),
                        ("hy_i1", [128, 2, 256], BF16), ("hy_z", [NSA, 128, 2, NT], BF16),
                        ("flt_feat", [33, 2 * T], F32), ("flt_msk", [3, 2 * T], F32), ("flt_delta", [128, 2], F32),
                        ("scal", [128, 4], F32), ("hal", [128, NB * 2], F32), ("rot", [T, 2, 32], F32),
                        ("ret_e", [128, 2, 6, 128], F32), ("ret_tok", [128, 2, 256], F32), ("ret_dec", [128, 2, 2], F32),
                        ("mh_ret", [128, 2], F32), ("mh_gla", [128, 4], F32), ("bd_ret", [128, 2, 256], F32),
                        ("bd_gla", [128, 1, 256], F32), ("tri", [128, 6, 128], F32)):
        inp(nm, shp, dt)
    y_out = nc.dram_tensor("y", [T, D], F32, kind="ExternalOutput").ap()
    x1d = scratch("x1d", [T, D]); xmid = scratch("xmid", [T, D])
    projT = scratch("projT", [T, NTM]); zF = scratch("zF", [2, 256, T], BF16); uhF = scratch("uhF", [768, T])
    brF = scratch("brF", [1024, T], BF16); modrow = scratch("modrow", [2, 2048]); hF = scratch("hF", [D, T + 2], BF16); gFF = scratch("gFF", [DFF, T], BF16)
    gF = scratch("gF", [512, 2 * T], BF16); rnD = scratch("rnD", [512]); Gd = scratch("Gd", [16, 128, NSA, 64], BF16)

    es = ExitStack()
    with es:
        es.enter_context(nc.allow_non_contiguous_dma(reason="strided scratch layouts"))
        P = Prog(nc, es)
        dB = {k: Buf(k) for k in ("x1d", "xmid", "projT", "zF", "uhF", "brF", "modrow", "gF", "rnD", "Gd", "y", "in", "hF", "gFF")}
        IN = dB["in"]

        def ld(dst_buf, dst_ap, src_ap, src=IN):
            return P.dma('sp', dst_ap, src_ap, reads=[src], writes=[dst_buf])

        def stq(dst_ap, src_buf, src_ap, dst):
            return P.dma('pool', dst_ap, src_ap, reads=[src_buf], writes=[dst])

        def dve(fn, r, w):
            return P.op('dve', fn, r, w)

        def act(fn, r, w):
            return P.op('act', fn, r, w)

        def gps(fn, r, w):
            return P.op('pool' if USE_POOL else 'dve', fn, r, w)

        def pe(fn, r, w):
            return P.op('pe', fn, r, w, accum=True)

        identb = P.sb("identb", [128, 128], BF16); identf = P.sb("identf", [128, 128], F32)
        scal = P.sb("scal", [128, 4]); hal = P.sb("hal", [128, NB * 2]); epsb = P.sb("epsb", [128, 1])
        modc = P.sb("modc", [128, 48, 2]); Am = P.sb("Am", [128, 8, 2]); Af = P.sb("Af", [128, 8, 2])
        gtb = P.sb("gtb", [128, 2, 2, D])
        ld(identb, identb[:], I["ident_b"][:, :]); ld(identf, identf[:], I["ident_f"][:, :])
        ld(scal, scal[:], I["scal"][:, :]); ld(hal, hal[:], I["hal"][:, :])
        dve(lambda e: e.memset(epsb[:], EPS), [], [epsb])

        def phase_mod(l):
            P.push()
            cT = P.sb("cT", [128, 8, 2]); scT = P.sb("scT", [128, 8, 2])
            ld(cT, cT[:], I["cT"][:, :, :])
            act(lambda e: e.activation(out=scT[:], in_=cT[:], func=AF.Silu), [cT], [scT])
            psc = P.ps("psc", [128, 96]); psr = P.ps("psr", [2, 2048]); macc = P.sb("macc", [128, 96])
            wts = [P.sb(f"adaw{i}", [128, 6 * D]) for i in range(2)]
            rowcols = (2048, 2560, 5120, 5632)
            for kc in range(8):
                wt = wts[kc % 2]
                ld(wt, wt[:], I["ada_w"][l, kc * 128:(kc + 1) * 128, :])
                for q in range(48):
                    pe(lambda e, q=q: e.matmul(psc[:, 2 * q:2 * q + 2], lhsT=wt[:, q * 128:(q + 1) * 128], rhs=scT[:, kc, :],
                                               start=True, stop=True), [wt, scT], [psc])
                if kc == 0:
                    dve(lambda e: e.tensor_copy(out=macc[:], in_=psc[:]), [psc], [macc])
                else:
                    dve(lambda e: e.tensor_tensor(out=macc[:], in0=macc[:], in1=psc[:], op=ALU.add), [psc, macc], [macc])
                for bi, c0 in enumerate(rowcols):
                    pe(lambda e, bi=bi, c0=c0: e.matmul(psr[:, bi * 512:(bi + 1) * 512], lhsT=scT[:, kc, :], rhs=wt[:, c0:c0 + 512],
                                                        start=(kc == 0), stop=(kc == 7)), [wt, scT], [psr])
            abc = P.sb("abc", [128, 48]); nwc = P.sb("nwc", [128, 4, 8])
            ld(abc, abc[:], I["ada_b_col"][l]); ld(nwc, nwc[:], I["normw_col"][l])
            dve(lambda e: e.tensor_tensor(out=modc[:], in0=macc[:].rearrange("p (q s) -> p q s", s=2),
                                          in1=abc[:].unsqueeze(2).broadcast_to([128, 48, 2]), op=ALU.add), [macc, abc], [modc])
            dve(lambda e: e.scalar_tensor_tensor(out=Am[:], in0=modc[:, 8:16, :], scalar=1.0,
                                                 in1=nwc[:, 0, :].unsqueeze(2).broadcast_to([128, 8, 2]), op0=ALU.add, op1=ALU.mult),
                [modc, nwc], [Am])
            dve(lambda e: e.scalar_tensor_tensor(out=Af[:], in0=modc[:, 32:40, :], scalar=1.0,
                                                 in1=nwc[:, 2, :].unsqueeze(2).broadcast_to([128, 8, 2]), op0=ALU.add, op1=ALU.mult),
                [modc, nwc], [Af])
            abr = P.sb("abr", [2, 2048]); nwr = P.sb("nwr", [2, 2048]); gr = P.sb("gr", [2, 2048])
            ld(abr, abr[:, 0:1024], I["ada_b"][l, 2048:3072].partition_broadcast(2))
            ld(abr, abr[:, 1024:2048], I["ada_b"][l, 5120:6144].partition_broadcast(2))
            ld(nwr, nwr[:, 0:1024], I["norm_post_mix"][l].partition_broadcast(2))
            ld(nwr, nwr[:, 1024:2048], I["norm_post_ffn"][l].partition_broadcast(2))
            dve(lambda e: e.tensor_tensor(out=gr[:], in0=psr[:], in1=abr[:], op=ALU.add), [psr, abr], [gr])
            dve(lambda e: e.tensor_tensor(out=gr[:], in0=gr[:], in1=nwr[:], op=ALU.mult), [gr, nwr], [gr])
            stq(modrow[:, :], gr, gr[:], dB["modrow"])
            for sg in range(2):
                ld(gtb, gtb[:, sg, :, :].rearrange("p a d -> p (a d)"), modrow[sg].partition_broadcast(128), src=dB["modrow"])
            P.pop()

        def phase_norm(src_ap2d, src_buf, A, bq0):
            P.push()
            zt = P.sb("zt", [128, 8, 1], BF16)
            dve(lambda e: e.memset(zt[:], 0.0), [], [zt])
            hFv = hF.rearrange("(c p) t -> p c t", p=128)
            stq(hFv[:, :, 0:1], zt, zt[:], dB["hF"]); stq(hFv[:, :, T + 1:T + 2], zt, zt[:], dB["hF"])
            xts = [P.sb(f"xt{i}", [128, D]) for i in range(2)]
            sq = P.sb("sq", [128, D]); ss = P.sb("ss", [128, 1]); rs = P.sb("rs", [128, 1])
            xn = P.sb("xn", [128, D], BF16); ptr = P.ps("ptr", [128, 8, 128], BF16); tmp = P.sb("tmpT", [128, 8, 128])
            hts = [P.sb(f"ht{i}", [128, 8, 512], BF16) for i in range(2)]
            xns = [P.sb(f"xnN{i}", [128, D], BF16) for i in range(2)]

            def gen(it):
                xt, ht, xn = xts[it % 2], hts[(it // 4) % 2], xns[it % 2]
                q4 = it % 4
                seg = 0 if it < NT // 2 else 1
                ld(xt, xt[:], src_ap2d[it * 128:(it + 1) * 128, :], src=src_buf)
                act(lambda e: e.activation(out=sq[:], in_=xt[:], func=AF.Square, accum_out=ss[:, 0:1]), [xt], [sq, ss])
                act(lambda e: e.activation(out=rs[:], in_=ss[:], func=AF.Sqrt, scale=1.0 / D, bias=epsb[:, 0:1]), [ss, epsb], [rs])
                yield
                dve(lambda e: e.reciprocal(out=rs[:], in_=rs[:]), [rs], [rs])
                dve(lambda e: e.tensor_scalar(out=xn[:], in0=xt[:], scalar1=rs[:, 0:1], scalar2=None, op0=ALU.mult), [xt, rs], [xn])
                yield 'prev_done'
                for c8 in range(8):
                    pe(lambda e, c8=c8: e.transpose(ptr[:, c8, :], xn[:, c8 * 128:(c8 + 1) * 128], identb[:]), [xn, identb], [ptr])
                yield
                dve(lambda e: e.tensor_tensor(out=tmp[:], in0=ptr[:], in1=A[:, :, seg:seg + 1].broadcast_to([128, 8, 128]), op=ALU.mult),
                    [ptr, A], [tmp])
                dve(lambda e: e.tensor_tensor(out=ht[:, :, q4 * 128:(q4 + 1) * 128], in0=tmp[:], in1=modc[:, bq0:bq0 + 8, seg:seg + 1].broadcast_to([128, 8, 128]), op=ALU.add),
                    [tmp, modc], [ht])
                if q4 == 3:
                    stq(hFv[:, :, 1 + (it - 3) * 128:1 + (it + 1) * 128], ht, ht[:], dB["hF"])

            run_pipelined(gen, range(NT))
            P.pop()

        def load_w_bf16(dst, dst_ap_fn, src_ap_fn, nk, width, stg):
            for k in range(nk):
                st = stg[k % 2]
                ld(st, st[:, 0:width], src_ap_fn(k))
                if k % 2 == 0:
                    dve(lambda e, k=k, st=st: e.tensor_copy(out=dst_ap_fn(k), in_=st[:, 0:width]), [st], [dst])
                else:
                    act(lambda e, k=k, st=st: e.activation(out=dst_ap_fn(k), in_=st[:, 0:width], func=AF.Copy), [st], [dst])

        def load_window(hw, b):
            hFv = hF.rearrange("(c p) t -> p c t", p=128)
            ld(hw, hw[:], hFv[:, :, b * 512:b * 512 + 514], src=dB["hF"])
            for side, col in ((0, 0), (1, 513)):
                dve(lambda e, side=side, col=col: e.tensor_tensor(out=hw[:, :, col:col + 1], in0=hw[:, :, col:col + 1],
                                                                  in1=hal[:, 2 * b + side:2 * b + side + 1].unsqueeze(1).broadcast_to([128, 8, 1]),
                                                                  op=ALU.mult), [hw, hal], [hw])

        def conv3_fm(out_ap, pm, ph, cw, cb, ci, rbufs, wbuf):
            act(lambda e: e.activation(out=out_ap, in_=pm[:, 0:512], func=AF.Identity, scale=cw[:, ci, 1:2], bias=cb[:, ci:ci + 1]), rbufs, [wbuf])
            dve(lambda e: e.scalar_tensor_tensor(out=out_ap[:, 1:512], in0=pm[:, 0:511], scalar=cw[:, ci, 0:1], in1=out_ap[:, 1:512],
                                                 op0=ALU.mult, op1=ALU.add), rbufs + [wbuf], [wbuf])
            dve(lambda e: e.scalar_tensor_tensor(out=out_ap[:, 0:511], in0=pm[:, 1:512], scalar=cw[:, ci, 2:3], in1=out_ap[:, 0:511],
                                                 op0=ALU.mult, op1=ALU.add), rbufs + [wbuf], [wbuf])
            dve(lambda e: e.scalar_tensor_tensor(out=out_ap[:, 0:1], in0=ph[:, 0:1], scalar=cw[:, ci, 0:1], in1=out_ap[:, 0:1],
                                                 op0=ALU.mult, op1=ALU.add), rbufs + [wbuf], [wbuf])
            dve(lambda e: e.scalar_tensor_tensor(out=out_ap[:, 511:512], in0=ph[:, 1:2], scalar=cw[:, ci, 2:3], in1=out_ap[:, 511:512],
                                                 op0=ALU.mult, op1=ALU.add), rbufs + [wbuf], [wbuf])

        def phase_A(l):
            P.push()
            wA = P.sb("wA", [128, 8, OFF_GATES], BF16)
            stg = [P.sb(f"stgA{i}", [128, OFF_GATES]) for i in range(2)]
            load_w_bf16(wA, lambda k: wA[:, k, :], lambda k: I["w_in"][l, k * 128:(k + 1) * 128, 0:OFF_GATES], 8, OFF_GATES, stg)
            bdcs = P.sb("bdcs", [128, 2, 128], BF16); ld(bdcs, bdcs[:], I["bdcs"][:, :, :])
            cw = P.sb("hcw", [128, 6, 3]); cb = P.sb("hcb", [128, 6])
            ld(cw, cw[:], I["hy_cw"][l]); ld(cb, cb[:], I["hy_cb"][l])
            hws = [P.sb(f"hw{i}", [128, 8, 514], BF16) for i in range(2)]
            pms = [P.ps(f"pmA{i}", [128, 512]) for i in range(2)]
            ph = P.ps("phA", [128, 2])
            pjs = [P.sb(f"pj{i}", [128, NTM]) for i in range(2)]; uT = P.sb("uT", [128, 2, 512], BF16)
            zts = [P.sb(f"ztA{i}", [128, 512], BF16) for i in range(2)]
            cvs = [P.sb(f"cvA{i}", [128, 512]) for i in range(2)]
            tmcols = ((256, 512), (768, 512), (2048, 512), (2560, 288))
            zFv = zF
            npm = 0
            for b in range(NB):
                hw = hws[b % 2]
                load_window(hw, b)
                t0 = b * 512
                for s in range(4):
                    o = 0
                    pj = pjs[s % 2]
                    for (c0, wd) in tmcols:
                        pm = pms[npm % 2]; npm += 1
                        for kc in range(8):
                            pe(lambda e, kc=kc, pm=pm, c0=c0, wd=wd: e.matmul(pm[:, 0:wd], lhsT=hw[:, kc, 1 + s * 128:1 + (s + 1) * 128],
                                                                                rhs=wA[:, kc, c0:c0 + wd], start=(kc == 0), stop=(kc == 7)),
                               [hw, wA], [pm])
                        if (npm % 2) == 0:
                            dve(lambda e, pm=pm, o=o, wd=wd, pj=pj: e.tensor_copy(out=pj[:, o:o + wd], in_=pm[:, 0:wd]), [pm], [pj])
                        else:
                            act(lambda e, pm=pm, o=o, wd=wd, pj=pj: e.activation(out=pj[:, o:o + wd], in_=pm[:, 0:wd], func=AF.Copy), [pm], [pj])
                        o += wd
                        yield
                    stq(projT[t0 + s * 128:t0 + (s + 1) * 128, :], pj, pj[:], dB["projT"])
                for ch in range(2):
                    pm = pms[npm % 2]; npm += 1
                    for kc in range(8):
                        pe(lambda e, kc=kc, pm=pm, ch=ch: e.matmul(pm[:], lhsT=wA[:, kc, ch * 128:(ch + 1) * 128], rhs=hw[:, kc, 1:513],
                                                                     start=(kc == 0), stop=(kc == 7)), [hw, wA], [pm])
                    act(lambda e, pm=pm, ch=ch: e.activation(out=uT[:, ch, :], in_=pm[:], func=AF.Copy), [pm], [uT])
                    yield
                for ri in range(2):
                    for ch in range(2):
                        pm = pms[npm % 2]; zt = zts[npm % 2]; npm += 1
                        pe(lambda e, pm=pm, ri=ri, ch=ch: e.matmul(pm[:], lhsT=bdcs[:, ri, :], rhs=uT[:, ch, :], start=True, stop=True), [bdcs, uT], [pm])
                        act(lambda e, pm=pm, zt=zt: e.activation(out=zt[:], in_=pm[:], func=AF.Copy), [pm], [zt])
                        stq(zFv[ri, ch * 128:(ch + 1) * 128, t0:t0 + 512], zt, zt[:], dB["zF"])
                        yield
                for ch in range(6):
                    pm = pms[npm % 2]; cv = cvs[npm % 2]; npm += 1
                    c0 = OFF_HY + ch * 128
                    for kc in range(8):
                        pe(lambda e, kc=kc, pm=pm, c0=c0: e.matmul(pm[:], lhsT=wA[:, kc, c0:c0 + 128], rhs=hw[:, kc, 1:513],
                                                                     start=(kc == 0), stop=(kc == 7)), [hw, wA], [pm])
                    for kc in range(8):
                        pe(lambda e, kc=kc, c0=c0: e.matmul(ph[:], lhsT=wA[:, kc, c0:c0 + 128], rhs=hw[:, kc, 0:514:513],
                                                              start=(kc == 0), stop=(kc == 7)), [hw, wA], [ph])
                    conv3_fm(cv[:], pm, ph, cw, cb, ch, [pm, ph, cw, cb], cv)
                    stq(uhF[ch * 128:(ch + 1) * 128, t0:t0 + 512], cv, cv[:], dB["uhF"])
                    yield
            yield 'finished'
            P.pop()

        def rms_residual(py, xt, sq, ss, rs, tmpo, outt, seg, which, dst_ap, dst_buf):
            act(lambda e: e.activation(out=sq[:], in_=py[:], func=AF.Square, accum_out=ss[:, 0:1]), [py], [sq, ss])
            act(lambda e: e.activation(out=rs[:], in_=ss[:], func=AF.Sqrt, scale=1.0 / D, bias=epsb[:, 0:1]), [ss, epsb], [rs])
            dve(lambda e: e.reciprocal(out=rs[:], in_=rs[:]), [rs], [rs])
            dve(lambda e: e.scalar_tensor_tensor(out=tmpo[:], in0=py[:], scalar=rs[:, 0:1], in1=gtb[:, seg, which, :], op0=ALU.mult, op1=ALU.mult),
                [py, rs, gtb], [tmpo])
            dve(lambda e: e.tensor_tensor(out=outt[:], in0=tmpo[:], in1=xt[:], op=ALU.add), [tmpo, xt], [outt])
            stq(dst_ap, outt, outt[:], dst_buf)

        def phase_C(l, src_ap2d, src_buf):
            P.push()
            wbr = P.sb("wbr", [128, 8, D], BF16); wout = P.sb("wout", [128, 8, D], BF16); wg = P.sb("wg", [128, 8, 4 * D], BF16)
            stg = [P.sb(f"stgC{i}", [128, 4 * D]) for i in range(2)]
            load_w_bf16(wbr, lambda k: wbr[:, k, :], lambda k: I["w_branch"][l, k * 128:(k + 1) * 128, :], 8, D, stg)
            load_w_bf16(wout, lambda k: wout[:, k, :], lambda k: I["w_out"][l, k * 128:(k + 1) * 128, :], 8, D, stg)
            load_w_bf16(wg, lambda k: wg[:, k, :], lambda k: I["w_in"][l, k * 128:(k + 1) * 128, OFF_GATES:OFF_GATES + 4 * D], 8, 4 * D, stg)
            xts = [P.sb(f"xtC{i}", [128, D]) for i in range(2)]
            hts = [P.sb(f"htC{i}", [128, 8, 128], BF16) for i in range(2)]
            brs = [P.sb(f"brC{i}", [128, 8, 128], BF16) for i in range(2)]
            pgs = [P.ps(f"pg{i}", [128, 512]) for i in range(2)]; pbs = [P.ps(f"pb{i}", [128, 512]) for i in range(2)]
            py = P.ps("py", [128, D]); ptr = P.ps("ptrC", [128, 8, 128], BF16)
            sigs = [P.sb(f"sig{i}", [128, 512]) for i in range(2)]; tms = [P.sb(f"tmc{i}", [128, 512]) for i in range(2)]
            merged = P.sb("merged", [128, D]); tmpm = P.sb("tmpm", [128, D]); mb = P.sb("mb", [128, D], BF16)
            mT = P.sb("mT", [128, 8, 128], BF16); sq = P.sb("sqC", [128, D]); ss = P.sb("ssC", [128, 1]); rs = P.sb("rsC", [128, 1])
            outt = P.sb("outC", [128, D])
            hFv = hF.rearrange("(c p) t -> p c t", p=128); brv = brF.rearrange("(k p) t -> p k t", p=128)
            mergeds = [merged, P.sb("merged1", [128, D])]

            def gen(it):
                merged = mergeds[it % 2]
                xt, ht, brt = xts[it % 2], hts[it % 2], brs[it % 2]
                seg = 0 if it < NT // 2 else 1
                ld(xt, xt[:], src_ap2d[it * 128:(it + 1) * 128, :], src=src_buf)
                ld(ht, ht[:], hFv[:, :, 1 + it * 128:1 + (it + 1) * 128], src=dB["hF"])
                ld(brt, brt[:], brv[:, :, it * 128:(it + 1) * 128], src=dB["brF"])
                for br in range(4):
                    for cb in range(2):
                        u = br * 2 + cb
                        pgu, pbu, sgu, tmu = pgs[u % 2], pbs[u % 2], sigs[u % 2], tms[u % 2]
                        for kc in range(8):
                            pe(lambda e, kc=kc, cb=cb, pgu=pgu, br=br: e.matmul(pgu[:], lhsT=ht[:, kc, :],
                                                                               rhs=wg[:, kc, br * D + cb * 512:br * D + (cb + 1) * 512], start=(kc == 0), stop=(kc == 7)),
                               [ht, wg], [pgu])
                        for k2 in range(2):
                            pe(lambda e, k2=k2, cb=cb, pbu=pbu, br=br: e.matmul(pbu[:], lhsT=brt[:, br * 2 + k2, :],
                                                                               rhs=wbr[:, br * 2 + k2, cb * 512:(cb + 1) * 512], start=(k2 == 0), stop=(k2 == 1)),
                               [brt, wbr], [pbu])
                        act(lambda e, pgu=pgu, sgu=sgu: e.activation(out=sgu[:], in_=pgu[:], func=AF.Sigmoid), [pgu], [sgu])
                        mslice = merged[:, cb * 512:(cb + 1) * 512]
                        if br == 0:
                            dve(lambda e, sgu=sgu, pbu=pbu, mslice=mslice: e.tensor_tensor(out=mslice, in0=sgu[:], in1=pbu[:], op=ALU.mult), [sgu, pbu], [merged])
                        else:
                            dve(lambda e, sgu=sgu, pbu=pbu, tmu=tmu: e.tensor_tensor(out=tmu[:], in0=sgu[:], in1=pbu[:], op=ALU.mult), [sgu, pbu], [tmu])
                            dve(lambda e, tmu=tmu, mslice=mslice: e.tensor_tensor(out=mslice, in0=mslice, in1=tmu[:], op=ALU.add), [merged, tmu], [merged])
                        yield
                yield 'prev_done'
                act(lambda e: e.activation(out=mb[:], in_=merged[:], func=AF.Copy), [merged], [mb])
                yield
                for c8 in range(8):
                    pe(lambda e, c8=c8: e.transpose(ptr[:, c8, :], mb[:, c8 * 128:(c8 + 1) * 128], identb[:]), [mb, identb], [ptr])
                yield
                dve(lambda e: e.tensor_copy(out=mT[:], in_=ptr[:]), [ptr], [mT])
                yield
                for cb in range(2):
                    for kc in range(8):
                        pe(lambda e, kc=kc, cb=cb: e.matmul(py[:, cb * 512:(cb + 1) * 512], lhsT=mT[:, kc, :], rhs=wout[:, kc, cb * 512:(cb + 1) * 512],
                                                           start=(kc == 0), stop=(kc == 7)), [mT, wout], [py])
                rms_residual(py, xt, sq, ss, rs, tmpm, outt, seg, 0, xmid[it * 128:(it + 1) * 128, :], dB["xmid"])
            run_pipelined(gen, range(NT))
            P.pop()

        def phase_D(l, dst_ap2d, dst_buf):
            P.push()
            wup = P.sb("wup", [128, 8, 2 * DFF], BF16)
            stg = [P.sb(f"stgD{i}", [128, 1408]) for i in range(2)]
            for part in range(4):
                c0 = part * 1408
                load_w_bf16(wup, lambda k, c0=c0: wup[:, k, c0:c0 + 1408], lambda k, c0=c0: I["ffn_up"][l, k * 128:(k + 1) * 128, c0:c0 + 1408], 8, 1408, stg)
            cw = P.sb("fcw", [128, 44, 3]); cb_ = P.sb("fcb", [128, 44])
            ld(cw, cw[:], I["ffn_cw"][l]); ld(cb_, cb_[:], I["ffn_cb"][l])
            hws = [P.sb(f"hwD{i}", [128, 8, 514], BF16) for i in range(2)]
            pms = [P.ps(f"pmD{i}", [128, 512]) for i in range(4)]; phs = [P.ps(f"phD{i}", [128, 2]) for i in range(4)]
            cvs = [P.sb(f"cvD{i}", [128, 512]) for i in range(4)]; gls = [P.sb(f"glD{i}", [128, 512]) for i in range(2)]
            gts = [P.sb(f"gtD{i}", [128, 512], BF16) for i in range(2)]
            for b in range(NB):
                hw = hws[b % 2]
                load_window(hw, b)
                for pc in range(22):
                    for wi in range(2):
                        ci = pc + 22 * wi
                        bi = 2 * (pc % 2) + wi
                        pm, ph, cv = pms[bi], phs[bi], cvs[bi]
                        for kc in range(8):
                            pe(lambda e, kc=kc, pm=pm, ci=ci: e.matmul(pm[:], lhsT=wup[:, kc, ci * 128:(ci + 1) * 128], rhs=hw[:, kc, 1:513],
                                                                         start=(kc == 0), stop=(kc == 7)), [hw, wup], [pm])
                        for kc in range(8):
                            pe(lambda e, kc=kc, ph=ph, ci=ci: e.matmul(ph[:], lhsT=wup[:, kc, ci * 128:(ci + 1) * 128], rhs=hw[:, kc, 0:514:513],
                                                                         start=(kc == 0), stop=(kc == 7)), [hw, wup], [ph])
                        conv3_fm(cv[:], pm, ph, cw, cb_, ci, [pm, ph, cw, cb_], cv)
                    gt = gts[pc % 2]; gl = gls[pc % 2]; cg_, cv_ = cvs[2 * (pc % 2)], cvs[2 * (pc % 2) + 1]
                    act(lambda e, gl=gl, cg_=cg_: e.activation(out=gl[:], in_=cg_[:], func=AF.Gelu_apprx_tanh), [cg_], [gl])
                    dve(lambda e, gt=gt, gl=gl, cv_=cv_: e.tensor_tensor(out=gt[:], in0=gl[:], in1=cv_[:], op=ALU.mult), [gl, cv_], [gt])
                    stq(gFF[pc * 128:(pc + 1) * 128, b * 512:(b + 1) * 512], gt, gt[:], dB["gFF"])
            P.pop()
            P.push()
            wdn = P.sb("wdn", [128, 22, D], BF16)
            stg = [P.sb(f"stgE{i}", [128, D]) for i in range(2)]
            load_w_bf16(wdn, lambda k: wdn[:, k, :], lambda k: I["ffn_down"][l, k * 128:(k + 1) * 128, :], 22, D, stg)
            xts = [P.sb(f"xtE{i}", [128, D]) for i in range(2)]
            ggs = [P.sb(f"ggE{i}", [128, 22, 512], BF16) for i in range(2)]
            pys = [P.ps(f"pyE{i}", [128, D]) for i in range(2)]; sq = P.sb("sqE", [128, D])
            sss = [P.sb(f"ssE{i}", [128, 1]) for i in range(2)]; rss = [P.sb(f"rsE{i}", [128, 1]) for i in range(2)]
            tmpos = [P.sb(f"tmpE{i}", [128, D]) for i in range(2)]; outts = [P.sb(f"outE{i}", [128, D]) for i in range(2)]
            gv = gFF.rearrange("(k p) t -> p k t", p=128)
            for it in range(NT):
                xt, gg = xts[it % 2], ggs[(it // 4) % 2]
                py, ss, rs, tmpo, outt = pys[it % 2], sss[it % 2], rss[it % 2], tmpos[it % 2], outts[it % 2]
                q4 = it % 4
                seg = 0 if it < NT // 2 else 1
                ld(xt, xt[:], xmid[it * 128:(it + 1) * 128, :], src=dB["xmid"])
                if q4 == 0:
                    ld(gg, gg[:], gv[:, :, it * 128:(it + 4) * 128], src=dB["gFF"])
                for cb in range(2):
                    for pc in range(22):
                        pe(lambda e, pc=pc, cb=cb: e.matmul(py[:, cb * 512:(cb + 1) * 512], lhsT=gg[:, pc, q4 * 128:(q4 + 1) * 128], rhs=wdn[:, pc, cb * 512:(cb + 1) * 512],
                                                           start=(pc == 0), stop=(pc == 21)), [gg, wdn], [py])
                rms_residual(py, xt, sq, ss, rs, tmpo, outt, seg, 1, dst_ap2d[it * 128:(it + 1) * 128, :], dst_buf)
            P.pop()

        def phase_fnet(l):
            P.push()
            Cg = 64
            m1 = P.sb("fm1", [NT, 2, 2 * NS], BF16); ld(m1, m1[:], I["fn_m1"][:, :, :])
            zin = P.sb("zin", [NT, 2, Cg, 128], BF16); A = P.sb("fA", [128, NS, 2, Cg], BF16)
            osb = P.sb("osb", [Cg, T], BF16); osbv = osb[:].rearrange("c (k j) -> c k j", j=NT)
            ps1s = [P.ps(f"fps1{i}", [128, 2, 2 * NS]) for i in range(2)]
            ps2s = [P.ps(f"fps2{i}", [Cg, 4, 128]) for i in range(2)]
            m2s = [P.sb(f"fm2{i}", [128, 4, 2, 2, 128], BF16) for i in range(2)]
            n1 = 0
            for g in range(256 // Cg):
                c0 = g * Cg
                for ri in range(2):
                    ld(zin, zin[:, ri, :, :], zF[ri, c0:c0 + Cg, :].rearrange("c (g n) -> g c n", n=128), src=dB["zF"])
                for c in range(0, Cg, 2):
                    ps1 = ps1s[n1 % 2]; n1 += 1
                    for cc in range(2):
                        pe(lambda e, cc=cc, ps1=ps1: e.matmul(ps1[:, cc, :], lhsT=zin[:, 0, c + cc, :], rhs=m1[:, 0, :], start=True, stop=False), [zin, m1], [ps1])
                        pe(lambda e, cc=cc, ps1=ps1: e.matmul(ps1[:, cc, :], lhsT=zin[:, 1, c + cc, :], rhs=m1[:, 1, :], start=False, stop=True), [zin, m1], [ps1])
                    for ri in range(2):
                        src_ap = ps1[:, :, ri * NS:(ri + 1) * NS].rearrange("p c s -> p s c")
                        if ri == 0:
                            dve(lambda e, src_ap=src_ap: e.tensor_copy(out=A[:, :, 0, c:c + 2], in_=src_ap), [ps1], [A])
                        else:
                            act(lambda e, src_ap=src_ap: e.activation(out=A[:, :, 1, c:c + 2], in_=src_ap, func=AF.Copy), [ps1], [A])
                m2v = I["fn_m2"].rearrange("j p s r k -> p j s r k")
                for j in range(NT):
                    m2t = m2s[(j // 4) % 2]
                    if j % 4 == 0:
                        ld(m2t, m2t[:], m2v[:, j:j + 4, :, :, :])
                    ps2 = ps2s[(j // 4) % 2]
                    k = 0
                    for s_ in range(2):
                        for ri in range(2):
                            pe(lambda e, s_=s_, ri=ri, k=k, ps2=ps2, m2t=m2t: e.matmul(ps2[:, j % 4, :], lhsT=A[:, s_ * NT + j, ri, :], rhs=m2t[:, j % 4, s_, ri, :],
                                                                                      start=(k == 0), stop=(k == 3)), [A, m2t], [ps2])
                            k += 1
                    if j % 4 == 3:
                        j0 = j - 3
                        src_ap = ps2[:, :, :].rearrange("c j k -> c k j")
                        if (j // 4) % 2 == 0:
                            dve(lambda e, src_ap=src_ap, j0=j0: e.tensor_copy(out=osbv[:, :, j0:j0 + 4], in_=src_ap), [ps2], [osb])
                        else:
                            act(lambda e, src_ap=src_ap, j0=j0: e.activation(out=osbv[:, :, j0:j0 + 4], in_=src_ap, func=AF.Copy), [ps2], [osb])
                stq(brF[c0:c0 + Cg, :], osb, osb[:], dB["brF"])
            P.pop()

        def drain(g):
            for _ in g:
                pass

        def run_concurrent(primary, secondary, ratio=int(os.environ.get("CONC_RATIO", "1"))):
            p_alive, s_alive, p_fin = True, True, False
            while p_alive or s_alive:
                for _ in range(PRIM_STEPS):
                    if p_alive and not (p_fin and s_alive):
                        try:
                            if next(primary) == 'finished':
                                p_fin = True
                        except StopIteration:
                            p_alive = False
                for _ in range(ratio):
                    if s_alive:
                        try:
                            next(secondary)
                        except StopIteration:
                            s_alive = False

        def run_pipelined(make_gen, order, depth=PIPE_DEPTH):
            active = []
            order = list(order)
            pos = 0
            while pos < len(order) or active:
                if pos < len(order) and len(active) < depth and all(e[1] == 'second' for e in active):
                    active.append([make_gen(order[pos]), 'first']); pos += 1
                for ent in list(active):
                    if ent[1] == 'waiting':
                        if active[0] is ent:
                            ent[1] = 'second'
                        else:
                            continue
                    try:
                        v = next(ent[0])
                        if v == 'prev_done' and ent[1] == 'first':
                            ent[1] = 'second' if active[0] is ent else 'waiting'
                    except StopIteration:
                        active.remove(ent)

        def phase_scan(l, ret):
            P.push()
            nkc, hpc = (2, 2) if ret else (1, 4)
            KW = nkc * 128
            col0, width = (0, 1024) if ret else (1024, 800)
            qo, ko, vo, go = (0, 256, 512, 768) if ret else (0, 128, 256, 512)
            lro = 768
            kbr = 1 if ret else 3
            tri = P.sb("tri", [128, 6, 128]); ld(tri, tri[:], I["tri"][:, :, :])
            mh = P.sb("mh", [128, hpc]); ld(mh, mh[:], I["mh_ret" if ret else "mh_gla"][:, :])
            bd = P.sb("bd", [128, nkc, 256]); ld(bd, bd[:], I["bd_ret" if ret else "bd_gla"][:, :, :])
            gn = P.sb("gn", [128, 256]); ld(gn, gn[:], I["ret_gn" if ret else "gla_gn"][l].partition_broadcast(128))
            lns = math.log(32.0 ** -0.5)
            if ret:
                Ec = P.sb("E", [128, nkc, 6, 128]); Epc = P.sb("Epad", [128, nkc, 2, hpc, 128])
                dtokc = P.sb("dtok", [128, 2, KW]); decc = P.sb("dec", [128, nkc, 2])
                ld(Ec, Ec[:], I["ret_e"][:, :, :, :]); ld(dtokc, dtokc[:], I["ret_tok"][:, :, :]); ld(decc, decc[:], I["ret_dec"][:, :, :])
                for c in range(nkc):
                    for d_ in range(2):
                        dve(lambda e, c=c, d_=d_: e.tensor_tensor(out=Epc[:, c, d_, :, :], in0=Ec[:, c, 1 + 2 * d_, :].unsqueeze(1).broadcast_to([128, hpc, 128]),
                                                                  in1=mh[:].unsqueeze(2).broadcast_to([128, hpc, 128]), op=ALU.mult), [Ec, mh], [Epc])
            else:
                wd = P.sb("wd", [33, 256]); ld(wd, wd[:], I["gla_wd"][l])
                lnsb = P.sb("lnsb", [128, 1])
                dve(lambda e: e.memset(lnsb[:], lns), [], [lnsb])
                psZ = P.ps("psZ", [128, 512]); psB = P.ps("psB", [128, 4, 128])

            class TS:
                pass

            def mk_set(i):
                S = TS()
                S.pt = P.sb(f"pt{i}", [128, width])
                S.Qt = P.sb(f"Qt{i}", [128, nkc, 4, 128], BF16); S.Kp = P.sb(f"Kp{i}", [128, nkc, 2, hpc, 128], BF16)
                S.khat = P.sb(f"khat{i}", [128, 2, KW], BF16); S.Vb = P.sb(f"Vb{i}", [128, 256], BF16)
                S.st1 = P.sb(f"st1{i}", [128, 4, 128]); S.st2 = P.sb(f"st2{i}", [128, 4, 128]); S.PT = P.sb(f"PT{i}", [128, 4, 128], BF16)
                S.hn1 = P.sb(f"hn1{i}", [128, 4]); S.hn2 = P.sb(f"hn2{i}", [128, 4]); S.oc = P.sb(f"oc{i}", [128, 4, 64]); S.osq = P.sb(f"osq{i}", [128, 4, 64])
                S.sg = P.sb(f"sg{i}", [128, 256]); S.resb = P.sb(f"resb{i}", [128, 256], BF16); S.resT = P.sb(f"resT{i}", [128, 2, 128], BF16)
                S.tU = P.sb(f"tU{i}", [128, nkc, 256])
                if ret:
                    S.rot = P.sb(f"rot{i}", [128, 2, 32]); S.qkr = P.sb(f"qkr{i}", [128, 8, 64])
                    S.rt1 = P.sb(f"rt1{i}", [128, 8, 32]); S.rt2 = P.sb(f"rt2{i}", [128, 8, 32])
                    S.E, S.Epad, S.dtok, S.dec = Ec, Epc, dtokc, decc
                else:
                    S.lrT = P.sb(f"lrT{i}", [33, 128]); dve(lambda e: e.memset(S.lrT[:], 1.0), [], [S.lrT])
                    S.et = P.sb(f"et{i}", [128, 256]); S.lt = P.sb(f"lt{i}", [128, 256]); S.bsb = P.sb(f"bsb{i}", [128, 4, 128])
                    S.mids = P.sb(f"mids{i}", [128, 4])
                    S.E = P.sb(f"E{i}", [128, nkc, 6, 128]); S.Epad = P.sb(f"Epad{i}", [128, nkc, 2, hpc, 128])
                    S.dtok = P.sb(f"dtok{i}", [128, 2, KW]); S.dec = P.sb(f"dec{i}", [128, nkc, 2])
                return S

            sets = [mk_set(0), mk_set(1)]
            psT = P.ps("psT", [128, 2 * nkc, 128])
            psS = [P.ps(f"psS{i}", [128, 4, 128]) for i in range(2)]
            psO = P.ps("psO", [128, 256]); psU = P.ps("psU", [128, nkc, 256]); psR = P.ps("psR", [128, 2, 128], BF16)
            Sm = P.sb("Sm", [128, nkc, 256]); Sbf = P.sb("Sbf", [128, nkc, 256], BF16); Sball = P.sb("Sball", [128, NT, nkc, 256], BF16)
            brv = brF.rearrange("(k p) t -> p k t", p=128)

            def prep(n, S, full=True):
                pt = S.pt
                ld(pt, pt[:], projT[n * 128:(n + 1) * 128, col0:col0 + width], src=dB["projT"])
                if ret:
                    rot, qkr, rt1, rt2 = S.rot, S.qkr, S.rt1, S.rt2
                    ld(rot, rot[:], I["rot"][n * 128:(n + 1) * 128, :, :])
                    h0 = 0 if full else 4
                    nh_ = 8 - h0
                    src = pt[:, 0:512].rearrange("p (h d) -> p h d", d=64)[:, h0:8, :]
                    cosb = rot[:, 0, :].unsqueeze(1).broadcast_to([128, nh_, 32]); sinb = rot[:, 1, :].unsqueeze(1).broadcast_to([128, nh_, 32])
                    qkr_full = qkr
                    qkr = qkr[:, h0:8, :]; rt1 = rt1[:, h0:8, :]; rt2 = rt2[:, h0:8, :]
                    gps(lambda e: e.tensor_tensor(out=rt1, in0=src[:, :, 0:32], in1=cosb, op=ALU.mult), [pt, rot], [S.rt1])
                    gps(lambda e: e.tensor_tensor(out=rt2, in0=src[:, :, 32:64], in1=sinb, op=ALU.mult), [pt, rot], [S.rt2])
                    gps(lambda e: e.tensor_tensor(out=qkr[:, :, 0:32], in0=rt1, in1=rt2, op=ALU.subtract), [S.rt1, S.rt2], [S.qkr])
                    gps(lambda e: e.tensor_tensor(out=rt1, in0=src[:, :, 0:32], in1=sinb, op=ALU.mult), [pt, rot, S.qkr], [S.rt1])
                    gps(lambda e: e.tensor_tensor(out=rt2, in0=src[:, :, 32:64], in1=cosb, op=ALU.mult), [pt, rot, S.qkr], [S.rt2])
                    gps(lambda e: e.tensor_tensor(out=qkr[:, :, 32:64], in0=rt1, in1=rt2, op=ALU.add), [S.rt1, S.rt2], [S.qkr])
                    qk = qkr_full[:].rearrange("p h d -> p (h d)")
                    S.q_tok, S.k_tok, S.qkb = qk[:, 0:256], qk[:, 256:512], qkr_full
                else:
                    lrT, et, lt, bsb, mids, E, Epad, dtok, dec = S.lrT, S.et, S.lt, S.bsb, S.mids, S.E, S.Epad, S.dtok, S.dec
                    S.q_tok, S.k_tok, S.qkb = pt[:, qo:qo + 128], pt[:, ko:ko + 128], pt
                    pe(lambda e: e.transpose(psZ[0:32, 256:384], pt[:, lro:lro + 32], identf[:]), [pt, identf], [psZ])
                    yield
                    dve(lambda e: e.tensor_copy(out=lrT[0:32, :], in_=psZ[0:32, 256:384]), [psZ], [lrT])
                    yield
                    pe(lambda e: e.matmul(psZ[:, 0:256], lhsT=lrT[:], rhs=wd[:], start=True, stop=True), [lrT, wd], [psZ])
                    yield
                    act(lambda e: e.activation(out=et[:], in_=psZ[:, 0:256], func=AF.Exp, scale=-1.0), [psZ], [et])
                    act(lambda e: e.activation(out=lt[:], in_=et[:], func=AF.Ln, bias=1.0), [et], [lt])
                    yield
                    pe(lambda e: e.matmul(psB[:, 0, :], lhsT=lt[:, 0:128], rhs=tri[:, 2, :], start=True, stop=True), [lt, tri], [psB])
                    pe(lambda e: e.matmul(psB[:, 1, :], lhsT=lt[:, 128:256], rhs=tri[:, 3, :], start=True, stop=True), [lt, tri], [psB])
                    pe(lambda e: e.matmul(psB[:, 2, :], lhsT=tri[:, 4, :], rhs=lt[:, 0:128], start=True, stop=True), [lt, tri], [psB])
                    pe(lambda e: e.matmul(psB[:, 3, :], lhsT=tri[:, 5, :], rhs=lt[:, 128:256], start=True, stop=True), [lt, tri], [psB])
                    yield
                    dve(lambda e: e.tensor_copy(out=bsb[:], in_=psB[:]), [psB], [bsb])
                    dve(lambda e: e.tensor_scalar(out=mids[:, 0:2], in0=bsb[:, 0:2, 64], scalar1=-1.0, scalar2=lns, op0=ALU.mult, op1=ALU.add), [bsb], [mids])
                    dve(lambda e: e.tensor_copy(out=mids[:, 2:4], in_=bsb[:, 0:2, 64]), [bsb], [mids])
                    yield
                    for d_ in (range(2) if full else (1,)):
                        if full:
                            act(lambda e, d_=d_: e.activation(out=E[:, 0, 2 * d_, :], in_=bsb[:, d_, :], func=AF.Exp, bias=mids[:, d_:d_ + 1]), [bsb, mids], [E])
                            act(lambda e, d_=d_: e.activation(out=E[:, 0, 2 * d_ + 1, :], in_=bsb[:, d_, :], func=AF.Exp, scale=-1.0, bias=mids[:, 2 + d_:3 + d_]), [bsb, mids], [E])
                            act(lambda e, d_=d_: e.activation(out=E[:, 0, 4 + d_, :], in_=bsb[:, d_, :], func=AF.Exp, bias=lnsb[:, 0:1]), [bsb, lnsb], [E])
                        act(lambda e, d_=d_: e.activation(out=dtok[:, d_, :], in_=bsb[:, 2 + d_, :], func=AF.Exp), [bsb], [dtok])
                    if full:
                        act(lambda e: e.activation(out=dec[:, 0, 0:1], in_=bsb[:, 0, 127:128], func=AF.Exp), [bsb], [dec])
                    act(lambda e: e.activation(out=dec[:, 0, 1:2], in_=bsb[:, 1, 0:1], func=AF.Exp), [bsb], [dec])
                    yield
                    if full:
                        for d_ in range(2):
                            dve(lambda e, d_=d_: e.tensor_tensor(out=Epad[:, 0, d_, :, :], in0=E[:, 0, 1 + 2 * d_, :].unsqueeze(1).broadcast_to([128, hpc, 128]),
                                                                 in1=mh[:].unsqueeze(2).broadcast_to([128, hpc, 128]), op=ALU.mult), [E, mh], [Epad])
                act(lambda e: e.activation(out=S.Vb[:], in_=pt[:, vo:vo + 256], func=AF.Copy), [pt], [S.Vb])
                for d_ in (range(2) if full else (1,)):
                    gps(lambda e, d_=d_: e.tensor_tensor(out=S.khat[:, d_, :], in0=S.k_tok, in1=S.dtok[:, d_, :], op=ALU.mult), [S.qkb, S.dtok], [S.khat])
                yield

            def state_update(d_, S):
                for c in range(nkc):
                    pe(lambda e, c=c: e.matmul(psU[:, c, :], lhsT=S.khat[:, d_, c * 128:(c + 1) * 128], rhs=S.Vb[:], start=True, stop=True), [S.khat, S.Vb], [psU])
                yield
                dve(lambda e: e.tensor_tensor(out=S.tU[:], in0=psU[:], in1=bd[:], op=ALU.mult), [psU, bd], [S.tU])
                for c in range(nkc):
                    dve(lambda e, c=c: e.scalar_tensor_tensor(out=Sm[:, c, :], in0=Sm[:, c, :], scalar=S.dec[:, c, d_:d_ + 1], in1=S.tU[:, c, :],
                                                              op0=ALU.mult, op1=ALU.add), [Sm, S.dec, S.tU], [Sm])

            def keep_mul():
                dve(lambda e: e.tensor_scalar(out=Sm[:], in0=Sm[:], scalar1=scal[:, 0:1], scalar2=None, op0=ALU.mult), [Sm, scal], [Sm])

            def gen1(n):
                S = sets[n % 2]
                yield from prep(n, S, full=False)
                yield 'prev_done'
                act(lambda e: e.activation(out=Sball[:, n, :, :], in_=Sm[:], func=AF.Copy), [Sm], [Sball])
                yield from state_update(1, S)
                if n == NT // 2:
                    keep_mul()

            dve(lambda e: e.memset(Sm[:], 0.0), [], [Sm])
            run_pipelined(gen1, reversed(range(NT)), depth=int(os.environ.get('PIPE1', '2')))

            def gen2(n):
                S = sets[n % 2]
                if PD_POS == 0:
                    yield 'prev_done'
                yield from prep(n, S)
                if PD_POS == 1:
                    yield 'prev_done'
                Qt, Kp, PT, Vb, E, Epad = S.Qt, S.Kp, S.PT, S.Vb, S.E, S.Epad
                for c in range(nkc):
                    pe(lambda e, c=c: e.transpose(psT[:, c, :], S.q_tok[:, c * 128:(c + 1) * 128], identf[:]), [S.qkb, identf], [psT])
                    pe(lambda e, c=c: e.transpose(psT[:, nkc + c, :], S.k_tok[:, c * 128:(c + 1) * 128], identf[:]), [S.qkb, identf], [psT])
                yield
                if PD_POS == 2:
                    yield 'prev_done'
                for c in range(nkc):
                    for vi, ei in enumerate((0, 2, 4, 5)):
                        dve(lambda e, c=c, vi=vi, ei=ei: e.tensor_tensor(out=Qt[:, c, vi, :], in0=psT[:, c, :], in1=E[:, c, ei, :], op=ALU.mult), [psT, E], [Qt])
                    for d_ in range(2):
                        dve(lambda e, c=c, d_=d_: e.tensor_tensor(out=Kp[:, c, d_, :, :], in0=psT[:, nkc + c, :].unsqueeze(1).broadcast_to([128, hpc, 128]),
                                                                  in1=Epad[:, c, d_, :, :], op=ALU.mult), [psT, Epad], [Kp])
                yield
                if PD_POS == 3:
                    yield 'prev_done'
                for d_ in range(2):
                    for c in range(nkc):
                        for hh in range(hpc):
                            pe(lambda e, d_=d_, c=c, hh=hh: e.matmul(psS[d_][:, c * hpc + hh, :], lhsT=Kp[:, c, d_, hh, :], rhs=Qt[:, c, d_, :], start=True, stop=True),
                               [Kp, Qt], [psS[d_]])
                yield
                if PD_POS == 4:
                    yield 'prev_done'
                dve(lambda e: e.tensor_tensor(out=S.st1[:], in0=psS[0][:], in1=tri[:, 0, :].unsqueeze(1).broadcast_to([128, 4, 128]), op=ALU.mult), [psS[0], tri], [S.st1])
                dve(lambda e: e.tensor_tensor(out=S.st2[:], in0=psS[1][:], in1=tri[:, 1, :].unsqueeze(1).broadcast_to([128, 4, 128]), op=ALU.mult), [psS[1], tri], [S.st2])
                gps(lambda e: e.tensor_tensor(out=PT[:], in0=S.st1[:], in1=S.st2[:], op=ALU.add), [S.st1, S.st2], [PT])
                yield 'prev_done'
                if n == NT // 2:
                    keep_mul()
                act(lambda e: e.activation(out=Sbf[:], in_=Sm[:], func=AF.Copy), [Sm], [Sbf])
                yield
                for h_ in range(4):
                    c = h_ // hpc
                    hs = slice(h_ * 64, (h_ + 1) * 64)
                    pe(lambda e, c=c, hs=hs: e.matmul(psO[:, hs], lhsT=Qt[:, c, 2, :], rhs=Sbf[:, c, hs], start=True, stop=False), [Qt, Sbf], [psO])
                    pe(lambda e, c=c, hs=hs: e.matmul(psO[:, hs], lhsT=Qt[:, c, 3, :], rhs=Sball[:, n, c, hs], start=False, stop=False), [Qt, Sball], [psO])
                    pe(lambda e, h_=h_, hs=hs: e.matmul(psO[:, hs], lhsT=PT[:, h_, :], rhs=Vb[:, hs], start=False, stop=True), [PT, Vb], [psO])
                yield
                hn1, hn2, oc, osq, sg, resb, resT, pt = S.hn1, S.hn2, S.oc, S.osq, S.sg, S.resb, S.resT, S.pt
                O3 = psO[:].rearrange("p (h d) -> p h d", d=64)
                if ret:
                    dve(lambda e: e.tensor_reduce(out=hn1[:], in_=O3, axis=mybir.AxisListType.X, op=ALU.add), [psO], [hn1])
                    dve(lambda e: e.tensor_scalar(out=hn1[:], in0=hn1[:], scalar1=-1.0 / 64, scalar2=None, op0=ALU.mult), [hn1], [hn1])
                    dve(lambda e: e.tensor_tensor(out=oc[:], in0=O3, in1=hn1[:].unsqueeze(2).broadcast_to([128, 4, 64]), op=ALU.add), [psO, hn1], [oc])
                else:
                    dve(lambda e: e.tensor_copy(out=oc[:], in_=O3), [psO], [oc])
                gps(lambda e: e.tensor_tensor(out=osq[:], in0=oc[:], in1=oc[:], op=ALU.mult), [oc], [osq])
                dve(lambda e: e.tensor_reduce(out=hn2[:], in_=osq[:], axis=mybir.AxisListType.X, op=ALU.add), [osq], [hn2])
                act(lambda e: e.activation(out=sg[:], in_=pt[:, go:go + 256], func=AF.Silu), [pt], [sg])
                act(lambda e: e.activation(out=hn2[:], in_=hn2[:], func=AF.Sqrt, scale=1.0 / 64, bias=epsb[:, 0:1]), [hn2, epsb], [hn2])
                yield
                dve(lambda e: e.reciprocal(out=hn2[:], in_=hn2[:]), [hn2], [hn2])
                gps(lambda e: e.tensor_tensor(out=oc[:], in0=oc[:], in1=hn2[:].unsqueeze(2).broadcast_to([128, 4, 64]), op=ALU.mult), [oc, hn2], [oc])
                gps(lambda e: e.tensor_tensor(out=sg[:], in0=sg[:], in1=gn[:], op=ALU.mult), [sg, gn], [sg])
                gps(lambda e: e.tensor_tensor(out=resb[:], in0=oc[:].rearrange("p h d -> p (h d)"), in1=sg[:], op=ALU.mult), [oc, sg], [resb])
                yield
                for c2_ in range(2):
                    pe(lambda e, c2_=c2_: e.transpose(psR[:, c2_, :], resb[:, c2_ * 128:(c2_ + 1) * 128], identb[:]), [resb, identb], [psR])
                yield from state_update(0, S)
                dve(lambda e: e.tensor_copy(out=resT[:], in_=psR[:]), [psR], [resT])
                stq(brv[:, 2 * kbr:2 * kbr + 2, n * 128:(n + 1) * 128], resT, resT[:], dB["brF"])

            dve(lambda e: e.memset(Sm[:], 0.0), [], [Sm])
            run_pipelined(gen2, range(NT), depth=int(os.environ.get('PIPE2', '2')))
            P.pop()

        def phase_hyena(l, mode='all'):
            NBLK = 2 * T // 512
            Cg = 32
            TWO_PI = 2.0 * math.pi
            def part1():
                P.push()
                w1 = P.sb("w1", [33, 64]); w2 = P.sb("w2", [64, 64]); w3a = P.sb("w3a", [65, 1024])
                c1 = P.sb("c1", [64, 2]); c2_ = P.sb("c2", [64, 2]); fb = P.sb("fb", [64, 2]); delta = P.sb("delta", [128, 2])
                ld(w1, w1[:], I["flt_w1"][l]); ld(w2, w2[:], I["flt_w2"][l]); ld(w3a, w3a[0:64, :], I["flt_w3"][l])
                ld(w3a, w3a[64:65, :], I["flt_b3"][l:l + 1, :])
                ld(c1, c1[:], I["flt_c1"][l]); ld(c2_, c2_[:], I["flt_c2"][l]); ld(delta, delta[:], I["flt_delta"][:, :])
                dve(lambda e: e.tensor_tensor(out=fb[:, 0:1], in0=c1[:, 0:1], in1=c1[:, 1:2], op=ALU.mult), [c1], [fb])
                dve(lambda e: e.tensor_tensor(out=fb[:, 1:2], in0=c2_[:, 0:1], in1=c2_[:, 1:2], op=ALU.mult), [c2_, fb], [fb])
                h2a = P.sb("h2a", [65, 512], BF16); dve(lambda e: e.memset(h2a[:], 1.0), [], [h2a])
                w3b = P.sb("w3b", [65, 1024], BF16); dve(lambda e: e.tensor_copy(out=w3b[:], in_=w3a[:]), [w3a], [w3b])
                h1 = P.sb("h1", [64, 512]); a1 = P.sb("a1", [64, 512]); kk = P.sb("kk", [64, 512])
                nrm = P.sb("nrm", [128, 4, NBLK]); rn = P.sb("rn", [128, 4])
                fts = [P.sb(f"ft{i}", [33, 512]) for i in range(2)]; msks = [P.sb(f"msk{i}", [128, 3, 512]) for i in range(2)]
                win = P.sb("win", [128, 2, 512]); t1 = P.sb("ft1", [128, 512]); t2 = P.sb("ft2", [128, 512]); ab = P.sb("fab", [128, 512])
                gbs = [P.sb(f"gb{i}", [128, 512], BF16) for i in range(2)]
                psh = P.ps("psh", [64, 512]); psf = [P.ps(f"psf{i}", [128, 512]) for i in range(2)]

                def sin_layer(cc, col, dst):
                    dve(lambda e: e.tensor_scalar(out=a1[:], in0=psh[:], scalar1=cc[:, 0:1], scalar2=fb[:, col:col + 1], op0=ALU.mult, op1=ALU.add), [psh, cc, fb], [a1])
                    dve(lambda e: e.tensor_scalar(out=kk[:], in0=a1[:], scalar1=1.0 / TWO_PI, scalar2=MAGIC, op0=ALU.mult, op1=ALU.add), [a1], [kk])
                    dve(lambda e: e.tensor_scalar(out=kk[:], in0=kk[:], scalar1=-MAGIC, scalar2=None, op0=ALU.add), [kk], [kk])
                    dve(lambda e: e.scalar_tensor_tensor(out=a1[:], in0=kk[:], scalar=-TWO_PI, in1=a1[:], op0=ALU.mult, op1=ALU.add), [kk, a1], [a1])
                    act(lambda e: e.activation(out=dst, in_=a1[:], func=AF.Sin), [a1], [h1 if dst is not None and cc is c1 else h2a])

                ng = 0
                for blk in range(NBLK):
                    m0 = blk * 512
                    ft, msk = fts[blk % 2], msks[blk % 2]
                    ld(ft, ft[:], I["flt_feat"][:, m0:m0 + 512])
                    for r_ in range(3):
                        ld(msk, msk[:, r_, :], I["flt_msk"][r_, m0:m0 + 512].partition_broadcast(128))
                    pe(lambda e: e.matmul(psh[:], lhsT=w1[:], rhs=ft[:], start=True, stop=True), [w1, ft], [psh])
                    sin_layer(c1, 0, h1[:])
                    yield
                    pe(lambda e: e.matmul(psh[:], lhsT=w2[:], rhs=h1[:], start=True, stop=True), [w2, h1], [psh])
                    sin_layer(c2_, 1, h2a[0:64, :])
                    yield
                    for ch in range(2):
                        act(lambda e, ch=ch: e.activation(out=win[:, ch, :], in_=msk[:, 2, :], func=AF.Exp, scale=delta[:, ch:ch + 1]), [msk, delta], [win])
                    for o in range(2):
                        for ch in range(2):
                            for dr in range(2):
                                q = o * 4 + dr * 2 + ch
                                pe(lambda e, dr=dr, q=q: e.matmul(psf[dr][:], lhsT=w3b[:, q * 128:(q + 1) * 128], rhs=h2a[:], start=True, stop=True), [w3b, h2a], [psf[dr]])
                            dve(lambda e: e.tensor_tensor(out=t1[:], in0=psf[0][:], in1=msk[:, 0, :], op=ALU.mult), [psf[0], msk], [t1])
                            dve(lambda e: e.tensor_tensor(out=t2[:], in0=psf[1][:], in1=msk[:, 1, :], op=ALU.mult), [psf[1], msk], [t2])
                            dve(lambda e: e.tensor_tensor(out=t1[:], in0=t1[:], in1=t2[:], op=ALU.add), [t1, t2], [t1])
                            dve(lambda e, ch=ch: e.tensor_tensor(out=t1[:], in0=t1[:], in1=win[:, ch, :], op=ALU.mult), [t1, win], [t1])
                            gb = gbs[ng % 2]; ng += 1
                            idx = o * 2 + ch
                            act(lambda e, gb=gb: e.activation(out=gb[:], in_=t1[:], func=AF.Copy), [t1], [gb])
                            act(lambda e, idx=idx, blk=blk: e.activation(out=ab[:], in_=t1[:], func=AF.Abs, accum_out=nrm[:, idx, blk:blk + 1]), [t1], [ab, nrm])
                            stq(gF[idx * 128:(idx + 1) * 128, m0:m0 + 512], gb, gb[:], dB["gF"])
                            yield
                dve(lambda e: e.tensor_reduce(out=rn[:], in_=nrm[:], axis=mybir.AxisListType.X, op=ALU.add), [nrm], [rn])
                dve(lambda e: e.tensor_scalar(out=rn[:], in0=rn[:], scalar1=scal[:, 1:2], scalar2=EPS, op0=ALU.mult, op1=ALU.add), [rn, scal], [rn])
                dve(lambda e: e.reciprocal(out=rn[:], in_=rn[:]), [rn], [rn])
                stq(rnD.rearrange("(q p) -> p q", p=128), rn, rn[:], dB["rnD"])
                P.pop()

            def stage1(din, m1, AA, ps1s, cnt, Cg=Cg):
                for c in range(0, Cg, 2):
                    ps1 = ps1s[cnt[0] % 2]; cnt[0] += 1
                    for cc in range(2):
                        pe(lambda e, cc=cc, ps1=ps1, c=c: e.matmul(ps1[:, cc, :], lhsT=din[:, c + cc, :], rhs=m1[:], start=True, stop=True), [din, m1], [ps1])
                    for ri in range(2):
                        src_ap = ps1[:, :, ri * NSA:(ri + 1) * NSA].rearrange("p c s -> p s c")
                        if ri == 0:
                            dve(lambda e, src_ap=src_ap, c=c: e.tensor_copy(out=AA[:, :, 0, c:c + 2], in_=src_ap), [ps1], [AA])
                        else:
                            act(lambda e, src_ap=src_ap, c=c: e.activation(out=AA[:, :, 1, c:c + 2], in_=src_ap, func=AF.Copy), [ps1], [AA])
                    yield

            def stage2(AA, h2ts, psXs, evac, spb=8):
                h2v = I["hy_h2"].rearrange("s p a k -> p s a k")
                for j in range(NSA):
                    h2t = h2ts[(j // 8) % 2]; jj = j % spb
                    if j % 8 == 0:
                        nj_ = min(8, NSA - j)
                        ld(h2t, h2t[:, 0:nj_, :, :], h2v[:, j:j + nj_, :, :])
                    psX = psXs[(j // spb) % 2]
                    j8 = j % 8
                    pe(lambda e, psX=psX, jj=jj, h2t=h2t, j=j, j8=j8: e.matmul(psX[:, jj, 0, :], lhsT=h2t[:, j8, 0, :], rhs=AA[:, j, 0, :], start=True, stop=False), [h2t, AA], [psX])
                    pe(lambda e, psX=psX, jj=jj, h2t=h2t, j=j, j8=j8: e.matmul(psX[:, jj, 0, :], lhsT=h2t[:, j8, 2, :], rhs=AA[:, j, 1, :], start=False, stop=True), [h2t, AA], [psX])
                    pe(lambda e, psX=psX, jj=jj, h2t=h2t, j=j, j8=j8: e.matmul(psX[:, jj, 1, :], lhsT=h2t[:, j8, 0, :], rhs=AA[:, j, 1, :], start=True, stop=False), [h2t, AA], [psX])
                    pe(lambda e, psX=psX, jj=jj, h2t=h2t, j=j, j8=j8: e.matmul(psX[:, jj, 1, :], lhsT=h2t[:, j8, 1, :], rhs=AA[:, j, 0, :], start=False, stop=True), [h2t, AA], [psX])
                    if jj == spb - 1 or j == NSA - 1:
                        evac(psX, j - jj, jj + 1)
                        yield

            def part2():
                P.push()
                rnb = P.sb("rnb", [128, 512]); ld(rnb, rnb[:], rnD.partition_broadcast(128), src=dB["rnD"])
                hf1 = P.sb("hf1", [NS, 2 * NSA], BF16); ld(hf1, hf1[:], I["hy_hf1"][:, :])
                Cf = 64
                gin = P.sb("gin", [NS, Cf, 128], BF16); AA = P.sb("AAf", [128, NSA, 2, Cf], BF16)
                Gsb = P.sb("Gsbf", [128, NSA, 2, Cf], BF16)
                ps1s = [P.ps(f"hps1f{i}", [128, 2, 2 * NSA]) for i in range(2)]; psXs = [P.ps(f"hpsXf{i}", [128, 4, 2, Cf]) for i in range(2)]
                h2ts = [P.sb(f"h2tf{i}", [128, 8, 3, 128], BF16) for i in range(2)]
                cnt = [0]
                for gi in range(512 // Cf):
                    ld(gin, gin[:], gF[gi * Cf:(gi + 1) * Cf, :].rearrange("c (g n) -> g c n", n=128), src=dB["gF"])
                    yield from stage1(gin, hf1, AA, ps1s, cnt, Cg=Cf)

                    def evacG(psX, j0, nj, gi=gi):
                        dve(lambda e: e.tensor_tensor(out=Gsb[:, j0:j0 + nj, :, :], in0=psX[:, 0:nj, :, :],
                                                      in1=rnb[:, gi * Cf:(gi + 1) * Cf].unsqueeze(1).unsqueeze(1).broadcast_to([128, nj, 2, Cf]), op=ALU.mult),
                            [psX, rnb], [Gsb])
                    yield from stage2(AA, h2ts, psXs, evacG, spb=4)
                    for hh_ in range(2):
                        for s0_ in range(0, NSA, 32):
                            s1_ = min(NSA, s0_ + 32)
                            stq(Gd[2 * gi + hh_].rearrange("p s (r c) -> p s r c", r=2)[:, s0_:s1_], Gsb, Gsb[:, s0_:s1_, :, hh_ * 32:(hh_ + 1) * 32], dB["Gd"])
                P.pop()

            def part3():
                P.push()
                hz = P.sb("hz", [NSA, 128, 2, NT], BF16); ld(hz, hz[:], I["hy_z"][:, :, :, :])
                h1t = P.sb("hh1", [NT, 2 * NSA], BF16); ld(h1t, h1t[:], I["hy_h1"][:, :])
                i1 = P.sb("hi1", [128, 2, 256], BF16); ld(i1, i1[:], I["hy_i1"][:, :, :])
                skb = P.sb("skb", [128, 2, 256])
                for o in range(2):
                    ld(skb, skb[:, o, :], I["hy_skip"][l, o].partition_broadcast(128))
                AA = P.sb("AAd", [128, NSA, 2, Cg], BF16); Ysb = P.sb("Ysb", [128, 2, Cg, NSA], BF16); Bsb = P.sb("Bsb", [NSA, 128, 2, Cg], BF16)
                Gsb = P.sb("Gsbd", [128, NSA, 2, Cg], BF16)
                din = P.sb("din", [NT, Cg, 128], BF16); vt = P.sb("vt", [NT, Cg, 128]); x1t = P.sb("x1t", [NT, Cg, 128]); x2t = P.sb("x2t", [NT, Cg, 128])
                ob = P.sb("ob", [NT, Cg, 128], BF16)
                pw = [P.sb(f"pw{i}", [128, 8, Cg]) for i in range(4)]
                tcv = P.sb("tcv", [NT, Cg, 16])
                ps1s = [P.ps(f"hps1d{i}", [128, 2, 2 * NSA]) for i in range(2)]; psXs = [P.ps(f"hpsXd{i}", [128, 8, 2, Cg]) for i in range(2)]
                psIs = [P.ps(f"hpsI{i}", [NSA, 2, 256]) for i in range(2)]; psYs = [P.ps(f"hpsY{i}", [NT, 16, Cg]) for i in range(2)]
                h2ts = [P.sb(f"h2td{i}", [128, 8, 3, 128], BF16) for i in range(2)]
                cnt = [0]

                def evacY(psX, j0, nj):
                    Xre, Xim = psX[:, 0:nj, 0, :], psX[:, 0:nj, 1, :]
                    Gre, Gim = Gsb[:, j0:j0 + nj, 0, :], Gsb[:, j0:j0 + nj, 1, :]
                    dve(lambda e: e.tensor_tensor(out=pw[0][:, 0:nj, :], in0=Xre, in1=Gre, op=ALU.mult), [psX, Gsb], [pw[0]])
                    dve(lambda e: e.tensor_tensor(out=pw[1][:, 0:nj, :], in0=Xim, in1=Gim, op=ALU.mult), [psX, Gsb], [pw[1]])
                    dve(lambda e: e.tensor_tensor(out=Ysb[:, 0, :, j0:j0 + nj].rearrange("p c j -> p j c"), in0=pw[0][:, 0:nj, :], in1=pw[1][:, 0:nj, :], op=ALU.subtract),
                        [pw[0], pw[1]], [Ysb])
                    dve(lambda e: e.tensor_tensor(out=pw[2][:, 0:nj, :], in0=Xre, in1=Gim, op=ALU.mult), [psX, Gsb], [pw[2]])
                    dve(lambda e: e.tensor_tensor(out=pw[3][:, 0:nj, :], in0=Xim, in1=Gre, op=ALU.mult), [psX, Gsb], [pw[3]])
                    dve(lambda e: e.tensor_tensor(out=Ysb[:, 1, :, j0:j0 + nj].rearrange("p c j -> p j c"), in0=pw[2][:, 0:nj, :], in1=pw[3][:, 0:nj, :], op=ALU.add),
                        [pw[2], pw[3]], [Ysb])

                def long_conv(o, g, xg, svt):
                    ld(Gsb, Gsb[:].rearrange("p s r c -> p s (r c)"), Gd[o * (256 // Cg) + g], src=dB["Gd"])
                    drain(stage1(din, h1t, AA, ps1s, cnt))
                    drain(stage2(AA, h2ts, psXs, evacY))
                    for c in range(0, Cg, 2):
                        psI = psIs[(c // 2) % 2]
                        for cc in range(2):
                            pe(lambda e, cc=cc, psI=psI, c=c: e.matmul(psI[:, cc, :], lhsT=Ysb[:, 0, c + cc, :], rhs=i1[:, 0, :], start=True, stop=False), [Ysb, i1], [psI])
                            pe(lambda e, cc=cc, psI=psI, c=c: e.matmul(psI[:, cc, :], lhsT=Ysb[:, 1, c + cc, :], rhs=i1[:, 1, :], start=False, stop=True), [Ysb, i1], [psI])
                        for ri in range(2):
                            src_ap = psI[:, :, ri * 128:(ri + 1) * 128].rearrange("p c n -> p n c")
                            if ri == 0:
                                dve(lambda e, src_ap=src_ap, c=c: e.tensor_copy(out=Bsb[:, :, 0, c:c + 2], in_=src_ap), [psI], [Bsb])
                            else:
                                act(lambda e, src_ap=src_ap, c=c: e.activation(out=Bsb[:, :, 1, c:c + 2], in_=src_ap, func=AF.Copy), [psI], [Bsb])
                    for nb in range(8):
                        psY = psYs[nb % 2]
                        for q in range(16):
                            n2 = nb * 16 + q
                            pe(lambda e, psY=psY, q=q, n2=n2: e.matmul(psY[:, q, :], lhsT=hz[:, n2, 0, :], rhs=Bsb[:, n2, 0, :], start=True, stop=False), [hz, Bsb], [psY])
                            pe(lambda e, psY=psY, q=q, n2=n2: e.matmul(psY[:, q, :], lhsT=hz[:, n2, 1, :], rhs=Bsb[:, n2, 1, :], start=False, stop=True), [hz, Bsb], [psY])
                        sl = slice(nb * 16, (nb + 1) * 16)
                        dve(lambda e, psY=psY, sl=sl: e.tensor_tensor(out=tcv[:], in0=psY[:].rearrange("p n c -> p c n"), in1=svt[:, :, sl], op=ALU.add), [psY, svt], [tcv])
                        dve(lambda e, sl=sl: e.tensor_tensor(out=xg[:, :, sl], in0=tcv[:], in1=xg[:, :, sl], op=ALU.mult), [tcv, xg], [xg])

                uv = lambda r0: uhF[r0:r0 + Cg, :].rearrange("c (g n) -> g c n", n=128)
                for g in range(256 // Cg):
                    c0 = g * Cg
                    ld(vt, vt[:], uv(c0), src=dB["uhF"]); ld(x1t, x1t[:], uv(256 + c0), src=dB["uhF"]); ld(x2t, x2t[:], uv(512 + c0), src=dB["uhF"])
                    act(lambda e: e.activation(out=din[:], in_=vt[:], func=AF.Copy), [vt], [din])
                    dve(lambda e, c0=c0: e.tensor_tensor(out=vt[:], in0=vt[:], in1=skb[0:NT, 0, c0:c0 + Cg].unsqueeze(2).broadcast_to([NT, Cg, 128]), op=ALU.mult), [vt, skb], [vt])
                    long_conv(0, g, x1t, vt)
                    act(lambda e: e.activation(out=din[:], in_=x1t[:], func=AF.Copy), [x1t], [din])
                    dve(lambda e, c0=c0: e.tensor_tensor(out=vt[:], in0=x1t[:], in1=skb[0:NT, 1, c0:c0 + Cg].unsqueeze(2).broadcast_to([NT, Cg, 128]), op=ALU.mult), [x1t, skb], [vt])
                    long_conv(1, g, x2t, vt)
                    act(lambda e: e.activation(out=ob[:], in_=x2t[:], func=AF.Copy), [x2t], [ob])
                    stq(brF[512 + c0:512 + c0 + Cg, :].rearrange("c (g n) -> g c n", n=128), ob, ob[:], dB["brF"])
                P.pop()
            if mode == 'filtgen':
                def both():
                    yield from part1()
                    yield from part2()
                return both()
            if mode in ('all', 'filt'):
                drain(part1())
                drain(part2())
            if mode in ('all', 'conv'):
                part3()

        def phase_zero_br(l):
            P.push()
            zt = P.sb("zbr", [128, 2048], BF16)
            dve(lambda e: e.memset(zt[:], 0.0), [], [zt])
            for k in range(8):
                for t0 in range(0, T, 2048):
                    w_ = min(2048, T - t0)
                    stq(brF[k * 128:(k + 1) * 128, t0:t0 + w_], zt, zt[:, 0:w_], dB["brF"])
            P.pop()

        PHASES = dict(mod=phase_mod, norm=phase_norm, A=lambda l: drain(phase_A(l)), Afilt=lambda l: run_concurrent(phase_A(l), phase_hyena(l, 'filtgen')), C=phase_C, D=phase_D, zero=phase_zero_br, fnet=phase_fnet, ret=lambda l: phase_scan(l, True), gla=lambda l: phase_scan(l, False), hyena=phase_hyena, hyfilt=lambda l: phase_hyena(l, 'filt'), hyconv=lambda l: phase_hyena(l, 'conv'))
        nc._I = I
        return_hook(P, PHASES, locals())
    return nc


def return_hook(P, PHASES, env):
    sched = env.get('debug') or ()
    I, dB = env['I'], env['dB']
    stop = None
    for d in sched:
        if isinstance(d, str) and d.startswith("stop:"):
            stop = d[5:]
    x_in, x1d, xmid, y_out = env['x_in'], env['x1d'], env['xmid'], env['y_out']
    Am, Af = env['Am'], env['Af']
    only = [d[5:] for d in sched if isinstance(d, str) and d.startswith("only:")]
    if only:
        for nm in only:
            if nm == 'norm':
                PHASES['norm'](x_in, dB["in"], Am, 0)
            elif nm == 'C':
                PHASES['C'](0, x_in, dB["in"])
            elif nm == 'D':
                PHASES['D'](0, x1d, dB["x1d"])
            else:
                PHASES[nm](0)
        P.barrier()
        return
    for l in range(DEPTH):
        src, sbuf = (x_in, dB["in"]) if l == 0 else (x1d, dB["x1d"])
        dst, dbuf = (x1d, dB["x1d"]) if l == 0 else (y_out, dB["y"])
        if stop == "none":
            break
        PHASES['mod'](l)
        if stop == "mod":
            break
        PHASES['norm'](src, sbuf, Am, 0)
        if stop == "norm":
            break
        PHASES['Afilt' if CONC_FILT else 'A'](l)
        if stop == "A":
            break
        PHASES['zero'](l)
        for nm in ('fnet', 'ret', 'hyconv' if CONC_FILT else 'hyena', 'gla'):
            if nm in PHASES:
                PHASES[nm](l)
        if stop == "mix":
            break
        PHASES['C'](l, src, sbuf)
        PHASES['norm'](xmid, dB["xmid"], Af, 24)
        PHASES['D'](l, dst, dbuf)
        if stop == "L0":
            break
    P.barrier()


def prep_core_inputs(x, c2, W, tb):
    m = {"x": np.ascontiguousarray(x, np.float32)}
    m["cT"] = np.ascontiguousarray(c2.reshape(2, 8, 128).transpose(2, 1, 0), np.float32)
    m.update(W)
    m.update(tb)
    return m


def prep_weights(inp):
    f = lambda a: np.ascontiguousarray(a, np.float32)
    W = {}
    W["ada_w"] = f(inp["ada_w"]); W["ada_b"] = f(inp["ada_b"])
    W["ada_b_col"] = f(inp["ada_b"].reshape(DEPTH, 48, 128).transpose(0, 2, 1))
    nw = np.stack([inp["norm_pre_mix"], inp["norm_post_mix"], inp["norm_pre_ffn"], inp["norm_post_ffn"]], 1)
    W["normw_col"] = f(nw.reshape(DEPTH, 4, 8, 128).transpose(0, 3, 1, 2))
    W["norm_post_mix"] = f(inp["norm_post_mix"]); W["norm_post_ffn"] = f(inp["norm_post_ffn"])
    W["w_in"] = f(inp["w_in"])
    W["hy_cw"] = f(inp["hy_conv_w"].reshape(DEPTH, 3, 6, 128).transpose(0, 3, 2, 1))
    W["hy_cb"] = f(inp["hy_conv_b"].reshape(DEPTH, 6, 128).transpose(0, 2, 1))
    W["flt_w1"] = f(inp["flt_w1"]); W["flt_w2"] = f(inp["flt_w2"]); W["flt_w3"] = f(inp["flt_w3"]); W["flt_b3"] = f(inp["flt_b3"])
    W["flt_c1"] = f(np.stack([inp["flt_freq"], inp["flt_b1"]], -1)); W["flt_c2"] = f(np.stack([inp["flt_freq"], inp["flt_b2"]], -1))
    W["hy_skip"] = f(inp["hy_skip"])
    wd = np.zeros((DEPTH, 33, 256), np.float32)
    wd[:, 0:16, 0:128] = inp["gla_w_decay"][:, 0]; wd[:, 16:32, 128:256] = inp["gla_w_decay"][:, 1]
    wd[:, 32, 0:128] = inp["gla_b_decay"][:, 0]; wd[:, 32, 128:256] = inp["gla_b_decay"][:, 1]
    W["gla_wd"] = wd
    W["ret_gn"] = f(inp["ret_gn"]); W["gla_gn"] = f(inp["gla_gn"])
    W["w_branch"] = f(inp["w_branch"].reshape(DEPTH, 1024, D)); W["w_out"] = f(inp["w_out"])
    W["ffn_up"] = f(inp["ffn_up"])
    W["ffn_cw"] = f(inp["ffn_conv_w"].reshape(DEPTH, 3, 44, 128).transpose(0, 3, 2, 1))
    W["ffn_cb"] = f(inp["ffn_conv_b"].reshape(DEPTH, 44, 128).transpose(0, 2, 1))
    W["ffn_down"] = f(inp["ffn_down"])
    return W


_T = 8192


def kernel(**inp):
    inp = {k: np.asarray(v) for k, v in inp.items()}
    T = _T
    W = prep_weights(inp)
    tbP, tbS = make_tables(T, 'P'), make_tables(T, 'S')
    xp, xs, cp, cs = inp["x_prompt"], inp["x_sample"], inp["c_prompt"], inp["c_sample"]
    in_maps = []
    for b in range(2):
        in_maps.append(prep_core_inputs(xp[b], np.stack([cp[b], cp[b]]), W, tbP))
    for b in range(2):
        in_maps.append(prep_core_inputs(xs[2 * b:2 * b + 2].reshape(T, D), cs[2 * b:2 * b + 2], W, tbS))
    nc = build_program(T)
    res = run_bass_kernel_spmd(nc, in_maps, core_ids=list(range(4)))
    outs = [np.asarray(r["y"], np.float32) for r in res.results]
    y_prompt = np.stack([outs[0], outs[1]], 0)
    y_sample = np.concatenate([outs[2].reshape(2, T // 2, D), outs[3].reshape(2, T // 2, D)], 0)
    return (y_prompt, y_sample)
```

```python
import math
from contextlib import ExitStack
import numpy as np
import ml_dtypes
import concourse.bass as bass
import concourse.mybir as mybir
from concourse.bass_utils import run_bass_kernel_spmd

F32 = mybir.dt.float32
BF16 = mybir.dt.bfloat16
AF = mybir.ActivationFunctionType
ALU = mybir.AluOpType
NPBF = ml_dtypes.bfloat16

D = 1024
DEPTH = 2
DFF = 2816
EPS = 1e-6
MAGIC = 12582912.0
import os
PIPE_DEPTH = int(os.environ.get("PIPE_DEPTH", "2"))
PD_POS = int(os.environ.get("PD_POS", "99"))
CONC_FILT = int(os.environ.get("CONC_FILT", "1"))
USE_POOL = int(os.environ.get("USE_POOL", "0"))
PRIM_STEPS = int(os.environ.get("PRIM_STEPS", "1"))


class Stream:
    def __init__(self, P, inc):
        self.P, self.inc = P, inc
        self.sem = P.new_sem()
        self.count = 0

    def bump(self):
        if self.count + self.inc > 30000:
            self.sem = self.P.new_sem()
            self.count = 0
        self.count += self.inc
        return (self.sem, self.count)

    def cur(self):
        return (self.sem, self.count) if self.count else None


class Buf:
    def __init__(self, name, t=None):
        self.name, self.t = name, t
        self.w = None
        self.r = {}

    def __getitem__(self, idx):
        return self.t[idx]


class Prog:
    def __init__(self, nc, es):
        self.nc, self.es = nc, es
        self.nsem = 0
        self.engs = {'pe': nc.tensor, 'act': nc.scalar, 'dve': nc.vector, 'pool': nc.gpsimd, 'sp': nc.sync}
        self.streams = {k: Stream(self, 1) for k in ('pe', 'act', 'dve', 'pool')}
        self.seen = {k: {} for k in self.engs}
        self.dma_pool = {q: [Stream(self, 16) for _ in range(8)] for q in ('sp', 'pool')}
        self.dma_rr = {q: 0 for q in self.dma_pool}
        self.scopes = [es]
        self.nuniq = 0

    def new_sem(self):
        self.nsem += 1
        return self.es.enter_context(self.nc.semaphore(f"s{self.nsem}"))

    def sb(self, name, shape, dt=F32):
        self.nuniq += 1
        return Buf(name, self.scopes[-1].enter_context(self.nc.sbuf_tensor(f"{name}_{self.nuniq}", shape, dt)))

    def ps(self, name, shape, dt=F32):
        self.nuniq += 1
        return Buf(name, self.scopes[-1].enter_context(self.nc.psum_tensor(f"{name}_{self.nuniq}", shape, dt)))

    def push(self):
        st = ExitStack()
        self.scopes.append(st)
        return st

    def pop(self):
        self.barrier()
        self.scopes.pop().close()

    def _wait(self, eng, tok):
        if tok is None:
            return
        sem, val = tok
        seen = self.seen[eng]
        if seen.get(id(sem), 0) >= val:
            return
        self.engs[eng].wait_ge(sem, val)
        seen[id(sem)] = val

    def barrier(self):
        toks = [s.cur() for s in self.streams.values()]
        for pool in self.dma_pool.values():
            toks += [s.cur() for s in pool]
        for eng in self.engs:
            for t in toks:
                self._wait(eng, t)

    def _deps(self, eng, reads, writes, accum):
        for b in reads:
            self._wait(eng, b.w)
        for b in writes:
            if not accum:
                self._wait(eng, b.w)
            for t in b.r.values():
                self._wait(eng, t)

    def _commit(self, tok, reads, writes):
        for b in writes:
            b.w = tok
            b.r = {}
        for b in reads:
            b.r[id(tok[0])] = tok

    def op(self, eng, fn, reads=(), writes=(), accum=False):
        self._deps(eng, reads, writes, accum)
        inst = fn(self.engs[eng])
        tok = self.streams[eng].bump()
        inst.then_inc(tok[0], 1)
        self._commit(tok, reads, writes)
        return tok

    def dma(self, q, out, in_, reads=(), writes=()):
        pool = self.dma_pool[q]
        st = pool[self.dma_rr[q] % len(pool)]
        self.dma_rr[q] += 1
        self._wait(q, st.cur())
        self._deps(q, reads, writes, False)
        inst = self.engs[q].dma_start(out=out, in_=in_)
        tok = st.bump()
        inst.then_inc(tok[0], 16)
        self._commit(tok, reads, writes)
        return tok


def _cplx_pair(M):
    return np.concatenate([M.real, M.imag], 1), np.concatenate([-M.imag, M.real], 1)


def make_tables(T, kind):
    NT = T // 128
    NS = 2 * NT
    H = NT // 2
    isS = (kind == 'S')
    tb = {}
    tb['ident_b'] = np.eye(128).astype(NPBF)
    tb['ident_f'] = np.eye(128, dtype=np.float32)
    cc = np.arange(64)
    ang = 2 * np.pi * np.outer(cc, cc) / 64
    bdc = np.zeros((128, 128)); bds = np.zeros((128, 128))
    for g in range(2):
        bdc[g * 64:(g + 1) * 64, g * 64:(g + 1) * 64] = np.cos(ang)
        bds[g * 64:(g + 1) * 64, g * 64:(g + 1) * 64] = -np.sin(ang)
    tb['bdcs'] = np.stack([bdc, bds], 1).astype(NPBF)
    n1 = np.arange(NT)
    M1 = np.zeros((NT, NS), np.complex128)
    M2 = np.zeros((NS, 128, 128), np.complex128)
    n2 = np.arange(128)[:, None]
    k2 = np.arange(128)[None, :]
    if not isS:
        L = T
        for j in range(NT):
            M1[:, j] = np.exp(-2j * np.pi * n1 * j / NT)
            M2[j] = np.exp(-2j * np.pi * n2 * (j + NT * k2) / T)
    else:
        L = T // 2
        for s in range(2):
            for j in range(NT):
                M1[s * H:(s + 1) * H, s * NT + j] = np.exp(-2j * np.pi * np.arange(H) * j / H)
                m = np.exp(-2j * np.pi * n2 * (NT * (k2 % 64) + j) / L) * ((k2 // 64) == s)
                M2[s * NT + j] = m
    M2 = M2 / math.sqrt(L * 64)
    a, b = _cplx_pair(M1)
    tb['fn_m1'] = np.stack([a, b], 1).astype(NPBF)
    fm2 = np.zeros((NT, 128, 2, 2, 128), np.float64)
    for s in range(2):
        for j in range(NT):
            fm2[j, :, s, 0] = M2[s * NT + j].real
            fm2[j, :, s, 1] = -M2[s * NT + j].imag
    tb['fn_m2'] = fm2.astype(NPBF)
    HF1 = np.zeros((NS, NS), np.complex128)
    H2 = np.zeros((NS, 128, 128), np.complex128)
    HZ = np.zeros((128, NS, NT), np.complex128)
    if not isS:
        N = 2 * T
        for j in range(NS):
            HF1[:, j] = np.exp(-2j * np.pi * np.arange(NS) * j / NS)
            H2[j] = np.exp(-2j * np.pi * n2 * (j + NS * k2) / N)
        for q in range(128):
            HZ[q] = np.exp(2j * np.pi * np.outer(np.arange(NS), 128 * np.arange(NT) + q) / N) / N
    else:
        N = T
        for s in range(2):
            for j in range(NT):
                HF1[s * NT:(s + 1) * NT, s * NT + j] = np.exp(-2j * np.pi * np.arange(NT) * j / NT)
                H2[s * NT + j] = np.exp(-2j * np.pi * n2 * (j + NT * k2) / N)
        for q in range(128):
            for s in range(2):
                HZ[q, s * NT:(s + 1) * NT, s * H:(s + 1) * H] = \
                    np.exp(2j * np.pi * np.outer(np.arange(NT), 128 * np.arange(H) + q) / N) / N
    if not isS:
        H1 = HF1[:NT]
    else:
        H1 = np.concatenate([HF1[0:H], HF1[NT:NT + H]], 0)
    if not isS:
        act = list(range(NS // 2 + 1)) + [None]
        wts = [1.0 if j in (0, NS // 2) else 2.0 for j in range(NS // 2 + 1)] + [0.0]
    else:
        act = [s * NT + j for s in range(2) for j in range(NT // 2 + 1)]
        wts = [1.0 if j in (0, NT // 2) else 2.0 for s in range(2) for j in range(NT // 2 + 1)]
    def sel(M, axis):
        parts = []
        for a in act:
            if a is None:
                parts.append(np.zeros_like(np.take(M, [0], axis=axis)))
            else:
                parts.append(np.take(M, [a], axis=axis))
        return np.concatenate(parts, axis=axis)
    H1 = sel(H1, 1); HF1 = sel(HF1, 1); H2 = sel(H2, 0)
    HZ = sel(HZ, 1) * np.asarray(wts)[None, :, None]
    tb['hy_h1'] = np.concatenate([H1.real, H1.imag], 1).astype(NPBF)
    tb['hy_hf1'] = np.concatenate([HF1.real, HF1.imag], 1).astype(NPBF)
    tb['hy_h2'] = np.stack([H2.real, H2.imag, -H2.imag], 2).astype(NPBF)
    Fi = np.exp(2j * np.pi * np.outer(np.arange(128), np.arange(128)) / 128)
    a, b = _cplx_pair(Fi)
    tb['hy_i1'] = np.stack([a, b], 1).astype(NPBF)
    tb['hy_z'] = np.stack([HZ.real, -HZ.imag], 2).transpose(1, 0, 2, 3).astype(NPBF).copy()
    Lf = L
    mpos = np.arange(2 * T)
    mloc = mpos % (2 * Lf)
    lag = np.where(mloc < Lf, mloc, 2 * Lf - mloc)
    lag = np.where(mloc == Lf, 0, lag)
    mf = (mloc < Lf).astype(np.float32)
    mb = (mloc > Lf).astype(np.float32)
    tl = np.linspace(0.0, 1.0, Lf, dtype=np.float32)
    wl = (2.0 * np.float32(math.pi) * np.arange(Lf, dtype=np.float32) / np.float32(Lf)).astype(np.float32)
    fb = np.linspace(1e-4, 15, 16, dtype=np.float32)[None, :]
    feat = np.concatenate([tl[:, None], np.cos(fb * wl[:, None]), -np.sin(fb * wl[:, None])], -1).astype(np.float32)
    tb['flt_feat'] = np.ascontiguousarray(feat[lag].T).astype(np.float32)
    tb['flt_msk'] = np.stack([mf, mb, -tl[lag]], 0).astype(np.float32)
    deltas = np.abs(np.linspace(math.log(1e-2) / 0.3, math.log(1e-2) / 1.5, 256, dtype=np.float32))
    tb['flt_delta'] = np.ascontiguousarray(deltas.reshape(2, 128).T).astype(np.float32)
    sc = np.zeros((128, 4), np.float32)
    sc[:, 0] = 0.0 if isS else 1.0
    sc[:, 1] = 0.5 if isS else 1.0
    tb['scal'] = sc
    NB = T // 512
    hal = np.ones((NB, 2), np.float32)
    hal[0, 0] = 0.0; hal[NB - 1, 1] = 0.0
    if isS:
        hal[NB // 2, 0] = 0.0; hal[NB // 2 - 1, 1] = 0.0
    tb['hal'] = np.broadcast_to(hal.reshape(1, NB * 2), (128, NB * 2)).astype(np.float32).copy()
    pos = (np.arange(T) % L).astype(np.float32)
    inv = (10000.0 ** (-np.arange(32, dtype=np.float32) / 32)).astype(np.float32)
    angr = pos[:, None] * inv[None, :]
    tb['rot'] = np.stack([np.cos(angr), np.sin(angr)], 1).astype(np.float32)
    lg = np.log(1.0 - 2.0 ** (-5.0 - np.arange(4)))
    i = np.arange(128)
    ret_e = np.zeros((128, 2, 6, 128), np.float64)
    ret_tok = np.zeros((128, 2, 256), np.float64)
    ret_dec = np.zeros((128, 2, 2), np.float64)
    for c in range(2):
        for p in range(128):
            h = 2 * c + p // 64
            bf = (i + 1) * lg[h]; bb = (128 - i) * lg[h]
            ret_e[p, c, 0] = np.exp(bf - bf[64]) / 8.0
            ret_e[p, c, 1] = np.exp(bf[64] - bf)
            ret_e[p, c, 2] = np.exp(bb - bb[64]) / 8.0
            ret_e[p, c, 3] = np.exp(bb[64] - bb)
            ret_e[p, c, 4] = np.exp(bf) / 8.0
            ret_e[p, c, 5] = np.exp(bb) / 8.0
            ret_dec[p, c, :] = np.exp(128 * lg[h])
    for h in range(4):
        ret_tok[:, 0, h * 64:(h + 1) * 64] = np.exp((127 - i) * lg[h])[:, None]
        ret_tok[:, 1, h * 64:(h + 1) * 64] = np.exp(i * lg[h])[:, None]
    tb['ret_e'] = ret_e.astype(np.float32)
    tb['ret_tok'] = ret_tok.astype(np.float32)
    tb['ret_dec'] = ret_dec.astype(np.float32)
    mh_ret = np.zeros((128, 2), np.float32); mh_gla = np.zeros((128, 4), np.float32)
    bd_ret = np.zeros((128, 2, 256), np.float32); bd_gla = np.zeros((128, 1, 256), np.float32)
    for p in range(128):
        mh_ret[p, p // 64] = 1.0; mh_gla[p, p // 32] = 1.0
        for c in range(2):
            h = 2 * c + p // 64
            bd_ret[p, c, h * 64:(h + 1) * 64] = 1.0
        h = p // 32
        bd_gla[p, 0, h * 64:(h + 1) * 64] = 1.0
    tb['mh_ret'] = mh_ret; tb['mh_gla'] = mh_gla; tb['bd_ret'] = bd_ret; tb['bd_gla'] = bd_gla
    jj = np.arange(128)[:, None]; ii = np.arange(128)[None, :]
    tri = np.zeros((128, 6, 128), np.float32)
    tri[:, 0] = (jj <= ii)
    tri[:, 1] = (jj > ii)
    tri[:, 2] = -(jj <= ii).astype(np.float32) / 16.0
    tri[:, 3] = -(jj >= ii).astype(np.float32) / 16.0
    tri[:, 4] = -(jj > ii).astype(np.float32) / 16.0
    tri[:, 5] = -(jj < ii).astype(np.float32) / 16.0
    tb['tri'] = tri
    return tb


OFF_FN, OFF_QR, OFF_HY, OFF_QG, OFF_GATES = 0, 256, 1280, 2048, 2848
NTM = 1824


def build_program(T, debug=()):
    NT, NS, NB, H = T // 128, T // 64, T // 512, T // 256
    NSA = NT + 2
    nc = bass.Bass("TRN2", target_bir_lowering=False)
    I = {}

    def inp(name, shape, dt=F32):
        I[name] = nc.dram_tensor(name, list(shape), dt, kind="ExternalInput").ap()
        return I[name]

    def scratch(name, shape, dt=F32):
        kind = "ExternalOutput" if name in debug else "Internal"
        return nc.dram_tensor(name, list(shape), dt, kind=kind).ap()

    x_in = inp("x", [T, D]); inp("cT", [128, 8, 2])
    inp("ada_w", [DEPTH, D, 6 * D]); inp("ada_b_col", [DEPTH, 128, 48]); inp("ada_b", [DEPTH, 6 * D])
    inp("normw_col", [DEPTH, 128, 4, 8]); inp("norm_post_mix", [DEPTH, D]); inp("norm_post_ffn", [DEPTH, D])
    inp("w_in", [DEPTH, D, 6944])
    inp("hy_cw", [DEPTH, 128, 6, 3]); inp("hy_cb", [DEPTH, 128, 6])
    inp("flt_w1", [DEPTH, 33, 64]); inp("flt_c1", [DEPTH, 64, 2]); inp("flt_w2", [DEPTH, 64, 64]); inp("flt_c2", [DEPTH, 64, 2])
    inp("flt_w3", [DEPTH, 64, 1024]); inp("flt_b3", [DEPTH, 1024]); inp("hy_skip", [DEPTH, 2, 256])
    inp("gla_wd", [DEPTH, 33, 256]); inp("ret_gn", [DEPTH, 256]); inp("gla_gn", [DEPTH, 256])
    inp("w_branch", [DEPTH, 1024, D]); inp("w_out", [DEPTH, D, D])
    inp("ffn_up", [DEPTH, D, 2 * DFF]); inp("ffn_cw", [DEPTH, 128, 44, 3]); inp("ffn_cb", [DEPTH, 128, 44])
    inp("ffn_down", [DEPTH, DFF, D])
    for nm, shp, dt in (("ident_b", [128, 128], BF16), ("ident_f", [128, 128], F32), ("bdcs", [128, 2, 128], BF16),
                        ("fn_m1", [NT, 2, 2 * NS], BF16), ("fn_m2", [NT, 128, 2, 2, 128], BF16),
                        ("hy_h1", [NT, 2 * NSA], BF16), ("hy_hf1", [NS, 2 * NSA], BF16), ("hy_h2", [NSA, 128, 3, 128], BF16),
                        ("hy_i1", [128, 2, 256], BF16), ("hy_z", [NSA, 128, 2, NT], BF16),
                        ("flt_feat", [33, 2 * T], F32), ("flt_msk", [3, 2 * T], F32), ("flt_delta", [128, 2], F32),
                        ("scal", [128, 4], F32), ("hal", [128, NB * 2], F32), ("rot", [T, 2, 32], F32),
                        ("ret_e", [128, 2, 6, 128], F32), ("ret_tok", [128, 2, 256], F32), ("ret_dec", [128, 2, 2], F32),
                        ("mh_ret", [128, 2], F32), ("mh_gla", [128, 4], F32), ("bd_ret", [128, 2, 256], F32),
                        ("bd_gla", [128, 1, 256], F32), ("tri", [128, 6, 128], F32)):
        inp(nm, shp, dt)
    y_out = nc.dram_tensor("y", [T, D], F32, kind="ExternalOutput").ap()
    x1d = scratch("x1d", [T, D]); xmid = scratch("xmid", [T, D])
    projT = scratch("projT", [T, NTM]); zF = scratch("zF", [2, 256, T], BF16); uhF = scratch("uhF", [768, T])
    brF = scratch("brF", [1024, T], BF16); modrow = scratch("modrow", [2, 2048]); hF = scratch("hF", [D, T + 2], BF16); gFF = scratch("gFF", [DFF, T], BF16)
    gF = scratch("gF", [512, 2 * T], BF16); rnD = scratch("rnD", [512]); Gd = scratch("Gd", [16, 128, NSA, 64], BF16)

    es = ExitStack()
    with es:
        es.enter_context(nc.allow_non_contiguous_dma(reason="strided scratch layouts"))
        P = Prog(nc, es)
        dB = {k: Buf(k) for k in ("x1d", "xmid", "projT", "zF", "uhF", "brF", "modrow", "gF", "rnD", "Gd", "y", "in", "hF", "gFF")}
        IN = dB["in"]

        def ld(dst_buf, dst_ap, src_ap, src=IN):
            return P.dma('sp', dst_ap, src_ap, reads=[src], writes=[dst_buf])

        def stq(dst_ap, src_buf, src_ap, dst):
            return P.dma('pool', dst_ap, src_ap, reads=[src_buf], writes=[dst])

        def dve(fn, r, w):
            return P.op('dve', fn, r, w)

        def act(fn, r, w):
            return P.op('act', fn, r, w)

        def gps(fn, r, w):
            return P.op('pool' if USE_POOL else 'dve', fn, r, w)

        def pe(fn, r, w):
            return P.op('pe', fn, r, w, accum=True)

        identb = P.sb("identb", [128, 128], BF16); identf = P.sb("identf", [128, 128], F32)
        scal = P.sb("scal", [128, 4]); hal = P.sb("hal", [128, NB * 2]); epsb = P.sb("epsb", [128, 1])
        modc = P.sb("modc", [128, 48, 2]); Am = P.sb("Am", [128, 8, 2]); Af = P.sb("Af", [128, 8, 2])
        gtb = P.sb("gtb", [128, 2, 2, D])
        ld(identb, identb[:], I["ident_b"][:, :]); ld(identf, identf[:], I["ident_f"][:, :])
        ld(scal, scal[:], I["scal"][:, :]); ld(hal, hal[:], I["hal"][:, :])
        dve(lambda e: e.memset(epsb[:], EPS), [], [epsb])

        def phase_mod(l):
            P.push()
            cT = P.sb("cT", [128, 8, 2]); scT = P.sb("scT", [128, 8, 2])
            ld(cT, cT[:], I["cT"][:, :, :])
            act(lambda e: e.activation(out=scT[:], in_=cT[:], func=AF.Silu), [cT], [scT])
            psc = P.ps("psc", [128, 96]); psr = P.ps("psr", [2, 2048]); macc = P.sb("macc", [128, 96])
            wts = [P.sb(f"adaw{i}", [128, 6 * D]) for i in range(2)]
            rowcols = (2048, 2560, 5120, 5632)
            for kc in range(8):
                wt = wts[kc % 2]
                ld(wt, wt[:], I["ada_w"][l, kc * 128:(kc + 1) * 128, :])
                for q in range(48):
                    pe(lambda e, q=q: e.matmul(psc[:, 2 * q:2 * q + 2], lhsT=wt[:, q * 128:(q + 1) * 128], rhs=scT[:, kc, :],
                                               start=True, stop=True), [wt, scT], [psc])
                if kc == 0:
                    dve(lambda e: e.tensor_copy(out=macc[:], in_=psc[:]), [psc], [macc])
                else:
                    dve(lambda e: e.tensor_tensor(out=macc[:], in0=macc[:], in1=psc[:], op=ALU.add), [psc, macc], [macc])
                for bi, c0 in enumerate(rowcols):
                    pe(lambda e, bi=bi, c0=c0: e.matmul(psr[:, bi * 512:(bi + 1) * 512], lhsT=scT[:, kc, :], rhs=wt[:, c0:c0 + 512],
                                                        start=(kc == 0), stop=(kc == 7)), [wt, scT], [psr])
            abc = P.sb("abc", [128, 48]); nwc = P.sb("nwc", [128, 4, 8])
            ld(abc, abc[:], I["ada_b_col"][l]); ld(nwc, nwc[:], I["normw_col"][l])
            dve(lambda e: e.tensor_tensor(out=modc[:], in0=macc[:].rearrange("p (q s) -> p q s", s=2),
                                          in1=abc[:].unsqueeze(2).broadcast_to([128, 48, 2]), op=ALU.add), [macc, abc], [modc])
            dve(lambda e: e.scalar_tensor_tensor(out=Am[:], in0=modc[:, 8:16, :], scalar=1.0,
                                                 in1=nwc[:, 0, :].unsqueeze(2).broadcast_to([128, 8, 2]), op0=ALU.add, op1=ALU.mult),
                [modc, nwc], [Am])
            dve(lambda e: e.scalar_tensor_tensor(out=Af[:], in0=modc[:, 32:40, :], scalar=1.0,
                                                 in1=nwc[:, 2, :].unsqueeze(2).broadcast_to([128, 8, 2]), op0=ALU.add, op1=ALU.mult),
                [modc, nwc], [Af])
            abr = P.sb("abr", [2, 2048]); nwr = P.sb("nwr", [2, 2048]); gr = P.sb("gr", [2, 2048])
            ld(abr, abr[:, 0:1024], I["ada_b"][l, 2048:3072].partition_broadcast(2))
            ld(abr, abr[:, 1024:2048], I["ada_b"][l, 5120:6144].partition_broadcast(2))
            ld(nwr, nwr[:, 0:1024], I["norm_post_mix"][l].partition_broadcast(2))
            ld(nwr, nwr[:, 1024:2048], I["norm_post_ffn"][l].partition_broadcast(2))
            dve(lambda e: e.tensor_tensor(out=gr[:], in0=psr[:], in1=abr[:], op=ALU.add), [psr, abr], [gr])
            dve(lambda e: e.tensor_tensor(out=gr[:], in0=gr[:], in1=nwr[:], op=ALU.mult), [gr, nwr], [gr])
            stq(modrow[:, :], gr, gr[:], dB["modrow"])
            for sg in range(2):
                ld(gtb, gtb[:, sg, :, :].rearrange("p a d -> p (a d)"), modrow[sg].partition_broadcast(128), src=dB["modrow"])
            P.pop()

        def phase_norm(src_ap2d, src_buf, A, bq0):
            P.push()
            zt = P.sb("zt", [128, 8, 1], BF16)
            dve(lambda e: e.memset(zt[:], 0.0), [], [zt])
            hFv = hF.rearrange("(c p) t -> p c t", p=128)
            stq(hFv[:, :, 0:1], zt, zt[:], dB["hF"]); stq(hFv[:, :, T + 1:T + 2], zt, zt[:], dB["hF"])
            xts = [P.sb(f"xt{i}", [128, D]) for i in range(2)]
            sq = P.sb("sq", [128, D]); ss = P.sb("ss", [128, 1]); rs = P.sb("rs", [128, 1])
            xn = P.sb("xn", [128, D], BF16); ptr = P.ps("ptr", [128, 8, 128], BF16); tmp = P.sb("tmpT", [128, 8, 128])
            hts = [P.sb(f"ht{i}", [128, 8, 512], BF16) for i in range(2)]
            xns = [P.sb(f"xnN{i}", [128, D], BF16) for i in range(2)]

            def gen(it):
                xt, ht, xn = xts[it % 2], hts[(it // 4) % 2], xns[it % 2]
                q4 = it % 4
                seg = 0 if it < NT // 2 else 1
                ld(xt, xt[:], src_ap2d[it * 128:(it + 1) * 128, :], src=src_buf)
                act(lambda e: e.activation(out=sq[:], in_=xt[:], func=AF.Square, accum_out=ss[:, 0:1]), [xt], [sq, ss])
                act(lambda e: e.activation(out=rs[:], in_=ss[:], func=AF.Sqrt, scale=1.0 / D, bias=epsb[:, 0:1]), [ss, epsb], [rs])
                yield
                dve(lambda e: e.reciprocal(out=rs[:], in_=rs[:]), [rs], [rs])
                dve(lambda e: e.tensor_scalar(out=xn[:], in0=xt[:], scalar1=rs[:, 0:1], scalar2=None, op0=ALU.mult), [xt, rs], [xn])
                yield 'prev_done'
                for c8 in range(8):
                    pe(lambda e, c8=c8: e.transpose(ptr[:, c8, :], xn[:, c8 * 128:(c8 + 1) * 128], identb[:]), [xn, identb], [ptr])
                yield
                dve(lambda e: e.tensor_tensor(out=tmp[:], in0=ptr[:], in1=A[:, :, seg:seg + 1].broadcast_to([128, 8, 128]), op=ALU.mult),
                    [ptr, A], [tmp])
                dve(lambda e: e.tensor_tensor(out=ht[:, :, q4 * 128:(q4 + 1) * 128], in0=tmp[:], in1=modc[:, bq0:bq0 + 8, seg:seg + 1].broadcast_to([128, 8, 128]), op=ALU.add),
                    [tmp, modc], [ht])
                if q4 == 3:
                    stq(hFv[:, :, 1 + (it - 3) * 128:1 + (it + 1) * 128], ht, ht[:], dB["hF"])

            run_pipelined(gen, range(NT))
            P.pop()

        def load_w_bf16(dst, dst_ap_fn, src_ap_fn, nk, width, stg):
            for k in range(nk):
                st = stg[k % 2]
                ld(st, st[:, 0:width], src_ap_fn(k))
                if k % 2 == 0:
                    dve(lambda e, k=k, st=st: e.tensor_copy(out=dst_ap_fn(k), in_=st[:, 0:width]), [st], [dst])
                else:
                    act(lambda e, k=k, st=st: e.activation(out=dst_ap_fn(k), in_=st[:, 0:width], func=AF.Copy), [st], [dst])

        def load_window(hw, b):
            hFv = hF.rearrange("(c p) t -> p c t", p=128)
            ld(hw, hw[:], hFv[:, :, b * 512:b * 512 + 514], src=dB["hF"])
            for side, col in ((0, 0), (1, 513)):
                dve(lambda e, side=side, col=col: e.tensor_tensor(out=hw[:, :, col:col + 1], in0=hw[:, :, col:col + 1],
                                                                  in1=hal[:, 2 * b + side:2 * b + side + 1].unsqueeze(1).broadcast_to([128, 8, 1]),
                                                                  op=ALU.mult), [hw, hal], [hw])

        def conv3_fm(out_ap, pm, ph, cw, cb, ci, rbufs, wbuf):
            act(lambda e: e.activation(out=out_ap, in_=pm[:, 0:512], func=AF.Identity, scale=cw[:, ci, 1:2], bias=cb[:, ci:ci + 1]), rbufs, [wbuf])
            dve(lambda e: e.scalar_tensor_tensor(out=out_ap[:, 1:512], in0=pm[:, 0:511], scalar=cw[:, ci, 0:1], in1=out_ap[:, 1:512],
                                                 op0=ALU.mult, op1=ALU.add), rbufs + [wbuf], [wbuf])
            dve(lambda e: e.scalar_tensor_tensor(out=out_ap[:, 0:511], in0=pm[:, 1:512], scalar=cw[:, ci, 2:3], in1=out_ap[:, 0:511],
                                                 op0=ALU.mult, op1=ALU.add), rbufs + [wbuf], [wbuf])
            dve(lambda e: e.scalar_tensor_tensor(out=out_ap[:, 0:1], in0=ph[:, 0:1], scalar=cw[:, ci, 0:1], in1=out_ap[:, 0:1],
                                                 op0=ALU.mult, op1=ALU.add), rbufs + [wbuf], [wbuf])
            dve(lambda e: e.scalar_tensor_tensor(out=out_ap[:, 511:512], in0=ph[:, 1:2], scalar=cw[:, ci, 2:3], in1=out_ap[:, 511:512],
                                                 op0=ALU.mult, op1=ALU.add), rbufs + [wbuf], [wbuf])

        def phase_A(l):
            P.push()
            wA = P.sb("wA", [128, 8, OFF_GATES], BF16)
            stg = [P.sb(f"stgA{i}", [128, OFF_GATES]) for i in range(2)]
            load_w_bf16(wA, lambda k: wA[:, k, :], lambda k: I["w_in"][l, k * 128:(k + 1) * 128, 0:OFF_GATES], 8, OFF_GATES, stg)
            bdcs = P.sb("bdcs", [128, 2, 128], BF16); ld(bdcs, bdcs[:], I["bdcs"][:, :, :])
            cw = P.sb("hcw", [128, 6, 3]); cb = P.sb("hcb", [128, 6])
            ld(cw, cw[:], I["hy_cw"][l]); ld(cb, cb[:], I["hy_cb"][l])
            hws = [P.sb(f"hw{i}", [128, 8, 514], BF16) for i in range(2)]
            pms = [P.ps(f"pmA{i}", [128, 512]) for i in range(2)]
            ph = P.ps("phA", [128, 2])
            pjs = [P.sb(f"pj{i}", [128, NTM]) for i in range(2)]; uT = P.sb("uT", [128, 2, 512], BF16)
            zts = [P.sb(f"ztA{i}", [128, 512], BF16) for i in range(2)]
            cvs = [P.sb(f"cvA{i}", [128, 512]) for i in range(2)]
            tmcols = ((256, 512), (768, 512), (2048, 512), (2560, 288))
            zFv = zF
            npm = 0
            load_window(hws[0], 0)
            for b in range(NB):
                hw = hws[b % 2]
                if b + 1 < NB:
                    load_window(hws[(b + 1) % 2], b + 1)
                t0 = b * 512
                for s in range(4):
                    o = 0
                    pj = pjs[s % 2]
                    for (c0, wd) in tmcols:
                        pm = pms[npm % 2]; npm += 1
                        for kc in range(8):
                            pe(lambda e, kc=kc, pm=pm, c0=c0, wd=wd: e.matmul(pm[:, 0:wd], lhsT=hw[:, kc, 1 + s * 128:1 + (s + 1) * 128],
                                                                                rhs=wA[:, kc, c0:c0 + wd], start=(kc == 0), stop=(kc == 7)),
                               [hw, wA], [pm])
                        if (npm % 2) == 0:
                            dve(lambda e, pm=pm, o=o, wd=wd, pj=pj: e.tensor_copy(out=pj[:, o:o + wd], in_=pm[:, 0:wd]), [pm], [pj])
                        else:
                            act(lambda e, pm=pm, o=o, wd=wd, pj=pj: e.activation(out=pj[:, o:o + wd], in_=pm[:, 0:wd], func=AF.Copy), [pm], [pj])
                        o += wd
                        yield
                    stq(projT[t0 + s * 128:t0 + (s + 1) * 128, :], pj, pj[:], dB["projT"])
                for ch in range(2):
                    pm = pms[npm % 2]; npm += 1
                    for kc in range(8):
                        pe(lambda e, kc=kc, pm=pm, ch=ch: e.matmul(pm[:], lhsT=wA[:, kc, ch * 128:(ch + 1) * 128], rhs=hw[:, kc, 1:513],
                                                                     start=(kc == 0), stop=(kc == 7)), [hw, wA], [pm])
                    act(lambda e, pm=pm, ch=ch: e.activation(out=uT[:, ch, :], in_=pm[:], func=AF.Copy), [pm], [uT])
                    yield
                for ri in range(2):
                    for ch in range(2):
                        pm = pms[npm % 2]; zt = zts[npm % 2]; npm += 1
                        pe(lambda e, pm=pm, ri=ri, ch=ch: e.matmul(pm[:], lhsT=bdcs[:, ri, :], rhs=uT[:, ch, :], start=True, stop=True), [bdcs, uT], [pm])
                        act(lambda e, pm=pm, zt=zt: e.activation(out=zt[:], in_=pm[:], func=AF.Copy), [pm], [zt])
                        stq(zFv[ri, ch * 128:(ch + 1) * 128, t0:t0 + 512], zt, zt[:], dB["zF"])
                        yield
                for ch in range(6):
                    pm = pms[npm % 2]; cv = cvs[npm % 2]; npm += 1
                    c0 = OFF_HY + ch * 128
                    for kc in range(8):
                        pe(lambda e, kc=kc, pm=pm, c0=c0: e.matmul(pm[:], lhsT=wA[:, kc, c0:c0 + 128], rhs=hw[:, kc, 1:513],
                                                                     start=(kc == 0), stop=(kc == 7)), [hw, wA], [pm])
                    for kc in range(8):
                        pe(lambda e, kc=kc, c0=c0: e.matmul(ph[:], lhsT=wA[:, kc, c0:c0 + 128], rhs=hw[:, kc, 0:514:513],
                                                              start=(kc == 0), stop=(kc == 7)), [hw, wA], [ph])
                    conv3_fm(cv[:], pm, ph, cw, cb, ch, [pm, ph, cw, cb], cv)
                    stq(uhF[ch * 128:(ch + 1) * 128, t0:t0 + 512], cv, cv[:], dB["uhF"])
                    yield
            yield 'finished'
            P.pop()

        def rms_residual(py, xt, sq, ss, rs, tmpo, outt, seg, which, dst_ap, dst_buf):
            act(lambda e: e.activation(out=sq[:], in_=py[:], func=AF.Square, accum_out=ss[:, 0:1]), [py], [sq, ss])
            act(lambda e: e.activation(out=rs[:], in_=ss[:], func=AF.Sqrt, scale=1.0 / D, bias=epsb[:, 0:1]), [ss, epsb], [rs])
            dve(lambda e: e.reciprocal(out=rs[:], in_=rs[:]), [rs], [rs])
            dve(lambda e: e.scalar_tensor_tensor(out=tmpo[:], in0=py[:], scalar=rs[:, 0:1], in1=gtb[:, seg, which, :], op0=ALU.mult, op1=ALU.mult),
                [py, rs, gtb], [tmpo])
            dve(lambda e: e.tensor_tensor(out=outt[:], in0=tmpo[:], in1=xt[:], op=ALU.add), [tmpo, xt], [outt])
            stq(dst_ap, outt, outt[:], dst_buf)

        def phase_C(l, src_ap2d, src_buf):
            P.push()
            wbr = P.sb("wbr", [128, 8, D], BF16); wout = P.sb("wout", [128, 8, D], BF16); wg = P.sb("wg", [128, 8, 4 * D], BF16)
            stg = [P.sb(f"stgC{i}", [128, 4 * D]) for i in range(2)]
            load_w_bf16(wbr, lambda k: wbr[:, k, :], lambda k: I["w_branch"][l, k * 128:(k + 1) * 128, :], 8, D, stg)
            load_w_bf16(wout, lambda k: wout[:, k, :], lambda k: I["w_out"][l, k * 128:(k + 1) * 128, :], 8, D, stg)
            load_w_bf16(wg, lambda k: wg[:, k, :], lambda k: I["w_in"][l, k * 128:(k + 1) * 128, OFF_GATES:OFF_GATES + 4 * D], 8, 4 * D, stg)
            xts = [P.sb(f"xtC{i}", [128, D]) for i in range(2)]
            hts = [P.sb(f"htC{i}", [128, 8, 128], BF16) for i in range(2)]
            brs = [P.sb(f"brC{i}", [128, 8, 128], BF16) for i in range(2)]
            pgs = [P.ps(f"pg{i}", [128, 512]) for i in range(2)]; pbs = [P.ps(f"pb{i}", [128, 512]) for i in range(2)]
            py = P.ps("py", [128, D]); ptr = P.ps("ptrC", [128, 8, 128], BF16)
            sigs = [P.sb(f"sig{i}", [128, 512]) for i in range(2)]; tms = [P.sb(f"tmc{i}", [128, 512]) for i in range(2)]
            merged = P.sb("merged", [128, D]); tmpm = P.sb("tmpm", [128, D]); mb = P.sb("mb", [128, D], BF16)
            mT = P.sb("mT", [128, 8, 128], BF16); sq = P.sb("sqC", [128, D]); ss = P.sb("ssC", [128, 1]); rs = P.sb("rsC", [128, 1])
            outt = P.sb("outC", [128, D])
            hFv = hF.rearrange("(c p) t -> p c t", p=128); brv = brF.rearrange("(k p) t -> p k t", p=128)
            mergeds = [merged, P.sb("merged1", [128, D])]

            def gen(it):
                merged = mergeds[it % 2]
                xt, ht, brt = xts[it % 2], hts[it % 2], brs[it % 2]
                seg = 0 if it < NT // 2 else 1
                ld(xt, xt[:], src_ap2d[it * 128:(it + 1) * 128, :], src=src_buf)
                ld(ht, ht[:], hFv[:, :, 1 + it * 128:1 + (it + 1) * 128], src=dB["hF"])
                ld(brt, brt[:], brv[:, :, it * 128:(it + 1) * 128], src=dB["brF"])
                for br in range(4):
                    for cb in range(2):
                        u = br * 2 + cb
                        pgu, pbu, sgu, tmu = pgs[u % 2], pbs[u % 2], sigs[u % 2], tms[u % 2]
                        for kc in range(8):
                            pe(lambda e, kc=kc, cb=cb, pgu=pgu, br=br: e.matmul(pgu[:], lhsT=ht[:, kc, :],
                                                                               rhs=wg[:, kc, br * D + cb * 512:br * D + (cb + 1) * 512], start=(kc == 0), stop=(kc == 7)),
                               [ht, wg], [pgu])
                        for k2 in range(2):
                            pe(lambda e, k2=k2, cb=cb, pbu=pbu, br=br: e.matmul(pbu[:], lhsT=brt[:, br * 2 + k2, :],
                                                                               rhs=wbr[:, br * 2 + k2, cb * 512:(cb + 1) * 512], start=(k2 == 0), stop=(k2 == 1)),
                               [brt, wbr], [pbu])
                        act(lambda e, pgu=pgu, sgu=sgu: e.activation(out=sgu[:], in_=pgu[:], func=AF.Sigmoid), [pgu], [sgu])
                        mslice = merged[:, cb * 512:(cb + 1) * 512]
                        if br == 0:
                            dve(lambda e, sgu=sgu, pbu=pbu, mslice=mslice: e.tensor_tensor(out=mslice, in0=sgu[:], in1=pbu[:], op=ALU.mult), [sgu, pbu], [merged])
                        else:
                            dve(lambda e, sgu=sgu, pbu=pbu, tmu=tmu: e.tensor_tensor(out=tmu[:], in0=sgu[:], in1=pbu[:], op=ALU.mult), [sgu, pbu], [tmu])
                            dve(lambda e, tmu=tmu, mslice=mslice: e.tensor_tensor(out=mslice, in0=mslice, in1=tmu[:], op=ALU.add), [merged, tmu], [merged])
                        yield
                yield 'prev_done'
                act(lambda e: e.activation(out=mb[:], in_=merged[:], func=AF.Copy), [merged], [mb])
                yield
                for c8 in range(8):
                    pe(lambda e, c8=c8: e.transpose(ptr[:, c8, :], mb[:, c8 * 128:(c8 + 1) * 128], identb[:]), [mb, identb], [ptr])
                yield
                dve(lambda e: e.tensor_copy(out=mT[:], in_=ptr[:]), [ptr], [mT])
                yield
                for cb in range(2):
                    for kc in range(8):
                        pe(lambda e, kc=kc, cb=cb: e.matmul(py[:, cb * 512:(cb + 1) * 512], lhsT=mT[:, kc, :], rhs=wout[:, kc, cb * 512:(cb + 1) * 512],
                                                           start=(kc == 0), stop=(kc == 7)), [mT, wout], [py])
                rms_residual(py, xt, sq, ss, rs, tmpm, outt, seg, 0, xmid[it * 128:(it + 1) * 128, :], dB["xmid"])
            run_pipelined(gen, range(NT))
            P.pop()

        def phase_D(l, dst_ap2d, dst_buf):
            P.push()
            wup = P.sb("wup", [128, 8, 2 * DFF], BF16)
            stg = [P.sb(f"stgD{i}", [128, 1408]) for i in range(2)]
            for part in range(4):
                c0 = part * 1408
                load_w_bf16(wup, lambda k, c0=c0: wup[:, k, c0:c0 + 1408], lambda k, c0=c0: I["ffn_up"][l, k * 128:(k + 1) * 128, c0:c0 + 1408], 8, 1408, stg)
            cw = P.sb("fcw", [128, 44, 3]); cb_ = P.sb("fcb", [128, 44])
            ld(cw, cw[:], I["ffn_cw"][l]); ld(cb_, cb_[:], I["ffn_cb"][l])
            hws = [P.sb(f"hwD{i}", [128, 8, 514], BF16) for i in range(2)]
            pms = [P.ps(f"pmD{i}", [128, 512]) for i in range(4)]; phs = [P.ps(f"phD{i}", [128, 2]) for i in range(4)]
            cvs = [P.sb(f"cvD{i}", [128, 512]) for i in range(4)]; gls = [P.sb(f"glD{i}", [128, 512]) for i in range(2)]
            gts = [P.sb(f"gtD{i}", [128, 512], BF16) for i in range(2)]
            load_window(hws[0], 0)
            for b in range(NB):
                hw = hws[b % 2]
                if b + 1 < NB:
                    load_window(hws[(b + 1) % 2], b + 1)
                for pc in range(22):
                    for wi in range(2):
                        ci = pc + 22 * wi
                        bi = 2 * (pc % 2) + wi
                        pm, ph, cv = pms[bi], phs[bi], cvs[bi]
                        for kc in range(8):
                            pe(lambda e, kc=kc, pm=pm, ci=ci: e.matmul(pm[:], lhsT=wup[:, kc, ci * 128:(ci + 1) * 128], rhs=hw[:, kc, 1:513],
                                                                         start=(kc == 0), stop=(kc == 7)), [hw, wup], [pm])
                        for kc in range(8):
                            pe(lambda e, kc=kc, ph=ph, ci=ci: e.matmul(ph[:], lhsT=wup[:, kc, ci * 128:(ci + 1) * 128], rhs=hw[:, kc, 0:514:513],
                                                                         start=(kc == 0), stop=(kc == 7)), [hw, wup], [ph])
                        conv3_fm(cv[:], pm, ph, cw, cb_, ci, [pm, ph, cw, cb_], cv)
                    gt = gts[pc % 2]; gl = gls[pc % 2]; cg_, cv_ = cvs[2 * (pc % 2)], cvs[2 * (pc % 2) + 1]
                    act(lambda e, gl=gl, cg_=cg_: e.activation(out=gl[:], in_=cg_[:], func=AF.Gelu_apprx_tanh), [cg_], [gl])
                    dve(lambda e, gt=gt, gl=gl, cv_=cv_: e.tensor_tensor(out=gt[:], in0=gl[:], in1=cv_[:], op=ALU.mult), [gl, cv_], [gt])
                    stq(gFF[pc * 128:(pc + 1) * 128, b * 512:(b + 1) * 512], gt, gt[:], dB["gFF"])
            P.pop()
            P.push()
            wdn = P.sb("wdn", [128, 22, D], BF16)
            stg = [P.sb(f"stgE{i}", [128, D]) for i in range(2)]
            load_w_bf16(wdn, lambda k: wdn[:, k, :], lambda k: I["ffn_down"][l, k * 128:(k + 1) * 128, :], 22, D, stg)
            xts = [P.sb(f"xtE{i}", [128, D]) for i in range(2)]
            ggs = [P.sb(f"ggE{i}", [128, 22, 512], BF16) for i in range(2)]
            pys = [P.ps(f"pyE{i}", [128, D]) for i in range(2)]; sq = P.sb("sqE", [128, D])
            sss = [P.sb(f"ssE{i}", [128, 1]) for i in range(2)]; rss = [P.sb(f"rsE{i}", [128, 1]) for i in range(2)]
            tmpos = [P.sb(f"tmpE{i}", [128, D]) for i in range(2)]; outts = [P.sb(f"outE{i}", [128, D]) for i in range(2)]
            gv = gFF.rearrange("(k p) t -> p k t", p=128)
            for it in range(NT):
                xt, gg = xts[it % 2], ggs[(it // 4) % 2]
                py, ss, rs, tmpo, outt = pys[it % 2], sss[it % 2], rss[it % 2], tmpos[it % 2], outts[it % 2]
                q4 = it % 4
                seg = 0 if it < NT // 2 else 1
                ld(xt, xt[:], xmid[it * 128:(it + 1) * 128, :], src=dB["xmid"])
                if q4 == 0:
                    ld(gg, gg[:], gv[:, :, it * 128:(it + 4) * 128], src=dB["gFF"])
                for cb in range(2):
                    for pc in range(22):
                        pe(lambda e, pc=pc, cb=cb: e.matmul(py[:, cb * 512:(cb + 1) * 512], lhsT=gg[:, pc, q4 * 128:(q4 + 1) * 128], rhs=wdn[:, pc, cb * 512:(cb + 1) * 512],
                                                           start=(pc == 0), stop=(pc == 21)), [gg, wdn], [py])
                rms_residual(py, xt, sq, ss, rs, tmpo, outt, seg, 1, dst_ap2d[it * 128:(it + 1) * 128, :], dst_buf)
            P.pop()

        def phase_fnet(l):
            P.push()
            Cg = 64
            m1 = P.sb("fm1", [NT, 2, 2 * NS], BF16); ld(m1, m1[:], I["fn_m1"][:, :, :])
            zin = P.sb("zin", [NT, 2, Cg, 128], BF16); A = P.sb("fA", [128, NS, 2, Cg], BF16)
            osb = P.sb("osb", [Cg, T], BF16); osbv = osb[:].rearrange("c (k j) -> c k j", j=NT)
            ps1s = [P.ps(f"fps1{i}", [128, 2, 2 * NS]) for i in range(2)]
            ps2s = [P.ps(f"fps2{i}", [Cg, 4, 128]) for i in range(2)]
            m2s = [P.sb(f"fm2{i}", [128, 4, 2, 2, 128], BF16) for i in range(2)]
            n1 = 0
            for g in range(256 // Cg):
                c0 = g * Cg
                for ri in range(2):
                    ld(zin, zin[:, ri, :, :], zF[ri, c0:c0 + Cg, :].rearrange("c (g n) -> g c n", n=128), src=dB["zF"])
                for c in range(0, Cg, 2):
                    ps1 = ps1s[n1 % 2]; n1 += 1
                    for cc in range(2):
                        pe(lambda e, cc=cc, ps1=ps1: e.matmul(ps1[:, cc, :], lhsT=zin[:, 0, c + cc, :], rhs=m1[:, 0, :], start=True, stop=False), [zin, m1], [ps1])
                        pe(lambda e, cc=cc, ps1=ps1: e.matmul(ps1[:, cc, :], lhsT=zin[:, 1, c + cc, :], rhs=m1[:, 1, :], start=False, stop=True), [zin, m1], [ps1])
                    for ri in range(2):
                        src_ap = ps1[:, :, ri * NS:(ri + 1) * NS].rearrange("p c s -> p s c")
                        if ri == 0:
                            dve(lambda e, src_ap=src_ap: e.tensor_copy(out=A[:, :, 0, c:c + 2], in_=src_ap), [ps1], [A])
                        else:
                            act(lambda e, src_ap=src_ap: e.activation(out=A[:, :, 1, c:c + 2], in_=src_ap, func=AF.Copy), [ps1], [A])
                m2v = I["fn_m2"].rearrange("j p s r k -> p j s r k")
                for j in range(NT):
                    m2t = m2s[(j // 4) % 2]
                    if j % 4 == 0:
                        ld(m2t, m2t[:], m2v[:, j:j + 4, :, :, :])
                    ps2 = ps2s[(j // 4) % 2]
                    k = 0
                    for s_ in range(2):
                        for ri in range(2):
                            pe(lambda e, s_=s_, ri=ri, k=k, ps2=ps2, m2t=m2t: e.matmul(ps2[:, j % 4, :], lhsT=A[:, s_ * NT + j, ri, :], rhs=m2t[:, j % 4, s_, ri, :],
                                                                                      start=(k == 0), stop=(k == 3)), [A, m2t], [ps2])
                            k += 1
                    if j % 4 == 3:
                        j0 = j - 3
                        src_ap = ps2[:, :, :].rearrange("c j k -> c k j")
                        if (j // 4) % 2 == 0:
                            dve(lambda e, src_ap=src_ap, j0=j0: e.tensor_copy(out=osbv[:, :, j0:j0 + 4], in_=src_ap), [ps2], [osb])
                        else:
                            act(lambda e, src_ap=src_ap, j0=j0: e.activation(out=osbv[:, :, j0:j0 + 4], in_=src_ap, func=AF.Copy), [ps2], [osb])
                stq(brF[c0:c0 + Cg, :], osb, osb[:], dB["brF"])
            P.pop()

        def drain(g):
            for _ in g:
                pass

        def run_concurrent(primary, secondary, ratio=int(os.environ.get("CONC_RATIO", "1"))):
            p_alive, s_alive, p_fin = True, True, False
            while p_alive or s_alive:
                for _ in range(PRIM_STEPS):
                    if p_alive and not (p_fin and s_alive):
                        try:
                            if next(primary) == 'finished':
                                p_fin = True
                        except StopIteration:
                            p_alive = False
                for _ in range(ratio):
                    if s_alive:
                        try:
                            next(secondary)
                        except StopIteration:
                            s_alive = False

        def run_pipelined(make_gen, order, depth=PIPE_DEPTH):
            active = []
            order = list(order)
            pos = 0
            while pos < len(order) or active:
                if pos < len(order) and len(active) < depth and all(e[1] == 'second' for e in active):
                    active.append([make_gen(order[pos]), 'first']); pos += 1
                for ent in list(active):
                    if ent[1] == 'waiting':
                        if active[0] is ent:
                            ent[1] = 'second'
                        else:
                            continue
                    try:
                        v = next(ent[0])
                        if v == 'prev_done' and ent[1] == 'first':
                            ent[1] = 'second' if active[0] is ent else 'waiting'
                    except StopIteration:
                        active.remove(ent)

        def phase_scan(l, ret):
            P.push()
            nkc, hpc = (2, 2) if ret else (1, 4)
            KW = nkc * 128
            col0, width = (0, 1024) if ret else (1024, 800)
            qo, ko, vo, go = (0, 256, 512, 768) if ret else (0, 128, 256, 512)
            lro = 768
            kbr = 1 if ret else 3
            tri = P.sb("tri", [128, 6, 128]); ld(tri, tri[:], I["tri"][:, :, :])
            mh = P.sb("mh", [128, hpc]); ld(mh, mh[:], I["mh_ret" if ret else "mh_gla"][:, :])
            bd = P.sb("bd", [128, nkc, 256]); ld(bd, bd[:], I["bd_ret" if ret else "bd_gla"][:, :, :])
            gn = P.sb("gn", [128, 256]); ld(gn, gn[:], I["ret_gn" if ret else "gla_gn"][l].partition_broadcast(128))
            lns = math.log(32.0 ** -0.5)
            if ret:
                Ec = P.sb("E", [128, nkc, 6, 128]); Epc = P.sb("Epad", [128, nkc, 2, hpc, 128])
                dtokc = P.sb("dtok", [128, 2, KW]); decc = P.sb("dec", [128, nkc, 2])
                ld(Ec, Ec[:], I["ret_e"][:, :, :, :]); ld(dtokc, dtokc[:], I["ret_tok"][:, :, :]); ld(decc, decc[:], I["ret_dec"][:, :, :])
                for c in range(nkc):
                    for d_ in range(2):
                        dve(lambda e, c=c, d_=d_: e.tensor_tensor(out=Epc[:, c, d_, :, :], in0=Ec[:, c, 1 + 2 * d_, :].unsqueeze(1).broadcast_to([128, hpc, 128]),
                                                                  in1=mh[:].unsqueeze(2).broadcast_to([128, hpc, 128]), op=ALU.mult), [Ec, mh], [Epc])
            else:
                wd = P.sb("wd", [33, 256]); ld(wd, wd[:], I["gla_wd"][l])
                lnsb = P.sb("lnsb", [128, 1])
                dve(lambda e: e.memset(lnsb[:], lns), [], [lnsb])
                psZ = P.ps("psZ", [128, 512]); psB = P.ps("psB", [128, 4, 128])

            class TS:
                pass

            def mk_set(i):
                S = TS()
                S.pt = P.sb(f"pt{i}", [128, width])
                S.Qt = P.sb(f"Qt{i}", [128, nkc, 4, 128], BF16); S.Kp = P.sb(f"Kp{i}", [128, nkc, 2, hpc, 128], BF16)
                S.khat = P.sb(f"khat{i}", [128, 2, KW], BF16); S.Vb = P.sb(f"Vb{i}", [128, 256], BF16)
                S.st1 = P.sb(f"st1{i}", [128, 4, 128]); S.st2 = P.sb(f"st2{i}", [128, 4, 128]); S.PT = P.sb(f"PT{i}", [128, 4, 128], BF16)
                S.hn1 = P.sb(f"hn1{i}", [128, 4]); S.hn2 = P.sb(f"hn2{i}", [128, 4]); S.oc = P.sb(f"oc{i}", [128, 4, 64]); S.osq = P.sb(f"osq{i}", [128, 4, 64])
                S.sg = P.sb(f"sg{i}", [128, 256]); S.resb = P.sb(f"resb{i}", [128, 256], BF16); S.resT = P.sb(f"resT{i}", [128, 2, 128], BF16)
                S.tU = P.sb(f"tU{i}", [128, nkc, 256])
                if ret:
                    S.rot = P.sb(f"rot{i}", [128, 2, 32]); S.qkr = P.sb(f"qkr{i}", [128, 8, 64])
                    S.rt1 = P.sb(f"rt1{i}", [128, 8, 32]); S.rt2 = P.sb(f"rt2{i}", [128, 8, 32])
                    S.E, S.Epad, S.dtok, S.dec = Ec, Epc, dtokc, decc
                else:
                    S.lrT = P.sb(f"lrT{i}", [33, 128]); dve(lambda e: e.memset(S.lrT[:], 1.0), [], [S.lrT])
                    S.et = P.sb(f"et{i}", [128, 256]); S.lt = P.sb(f"lt{i}", [128, 256]); S.bsb = P.sb(f"bsb{i}", [128, 4, 128])
                    S.mids = P.sb(f"mids{i}", [128, 4])
                    S.E = P.sb(f"E{i}", [128, nkc, 6, 128]); S.Epad = P.sb(f"Epad{i}", [128, nkc, 2, hpc, 128])
                    S.dtok = P.sb(f"dtok{i}", [128, 2, KW]); S.dec = P.sb(f"dec{i}", [128, nkc, 2])
                return S

            sets = [mk_set(0), mk_set(1)]
            psT = P.ps("psT", [128, 2 * nkc, 128])
            psS = [P.ps(f"psS{i}", [128, 4, 128]) for i in range(2)]
            psO = P.ps("psO", [128, 256]); psU = P.ps("psU", [128, nkc, 256]); psR = P.ps("psR", [128, 2, 128], BF16)
            Sm = P.sb("Sm", [128, nkc, 256]); Sbf = P.sb("Sbf", [128, nkc, 256], BF16); Sball = P.sb("Sball", [128, NT, nkc, 256], BF16)
            brv = brF.rearrange("(k p) t -> p k t", p=128)

            def prep(n, S, full=True):
                pt = S.pt
                ld(pt, pt[:], projT[n * 128:(n + 1) * 128, col0:col0 + width], src=dB["projT"])
                if ret:
                    rot, qkr, rt1, rt2 = S.rot, S.qkr, S.rt1, S.rt2
                    ld(rot, rot[:], I["rot"][n * 128:(n + 1) * 128, :, :])
                    h0 = 0 if full else 4
                    nh_ = 8 - h0
                    src = pt[:, 0:512].rearrange("p (h d) -> p h d", d=64)[:, h0:8, :]
                    cosb = rot[:, 0, :].unsqueeze(1).broadcast_to([128, nh_, 32]); sinb = rot[:, 1, :].unsqueeze(1).broadcast_to([128, nh_, 32])
                    qkr_full = qkr
                    qkr = qkr[:, h0:8, :]; rt1 = rt1[:, h0:8, :]; rt2 = rt2[:, h0:8, :]
                    gps(lambda e: e.tensor_tensor(out=rt1, in0=src[:, :, 0:32], in1=cosb, op=ALU.mult), [pt, rot], [S.rt1])
                    gps(lambda e: e.tensor_tensor(out=rt2, in0=src[:, :, 32:64], in1=sinb, op=ALU.mult), [pt, rot], [S.rt2])
                    gps(lambda e: e.tensor_tensor(out=qkr[:, :, 0:32], in0=rt1, in1=rt2, op=ALU.subtract), [S.rt1, S.rt2], [S.qkr])
                    gps(lambda e: e.tensor_tensor(out=rt1, in0=src[:, :, 0:32], in1=sinb, op=ALU.mult), [pt, rot, S.qkr], [S.rt1])
                    gps(lambda e: e.tensor_tensor(out=rt2, in0=src[:, :, 32:64], in1=cosb, op=ALU.mult), [pt, rot, S.qkr], [S.rt2])
                    gps(lambda e: e.tensor_tensor(out=qkr[:, :, 32:64], in0=rt1, in1=rt2, op=ALU.add), [S.rt1, S.rt2], [S.qkr])
                    qk = qkr_full[:].rearrange("p h d -> p (h d)")
                    S.q_tok, S.k_tok, S.qkb = qk[:, 0:256], qk[:, 256:512], qkr_full
                else:
                    lrT, et, lt, bsb, mids, E, Epad, dtok, dec = S.lrT, S.et, S.lt, S.bsb, S.mids, S.E, S.Epad, S.dtok, S.dec
                    S.q_tok, S.k_tok, S.qkb = pt[:, qo:qo + 128], pt[:, ko:ko + 128], pt
                    pe(lambda e: e.transpose(psZ[0:32, 256:384], pt[:, lro:lro + 32], identf[:]), [pt, identf], [psZ])
                    yield
                    dve(lambda e: e.tensor_copy(out=lrT[0:32, :], in_=psZ[0:32, 256:384]), [psZ], [lrT])
                    yield
                    pe(lambda e: e.matmul(psZ[:, 0:256], lhsT=lrT[:], rhs=wd[:], start=True, stop=True), [lrT, wd], [psZ])
                    yield
                    act(lambda e: e.activation(out=et[:], in_=psZ[:, 0:256], func=AF.Exp, scale=-1.0), [psZ], [et])
                    act(lambda e: e.activation(out=lt[:], in_=et[:], func=AF.Ln, bias=1.0), [et], [lt])
                    yield
                    pe(lambda e: e.matmul(psB[:, 0, :], lhsT=lt[:, 0:128], rhs=tri[:, 2, :], start=True, stop=True), [lt, tri], [psB])
                    pe(lambda e: e.matmul(psB[:, 1, :], lhsT=lt[:, 128:256], rhs=tri[:, 3, :], start=True, stop=True), [lt, tri], [psB])
                    pe(lambda e: e.matmul(psB[:, 2, :], lhsT=tri[:, 4, :], rhs=lt[:, 0:128], start=True, stop=True), [lt, tri], [psB])
                    pe(lambda e: e.matmul(psB[:, 3, :], lhsT=tri[:, 5, :], rhs=lt[:, 128:256], start=True, stop=True), [lt, tri], [psB])
                    yield
                    dve(lambda e: e.tensor_copy(out=bsb[:], in_=psB[:]), [psB], [bsb])
                    dve(lambda e: e.tensor_scalar(out=mids[:, 0:2], in0=bsb[:, 0:2, 64], scalar1=-1.0, scalar2=lns, op0=ALU.mult, op1=ALU.add), [bsb], [mids])
                    dve(lambda e: e.tensor_copy(out=mids[:, 2:4], in_=bsb[:, 0:2, 64]), [bsb], [mids])
                    yield
                    for d_ in (range(2) if full else (1,)):
                        if full:
                            act(lambda e, d_=d_: e.activation(out=E[:, 0, 2 * d_, :], in_=bsb[:, d_, :], func=AF.Exp, bias=mids[:, d_:d_ + 1]), [bsb, mids], [E])
                            act(lambda e, d_=d_: e.activation(out=E[:, 0, 2 * d_ + 1, :], in_=bsb[:, d_, :], func=AF.Exp, scale=-1.0, bias=mids[:, 2 + d_:3 + d_]), [bsb, mids], [E])
                            act(lambda e, d_=d_: e.activation(out=E[:, 0, 4 + d_, :], in_=bsb[:, d_, :], func=AF.Exp, bias=lnsb[:, 0:1]), [bsb, lnsb], [E])
                        act(lambda e, d_=d_: e.activation(out=dtok[:, d_, :], in_=bsb[:, 2 + d_, :], func=AF.Exp), [bsb], [dtok])
                    if full:
                        act(lambda e: e.activation(out=dec[:, 0, 0:1], in_=bsb[:, 0, 127:128], func=AF.Exp), [bsb], [dec])
                    act(lambda e: e.activation(out=dec[:, 0, 1:2], in_=bsb[:, 1, 0:1], func=AF.Exp), [bsb], [dec])
                    yield
                    if full:
                        for d_ in range(2):
                            dve(lambda e, d_=d_: e.tensor_tensor(out=Epad[:, 0, d_, :, :], in0=E[:, 0, 1 + 2 * d_, :].unsqueeze(1).broadcast_to([128, hpc, 128]),
                                                                 in1=mh[:].unsqueeze(2).broadcast_to([128, hpc, 128]), op=ALU.mult), [E, mh], [Epad])
                act(lambda e: e.activation(out=S.Vb[:], in_=pt[:, vo:vo + 256], func=AF.Copy), [pt], [S.Vb])
                for d_ in (range(2) if full else (1,)):
                    gps(lambda e, d_=d_: e.tensor_tensor(out=S.khat[:, d_, :], in0=S.k_tok, in1=S.dtok[:, d_, :], op=ALU.mult), [S.qkb, S.dtok], [S.khat])
                yield

            def state_update(d_, S):
                for c in range(nkc):
                    pe(lambda e, c=c: e.matmul(psU[:, c, :], lhsT=S.khat[:, d_, c * 128:(c + 1) * 128], rhs=S.Vb[:], start=True, stop=True), [S.khat, S.Vb], [psU])
                yield
                dve(lambda e: e.tensor_tensor(out=S.tU[:], in0=psU[:], in1=bd[:], op=ALU.mult), [psU, bd], [S.tU])
                for c in range(nkc):
                    dve(lambda e, c=c: e.scalar_tensor_tensor(out=Sm[:, c, :], in0=Sm[:, c, :], scalar=S.dec[:, c, d_:d_ + 1], in1=S.tU[:, c, :],
                                                              op0=ALU.mult, op1=ALU.add), [Sm, S.dec, S.tU], [Sm])

            def keep_mul():
                dve(lambda e: e.tensor_scalar(out=Sm[:], in0=Sm[:], scalar1=scal[:, 0:1], scalar2=None, op0=ALU.mult), [Sm, scal], [Sm])

            def gen1(n):
                S = sets[n % 2]
                yield from prep(n, S, full=False)
                yield 'prev_done'
                act(lambda e: e.activation(out=Sball[:, n, :, :], in_=Sm[:], func=AF.Copy), [Sm], [Sball])
                yield from state_update(1, S)
                if n == NT // 2:
                    keep_mul()

            dve(lambda e: e.memset(Sm[:], 0.0), [], [Sm])
            run_pipelined(gen1, reversed(range(NT)), depth=int(os.environ.get('PIPE1', '2')))

            def gen2(n):
                S = sets[n % 2]
                if PD_POS == 0:
                    yield 'prev_done'
                yield from prep(n, S)
                if PD_POS == 1:
                    yield 'prev_done'
                Qt, Kp, PT, Vb, E, Epad = S.Qt, S.Kp, S.PT, S.Vb, S.E, S.Epad
                for c in range(nkc):
                    pe(lambda e, c=c: e.transpose(psT[:, c, :], S.q_tok[:, c * 128:(c + 1) * 128], identf[:]), [S.qkb, identf], [psT])
                    pe(lambda e, c=c: e.transpose(psT[:, nkc + c, :], S.k_tok[:, c * 128:(c + 1) * 128], identf[:]), [S.qkb, identf], [psT])
                yield
                if PD_POS == 2:
                    yield 'prev_done'
                for c in range(nkc):
                    for vi, ei in enumerate((0, 2, 4, 5)):
                        dve(lambda e, c=c, vi=vi, ei=ei: e.tensor_tensor(out=Qt[:, c, vi, :], in0=psT[:, c, :], in1=E[:, c, ei, :], op=ALU.mult), [psT, E], [Qt])
                    for d_ in range(2):
                        dve(lambda e, c=c, d_=d_: e.tensor_tensor(out=Kp[:, c, d_, :, :], in0=psT[:, nkc + c, :].unsqueeze(1).broadcast_to([128, hpc, 128]),
                                                                  in1=Epad[:, c, d_, :, :], op=ALU.mult), [psT, Epad], [Kp])
                yield
                if PD_POS == 3:
                    yield 'prev_done'
                for d_ in range(2):
                    for c in range(nkc):
                        for hh in range(hpc):
                            pe(lambda e, d_=d_, c=c, hh=hh: e.matmul(psS[d_][:, c * hpc + hh, :], lhsT=Kp[:, c, d_, hh, :], rhs=Qt[:, c, d_, :], start=True, stop=True),
                               [Kp, Qt], [psS[d_]])
                yield
                if PD_POS == 4:
                    yield 'prev_done'
                dve(lambda e: e.tensor_tensor(out=S.st1[:], in0=psS[0][:], in1=tri[:, 0, :].unsqueeze(1).broadcast_to([128, 4, 128]), op=ALU.mult), [psS[0], tri], [S.st1])
                dve(lambda e: e.tensor_tensor(out=S.st2[:], in0=psS[1][:], in1=tri[:, 1, :].unsqueeze(1).broadcast_to([128, 4, 128]), op=ALU.mult), [psS[1], tri], [S.st2])
                gps(lambda e: e.tensor_tensor(out=PT[:], in0=S.st1[:], in1=S.st2[:], op=ALU.add), [S.st1, S.st2], [PT])
                yield 'prev_done'
                if n == NT // 2:
                    keep_mul()
                act(lambda e: e.activation(out=Sbf[:], in_=Sm[:], func=AF.Copy), [Sm], [Sbf])
                yield
                for h_ in range(4):
                    c = h_ // hpc
                    hs = slice(h_ * 64, (h_ + 1) * 64)
                    pe(lambda e, c=c, hs=hs: e.matmul(psO[:, hs], lhsT=Qt[:, c, 2, :], rhs=Sbf[:, c, hs], start=True, stop=False), [Qt, Sbf], [psO])
                    pe(lambda e, c=c, hs=hs: e.matmul(psO[:, hs], lhsT=Qt[:, c, 3, :], rhs=Sball[:, n, c, hs], start=False, stop=False), [Qt, Sball], [psO])
                    pe(lambda e, h_=h_, hs=hs: e.matmul(psO[:, hs], lhsT=PT[:, h_, :], rhs=Vb[:, hs], start=False, stop=True), [PT, Vb], [psO])
                yield
                hn1, hn2, oc, osq, sg, resb, resT, pt = S.hn1, S.hn2, S.oc, S.osq, S.sg, S.resb, S.resT, S.pt
                O3 = psO[:].rearrange("p (h d) -> p h d", d=64)
                if ret:
                    dve(lambda e: e.tensor_reduce(out=hn1[:], in_=O3, axis=mybir.AxisListType.X, op=ALU.add), [psO], [hn1])
                    dve(lambda e: e.tensor_scalar(out=hn1[:], in0=hn1[:], scalar1=-1.0 / 64, scalar2=None, op0=ALU.mult), [hn1], [hn1])
                    dve(lambda e: e.tensor_tensor(out=oc[:], in0=O3, in1=hn1[:].unsqueeze(2).broadcast_to([128, 4, 64]), op=ALU.add), [psO, hn1], [oc])
                else:
                    dve(lambda e: e.tensor_copy(out=oc[:], in_=O3), [psO], [oc])
                gps(lambda e: e.tensor_tensor(out=osq[:], in0=oc[:], in1=oc[:], op=ALU.mult), [oc], [osq])
                dve(lambda e: e.tensor_reduce(out=hn2[:], in_=osq[:], axis=mybir.AxisListType.X, op=ALU.add), [osq], [hn2])
                act(lambda e: e.activation(out=sg[:], in_=pt[:, go:go + 256], func=AF.Silu), [pt], [sg])
                act(lambda e: e.activation(out=hn2[:], in_=hn2[:], func=AF.Sqrt, scale=1.0 / 64, bias=epsb[:, 0:1]), [hn2, epsb], [hn2])
                yield
                dve(lambda e: e.reciprocal(out=hn2[:], in_=hn2[:]), [hn2], [hn2])
                gps(lambda e: e.tensor_tensor(out=oc[:], in0=oc[:], in1=hn2[:].unsqueeze(2).broadcast_to([128, 4, 64]), op=ALU.mult), [oc, hn2], [oc])
                gps(lambda e: e.tensor_tensor(out=sg[:], in0=sg[:], in1=gn[:], op=ALU.mult), [sg, gn], [sg])
                gps(lambda e: e.tensor_tensor(out=resb[:], in0=oc[:].rearrange("p h d -> p (h d)"), in1=sg[:], op=ALU.mult), [oc, sg], [resb])
                yield
                for c2_ in range(2):
                    pe(lambda e, c2_=c2_: e.transpose(psR[:, c2_, :], resb[:, c2_ * 128:(c2_ + 1) * 128], identb[:]), [resb, identb], [psR])
                yield from state_update(0, S)
                dve(lambda e: e.tensor_copy(out=resT[:], in_=psR[:]), [psR], [resT])
                stq(brv[:, 2 * kbr:2 * kbr + 2, n * 128:(n + 1) * 128], resT, resT[:], dB["brF"])

            dve(lambda e: e.memset(Sm[:], 0.0), [], [Sm])
            run_pipelined(gen2, range(NT), depth=int(os.environ.get('PIPE2', '2')))
            P.pop()

        def phase_hyena(l, mode='all'):
            NBLK = 2 * T // 512
            Cg = 32
            TWO_PI = 2.0 * math.pi
            def part1():
                P.push()
                w1 = P.sb("w1", [33, 64]); w2 = P.sb("w2", [64, 64]); w3a = P.sb("w3a", [65, 1024])
                c1 = P.sb("c1", [64, 2]); c2_ = P.sb("c2", [64, 2]); fb = P.sb("fb", [64, 2]); delta = P.sb("delta", [128, 2])
                ld(w1, w1[:], I["flt_w1"][l]); ld(w2, w2[:], I["flt_w2"][l]); ld(w3a, w3a[0:64, :], I["flt_w3"][l])
                ld(w3a, w3a[64:65, :], I["flt_b3"][l:l + 1, :])
                ld(c1, c1[:], I["flt_c1"][l]); ld(c2_, c2_[:], I["flt_c2"][l]); ld(delta, delta[:], I["flt_delta"][:, :])
                dve(lambda e: e.tensor_tensor(out=fb[:, 0:1], in0=c1[:, 0:1], in1=c1[:, 1:2], op=ALU.mult), [c1], [fb])
                dve(lambda e: e.tensor_tensor(out=fb[:, 1:2], in0=c2_[:, 0:1], in1=c2_[:, 1:2], op=ALU.mult), [c2_, fb], [fb])
                h2a = P.sb("h2a", [65, 512], BF16); dve(lambda e: e.memset(h2a[:], 1.0), [], [h2a])
                w3b = P.sb("w3b", [65, 1024], BF16); dve(lambda e: e.tensor_copy(out=w3b[:], in_=w3a[:]), [w3a], [w3b])
                h1 = P.sb("h1", [64, 512]); a1 = P.sb("a1", [64, 512]); kk = P.sb("kk", [64, 512])
                nrm = P.sb("nrm", [128, 4, NBLK]); rn = P.sb("rn", [128, 4])
                fts = [P.sb(f"ft{i}", [33, 512]) for i in range(2)]; msks = [P.sb(f"msk{i}", [128, 3, 512]) for i in range(2)]
                win = P.sb("win", [128, 2, 512]); t1 = P.sb("ft1", [128, 512]); t2 = P.sb("ft2", [128, 512]); ab = P.sb("fab", [128, 512])
                gbs = [P.sb(f"gb{i}", [128, 512], BF16) for i in range(2)]
                psh = P.ps("psh", [64, 512]); psf = [P.ps(f"psf{i}", [128, 512]) for i in range(2)]

                def sin_layer(cc, col, dst):
                    dve(lambda e: e.tensor_scalar(out=a1[:], in0=psh[:], scalar1=cc[:, 0:1], scalar2=fb[:, col:col + 1], op0=ALU.mult, op1=ALU.add), [psh, cc, fb], [a1])
                    dve(lambda e: e.tensor_scalar(out=kk[:], in0=a1[:], scalar1=1.0 / TWO_PI, scalar2=MAGIC, op0=ALU.mult, op1=ALU.add), [a1], [kk])
                    dve(lambda e: e.tensor_scalar(out=kk[:], in0=kk[:], scalar1=-MAGIC, scalar2=None, op0=ALU.add), [kk], [kk])
                    dve(lambda e: e.scalar_tensor_tensor(out=a1[:], in0=kk[:], scalar=-TWO_PI, in1=a1[:], op0=ALU.mult, op1=ALU.add), [kk, a1], [a1])
                    act(lambda e: e.activation(out=dst, in_=a1[:], func=AF.Sin), [a1], [h1 if dst is not None and cc is c1 else h2a])

                ng = 0
                for blk in range(NBLK):
                    m0 = blk * 512
                    ft, msk = fts[blk % 2], msks[blk % 2]
                    ld(ft, ft[:], I["flt_feat"][:, m0:m0 + 512])
                    for r_ in range(3):
                        ld(msk, msk[:, r_, :], I["flt_msk"][r_, m0:m0 + 512].partition_broadcast(128))
                    pe(lambda e: e.matmul(psh[:], lhsT=w1[:], rhs=ft[:], start=True, stop=True), [w1, ft], [psh])
                    sin_layer(c1, 0, h1[:])
                    yield
                    pe(lambda e: e.matmul(psh[:], lhsT=w2[:], rhs=h1[:], start=True, stop=True), [w2, h1], [psh])
                    sin_layer(c2_, 1, h2a[0:64, :])
                    yield
                    for ch in range(2):
                        act(lambda e, ch=ch: e.activation(out=win[:, ch, :], in_=msk[:, 2, :], func=AF.Exp, scale=delta[:, ch:ch + 1]), [msk, delta], [win])
                    for o in range(2):
                        for ch in range(2):
                            for dr in range(2):
                                q = o * 4 + dr * 2 + ch
                                pe(lambda e, dr=dr, q=q: e.matmul(psf[dr][:], lhsT=w3b[:, q * 128:(q + 1) * 128], rhs=h2a[:], start=True, stop=True), [w3b, h2a], [psf[dr]])
                            dve(lambda e: e.tensor_tensor(out=t1[:], in0=psf[0][:], in1=msk[:, 0, :], op=ALU.mult), [psf[0], msk], [t1])
                            dve(lambda e: e.tensor_tensor(out=t2[:], in0=psf[1][:], in1=msk[:, 1, :], op=ALU.mult), [psf[1], msk], [t2])
                            dve(lambda e: e.tensor_tensor(out=t1[:], in0=t1[:], in1=t2[:], op=ALU.add), [t1, t2], [t1])
                            dve(lambda e, ch=ch: e.tensor_tensor(out=t1[:], in0=t1[:], in1=win[:, ch, :], op=ALU.mult), [t1, win], [t1])
                            gb = gbs[ng % 2]; ng += 1
                            idx = o * 2 + ch
                            act(lambda e, gb=gb: e.activation(out=gb[:], in_=t1[:], func=AF.Copy), [t1], [gb])
                            act(lambda e, idx=idx, blk=blk: e.activation(out=ab[:], in_=t1[:], func=AF.Abs, accum_out=nrm[:, idx, blk:blk + 1]), [t1], [ab, nrm])
                            stq(gF[idx * 128:(idx + 1) * 128, m0:m0 + 512], gb, gb[:], dB["gF"])
                            yield
                dve(lambda e: e.tensor_reduce(out=rn[:], in_=nrm[:], axis=mybir.AxisListType.X, op=ALU.add), [nrm], [rn])
                dve(lambda e: e.tensor_scalar(out=rn[:], in0=rn[:], scalar1=scal[:, 1:2], scalar2=EPS, op0=ALU.mult, op1=ALU.add), [rn, scal], [rn])
                dve(lambda e: e.reciprocal(out=rn[:], in_=rn[:]), [rn], [rn])
                stq(rnD.rearrange("(q p) -> p q", p=128), rn, rn[:], dB["rnD"])
                P.pop()

            def stage1(din, m1, AA, ps1s, cnt, Cg=Cg):
                for c in range(0, Cg, 2):
                    ps1 = ps1s[cnt[0] % 2]; cnt[0] += 1
                    for cc in range(2):
                        pe(lambda e, cc=cc, ps1=ps1, c=c: e.matmul(ps1[:, cc, :], lhsT=din[:, c + cc, :], rhs=m1[:], start=True, stop=True), [din, m1], [ps1])
                    for ri in range(2):
                        src_ap = ps1[:, :, ri * NSA:(ri + 1) * NSA].rearrange("p c s -> p s c")
                        if ri == 0:
                            dve(lambda e, src_ap=src_ap, c=c: e.tensor_copy(out=AA[:, :, 0, c:c + 2], in_=src_ap), [ps1], [AA])
                        else:
                            act(lambda e, src_ap=src_ap, c=c: e.activation(out=AA[:, :, 1, c:c + 2], in_=src_ap, func=AF.Copy), [ps1], [AA])
                    yield

            def stage2(AA, h2ts, psXs, evac, spb=8):
                h2v = I["hy_h2"].rearrange("s p a k -> p s a k")
                for j in range(NSA):
                    h2t = h2ts[(j // 8) % 2]; jj = j % spb
                    if j % 8 == 0:
                        nj_ = min(8, NSA - j)
                        ld(h2t, h2t[:, 0:nj_, :, :], h2v[:, j:j + nj_, :, :])
                    psX = psXs[(j // spb) % 2]
                    j8 = j % 8
                    pe(lambda e, psX=psX, jj=jj, h2t=h2t, j=j, j8=j8: e.matmul(psX[:, jj, 0, :], lhsT=h2t[:, j8, 0, :], rhs=AA[:, j, 0, :], start=True, stop=False), [h2t, AA], [psX])
                    pe(lambda e, psX=psX, jj=jj, h2t=h2t, j=j, j8=j8: e.matmul(psX[:, jj, 0, :], lhsT=h2t[:, j8, 2, :], rhs=AA[:, j, 1, :], start=False, stop=True), [h2t, AA], [psX])
                    pe(lambda e, psX=psX, jj=jj, h2t=h2t, j=j, j8=j8: e.matmul(psX[:, jj, 1, :], lhsT=h2t[:, j8, 0, :], rhs=AA[:, j, 1, :], start=True, stop=False), [h2t, AA], [psX])
                    pe(lambda e, psX=psX, jj=jj, h2t=h2t, j=j, j8=j8: e.matmul(psX[:, jj, 1, :], lhsT=h2t[:, j8, 1, :], rhs=AA[:, j, 0, :], start=False, stop=True), [h2t, AA], [psX])
                    if jj == spb - 1 or j == NSA - 1:
                        evac(psX, j - jj, jj + 1)
                        yield

            def part2():
                P.push()
                rnb = P.sb("rnb", [128, 512]); ld(rnb, rnb[:], rnD.partition_broadcast(128), src=dB["rnD"])
                hf1 = P.sb("hf1", [NS, 2 * NSA], BF16); ld(hf1, hf1[:], I["hy_hf1"][:, :])
                Cf = 64
                gin = P.sb("gin", [NS, Cf, 128], BF16); AA = P.sb("AAf", [128, NSA, 2, Cf], BF16)
                Gsb = P.sb("Gsbf", [128, NSA, 2, Cf], BF16)
                ps1s = [P.ps(f"hps1f{i}", [128, 2, 2 * NSA]) for i in range(2)]; psXs = [P.ps(f"hpsXf{i}", [128, 4, 2, Cf]) for i in range(2)]
                h2ts = [P.sb(f"h2tf{i}", [128, 8, 3, 128], BF16) for i in range(2)]
                cnt = [0]
                for gi in range(512 // Cf):
                    ld(gin, gin[:], gF[gi * Cf:(gi + 1) * Cf, :].rearrange("c (g n) -> g c n", n=128), src=dB["gF"])
                    yield from stage1(gin, hf1, AA, ps1s, cnt, Cg=Cf)

                    def evacG(psX, j0, nj, gi=gi):
                        dve(lambda e: e.tensor_tensor(out=Gsb[:, j0:j0 + nj, :, :], in0=psX[:, 0:nj, :, :],
                                                      in1=rnb[:, gi * Cf:(gi + 1) * Cf].unsqueeze(1).unsqueeze(1).broadcast_to([128, nj, 2, Cf]), op=ALU.mult),
                            [psX, rnb], [Gsb])
                    yield from stage2(AA, h2ts, psXs, evacG, spb=4)
                    for hh_ in range(2):
                        for s0_ in range(0, NSA, 32):
                            s1_ = min(NSA, s0_ + 32)
                            stq(Gd[2 * gi + hh_].rearrange("p s (r c) -> p s r c", r=2)[:, s0_:s1_], Gsb, Gsb[:, s0_:s1_, :, hh_ * 32:(hh_ + 1) * 32], dB["Gd"])
                P.pop()

            def part3():
                P.push()
                hz = P.sb("hz", [NSA, 128, 2, NT], BF16); ld(hz, hz[:], I["hy_z"][:, :, :, :])
                h1t = P.sb("hh1", [NT, 2 * NSA], BF16); ld(h1t, h1t[:], I["hy_h1"][:, :])
                i1 = P.sb("hi1", [128, 2, 256], BF16); ld(i1, i1[:], I["hy_i1"][:, :, :])
                skb = P.sb("skb", [128, 2, 256])
                for o in range(2):
                    ld(skb, skb[:, o, :], I["hy_skip"][l, o].partition_broadcast(128))
                AA = P.sb("AAd", [128, NSA, 2, Cg], BF16); Ysb = P.sb("Ysb", [128, 2, Cg, NSA], BF16); Bsb = P.sb("Bsb", [NSA, 128, 2, Cg], BF16)
                Gsb = P.sb("Gsbd", [128, NSA, 2, Cg], BF16)
                din = P.sb("din", [NT, Cg, 128], BF16); vt = P.sb("vt", [NT, Cg, 128]); x1t = P.sb("x1t", [NT, Cg, 128]); x2t = P.sb("x2t", [NT, Cg, 128])
                ob = P.sb("ob", [NT, Cg, 128], BF16)
                pw = [P.sb(f"pw{i}", [128, 8, Cg]) for i in range(4)]
                tcv = P.sb("tcv", [NT, Cg, 16])
                ps1s = [P.ps(f"hps1d{i}", [128, 2, 2 * NSA]) for i in range(2)]; psXs = [P.ps(f"hpsXd{i}", [128, 8, 2, Cg]) for i in range(2)]
                psIs = [P.ps(f"hpsI{i}", [NSA, 2, 256]) for i in range(2)]; psYs = [P.ps(f"hpsY{i}", [NT, 16, Cg]) for i in range(2)]
                h2ts = [P.sb(f"h2td{i}", [128, 8, 3, 128], BF16) for i in range(2)]
                cnt = [0]

                def evacY(psX, j0, nj):
                    Xre, Xim = psX[:, 0:nj, 0, :], psX[:, 0:nj, 1, :]
                    Gre, Gim = Gsb[:, j0:j0 + nj, 0, :], Gsb[:, j0:j0 + nj, 1, :]
                    dve(lambda e: e.tensor_tensor(out=pw[0][:, 0:nj, :], in0=Xre, in1=Gre, op=ALU.mult), [psX, Gsb], [pw[0]])
                    dve(lambda e: e.tensor_tensor(out=pw[1][:, 0:nj, :], in0=Xim, in1=Gim, op=ALU.mult), [psX, Gsb], [pw[1]])
                    dve(lambda e: e.tensor_tensor(out=Ysb[:, 0, :, j0:j0 + nj].rearrange("p c j -> p j c"), in0=pw[0][:, 0:nj, :], in1=pw[1][:, 0:nj, :], op=ALU.subtract),
                        [pw[0], pw[1]], [Ysb])
                    dve(lambda e: e.tensor_tensor(out=pw[2][:, 0:nj, :], in0=Xre, in1=Gim, op=ALU.mult), [psX, Gsb], [pw[2]])
                    dve(lambda e: e.tensor_tensor(out=pw[3][:, 0:nj, :], in0=Xim, in1=Gre, op=ALU.mult), [psX, Gsb], [pw[3]])
                    dve(lambda e: e.tensor_tensor(out=Ysb[:, 1, :, j0:j0 + nj].rearrange("p c j -> p j c"), in0=pw[2][:, 0:nj, :], in1=pw[3][:, 0:nj, :], op=ALU.add),
                        [pw[2], pw[3]], [Ysb])

                def long_conv(o, g, xg, svt):
                    ld(Gsb, Gsb[:].rearrange("p s r c -> p s (r c)"), Gd[o * (256 // Cg) + g], src=dB["Gd"])
                    drain(stage1(din, h1t, AA, ps1s, cnt))
                    drain(stage2(AA, h2ts, psXs, evacY))
                    for c in range(0, Cg, 2):
                        psI = psIs[(c // 2) % 2]
                        for cc in range(2):
                            pe(lambda e, cc=cc, psI=psI, c=c: e.matmul(psI[:, cc, :], lhsT=Ysb[:, 0, c + cc, :], rhs=i1[:, 0, :], start=True, stop=False), [Ysb, i1], [psI])
                            pe(lambda e, cc=cc, psI=psI, c=c: e.matmul(psI[:, cc, :], lhsT=Ysb[:, 1, c + cc, :], rhs=i1[:, 1, :], start=False, stop=True), [Ysb, i1], [psI])
                        for ri in range(2):
                            src_ap = psI[:, :, ri * 128:(ri + 1) * 128].rearrange("p c n -> p n c")
                            if ri == 0:
                                dve(lambda e, src_ap=src_ap, c=c: e.tensor_copy(out=Bsb[:, :, 0, c:c + 2], in_=src_ap), [psI], [Bsb])
                            else:
                                act(lambda e, src_ap=src_ap, c=c: e.activation(out=Bsb[:, :, 1, c:c + 2], in_=src_ap, func=AF.Copy), [psI], [Bsb])
                    for nb in range(8):
                        psY = psYs[nb % 2]
                        for q in range(16):
                            n2 = nb * 16 + q
                            pe(lambda e, psY=psY, q=q, n2=n2: e.matmul(psY[:, q, :], lhsT=hz[:, n2, 0, :], rhs=Bsb[:, n2, 0, :], start=True, stop=False), [hz, Bsb], [psY])
                            pe(lambda e, psY=psY, q=q, n2=n2: e.matmul(psY[:, q, :], lhsT=hz[:, n2, 1, :], rhs=Bsb[:, n2, 1, :], start=False, stop=True), [hz, Bsb], [psY])
                        sl = slice(nb * 16, (nb + 1) * 16)
                        dve(lambda e, psY=psY, sl=sl: e.tensor_tensor(out=tcv[:], in0=psY[:].rearrange("p n c -> p c n"), in1=svt[:, :, sl], op=ALU.add), [psY, svt], [tcv])
                        dve(lambda e, sl=sl: e.tensor_tensor(out=xg[:, :, sl], in0=tcv[:], in1=xg[:, :, sl], op=ALU.mult), [tcv, xg], [xg])

                uv = lambda r0: uhF[r0:r0 + Cg, :].rearrange("c (g n) -> g c n", n=128)
                for g in range(256 // Cg):
                    c0 = g * Cg
                    ld(vt, vt[:], uv(c0), src=dB["uhF"]); ld(x1t, x1t[:], uv(256 + c0), src=dB["uhF"]); ld(x2t, x2t[:], uv(512 + c0), src=dB["uhF"])
                    act(lambda e: e.activation(out=din[:], in_=vt[:], func=AF.Copy), [vt], [din])
                    dve(lambda e, c0=c0: e.tensor_tensor(out=vt[:], in0=vt[:], in1=skb[0:NT, 0, c0:c0 + Cg].unsqueeze(2).broadcast_to([NT, Cg, 128]), op=ALU.mult), [vt, skb], [vt])
                    long_conv(0, g, x1t, vt)
                    act(lambda e: e.activation(out=din[:], in_=x1t[:], func=AF.Copy), [x1t], [din])
                    dve(lambda e, c0=c0: e.tensor_tensor(out=vt[:], in0=x1t[:], in1=skb[0:NT, 1, c0:c0 + Cg].unsqueeze(2).broadcast_to([NT, Cg, 128]), op=ALU.mult), [x1t, skb], [vt])
                    long_conv(1, g, x2t, vt)
                    act(lambda e: e.activation(out=ob[:], in_=x2t[:], func=AF.Copy), [x2t], [ob])
                    stq(brF[512 + c0:512 + c0 + Cg, :].rearrange("c (g n) -> g c n", n=128), ob, ob[:], dB["brF"])
                P.pop()
            if mode == 'filtgen':
                def both():
                    yield from part1()
                    yield from part2()
                return both()
            if mode in ('all', 'filt'):
                drain(part1())
                drain(part2())
            if mode in ('all', 'conv'):
                part3()

        def phase_zero_br(l):
            P.push()
            zt = P.sb("zbr", [128, 2048], BF16)
            dve(lambda e: e.memset(zt[:], 0.0), [], [zt])
            for k in range(8):
                for t0 in range(0, T, 2048):
                    w_ = min(2048, T - t0)
                    stq(brF[k * 128:(k + 1) * 128, t0:t0 + w_], zt, zt[:, 0:w_], dB["brF"])
            P.pop()

        PHASES = dict(mod=phase_mod, norm=phase_norm, A=lambda l: drain(phase_A(l)), Afilt=lambda l: run_concurrent(phase_A(l), phase_hyena(l, 'filtgen')), C=phase_C, D=phase_D, zero=phase_zero_br, fnet=phase_fnet, ret=lambda l: phase_scan(l, True), gla=lambda l: phase_scan(l, False), hyena=phase_hyena, hyfilt=lambda l: phase_hyena(l, 'filt'), hyconv=lambda l: phase_hyena(l, 'conv'))
        nc._I = I
        return_hook(P, PHASES, locals())
    return nc


def return_hook(P, PHASES, env):
    sched = env.get('debug') or ()
    I, dB = env['I'], env['dB']
    stop = None
    for d in sched:
        if isinstance(d, str) and d.startswith("stop:"):
            stop = d[5:]
    x_in, x1d, xmid, y_out = env['x_in'], env['x1d'], env['xmid'], env['y_out']
    Am, Af = env['Am'], env['Af']
    only = [d[5:] for d in sched if isinstance(d, str) and d.startswith("only:")]
    if only:
        for nm in only:
            if nm == 'norm':
                PHASES['norm'](x_in, dB["in"], Am, 0)
            elif nm == 'C':
                PHASES['C'](0, x_in, dB["in"])
            elif nm == 'D':
                PHASES['D'](0, x1d, dB["x1d"])
            else:
                PHASES[nm](0)
        P.barrier()
        return
    for l in range(DEPTH):
        src, sbuf = (x_in, dB["in"]) if l == 0 else (x1d, dB["x1d"])
        dst, dbuf = (x1d, dB["x1d"]) if l == 0 else (y_out, dB["y"])
        if stop == "none":
            break
        PHASES['mod'](l)
        if stop == "mod":
            break
        PHASES['norm'](src, sbuf, Am, 0)
        if stop == "norm":
            break
        PHASES['Afilt' if CONC_FILT else 'A'](l)
        if stop == "A":
            break
        PHASES['zero'](l)
        for nm in ('fnet', 'ret', 'hyconv' if CONC_FILT else 'hyena', 'gla'):
            if nm in PHASES:
                PHASES[nm](l)
        if stop == "mix":
            break
        PHASES['C'](l, src, sbuf)
        PHASES['norm'](xmid, dB["xmid"], Af, 24)
        PHASES['D'](l, dst, dbuf)
        if stop == "L0":
            break
    P.barrier()


def prep_core_inputs(x, c2, W, tb):
    m = {"x": np.ascontiguousarray(x, np.float32)}
    m["cT"] = np.ascontiguousarray(c2.reshape(2, 8, 128).transpose(2, 1, 0), np.float32)
    m.update(W)
    m.update(tb)
    return m


def prep_weights(inp):
    f = lambda a: np.ascontiguousarray(a, np.float32)
    W = {}
    W["ada_w"] = f(inp["ada_w"]); W["ada_b"] = f(inp["ada_b"])
    W["ada_b_col"] = f(inp["ada_b"].reshape(DEPTH, 48, 128).transpose(0, 2, 1))
    nw = np.stack([inp["norm_pre_mix"], inp["norm_post_mix"], inp["norm_pre_ffn"], inp["norm_post_ffn"]], 1)
    W["normw_col"] = f(nw.reshape(DEPTH, 4, 8, 128).transpose(0, 3, 1, 2))
    W["norm_post_mix"] = f(inp["norm_post_mix"]); W["norm_post_ffn"] = f(inp["norm_post_ffn"])
    W["w_in"] = f(inp["w_in"])
    W["hy_cw"] = f(inp["hy_conv_w"].reshape(DEPTH, 3, 6, 128).transpose(0, 3, 2, 1))
    W["hy_cb"] = f(inp["hy_conv_b"].reshape(DEPTH, 6, 128).transpose(0, 2, 1))
    W["flt_w1"] = f(inp["flt_w1"]); W["flt_w2"] = f(inp["flt_w2"]); W["flt_w3"] = f(inp["flt_w3"]); W["flt_b3"] = f(inp["flt_b3"])
    W["flt_c1"] = f(np.stack([inp["flt_freq"], inp["flt_b1"]], -1)); W["flt_c2"] = f(np.stack([inp["flt_freq"], inp["flt_b2"]], -1))
    W["hy_skip"] = f(inp["hy_skip"])
    wd = np.zeros((DEPTH, 33, 256), np.float32)
    wd[:, 0:16, 0:128] = inp["gla_w_decay"][:, 0]; wd[:, 16:32, 128:256] = inp["gla_w_decay"][:, 1]
    wd[:, 32, 0:128] = inp["gla_b_decay"][:, 0]; wd[:, 32, 128:256] = inp["gla_b_decay"][:, 1]
    W["gla_wd"] = wd
    W["ret_gn"] = f(inp["ret_gn"]); W["gla_gn"] = f(inp["gla_gn"])
    W["w_branch"] = f(inp["w_branch"].reshape(DEPTH, 1024, D)); W["w_out"] = f(inp["w_out"])
    W["ffn_up"] = f(inp["ffn_up"])
    W["ffn_cw"] = f(inp["ffn_conv_w"].reshape(DEPTH, 3, 44, 128).transpose(0, 3, 2, 1))
    W["ffn_cb"] = f(inp["ffn_conv_b"].reshape(DEPTH, 44, 128).transpose(0, 2, 1))
    W["ffn_down"] = f(inp["ffn_down"])
    return W


_T = 8192


def kernel(**inp):
    inp = {k: np.asarray(v) for k, v in inp.items()}
    T = _T
    W = prep_weights(inp)
    tbP, tbS = make_tables(T, 'P'), make_tables(T, 'S')
    xp, xs, cp, cs = inp["x_prompt"], inp["x_sample"], inp["c_prompt"], inp["c_sample"]
    in_maps = []
    for b in range(2):
        in_maps.append(prep_core_inputs(xp[b], np.stack([cp[b], cp[b]]), W, tbP))
    for b in range(2):
        in_maps.append(prep_core_inputs(xs[2 * b:2 * b + 2].reshape(T, D), cs[2 * b:2 * b + 2], W, tbS))
    nc = build_program(T)
    res = run_bass_kernel_spmd(nc, in_maps, core_ids=list(range(4)))
    outs = [np.asarray(r["y"], np.float32) for r in res.results]
    y_prompt = np.stack([outs[0], outs[1]], 0)
    y_sample = np.concatenate([outs[2].reshape(2, T // 2, D), outs[3].reshape(2, T // 2, D)], 0)
    return (y_prompt, y_sample)
```

```python
import math
from contextlib import ExitStack
import numpy as np
import ml_dtypes
import concourse.bass as bass
import concourse.mybir as mybir
from concourse.bass_utils import run_bass_kernel_spmd

F32 = mybir.dt.float32
BF16 = mybir.dt.bfloat16
AF = mybir.ActivationFunctionType
ALU = mybir.AluOpType
NPBF = ml_dtypes.bfloat16

D = 1024
DEPTH = 2
DFF = 2816
EPS = 1e-6
MAGIC = 12582912.0
import os
PIPE_DEPTH = int(os.environ.get("PIPE_DEPTH", "2"))
PD_POS = int(os.environ.get("PD_POS", "99"))
CONC_FILT = int(os.environ.get("CONC_FILT", "1"))
USE_POOL = int(os.environ.get("USE_POOL", "0"))
PRIM_STEPS = int(os.environ.get("PRIM_STEPS", "1"))


class Stream:
    def __init__(self, P, inc):
        self.P, self.inc = P, inc
        self.sem = P.new_sem()
        self.count = 0

    def bump(self):
        if self.count + self.inc > 30000:
            self.sem = self.P.new_sem()
            self.count = 0
        self.count += self.inc
        return (self.sem, self.count)

    def cur(self):
        return (self.sem, self.count) if self.count else None


class Buf:
    def __init__(self, name, t=None):
        self.name, self.t = name, t
        self.w = None
        self.r = {}

    def __getitem__(self, idx):
        return self.t[idx]


class Prog:
    def __init__(self, nc, es):
        self.nc, self.es = nc, es
        self.nsem = 0
        self.engs = {'pe': nc.tensor, 'act': nc.scalar, 'dve': nc.vector, 'pool': nc.gpsimd, 'sp': nc.sync}
        self.streams = {k: Stream(self, 1) for k in ('pe', 'act', 'dve', 'pool')}
        self.seen = {k: {} for k in self.engs}
        self.dma_pool = {q: [Stream(self, 16) for _ in range(8)] for q in ('sp', 'pool')}
        self.dma_rr = {q: 0 for q in self.dma_pool}
        self.scopes = [es]
        self.nuniq = 0

    def new_sem(self):
        self.nsem += 1
        return self.es.enter_context(self.nc.semaphore(f"s{self.nsem}"))

    def sb(self, name, shape, dt=F32):
        self.nuniq += 1
        return Buf(name, self.scopes[-1].enter_context(self.nc.sbuf_tensor(f"{name}_{self.nuniq}", shape, dt)))

    def ps(self, name, shape, dt=F32):
        self.nuniq += 1
        return Buf(name, self.scopes[-1].enter_context(self.nc.psum_tensor(f"{name}_{self.nuniq}", shape, dt)))

    def push(self):
        st = ExitStack()
        self.scopes.append(st)
        return st

    def pop(self):
        self.barrier()
        self.scopes.pop().close()

    def _wait(self, eng, tok):
        if tok is None:
            return
        sem, val = tok
        seen = self.seen[eng]
        if seen.get(id(sem), 0) >= val:
            return
        self.engs[eng].wait_ge(sem, val)
        seen[id(sem)] = val

    def barrier(self):
        toks = [s.cur() for s in self.streams.values()]
        for pool in self.dma_pool.values():
            toks += [s.cur() for s in pool]
        for eng in self.engs:
            for t in toks:
                self._wait(eng, t)

    def _deps(self, eng, reads, writes, accum):
        for b in reads:
            self._wait(eng, b.w)
        for b in writes:
            if not accum:
                self._wait(eng, b.w)
            for t in b.r.values():
                self._wait(eng, t)

    def _commit(self, tok, reads, writes):
        for b in writes:
            b.w = tok
            b.r = {}
        for b in reads:
            b.r[id(tok[0])] = tok

    def op(self, eng, fn, reads=(), writes=(), accum=False):
        self._deps(eng, reads, writes, accum)
        inst = fn(self.engs[eng])
        tok = self.streams[eng].bump()
        inst.then_inc(tok[0], 1)
        self._commit(tok, reads, writes)
        return tok

    def dma(self, q, out, in_, reads=(), writes=()):
        pool = self.dma_pool[q]
        st = pool[self.dma_rr[q] % len(pool)]
        self.dma_rr[q] += 1
        self._wait(q, st.cur())
        self._deps(q, reads, writes, False)
        inst = self.engs[q].dma_start(out=out, in_=in_)
        tok = st.bump()
        inst.then_inc(tok[0], 16)
        self._commit(tok, reads, writes)
        return tok


def _cplx_pair(M):
    return np.concatenate([M.real, M.imag], 1), np.concatenate([-M.imag, M.real], 1)


def make_tables(T, kind):
    NT = T // 128
    NS = 2 * NT
    H = NT // 2
    isS = (kind == 'S')
    tb = {}
    tb['ident_b'] = np.eye(128).astype(NPBF)
    tb['ident_f'] = np.eye(128, dtype=np.float32)
    cc = np.arange(64)
    ang = 2 * np.pi * np.outer(cc, cc) / 64
    bdc = np.zeros((128, 128)); bds = np.zeros((128, 128))
    for g in range(2):
        bdc[g * 64:(g + 1) * 64, g * 64:(g + 1) * 64] = np.cos(ang)
        bds[g * 64:(g + 1) * 64, g * 64:(g + 1) * 64] = -np.sin(ang)
    tb['bdcs'] = np.stack([bdc, bds], 1).astype(NPBF)
    n1 = np.arange(NT)
    M1 = np.zeros((NT, NS), np.complex128)
    M2 = np.zeros((NS, 128, 128), np.complex128)
    n2 = np.arange(128)[:, None]
    k2 = np.arange(128)[None, :]
    if not isS:
        L = T
        for j in range(NT):
            M1[:, j] = np.exp(-2j * np.pi * n1 * j / NT)
            M2[j] = np.exp(-2j * np.pi * n2 * (j + NT * k2) / T)
    else:
        L = T // 2
        for s in range(2):
            for j in range(NT):
                M1[s * H:(s + 1) * H, s * NT + j] = np.exp(-2j * np.pi * np.arange(H) * j / H)
                m = np.exp(-2j * np.pi * n2 * (NT * (k2 % 64) + j) / L) * ((k2 // 64) == s)
                M2[s * NT + j] = m
    M2 = M2 / math.sqrt(L * 64)
    a, b = _cplx_pair(M1)
    tb['fn_m1'] = np.stack([a, b], 1).astype(NPBF)
    fm2 = np.zeros((NT, 128, 2, 2, 128), np.float64)
    for s in range(2):
        for j in range(NT):
            fm2[j, :, s, 0] = M2[s * NT + j].real
            fm2[j, :, s, 1] = -M2[s * NT + j].imag
    tb['fn_m2'] = fm2.astype(NPBF)
    HF1 = np.zeros((NS, NS), np.complex128)
    H2 = np.zeros((NS, 128, 128), np.complex128)
    HZ = np.zeros((128, NS, NT), np.complex128)
    if not isS:
        N = 2 * T
        for j in range(NS):
            HF1[:, j] = np.exp(-2j * np.pi * np.arange(NS) * j / NS)
            H2[j] = np.exp(-2j * np.pi * n2 * (j + NS * k2) / N)
        for q in range(128):
            HZ[q] = np.exp(2j * np.pi * np.outer(np.arange(NS), 128 * np.arange(NT) + q) / N) / N
    else:
        N = T
        for s in range(2):
            for j in range(NT):
                HF1[s * NT:(s + 1) * NT, s * NT + j] = np.exp(-2j * np.pi * np.arange(NT) * j / NT)
                H2[s * NT + j] = np.exp(-2j * np.pi * n2 * (j + NT * k2) / N)
        for q in range(128):
            for s in range(2):
                HZ[q, s * NT:(s + 1) * NT, s * H:(s + 1) * H] = \
                    np.exp(2j * np.pi * np.outer(np.arange(NT), 128 * np.arange(H) + q) / N) / N
    if not isS:
        H1 = HF1[:NT]
    else:
        H1 = np.concatenate([HF1[0:H], HF1[NT:NT + H]], 0)
    if not isS:
        act = list(range(NS // 2 + 1)) + [None]
        wts = [1.0 if j in (0, NS // 2) else 2.0 for j in range(NS // 2 + 1)] + [0.0]
    else:
        act = [s * NT + j for s in range(2) for j in range(NT // 2 + 1)]
        wts = [1.0 if j in (0, NT // 2) else 2.0 for s in range(2) for j in range(NT // 2 + 1)]
    def sel(M, axis):
        parts = []
        for a in act:
            if a is None:
                parts.append(np.zeros_like(np.take(M, [0], axis=axis)))
            else:
                parts.append(np.take(M, [a], axis=axis))
        return np.concatenate(parts, axis=axis)
    H1 = sel(H1, 1); HF1 = sel(HF1, 1); H2 = sel(H2, 0)
    HZ = sel(HZ, 1) * np.asarray(wts)[None, :, None]
    tb['hy_h1'] = np.concatenate([H1.real, H1.imag], 1).astype(NPBF)
    tb['hy_hf1'] = np.concatenate([HF1.real, HF1.imag], 1).astype(NPBF)
    tb['hy_h2'] = np.stack([H2.real, H2.imag, -H2.imag], 2).astype(NPBF)
    Fi = np.exp(2j * np.pi * np.outer(np.arange(128), np.arange(128)) / 128)
    a, b = _cplx_pair(Fi)
    tb['hy_i1'] = np.stack([a, b], 1).astype(NPBF)
    tb['hy_z'] = np.stack([HZ.real, -HZ.imag], 2).transpose(1, 0, 2, 3).astype(NPBF).copy()
    Lf = L
    mpos = np.arange(2 * T)
    mloc = mpos % (2 * Lf)
    lag = np.where(mloc < Lf, mloc, 2 * Lf - mloc)
    lag = np.where(mloc == Lf, 0, lag)
    mf = (mloc < Lf).astype(np.float32)
    mb = (mloc > Lf).astype(np.float32)
    tl = np.linspace(0.0, 1.0, Lf, dtype=np.float32)
    wl = (2.0 * np.float32(math.pi) * np.arange(Lf, dtype=np.float32) / np.float32(Lf)).astype(np.float32)
    fb = np.linspace(1e-4, 15, 16, dtype=np.float32)[None, :]
    feat = np.concatenate([tl[:, None], np.cos(fb * wl[:, None]), -np.sin(fb * wl[:, None])], -1).astype(np.float32)
    tb['flt_feat'] = np.ascontiguousarray(feat[lag].T).astype(np.float32)
    tb['flt_msk'] = np.stack([mf, mb, -tl[lag]], 0).astype(np.float32)
    deltas = np.abs(np.linspace(math.log(1e-2) / 0.3, math.log(1e-2) / 1.5, 256, dtype=np.float32))
    tb['flt_delta'] = np.ascontiguousarray(deltas.reshape(2, 128).T).astype(np.float32)
    sc = np.zeros((128, 4), np.float32)
    sc[:, 0] = 0.0 if isS else 1.0
    sc[:, 1] = 0.5 if isS else 1.0
    tb['scal'] = sc
    NB = T // 512
    hal = np.ones((NB, 2), np.float32)
    hal[0, 0] = 0.0; hal[NB - 1, 1] = 0.0
    if isS:
        hal[NB // 2, 0] = 0.0; hal[NB // 2 - 1, 1] = 0.0
    tb['hal'] = np.broadcast_to(hal.reshape(1, NB * 2), (128, NB * 2)).astype(np.float32).copy()
    pos = (np.arange(T) % L).astype(np.float32)
    inv = (10000.0 ** (-np.arange(32, dtype=np.float32) / 32)).astype(np.float32)
    angr = pos[:, None] * inv[None, :]
    tb['rot'] = np.stack([np.cos(angr), np.sin(angr)], 1).astype(np.float32)
    lg = np.log(1.0 - 2.0 ** (-5.0 - np.arange(4)))
    i = np.arange(128)
    ret_e = np.zeros((128, 2, 6, 128), np.float64)
    ret_tok = np.zeros((128, 2, 256), np.float64)
    ret_dec = np.zeros((128, 2, 2), np.float64)
    for c in range(2):
        for p in range(128):
            h = 2 * c + p // 64
            bf = (i + 1) * lg[h]; bb = (128 - i) * lg[h]
            ret_e[p, c, 0] = np.exp(bf - bf[64]) / 8.0
            ret_e[p, c, 1] = np.exp(bf[64] - bf)
            ret_e[p, c, 2] = np.exp(bb - bb[64]) / 8.0
            ret_e[p, c, 3] = np.exp(bb[64] - bb)
            ret_e[p, c, 4] = np.exp(bf) / 8.0
            ret_e[p, c, 5] = np.exp(bb) / 8.0
            ret_dec[p, c, :] = np.exp(128 * lg[h])
    for h in range(4):
        ret_tok[:, 0, h * 64:(h + 1) * 64] = np.exp((127 - i) * lg[h])[:, None]
        ret_tok[:, 1, h * 64:(h + 1) * 64] = np.exp(i * lg[h])[:, None]
    tb['ret_e'] = ret_e.astype(np.float32)
    tb['ret_tok'] = ret_tok.astype(np.float32)
    tb['ret_dec'] = ret_dec.astype(np.float32)
    mh_ret = np.zeros((128, 2), np.float32); mh_gla = np.zeros((128, 4), np.float32)
    bd_ret = np.zeros((128, 2, 256), np.float32); bd_gla = np.zeros((128, 1, 256), np.float32)
    for p in range(128):
        mh_ret[p, p // 64] = 1.0; mh_gla[p, p // 32] = 1.0
        for c in range(2):
            h = 2 * c + p // 64
            bd_ret[p, c, h * 64:(h + 1) * 64] = 1.0
        h = p // 32
        bd_gla[p, 0, h * 64:(h + 1) * 64] = 1.0
    tb['mh_ret'] = mh_ret; tb['mh_gla'] = mh_gla; tb['bd_ret'] = bd_ret; tb['bd_gla'] = bd_gla
    jj = np.arange(128)[:, None]; ii = np.arange(128)[None, :]
    tri = np.zeros((128, 6, 128), np.float32)
    tri[:, 0] = (jj <= ii)
    tri[:, 1] = (jj > ii)
    tri[:, 2] = -(jj <= ii).astype(np.float32) / 16.0
    tri[:, 3] = -(jj >= ii).astype(np.float32) / 16.0
    tri[:, 4] = -(jj > ii).astype(np.float32) / 16.0
    tri[:, 5] = -(jj < ii).astype(np.float32) / 16.0
    tb['tri'] = tri
    return tb


OFF_FN, OFF_QR, OFF_HY, OFF_QG, OFF_GATES = 0, 256, 1280, 2048, 2848
NTM = 1824


def build_program(T, debug=()):
    NT, NS, NB, H = T // 128, T // 64, T // 512, T // 256
    NSA = NT + 2
    nc = bass.Bass("TRN2", target_bir_lowering=False)
    I = {}

    def inp(name, shape, dt=F32):
        I[name] = nc.dram_tensor(name, list(shape), dt, kind="ExternalInput").ap()
        return I[name]

    def scratch(name, shape, dt=F32):
        kind = "ExternalOutput" if name in debug else "Internal"
        return nc.dram_tensor(name, list(shape), dt, kind=kind).ap()

    x_in = inp("x", [T, D]); inp("cT", [128, 8, 2])
    inp("ada_w", [DEPTH, D, 6 * D]); inp("ada_b_col", [DEPTH, 128, 48]); inp("ada_b", [DEPTH, 6 * D])
    inp("normw_col", [DEPTH, 128, 4, 8]); inp("norm_post_mix", [DEPTH, D]); inp("norm_post_ffn", [DEPTH, D])
    inp("w_in", [DEPTH, D, 6944])
    inp("hy_cw", [DEPTH, 128, 6, 3]); inp("hy_cb", [DEPTH, 128, 6])
    inp("flt_w1", [DEPTH, 33, 64]); inp("flt_c1", [DEPTH, 64, 2]); inp("flt_w2", [DEPTH, 64, 64]); inp("flt_c2", [DEPTH, 64, 2])
    inp("flt_w3", [DEPTH, 64, 1024]); inp("flt_b3", [DEPTH, 1024]); inp("hy_skip", [DEPTH, 2, 256])
    inp("gla_wd", [DEPTH, 33, 256]); inp("ret_gn", [DEPTH, 256]); inp("gla_gn", [DEPTH, 256])
    inp("w_branch", [DEPTH, 1024, D]); inp("w_out", [DEPTH, D, D])
    inp("ffn_up", [DEPTH, D, 2 * DFF]); inp("ffn_cw", [DEPTH, 128, 44, 3]); inp("ffn_cb", [DEPTH, 128, 44])
    inp("ffn_down", [DEPTH, DFF, D])
    for nm, shp, dt in (("ident_b", [128, 128], BF16), ("ident_f", [128, 128], F32), ("bdcs", [128, 2, 128], BF16),
                        ("fn_m1", [NT, 2, 2 * NS], BF16), ("fn_m2", [NT, 128, 2, 2, 128], BF16),
                        ("hy_h1", [NT, 2 * NSA], BF16), ("hy_hf1", [NS, 2 * NSA], BF16), ("hy_h2", [NSA, 128, 3, 128], BF16),
                        ("hy_i1", [128, 2, 256], BF16), ("hy_z", [NSA, 128, 2, NT], BF16),
                        ("flt_feat", [33, 2 * T], F32), ("flt_msk", [3, 2 * T], F32), ("flt_delta", [128, 2], F32),
                        ("scal", [128, 4], F32), ("hal", [128, NB * 2], F32), ("rot", [T, 2, 32], F32),
                        ("ret_e", [128, 2, 6, 128], F32), ("ret_tok", [128, 2, 256], F32), ("ret_dec", [128, 2, 2], F32),
                        ("mh_ret", [128, 2], F32), ("mh_gla", [128, 4], F32), ("bd_ret", [128, 2, 256], F32),
                        ("bd_gla", [128, 1, 256], F32), ("tri", [128, 6, 128], F32)):
        inp(nm, shp, dt)
    y_out = nc.dram_tensor("y", [T, D], F32, kind="ExternalOutput").ap()
    x1d = scratch("x1d", [T, D]); xmid = scratch("xmid", [T, D])
    projT = scratch("projT", [T, NTM]); zF = scratch("zF", [2, 256, T], BF16); uhF = scratch("uhF", [768, T])
    brF = scratch("brF", [1024, T], BF16); modrow = scratch("modrow", [2, 2048]); hF = scratch("hF", [D, T + 2], BF16); gFF = scratch("gFF", [DFF, T], BF16)
    gF = scratch("gF", [512, 2 * T], BF16); rnD = scratch("rnD", [512]); Gd = scratch("Gd", [16, 128, NSA, 64], BF16)

    es = ExitStack()
    with es:
        es.enter_context(nc.allow_non_contiguous_dma(reason="strided scratch layouts"))
        P = Prog(nc, es)
        dB = {k: Buf(k) for k in ("x1d", "xmid", "projT", "zF", "uhF", "brF", "modrow", "gF", "rnD", "Gd", "y", "in", "hF", "gFF")}
        IN = dB["in"]

        def ld(dst_buf, dst_ap, src_ap, src=IN):
            return P.dma('sp', dst_ap, src_ap, reads=[src], writes=[dst_buf])

        def stq(dst_ap, src_buf, src_ap, dst):
            return P.dma('pool', dst_ap, src_ap, reads=[src_buf], writes=[dst])

        def dve(fn, r, w):
            return P.op('dve', fn, r, w)

        def act(fn, r, w):
            return P.op('act', fn, r, w)

        def gps(fn, r, w):
            return P.op('pool' if USE_POOL else 'dve', fn, r, w)

        def pe(fn, r, w):
            return P.op('pe', fn, r, w, accum=True)

        identb = P.sb("identb", [128, 128], BF16); identf = P.sb("identf", [128, 128], F32)
        scal = P.sb("scal", [128, 4]); hal = P.sb("hal", [128, NB * 2]); epsb = P.sb("epsb", [128, 1])
        modc = P.sb("modc", [128, 48, 2]); Am = P.sb("Am", [128, 8, 2]); Af = P.sb("Af", [128, 8, 2])
        gtb = P.sb("gtb", [128, 2, 2, D])
        ld(identb, identb[:], I["ident_b"][:, :]); ld(identf, identf[:], I["ident_f"][:, :])
        ld(scal, scal[:], I["scal"][:, :]); ld(hal, hal[:], I["hal"][:, :])
        dve(lambda e: e.memset(epsb[:], EPS), [], [epsb])

        def phase_mod(l):
            P.push()
            cT = P.sb("cT", [128, 8, 2]); scT = P.sb("scT", [128, 8, 2])
            ld(cT, cT[:], I["cT"][:, :, :])
            act(lambda e: e.activation(out=scT[:], in_=cT[:], func=AF.Silu), [cT], [scT])
            psc = P.ps("psc", [128, 96]); psr = P.ps("psr", [2, 2048]); macc = P.sb("macc", [128, 96])
            wts = [P.sb(f"adaw{i}", [128, 6 * D]) for i in range(2)]
            rowcols = (2048, 2560, 5120, 5632)
            for kc in range(8):
                wt = wts[kc % 2]
                ld(wt, wt[:], I["ada_w"][l, kc * 128:(kc + 1) * 128, :])
                for q in range(48):
                    pe(lambda e, q=q: e.matmul(psc[:, 2 * q:2 * q + 2], lhsT=wt[:, q * 128:(q + 1) * 128], rhs=scT[:, kc, :],
                                               start=True, stop=True), [wt, scT], [psc])
                if kc == 0:
                    dve(lambda e: e.tensor_copy(out=macc[:], in_=psc[:]), [psc], [macc])
                else:
                    dve(lambda e: e.tensor_tensor(out=macc[:], in0=macc[:], in1=psc[:], op=ALU.add), [psc, macc], [macc])
                for bi, c0 in enumerate(rowcols):
                    pe(lambda e, bi=bi, c0=c0: e.matmul(psr[:, bi * 512:(bi + 1) * 512], lhsT=scT[:, kc, :], rhs=wt[:, c0:c0 + 512],
                                                        start=(kc == 0), stop=(kc == 7)), [wt, scT], [psr])
            abc = P.sb("abc", [128, 48]); nwc = P.sb("nwc", [128, 4, 8])
            ld(abc, abc[:], I["ada_b_col"][l]); ld(nwc, nwc[:], I["normw_col"][l])
            dve(lambda e: e.tensor_tensor(out=modc[:], in0=macc[:].rearrange("p (q s) -> p q s", s=2),
                                          in1=abc[:].unsqueeze(2).broadcast_to([128, 48, 2]), op=ALU.add), [macc, abc], [modc])
            dve(lambda e: e.scalar_tensor_tensor(out=Am[:], in0=modc[:, 8:16, :], scalar=1.0,
                                                 in1=nwc[:, 0, :].unsqueeze(2).broadcast_to([128, 8, 2]), op0=ALU.add, op1=ALU.mult),
                [modc, nwc], [Am])
            dve(lambda e: e.scalar_tensor_tensor(out=Af[:], in0=modc[:, 32:40, :], scalar=1.0,
                                                 in1=nwc[:, 2, :].unsqueeze(2).broadcast_to([128, 8, 2]), op0=ALU.add, op1=ALU.mult),
                [modc, nwc], [Af])
            abr = P.sb("abr", [2, 2048]); nwr = P.sb("nwr", [2, 2048]); gr = P.sb("gr", [2, 2048])
            ld(abr, abr[:, 0:1024], I["ada_b"][l, 2048:3072].partition_broadcast(2))
            ld(abr, abr[:, 1024:2048], I["ada_b"][l, 5120:6144].partition_broadcast(2))
            ld(nwr, nwr[:, 0:1024], I["norm_post_mix"][l].partition_broadcast(2))
            ld(nwr, nwr[:, 1024:2048], I["norm_post_ffn"][l].partition_broadcast(2))
            dve(lambda e: e.tensor_tensor(out=gr[:], in0=psr[:], in1=abr[:], op=ALU.add), [psr, abr], [gr])
            dve(lambda e: e.tensor_tensor(out=gr[:], in0=gr[:], in1=nwr[:], op=ALU.mult), [gr, nwr], [gr])
            stq(modrow[:, :], gr, gr[:], dB["modrow"])
            for sg in range(2):
                ld(gtb, gtb[:, sg, :, :].rearrange("p a d -> p (a d)"), modrow[sg].partition_broadcast(128), src=dB["modrow"])
            P.pop()

        def phase_norm(src_ap2d, src_buf, A, bq0):
            P.push()
            zt = P.sb("zt", [128, 8, 1], BF16)
            dve(lambda e: e.memset(zt[:], 0.0), [], [zt])
            hFv = hF.rearrange("(c p) t -> p c t", p=128)
            stq(hFv[:, :, 0:1], zt, zt[:], dB["hF"]); stq(hFv[:, :, T + 1:T + 2], zt, zt[:], dB["hF"])
            xts = [P.sb(f"xt{i}", [128, D]) for i in range(2)]
            sq = P.sb("sq", [128, D]); ss = P.sb("ss", [128, 1]); rs = P.sb("rs", [128, 1])
            xn = P.sb("xn", [128, D], BF16); ptr = P.ps("ptr", [128, 8, 128], BF16); tmp = P.sb("tmpT", [128, 8, 128])
            hts = [P.sb(f"ht{i}", [128, 8, 512], BF16) for i in range(2)]
            xns = [P.sb(f"xnN{i}", [128, D], BF16) for i in range(2)]

            def gen(it):
                xt, ht, xn = xts[it % 2], hts[(it // 4) % 2], xns[it % 2]
                q4 = it % 4
                seg = 0 if it < NT // 2 else 1
                ld(xt, xt[:], src_ap2d[it * 128:(it + 1) * 128, :], src=src_buf)
                act(lambda e: e.activation(out=sq[:], in_=xt[:], func=AF.Square, accum_out=ss[:, 0:1]), [xt], [sq, ss])
                act(lambda e: e.activation(out=rs[:], in_=ss[:], func=AF.Sqrt, scale=1.0 / D, bias=epsb[:, 0:1]), [ss, epsb], [rs])
                yield
                dve(lambda e: e.reciprocal(out=rs[:], in_=rs[:]), [rs], [rs])
                dve(lambda e: e.tensor_scalar(out=xn[:], in0=xt[:], scalar1=rs[:, 0:1], scalar2=None, op0=ALU.mult), [xt, rs], [xn])
                yield 'prev_done'
                for c8 in range(8):
                    pe(lambda e, c8=c8: e.transpose(ptr[:, c8, :], xn[:, c8 * 128:(c8 + 1) * 128], identb[:]), [xn, identb], [ptr])
                yield
                dve(lambda e: e.tensor_tensor(out=tmp[:], in0=ptr[:], in1=A[:, :, seg:seg + 1].broadcast_to([128, 8, 128]), op=ALU.mult),
                    [ptr, A], [tmp])
                dve(lambda e: e.tensor_tensor(out=ht[:, :, q4 * 128:(q4 + 1) * 128], in0=tmp[:], in1=modc[:, bq0:bq0 + 8, seg:seg + 1].broadcast_to([128, 8, 128]), op=ALU.add),
                    [tmp, modc], [ht])
                if q4 == 3:
                    stq(hFv[:, :, 1 + (it - 3) * 128:1 + (it + 1) * 128], ht, ht[:], dB["hF"])

            run_pipelined(gen, range(NT))
            P.pop()

        def load_w_bf16(dst, dst_ap_fn, src_ap_fn, nk, width, stg):
            for k in range(nk):
                st = stg[k % 2]
                ld(st, st[:, 0:width], src_ap_fn(k))
                if k % 2 == 0:
                    dve(lambda e, k=k, st=st: e.tensor_copy(out=dst_ap_fn(k), in_=st[:, 0:width]), [st], [dst])
                else:
                    act(lambda e, k=k, st=st: e.activation(out=dst_ap_fn(k), in_=st[:, 0:width], func=AF.Copy), [st], [dst])

        def load_window(hw, b):
            hFv = hF.rearrange("(c p) t -> p c t", p=128)
            ld(hw, hw[:], hFv[:, :, b * 512:b * 512 + 514], src=dB["hF"])
            for side, col in ((0, 0), (1, 513)):
                dve(lambda e, side=side, col=col: e.tensor_tensor(out=hw[:, :, col:col + 1], in0=hw[:, :, col:col + 1],
                                                                  in1=hal[:, 2 * b + side:2 * b + side + 1].unsqueeze(1).broadcast_to([128, 8, 1]),
                                                                  op=ALU.mult), [hw, hal], [hw])

        def conv3_fm(out_ap, pm, ph, cw, cb, ci, rbufs, wbuf):
            act(lambda e: e.activation(out=out_ap, in_=pm[:, 0:512], func=AF.Identity, scale=cw[:, ci, 1:2], bias=cb[:, ci:ci + 1]), rbufs, [wbuf])
            dve(lambda e: e.scalar_tensor_tensor(out=out_ap[:, 1:512], in0=pm[:, 0:511], scalar=cw[:, ci, 0:1], in1=out_ap[:, 1:512],
                                                 op0=ALU.mult, op1=ALU.add), rbufs + [wbuf], [wbuf])
            dve(lambda e: e.scalar_tensor_tensor(out=out_ap[:, 0:511], in0=pm[:, 1:512], scalar=cw[:, ci, 2:3], in1=out_ap[:, 0:511],
                                                 op0=ALU.mult, op1=ALU.add), rbufs + [wbuf], [wbuf])
            dve(lambda e: e.scalar_tensor_tensor(out=out_ap[:, 0:1], in0=ph[:, 0:1], scalar=cw[:, ci, 0:1], in1=out_ap[:, 0:1],
                                                 op0=ALU.mult, op1=ALU.add), rbufs + [wbuf], [wbuf])
            dve(lambda e: e.scalar_tensor_tensor(out=out_ap[:, 511:512], in0=ph[:, 1:2], scalar=cw[:, ci, 2:3], in1=out_ap[:, 511:512],
                                                 op0=ALU.mult, op1=ALU.add), rbufs + [wbuf], [wbuf])

        def phase_A(l):
            P.push()
            wA = P.sb("wA", [128, 8, OFF_GATES], BF16)
            stg = [P.sb(f"stgA{i}", [128, OFF_GATES]) for i in range(2)]
            load_w_bf16(wA, lambda k: wA[:, k, :], lambda k: I["w_in"][l, k * 128:(k + 1) * 128, 0:OFF_GATES], 8, OFF_GATES, stg)
            bdcs = P.sb("bdcs", [128, 2, 128], BF16); ld(bdcs, bdcs[:], I["bdcs"][:, :, :])
            cw = P.sb("hcw", [128, 6, 3]); cb = P.sb("hcb", [128, 6])
            ld(cw, cw[:], I["hy_cw"][l]); ld(cb, cb[:], I["hy_cb"][l])
            hws = [P.sb(f"hw{i}", [128, 8, 514], BF16) for i in range(2)]
            pms = [P.ps(f"pmA{i}", [128, 512]) for i in range(2)]
            ph = P.ps("phA", [128, 2])
            pjs = [P.sb(f"pj{i}", [128, NTM]) for i in range(2)]; uT = P.sb("uT", [128, 2, 512], BF16)
            zts = [P.sb(f"ztA{i}", [128, 512], BF16) for i in range(2)]
            cvs = [P.sb(f"cvA{i}", [128, 512]) for i in range(2)]
            tmcols = ((256, 512), (768, 512), (2048, 512), (2560, 288))
            zFv = zF
            npm = 0
            load_window(hws[0], 0)
            for b in range(NB):
                hw = hws[b % 2]
                if b + 1 < NB:
                    load_window(hws[(b + 1) % 2], b + 1)
                t0 = b * 512
                for s in range(4):
                    o = 0
                    pj = pjs[s % 2]
                    for (c0, wd) in tmcols:
                        pm = pms[npm % 2]; npm += 1
                        for kc in range(8):
                            pe(lambda e, kc=kc, pm=pm, c0=c0, wd=wd: e.matmul(pm[:, 0:wd], lhsT=hw[:, kc, 1 + s * 128:1 + (s + 1) * 128],
                                                                                rhs=wA[:, kc, c0:c0 + wd], start=(kc == 0), stop=(kc == 7)),
                               [hw, wA], [pm])
                        if (npm % 2) == 0:
                            dve(lambda e, pm=pm, o=o, wd=wd, pj=pj: e.tensor_copy(out=pj[:, o:o + wd], in_=pm[:, 0:wd]), [pm], [pj])
                        else:
                            act(lambda e, pm=pm, o=o, wd=wd, pj=pj: e.activation(out=pj[:, o:o + wd], in_=pm[:, 0:wd], func=AF.Copy), [pm], [pj])
                        o += wd
                        yield
                    stq(projT[t0 + s * 128:t0 + (s + 1) * 128, :], pj, pj[:], dB["projT"])
                for ch in range(2):
                    pm = pms[npm % 2]; npm += 1
                    for kc in range(8):
                        pe(lambda e, kc=kc, pm=pm, ch=ch: e.matmul(pm[:], lhsT=wA[:, kc, ch * 128:(ch + 1) * 128], rhs=hw[:, kc, 1:513],
                                                                     start=(kc == 0), stop=(kc == 7)), [hw, wA], [pm])
                    act(lambda e, pm=pm, ch=ch: e.activation(out=uT[:, ch, :], in_=pm[:], func=AF.Copy), [pm], [uT])
                    yield
                for ri in range(2):
                    for ch in range(2):
                        pm = pms[npm % 2]; zt = zts[npm % 2]; npm += 1
                        pe(lambda e, pm=pm, ri=ri, ch=ch: e.matmul(pm[:], lhsT=bdcs[:, ri, :], rhs=uT[:, ch, :], start=True, stop=True), [bdcs, uT], [pm])
                        act(lambda e, pm=pm, zt=zt: e.activation(out=zt[:], in_=pm[:], func=AF.Copy), [pm], [zt])
                        stq(zFv[ri, ch * 128:(ch + 1) * 128, t0:t0 + 512], zt, zt[:], dB["zF"])
                        yield
                for ch in range(6):
                    pm = pms[npm % 2]; cv = cvs[npm % 2]; npm += 1
                    c0 = OFF_HY + ch * 128
                    for kc in range(8):
                        pe(lambda e, kc=kc, pm=pm, c0=c0: e.matmul(pm[:], lhsT=wA[:, kc, c0:c0 + 128], rhs=hw[:, kc, 1:513],
                                                                     start=(kc == 0), stop=(kc == 7)), [hw, wA], [pm])
                    for kc in range(8):
                        pe(lambda e, kc=kc, c0=c0: e.matmul(ph[:], lhsT=wA[:, kc, c0:c0 + 128], rhs=hw[:, kc, 0:514:513],
                                                              start=(kc == 0), stop=(kc == 7)), [hw, wA], [ph])
                    conv3_fm(cv[:], pm, ph, cw, cb, ch, [pm, ph, cw, cb], cv)
                    stq(uhF[ch * 128:(ch + 1) * 128, t0:t0 + 512], cv, cv[:], dB["uhF"])
                    yield
            yield 'finished'
            P.pop()

        def rms_residual(py, xt, sq, ss, rs, tmpo, outt, seg, which, dst_ap, dst_buf):
            act(lambda e: e.activation(out=sq[:], in_=py[:], func=AF.Square, accum_out=ss[:, 0:1]), [py], [sq, ss])
            act(lambda e: e.activation(out=rs[:], in_=ss[:], func=AF.Sqrt, scale=1.0 / D, bias=epsb[:, 0:1]), [ss, epsb], [rs])
            dve(lambda e: e.reciprocal(out=rs[:], in_=rs[:]), [rs], [rs])
            dve(lambda e: e.scalar_tensor_tensor(out=tmpo[:], in0=py[:], scalar=rs[:, 0:1], in1=gtb[:, seg, which, :], op0=ALU.mult, op1=ALU.mult),
                [py, rs, gtb], [tmpo])
            dve(lambda e: e.tensor_tensor(out=outt[:], in0=tmpo[:], in1=xt[:], op=ALU.add), [tmpo, xt], [outt])
            stq(dst_ap, outt, outt[:], dst_buf)

        def phase_C(l, src_ap2d, src_buf):
            P.push()
            wbr = P.sb("wbr", [128, 8, D], BF16); wout = P.sb("wout", [128, 8, D], BF16); wg = P.sb("wg", [128, 8, 4 * D], BF16)
            stg = [P.sb(f"stgC{i}", [128, 4 * D]) for i in range(2)]
            load_w_bf16(wbr, lambda k: wbr[:, k, :], lambda k: I["w_branch"][l, k * 128:(k + 1) * 128, :], 8, D, stg)
            load_w_bf16(wout, lambda k: wout[:, k, :], lambda k: I["w_out"][l, k * 128:(k + 1) * 128, :], 8, D, stg)
            load_w_bf16(wg, lambda k: wg[:, k, :], lambda k: I["w_in"][l, k * 128:(k + 1) * 128, OFF_GATES:OFF_GATES + 4 * D], 8, 4 * D, stg)
            xts = [P.sb(f"xtC{i}", [128, D]) for i in range(2)]
            hts = [P.sb(f"htC{i}", [128, 8, 128], BF16) for i in range(2)]
            brs = [P.sb(f"brC{i}", [128, 8, 128], BF16) for i in range(2)]
            pgs = [P.ps(f"pg{i}", [128, 512]) for i in range(2)]; pbs = [P.ps(f"pb{i}", [128, 512]) for i in range(2)]
            py = P.ps("py", [128, D]); ptr = P.ps("ptrC", [128, 8, 128], BF16)
            sigs = [P.sb(f"sig{i}", [128, 512]) for i in range(2)]; tms = [P.sb(f"tmc{i}", [128, 512]) for i in range(2)]
            merged = P.sb("merged", [128, D]); tmpm = P.sb("tmpm", [128, D]); mb = P.sb("mb", [128, D], BF16)
            mT = P.sb("mT", [128, 8, 128], BF16); sq = P.sb("sqC", [128, D]); ss = P.sb("ssC", [128, 1]); rs = P.sb("rsC", [128, 1])
            outt = P.sb("outC", [128, D])
            hFv = hF.rearrange("(c p) t -> p c t", p=128); brv = brF.rearrange("(k p) t -> p k t", p=128)
            mergeds = [merged, P.sb("merged1", [128, D])]

            def gen(it):
                merged = mergeds[it % 2]
                xt, ht, brt = xts[it % 2], hts[it % 2], brs[it % 2]
                seg = 0 if it < NT // 2 else 1
                ld(xt, xt[:], src_ap2d[it * 128:(it + 1) * 128, :], src=src_buf)
                ld(ht, ht[:], hFv[:, :, 1 + it * 128:1 + (it + 1) * 128], src=dB["hF"])
                ld(brt, brt[:], brv[:, :, it * 128:(it + 1) * 128], src=dB["brF"])
                for br in range(4):
                    for cb in range(2):
                        u = br * 2 + cb
                        pgu, pbu, sgu, tmu = pgs[u % 2], pbs[u % 2], sigs[u % 2], tms[u % 2]
                        for kc in range(8):
                            pe(lambda e, kc=kc, cb=cb, pgu=pgu, br=br: e.matmul(pgu[:], lhsT=ht[:, kc, :],
                                                                               rhs=wg[:, kc, br * D + cb * 512:br * D + (cb + 1) * 512], start=(kc == 0), stop=(kc == 7)),
                               [ht, wg], [pgu])
                        for k2 in range(2):
                            pe(lambda e, k2=k2, cb=cb, pbu=pbu, br=br: e.matmul(pbu[:], lhsT=brt[:, br * 2 + k2, :],
                                                                               rhs=wbr[:, br * 2 + k2, cb * 512:(cb + 1) * 512], start=(k2 == 0), stop=(k2 == 1)),
                               [brt, wbr], [pbu])
                        act(lambda e, pgu=pgu, sgu=sgu: e.activation(out=sgu[:], in_=pgu[:], func=AF.Sigmoid), [pgu], [sgu])
                        mslice = merged[:, cb * 512:(cb + 1) * 512]
                        if br == 0:
                            dve(lambda e, sgu=sgu, pbu=pbu, mslice=mslice: e.tensor_tensor(out=mslice, in0=sgu[:], in1=pbu[:], op=ALU.mult), [sgu, pbu], [merged])
                        else:
                            dve(lambda e, sgu=sgu, pbu=pbu, tmu=tmu: e.tensor_tensor(out=tmu[:], in0=sgu[:], in1=pbu[:], op=ALU.mult), [sgu, pbu], [tmu])
                            dve(lambda e, tmu=tmu, mslice=mslice: e.tensor_tensor(out=mslice, in0=mslice, in1=tmu[:], op=ALU.add), [merged, tmu], [merged])
                        yield
                yield 'prev_done'
                act(lambda e: e.activation(out=mb[:], in_=merged[:], func=AF.Copy), [merged], [mb])
                yield
                for c8 in range(8):
                    pe(lambda e, c8=c8: e.transpose(ptr[:, c8, :], mb[:, c8 * 128:(c8 + 1) * 128], identb[:]), [mb, identb], [ptr])
                yield
                dve(lambda e: e.tensor_copy(out=mT[:], in_=ptr[:]), [ptr], [mT])
                yield
                for cb in range(2):
                    for kc in range(8):
                        pe(lambda e, kc=kc, cb=cb: e.matmul(py[:, cb * 512:(cb + 1) * 512], lhsT=mT[:, kc, :], rhs=wout[:, kc, cb * 512:(cb + 1) * 512],
                                                           start=(kc == 0), stop=(kc == 7)), [mT, wout], [py])
                rms_residual(py, xt, sq, ss, rs, tmpm, outt, seg, 0, xmid[it * 128:(it + 1) * 128, :], dB["xmid"])
            run_pipelined(gen, range(NT))
            P.pop()

        def phase_D(l, dst_ap2d, dst_buf):
            P.push()
            wup = P.sb("wup", [128, 8, 2 * DFF], BF16)
            stg = [P.sb(f"stgD{i}", [128, 1408]) for i in range(2)]
            for part in range(4):
                c0 = part * 1408
                load_w_bf16(wup, lambda k, c0=c0: wup[:, k, c0:c0 + 1408], lambda k, c0=c0: I["ffn_up"][l, k * 128:(k + 1) * 128, c0:c0 + 1408], 8, 1408, stg)
            cw = P.sb("fcw", [128, 44, 3]); cb_ = P.sb("fcb", [128, 44])
            ld(cw, cw[:], I["ffn_cw"][l]); ld(cb_, cb_[:], I["ffn_cb"][l])
            hws = [P.sb(f"hwD{i}", [128, 8, 514], BF16) for i in range(2)]
            pms = [P.ps(f"pmD{i}", [128, 512]) for i in range(4)]; phs = [P.ps(f"phD{i}", [128, 2]) for i in range(4)]
            cvs = [P.sb(f"cvD{i}", [128, 512]) for i in range(4)]; gls = [P.sb(f"glD{i}", [128, 512]) for i in range(2)]
            gts = [P.sb(f"gtD{i}", [128, 512], BF16) for i in range(2)]
            load_window(hws[0], 0)
            for b in range(NB):
                hw = hws[b % 2]
                if b + 1 < NB:
                    load_window(hws[(b + 1) % 2], b + 1)
                for pc in range(22):
                    for wi in range(2):
                        ci = pc + 22 * wi
                        bi = 2 * (pc % 2) + wi
                        pm, ph, cv = pms[bi], phs[bi], cvs[bi]
                        for kc in range(8):
                            pe(lambda e, kc=kc, pm=pm, ci=ci: e.matmul(pm[:], lhsT=wup[:, kc, ci * 128:(ci + 1) * 128], rhs=hw[:, kc, 1:513],
                                                                         start=(kc == 0), stop=(kc == 7)), [hw, wup], [pm])
                        for kc in range(8):
                            pe(lambda e, kc=kc, ph=ph, ci=ci: e.matmul(ph[:], lhsT=wup[:, kc, ci * 128:(ci + 1) * 128], rhs=hw[:, kc, 0:514:513],
                                                                         start=(kc == 0), stop=(kc == 7)), [hw, wup], [ph])
                        conv3_fm(cv[:], pm, ph, cw, cb_, ci, [pm, ph, cw, cb_], cv)
                    gt = gts[pc % 2]; gl = gls[pc % 2]; cg_, cv_ = cvs[2 * (pc % 2)], cvs[2 * (pc % 2) + 1]
                    act(lambda e, gl=gl, cg_=cg_: e.activation(out=gl[:], in_=cg_[:], func=AF.Gelu_apprx_tanh), [cg_], [gl])
                    P.op('pool', lambda e, gt=gt, gl=gl, cv_=cv_: e.tensor_tensor(out=gt[:], in0=gl[:], in1=cv_[:], op=ALU.mult), [gl, cv_], [gt])
                    stq(gFF[pc * 128:(pc + 1) * 128, b * 512:(b + 1) * 512], gt, gt[:], dB["gFF"])
            P.pop()
            P.push()
            wdn = P.sb("wdn", [128, 22, D], BF16)
            stg = [P.sb(f"stgE{i}", [128, D]) for i in range(2)]
            load_w_bf16(wdn, lambda k: wdn[:, k, :], lambda k: I["ffn_down"][l, k * 128:(k + 1) * 128, :], 22, D, stg)
            xts = [P.sb(f"xtE{i}", [128, D]) for i in range(2)]
            ggs = [P.sb(f"ggE{i}", [128, 22, 512], BF16) for i in range(2)]
            pys = [P.ps(f"pyE{i}", [128, D]) for i in range(2)]; sq = P.sb("sqE", [128, D])
            sss = [P.sb(f"ssE{i}", [128, 1]) for i in range(2)]; rss = [P.sb(f"rsE{i}", [128, 1]) for i in range(2)]
            tmpos = [P.sb(f"tmpE{i}", [128, D]) for i in range(2)]; outts = [P.sb(f"outE{i}", [128, D]) for i in range(2)]
            gv = gFF.rearrange("(k p) t -> p k t", p=128)
            for it in range(NT):
                xt, gg = xts[it % 2], ggs[(it // 4) % 2]
                py, ss, rs, tmpo, outt = pys[it % 2], sss[it % 2], rss[it % 2], tmpos[it % 2], outts[it % 2]
                q4 = it % 4
                seg = 0 if it < NT // 2 else 1
                ld(xt, xt[:], xmid[it * 128:(it + 1) * 128, :], src=dB["xmid"])
                if q4 == 0:
                    ld(gg, gg[:], gv[:, :, it * 128:(it + 4) * 128], src=dB["gFF"])
                for cb in range(2):
                    for pc in range(22):
                        pe(lambda e, pc=pc, cb=cb: e.matmul(py[:, cb * 512:(cb + 1) * 512], lhsT=gg[:, pc, q4 * 128:(q4 + 1) * 128], rhs=wdn[:, pc, cb * 512:(cb + 1) * 512],
                                                           start=(pc == 0), stop=(pc == 21)), [gg, wdn], [py])
                rms_residual(py, xt, sq, ss, rs, tmpo, outt, seg, 1, dst_ap2d[it * 128:(it + 1) * 128, :], dst_buf)
            P.pop()

        def phase_fnet(l):
            P.push()
            Cg = 64
            m1 = P.sb("fm1", [NT, 2, 2 * NS], BF16); ld(m1, m1[:], I["fn_m1"][:, :, :])
            zin = P.sb("zin", [NT, 2, Cg, 128], BF16); A = P.sb("fA", [128, NS, 2, Cg], BF16)
            osb = P.sb("osb", [Cg, T], BF16); osbv = osb[:].rearrange("c (k j) -> c k j", j=NT)
            ps1s = [P.ps(f"fps1{i}", [128, 2, 2 * NS]) for i in range(2)]
            ps2s = [P.ps(f"fps2{i}", [Cg, 4, 128]) for i in range(2)]
            m2s = [P.sb(f"fm2{i}", [128, 4, 2, 2, 128], BF16) for i in range(2)]
            n1 = 0
            for g in range(256 // Cg):
                c0 = g * Cg
                for ri in range(2):
                    ld(zin, zin[:, ri, :, :], zF[ri, c0:c0 + Cg, :].rearrange("c (g n) -> g c n", n=128), src=dB["zF"])
                for c in range(0, Cg, 2):
                    ps1 = ps1s[n1 % 2]; n1 += 1
                    for cc in range(2):
                        pe(lambda e, cc=cc, ps1=ps1: e.matmul(ps1[:, cc, :], lhsT=zin[:, 0, c + cc, :], rhs=m1[:, 0, :], start=True, stop=False), [zin, m1], [ps1])
                        pe(lambda e, cc=cc, ps1=ps1: e.matmul(ps1[:, cc, :], lhsT=zin[:, 1, c + cc, :], rhs=m1[:, 1, :], start=False, stop=True), [zin, m1], [ps1])
                    for ri in range(2):
                        src_ap = ps1[:, :, ri * NS:(ri + 1) * NS].rearrange("p c s -> p s c")
                        if ri == 0:
                            dve(lambda e, src_ap=src_ap: e.tensor_copy(out=A[:, :, 0, c:c + 2], in_=src_ap), [ps1], [A])
                        else:
                            act(lambda e, src_ap=src_ap: e.activation(out=A[:, :, 1, c:c + 2], in_=src_ap, func=AF.Copy), [ps1], [A])
                m2v = I["fn_m2"].rearrange("j p s r k -> p j s r k")
                for j in range(NT):
                    m2t = m2s[(j // 4) % 2]
                    if j % 4 == 0:
                        ld(m2t, m2t[:], m2v[:, j:j + 4, :, :, :])
                    ps2 = ps2s[(j // 4) % 2]
                    k = 0
                    for s_ in range(2):
                        for ri in range(2):
                            pe(lambda e, s_=s_, ri=ri, k=k, ps2=ps2, m2t=m2t: e.matmul(ps2[:, j % 4, :], lhsT=A[:, s_ * NT + j, ri, :], rhs=m2t[:, j % 4, s_, ri, :],
                                                                                      start=(k == 0), stop=(k == 3)), [A, m2t], [ps2])
                            k += 1
                    if j % 4 == 3:
                        j0 = j - 3
                        src_ap = ps2[:, :, :].rearrange("c j k -> c k j")
                        if (j // 4) % 2 == 0:
                            dve(lambda e, src_ap=src_ap, j0=j0: e.tensor_copy(out=osbv[:, :, j0:j0 + 4], in_=src_ap), [ps2], [osb])
                        else:
                            act(lambda e, src_ap=src_ap, j0=j0: e.activation(out=osbv[:, :, j0:j0 + 4], in_=src_ap, func=AF.Copy), [ps2], [osb])
                stq(brF[c0:c0 + Cg, :], osb, osb[:], dB["brF"])
            P.pop()

        def drain(g):
            for _ in g:
                pass

        def run_concurrent(primary, secondary, ratio=int(os.environ.get("CONC_RATIO", "1"))):
            p_alive, s_alive, p_fin = True, True, False
            while p_alive or s_alive:
                for _ in range(PRIM_STEPS):
                    if p_alive and not (p_fin and s_alive):
                        try:
                            if next(primary) == 'finished':
                                p_fin = True
                        except StopIteration:
                            p_alive = False
                for _ in range(ratio):
                    if s_alive:
                        try:
                            next(secondary)
                        except StopIteration:
                            s_alive = False

        def run_pipelined(make_gen, order, depth=PIPE_DEPTH):
            active = []
            order = list(order)
            pos = 0
            while pos < len(order) or active:
                if pos < len(order) and len(active) < depth and all(e[1] == 'second' for e in active):
                    active.append([make_gen(order[pos]), 'first']); pos += 1
                for ent in list(active):
                    if ent[1] == 'waiting':
                        if active[0] is ent:
                            ent[1] = 'second'
                        else:
                            continue
                    try:
                        v = next(ent[0])
                        if v == 'prev_done' and ent[1] == 'first':
                            ent[1] = 'second' if active[0] is ent else 'waiting'
                    except StopIteration:
                        active.remove(ent)

        def phase_scan(l, ret):
            P.push()
            nkc, hpc = (2, 2) if ret else (1, 4)
            KW = nkc * 128
            col0, width = (0, 1024) if ret else (1024, 800)
            qo, ko, vo, go = (0, 256, 512, 768) if ret else (0, 128, 256, 512)
            lro = 768
            kbr = 1 if ret else 3
            tri = P.sb("tri", [128, 6, 128]); ld(tri, tri[:], I["tri"][:, :, :])
            mh = P.sb("mh", [128, hpc]); ld(mh, mh[:], I["mh_ret" if ret else "mh_gla"][:, :])
            bd = P.sb("bd", [128, nkc, 256]); ld(bd, bd[:], I["bd_ret" if ret else "bd_gla"][:, :, :])
            gn = P.sb("gn", [128, 256]); ld(gn, gn[:], I["ret_gn" if ret else "gla_gn"][l].partition_broadcast(128))
            lns = math.log(32.0 ** -0.5)
            if ret:
                Ec = P.sb("E", [128, nkc, 6, 128]); Epc = P.sb("Epad", [128, nkc, 2, hpc, 128])
                dtokc = P.sb("dtok", [128, 2, KW]); decc = P.sb("dec", [128, nkc, 2])
                ld(Ec, Ec[:], I["ret_e"][:, :, :, :]); ld(dtokc, dtokc[:], I["ret_tok"][:, :, :]); ld(decc, decc[:], I["ret_dec"][:, :, :])
                for c in range(nkc):
                    for d_ in range(2):
                        dve(lambda e, c=c, d_=d_: e.tensor_tensor(out=Epc[:, c, d_, :, :], in0=Ec[:, c, 1 + 2 * d_, :].unsqueeze(1).broadcast_to([128, hpc, 128]),
                                                                  in1=mh[:].unsqueeze(2).broadcast_to([128, hpc, 128]), op=ALU.mult), [Ec, mh], [Epc])
            else:
                wd = P.sb("wd", [33, 256]); ld(wd, wd[:], I["gla_wd"][l])
                lnsb = P.sb("lnsb", [128, 1])
                dve(lambda e: e.memset(lnsb[:], lns), [], [lnsb])
                psZ = P.ps("psZ", [128, 512]); psB = P.ps("psB", [128, 4, 128])

            class TS:
                pass

            def mk_set(i):
                S = TS()
                S.pt = P.sb(f"pt{i}", [128, width])
                S.Qt = P.sb(f"Qt{i}", [128, nkc, 4, 128], BF16); S.Kp = P.sb(f"Kp{i}", [128, nkc, 2, hpc, 128], BF16)
                S.khat = P.sb(f"khat{i}", [128, 2, KW], BF16); S.Vb = P.sb(f"Vb{i}", [128, 256], BF16)
                S.st1 = P.sb(f"st1{i}", [128, 4, 128]); S.st2 = P.sb(f"st2{i}", [128, 4, 128]); S.PT = P.sb(f"PT{i}", [128, 4, 128], BF16)
                S.hn1 = P.sb(f"hn1{i}", [128, 4]); S.hn2 = P.sb(f"hn2{i}", [128, 4]); S.oc = P.sb(f"oc{i}", [128, 4, 64]); S.osq = P.sb(f"osq{i}", [128, 4, 64])
                S.sg = P.sb(f"sg{i}", [128, 256]); S.resb = P.sb(f"resb{i}", [128, 256], BF16); S.resT = P.sb(f"resT{i}", [128, 2, 128], BF16)
                S.tU = P.sb(f"tU{i}", [128, nkc, 256])
                if ret:
                    S.rot = P.sb(f"rot{i}", [128, 2, 32]); S.qkr = P.sb(f"qkr{i}", [128, 8, 64])
                    S.rt1 = P.sb(f"rt1{i}", [128, 8, 32]); S.rt2 = P.sb(f"rt2{i}", [128, 8, 32])
                    S.E, S.Epad, S.dtok, S.dec = Ec, Epc, dtokc, decc
                else:
                    S.lrT = P.sb(f"lrT{i}", [33, 128]); dve(lambda e: e.memset(S.lrT[:], 1.0), [], [S.lrT])
                    S.et = P.sb(f"et{i}", [128, 256]); S.lt = P.sb(f"lt{i}", [128, 256]); S.bsb = P.sb(f"bsb{i}", [128, 4, 128])
                    S.mids = P.sb(f"mids{i}", [128, 4])
                    S.E = P.sb(f"E{i}", [128, nkc, 6, 128]); S.Epad = P.sb(f"Epad{i}", [128, nkc, 2, hpc, 128])
                    S.dtok = P.sb(f"dtok{i}", [128, 2, KW]); S.dec = P.sb(f"dec{i}", [128, nkc, 2])
                return S

            sets = [mk_set(0), mk_set(1)]
            psT = P.ps("psT", [128, 2 * nkc, 128])
            psS = [P.ps(f"psS{i}", [128, 4, 128]) for i in range(2)]
            psO = P.ps("psO", [128, 256]); psU = P.ps("psU", [128, nkc, 256]); psR = P.ps("psR", [128, 2, 128], BF16)
            Sm = P.sb("Sm", [128, nkc, 256]); Sbf = P.sb("Sbf", [128, nkc, 256], BF16); Sball = P.sb("Sball", [128, NT, nkc, 256], BF16)
            brv = brF.rearrange("(k p) t -> p k t", p=128)

            def prep(n, S, full=True):
                pt = S.pt
                ld(pt, pt[:], projT[n * 128:(n + 1) * 128, col0:col0 + width], src=dB["projT"])
                if ret:
                    rot, qkr, rt1, rt2 = S.rot, S.qkr, S.rt1, S.rt2
                    ld(rot, rot[:], I["rot"][n * 128:(n + 1) * 128, :, :])
                    h0 = 0 if full else 4
                    nh_ = 8 - h0
                    src = pt[:, 0:512].rearrange("p (h d) -> p h d", d=64)[:, h0:8, :]
                    cosb = rot[:, 0, :].unsqueeze(1).broadcast_to([128, nh_, 32]); sinb = rot[:, 1, :].unsqueeze(1).broadcast_to([128, nh_, 32])
                    qkr_full = qkr
                    qkr = qkr[:, h0:8, :]; rt1 = rt1[:, h0:8, :]; rt2 = rt2[:, h0:8, :]
                    gps(lambda e: e.tensor_tensor(out=rt1, in0=src[:, :, 0:32], in1=cosb, op=ALU.mult), [pt, rot], [S.rt1])
                    gps(lambda e: e.tensor_tensor(out=rt2, in0=src[:, :, 32:64], in1=sinb, op=ALU.mult), [pt, rot], [S.rt2])
                    gps(lambda e: e.tensor_tensor(out=qkr[:, :, 0:32], in0=rt1, in1=rt2, op=ALU.subtract), [S.rt1, S.rt2], [S.qkr])
                    gps(lambda e: e.tensor_tensor(out=rt1, in0=src[:, :, 0:32], in1=sinb, op=ALU.mult), [pt, rot, S.qkr], [S.rt1])
                    gps(lambda e: e.tensor_tensor(out=rt2, in0=src[:, :, 32:64], in1=cosb, op=ALU.mult), [pt, rot, S.qkr], [S.rt2])
                    gps(lambda e: e.tensor_tensor(out=qkr[:, :, 32:64], in0=rt1, in1=rt2, op=ALU.add), [S.rt1, S.rt2], [S.qkr])
                    qk = qkr_full[:].rearrange("p h d -> p (h d)")
                    S.q_tok, S.k_tok, S.qkb = qk[:, 0:256], qk[:, 256:512], qkr_full
                else:
                    lrT, et, lt, bsb, mids, E, Epad, dtok, dec = S.lrT, S.et, S.lt, S.bsb, S.mids, S.E, S.Epad, S.dtok, S.dec
                    S.q_tok, S.k_tok, S.qkb = pt[:, qo:qo + 128], pt[:, ko:ko + 128], pt
                    pe(lambda e: e.transpose(psZ[0:32, 256:384], pt[:, lro:lro + 32], identf[:]), [pt, identf], [psZ])
                    yield
                    dve(lambda e: e.tensor_copy(out=lrT[0:32, :], in_=psZ[0:32, 256:384]), [psZ], [lrT])
                    yield
                    pe(lambda e: e.matmul(psZ[:, 0:256], lhsT=lrT[:], rhs=wd[:], start=True, stop=True), [lrT, wd], [psZ])
                    yield
                    act(lambda e: e.activation(out=et[:], in_=psZ[:, 0:256], func=AF.Exp, scale=-1.0), [psZ], [et])
                    act(lambda e: e.activation(out=lt[:], in_=et[:], func=AF.Ln, bias=1.0), [et], [lt])
                    yield
                    pe(lambda e: e.matmul(psB[:, 0, :], lhsT=lt[:, 0:128], rhs=tri[:, 2, :], start=True, stop=True), [lt, tri], [psB])
                    pe(lambda e: e.matmul(psB[:, 1, :], lhsT=lt[:, 128:256], rhs=tri[:, 3, :], start=True, stop=True), [lt, tri], [psB])
                    pe(lambda e: e.matmul(psB[:, 2, :], lhsT=tri[:, 4, :], rhs=lt[:, 0:128], start=True, stop=True), [lt, tri], [psB])
                    pe(lambda e: e.matmul(psB[:, 3, :], lhsT=tri[:, 5, :], rhs=lt[:, 128:256], start=True, stop=True), [lt, tri], [psB])
                    yield
                    dve(lambda e: e.tensor_copy(out=bsb[:], in_=psB[:]), [psB], [bsb])
                    dve(lambda e: e.tensor_scalar(out=mids[:, 0:2], in0=bsb[:, 0:2, 64], scalar1=-1.0, scalar2=lns, op0=ALU.mult, op1=ALU.add), [bsb], [mids])
                    dve(lambda e: e.tensor_copy(out=mids[:, 2:4], in_=bsb[:, 0:2, 64]), [bsb], [mids])
                    yield
                    for d_ in (range(2) if full else (1,)):
                        if full:
                            act(lambda e, d_=d_: e.activation(out=E[:, 0, 2 * d_, :], in_=bsb[:, d_, :], func=AF.Exp, bias=mids[:, d_:d_ + 1]), [bsb, mids], [E])
                            act(lambda e, d_=d_: e.activation(out=E[:, 0, 2 * d_ + 1, :], in_=bsb[:, d_, :], func=AF.Exp, scale=-1.0, bias=mids[:, 2 + d_:3 + d_]), [bsb, mids], [E])
                            act(lambda e, d_=d_: e.activation(out=E[:, 0, 4 + d_, :], in_=bsb[:, d_, :], func=AF.Exp, bias=lnsb[:, 0:1]), [bsb, lnsb], [E])
                        act(lambda e, d_=d_: e.activation(out=dtok[:, d_, :], in_=bsb[:, 2 + d_, :], func=AF.Exp), [bsb], [dtok])
                    if full:
                        act(lambda e: e.activation(out=dec[:, 0, 0:1], in_=bsb[:, 0, 127:128], func=AF.Exp), [bsb], [dec])
                    act(lambda e: e.activation(out=dec[:, 0, 1:2], in_=bsb[:, 1, 0:1], func=AF.Exp), [bsb], [dec])
                    yield
                    if full:
                        for d_ in range(2):
                            dve(lambda e, d_=d_: e.tensor_tensor(out=Epad[:, 0, d_, :, :], in0=E[:, 0, 1 + 2 * d_, :].unsqueeze(1).broadcast_to([128, hpc, 128]),
                                                                 in1=mh[:].unsqueeze(2).broadcast_to([128, hpc, 128]), op=ALU.mult), [E, mh], [Epad])
                act(lambda e: e.activation(out=S.Vb[:], in_=pt[:, vo:vo + 256], func=AF.Copy), [pt], [S.Vb])
                for d_ in (range(2) if full else (1,)):
                    gps(lambda e, d_=d_: e.tensor_tensor(out=S.khat[:, d_, :], in0=S.k_tok, in1=S.dtok[:, d_, :], op=ALU.mult), [S.qkb, S.dtok], [S.khat])
                yield

            def state_update(d_, S):
                for c in range(nkc):
                    pe(lambda e, c=c: e.matmul(psU[:, c, :], lhsT=S.khat[:, d_, c * 128:(c + 1) * 128], rhs=S.Vb[:], start=True, stop=True), [S.khat, S.Vb], [psU])
                yield
                dve(lambda e: e.tensor_tensor(out=S.tU[:], in0=psU[:], in1=bd[:], op=ALU.mult), [psU, bd], [S.tU])
                for c in range(nkc):
                    dve(lambda e, c=c: e.scalar_tensor_tensor(out=Sm[:, c, :], in0=Sm[:, c, :], scalar=S.dec[:, c, d_:d_ + 1], in1=S.tU[:, c, :],
                                                              op0=ALU.mult, op1=ALU.add), [Sm, S.dec, S.tU], [Sm])

            def keep_mul():
                dve(lambda e: e.tensor_scalar(out=Sm[:], in0=Sm[:], scalar1=scal[:, 0:1], scalar2=None, op0=ALU.mult), [Sm, scal], [Sm])

            def gen1(n):
                S = sets[n % 2]
                yield from prep(n, S, full=False)
                yield 'prev_done'
                act(lambda e: e.activation(out=Sball[:, n, :, :], in_=Sm[:], func=AF.Copy), [Sm], [Sball])
                yield from state_update(1, S)
                if n == NT // 2:
                    keep_mul()

            dve(lambda e: e.memset(Sm[:], 0.0), [], [Sm])
            run_pipelined(gen1, reversed(range(NT)), depth=int(os.environ.get('PIPE1', '2')))

            def gen2(n):
                S = sets[n % 2]
                if PD_POS == 0:
                    yield 'prev_done'
                yield from prep(n, S)
                if PD_POS == 1:
                    yield 'prev_done'
                Qt, Kp, PT, Vb, E, Epad = S.Qt, S.Kp, S.PT, S.Vb, S.E, S.Epad
                for c in range(nkc):
                    pe(lambda e, c=c: e.transpose(psT[:, c, :], S.q_tok[:, c * 128:(c + 1) * 128], identf[:]), [S.qkb, identf], [psT])
                    pe(lambda e, c=c: e.transpose(psT[:, nkc + c, :], S.k_tok[:, c * 128:(c + 1) * 128], identf[:]), [S.qkb, identf], [psT])
                yield
                if PD_POS == 2:
                    yield 'prev_done'
                for c in range(nkc):
                    for vi, ei in enumerate((0, 2, 4, 5)):
                        dve(lambda e, c=c, vi=vi, ei=ei: e.tensor_tensor(out=Qt[:, c, vi, :], in0=psT[:, c, :], in1=E[:, c, ei, :], op=ALU.mult), [psT, E], [Qt])
                    for d_ in range(2):
                        dve(lambda e, c=c, d_=d_: e.tensor_tensor(out=Kp[:, c, d_, :, :], in0=psT[:, nkc + c, :].unsqueeze(1).broadcast_to([128, hpc, 128]),
                                                                  in1=Epad[:, c, d_, :, :], op=ALU.mult), [psT, Epad], [Kp])
                yield
                if PD_POS == 3:
                    yield 'prev_done'
                for d_ in range(2):
                    for c in range(nkc):
                        for hh in range(hpc):
                            pe(lambda e, d_=d_, c=c, hh=hh: e.matmul(psS[d_][:, c * hpc + hh, :], lhsT=Kp[:, c, d_, hh, :], rhs=Qt[:, c, d_, :], start=True, stop=True),
                               [Kp, Qt], [psS[d_]])
                yield
                if PD_POS == 4:
                    yield 'prev_done'
                dve(lambda e: e.tensor_tensor(out=S.st1[:], in0=psS[0][:], in1=tri[:, 0, :].unsqueeze(1).broadcast_to([128, 4, 128]), op=ALU.mult), [psS[0], tri], [S.st1])
                dve(lambda e: e.tensor_tensor(out=S.st2[:], in0=psS[1][:], in1=tri[:, 1, :].unsqueeze(1).broadcast_to([128, 4, 128]), op=ALU.mult), [psS[1], tri], [S.st2])
                gps(lambda e: e.tensor_tensor(out=PT[:], in0=S.st1[:], in1=S.st2[:], op=ALU.add), [S.st1, S.st2], [PT])
                yield 'prev_done'
                if n == NT // 2:
                    keep_mul()
                act(lambda e: e.activation(out=Sbf[:], in_=Sm[:], func=AF.Copy), [Sm], [Sbf])
                yield
                for h_ in range(4):
                    c = h_ // hpc
                    hs = slice(h_ * 64, (h_ + 1) * 64)
                    pe(lambda e, c=c, hs=hs: e.matmul(psO[:, hs], lhsT=Qt[:, c, 2, :], rhs=Sbf[:, c, hs], start=True, stop=False), [Qt, Sbf], [psO])
                    pe(lambda e, c=c, hs=hs: e.matmul(psO[:, hs], lhsT=Qt[:, c, 3, :], rhs=Sball[:, n, c, hs], start=False, stop=False), [Qt, Sball], [psO])
                    pe(lambda e, h_=h_, hs=hs: e.matmul(psO[:, hs], lhsT=PT[:, h_, :], rhs=Vb[:, hs], start=False, stop=True), [PT, Vb], [psO])
                yield
                hn1, hn2, oc, osq, sg, resb, resT, pt = S.hn1, S.hn2, S.oc, S.osq, S.sg, S.resb, S.resT, S.pt
                O3 = psO[:].rearrange("p (h d) -> p h d", d=64)
                if ret:
                    dve(lambda e: e.tensor_reduce(out=hn1[:], in_=O3, axis=mybir.AxisListType.X, op=ALU.add), [psO], [hn1])
                    dve(lambda e: e.tensor_scalar(out=hn1[:], in0=hn1[:], scalar1=-1.0 / 64, scalar2=None, op0=ALU.mult), [hn1], [hn1])
                    dve(lambda e: e.tensor_tensor(out=oc[:], in0=O3, in1=hn1[:].unsqueeze(2).broadcast_to([128, 4, 64]), op=ALU.add), [psO, hn1], [oc])
                else:
                    dve(lambda e: e.tensor_copy(out=oc[:], in_=O3), [psO], [oc])
                gps(lambda e: e.tensor_tensor(out=osq[:], in0=oc[:], in1=oc[:], op=ALU.mult), [oc], [osq])
                dve(lambda e: e.tensor_reduce(out=hn2[:], in_=osq[:], axis=mybir.AxisListType.X, op=ALU.add), [osq], [hn2])
                act(lambda e: e.activation(out=sg[:], in_=pt[:, go:go + 256], func=AF.Silu), [pt], [sg])
                act(lambda e: e.activation(out=hn2[:], in_=hn2[:], func=AF.Sqrt, scale=1.0 / 64, bias=epsb[:, 0:1]), [hn2, epsb], [hn2])
                yield
                dve(lambda e: e.reciprocal(out=hn2[:], in_=hn2[:]), [hn2], [hn2])
                gps(lambda e: e.tensor_tensor(out=oc[:], in0=oc[:], in1=hn2[:].unsqueeze(2).broadcast_to([128, 4, 64]), op=ALU.mult), [oc, hn2], [oc])
                gps(lambda e: e.tensor_tensor(out=sg[:], in0=sg[:], in1=gn[:], op=ALU.mult), [sg, gn], [sg])
                gps(lambda e: e.tensor_tensor(out=resb[:], in0=oc[:].rearrange("p h d -> p (h d)"), in1=sg[:], op=ALU.mult), [oc, sg], [resb])
                yield
                for c2_ in range(2):
                    pe(lambda e, c2_=c2_: e.transpose(psR[:, c2_, :], resb[:, c2_ * 128:(c2_ + 1) * 128], identb[:]), [resb, identb], [psR])
                yield from state_update(0, S)
                dve(lambda e: e.tensor_copy(out=resT[:], in_=psR[:]), [psR], [resT])
                stq(brv[:, 2 * kbr:2 * kbr + 2, n * 128:(n + 1) * 128], resT, resT[:], dB["brF"])

            dve(lambda e: e.memset(Sm[:], 0.0), [], [Sm])
            run_pipelined(gen2, range(NT), depth=int(os.environ.get('PIPE2', '2')))
            P.pop()

        def phase_hyena(l, mode='all'):
            NBLK = 2 * T // 512
            Cg = 32
            TWO_PI = 2.0 * math.pi
            def part1():
                P.push()
                w1 = P.sb("w1", [33, 64]); w2 = P.sb("w2", [64, 64]); w3a = P.sb("w3a", [65, 1024])
                c1 = P.sb("c1", [64, 2]); c2_ = P.sb("c2", [64, 2]); fb = P.sb("fb", [64, 2]); delta = P.sb("delta", [128, 2])
                ld(w1, w1[:], I["flt_w1"][l]); ld(w2, w2[:], I["flt_w2"][l]); ld(w3a, w3a[0:64, :], I["flt_w3"][l])
                ld(w3a, w3a[64:65, :], I["flt_b3"][l:l + 1, :])
                ld(c1, c1[:], I["flt_c1"][l]); ld(c2_, c2_[:], I["flt_c2"][l]); ld(delta, delta[:], I["flt_delta"][:, :])
                dve(lambda e: e.tensor_tensor(out=fb[:, 0:1], in0=c1[:, 0:1], in1=c1[:, 1:2], op=ALU.mult), [c1], [fb])
                dve(lambda e: e.tensor_tensor(out=fb[:, 1:2], in0=c2_[:, 0:1], in1=c2_[:, 1:2], op=ALU.mult), [c2_, fb], [fb])
                h2a = P.sb("h2a", [65, 512], BF16); dve(lambda e: e.memset(h2a[:], 1.0), [], [h2a])
                w3b = P.sb("w3b", [65, 1024], BF16); dve(lambda e: e.tensor_copy(out=w3b[:], in_=w3a[:]), [w3a], [w3b])
                h1 = P.sb("h1", [64, 512]); a1 = P.sb("a1", [64, 512]); kk = P.sb("kk", [64, 512])
                nrm = P.sb("nrm", [128, 4, NBLK]); rn = P.sb("rn", [128, 4])
                fts = [P.sb(f"ft{i}", [33, 512]) for i in range(2)]; msks = [P.sb(f"msk{i}", [128, 3, 512]) for i in range(2)]
                win = P.sb("win", [128, 2, 512]); t1 = P.sb("ft1", [128, 512]); t2 = P.sb("ft2", [128, 512]); ab = P.sb("fab", [128, 512])
                gbs = [P.sb(f"gb{i}", [128, 512], BF16) for i in range(2)]
                psh = P.ps("psh", [64, 512]); psf = [P.ps(f"psf{i}", [128, 512]) for i in range(2)]

                def sin_layer(cc, col, dst):
                    dve(lambda e: e.tensor_scalar(out=a1[:], in0=psh[:], scalar1=cc[:, 0:1], scalar2=fb[:, col:col + 1], op0=ALU.mult, op1=ALU.add), [psh, cc, fb], [a1])
                    dve(lambda e: e.tensor_scalar(out=kk[:], in0=a1[:], scalar1=1.0 / TWO_PI, scalar2=MAGIC, op0=ALU.mult, op1=ALU.add), [a1], [kk])
                    dve(lambda e: e.tensor_scalar(out=kk[:], in0=kk[:], scalar1=-MAGIC, scalar2=None, op0=ALU.add), [kk], [kk])
                    dve(lambda e: e.scalar_tensor_tensor(out=a1[:], in0=kk[:], scalar=-TWO_PI, in1=a1[:], op0=ALU.mult, op1=ALU.add), [kk, a1], [a1])
                    act(lambda e: e.activation(out=dst, in_=a1[:], func=AF.Sin), [a1], [h1 if dst is not None and cc is c1 else h2a])

                ng = 0
                for blk in range(NBLK):
                    m0 = blk * 512
                    ft, msk = fts[blk % 2], msks[blk % 2]
                    ld(ft, ft[:], I["flt_feat"][:, m0:m0 + 512])
                    for r_ in range(3):
                        ld(msk, msk[:, r_, :], I["flt_msk"][r_, m0:m0 + 512].partition_broadcast(128))
                    pe(lambda e: e.matmul(psh[:], lhsT=w1[:], rhs=ft[:], start=True, stop=True), [w1, ft], [psh])
                    sin_layer(c1, 0, h1[:])
                    yield
                    pe(lambda e: e.matmul(psh[:], lhsT=w2[:], rhs=h1[:], start=True, stop=True), [w2, h1], [psh])
                    sin_layer(c2_, 1, h2a[0:64, :])
                    yield
                    for ch in range(2):
                        act(lambda e, ch=ch: e.activation(out=win[:, ch, :], in_=msk[:, 2, :], func=AF.Exp, scale=delta[:, ch:ch + 1]), [msk, delta], [win])
                    for o in range(2):
                        for ch in range(2):
                            for dr in range(2):
                                q = o * 4 + dr * 2 + ch
                                pe(lambda e, dr=dr, q=q: e.matmul(psf[dr][:], lhsT=w3b[:, q * 128:(q + 1) * 128], rhs=h2a[:], start=True, stop=True), [w3b, h2a], [psf[dr]])
                            dve(lambda e: e.tensor_tensor(out=t1[:], in0=psf[0][:], in1=msk[:, 0, :], op=ALU.mult), [psf[0], msk], [t1])
                            dve(lambda e: e.tensor_tensor(out=t2[:], in0=psf[1][:], in1=msk[:, 1, :], op=ALU.mult), [psf[1], msk], [t2])
                            dve(lambda e: e.tensor_tensor(out=t1[:], in0=t1[:], in1=t2[:], op=ALU.add), [t1, t2], [t1])
                            dve(lambda e, ch=ch: e.tensor_tensor(out=t1[:], in0=t1[:], in1=win[:, ch, :], op=ALU.mult), [t1, win], [t1])
                            gb = gbs[ng % 2]; ng += 1
                            idx = o * 2 + ch
                            act(lambda e, gb=gb: e.activation(out=gb[:], in_=t1[:], func=AF.Copy), [t1], [gb])
                            act(lambda e, idx=idx, blk=blk: e.activation(out=ab[:], in_=t1[:], func=AF.Abs, accum_out=nrm[:, idx, blk:blk + 1]), [t1], [ab, nrm])
                            stq(gF[idx * 128:(idx + 1) * 128, m0:m0 + 512], gb, gb[:], dB["gF"])
                            yield
                dve(lambda e: e.tensor_reduce(out=rn[:], in_=nrm[:], axis=mybir.AxisListType.X, op=ALU.add), [nrm], [rn])
                dve(lambda e: e.tensor_scalar(out=rn[:], in0=rn[:], scalar1=scal[:, 1:2], scalar2=EPS, op0=ALU.mult, op1=ALU.add), [rn, scal], [rn])
                dve(lambda e: e.reciprocal(out=rn[:], in_=rn[:]), [rn], [rn])
                stq(rnD.rearrange("(q p) -> p q", p=128), rn, rn[:], dB["rnD"])
                P.pop()

            def stage1(din, m1, AA, ps1s, cnt, Cg=Cg):
                for c in range(0, Cg, 2):
                    ps1 = ps1s[cnt[0] % 2]; cnt[0] += 1
                    for cc in range(2):
                        pe(lambda e, cc=cc, ps1=ps1, c=c: e.matmul(ps1[:, cc, :], lhsT=din[:, c + cc, :], rhs=m1[:], start=True, stop=True), [din, m1], [ps1])
                    for ri in range(2):
                        src_ap = ps1[:, :, ri * NSA:(ri + 1) * NSA].rearrange("p c s -> p s c")
                        if ri == 0:
                            dve(lambda e, src_ap=src_ap, c=c: e.tensor_copy(out=AA[:, :, 0, c:c + 2], in_=src_ap), [ps1], [AA])
                        else:
                            act(lambda e, src_ap=src_ap, c=c: e.activation(out=AA[:, :, 1, c:c + 2], in_=src_ap, func=AF.Copy), [ps1], [AA])
                    yield

            def stage2(AA, h2ts, psXs, evac, spb=8):
                h2v = I["hy_h2"].rearrange("s p a k -> p s a k")
                for j in range(NSA):
                    h2t = h2ts[(j // 8) % 2]; jj = j % spb
                    if j % 8 == 0:
                        nj_ = min(8, NSA - j)
                        ld(h2t, h2t[:, 0:nj_, :, :], h2v[:, j:j + nj_, :, :])
                    psX = psXs[(j // spb) % 2]
                    j8 = j % 8
                    pe(lambda e, psX=psX, jj=jj, h2t=h2t, j=j, j8=j8: e.matmul(psX[:, jj, 0, :], lhsT=h2t[:, j8, 0, :], rhs=AA[:, j, 0, :], start=True, stop=False), [h2t, AA], [psX])
                    pe(lambda e, psX=psX, jj=jj, h2t=h2t, j=j, j8=j8: e.matmul(psX[:, jj, 0, :], lhsT=h2t[:, j8, 2, :], rhs=AA[:, j, 1, :], start=False, stop=True), [h2t, AA], [psX])
                    pe(lambda e, psX=psX, jj=jj, h2t=h2t, j=j, j8=j8: e.matmul(psX[:, jj, 1, :], lhsT=h2t[:, j8, 0, :], rhs=AA[:, j, 1, :], start=True, stop=False), [h2t, AA], [psX])
                    pe(lambda e, psX=psX, jj=jj, h2t=h2t, j=j, j8=j8: e.matmul(psX[:, jj, 1, :], lhsT=h2t[:, j8, 1, :], rhs=AA[:, j, 0, :], start=False, stop=True), [h2t, AA], [psX])
                    if jj == spb - 1 or j == NSA - 1:
                        evac(psX, j - jj, jj + 1)
                        yield

            def part2():
                P.push()
                rnb = P.sb("rnb", [128, 512]); ld(rnb, rnb[:], rnD.partition_broadcast(128), src=dB["rnD"])
                hf1 = P.sb("hf1", [NS, 2 * NSA], BF16); ld(hf1, hf1[:], I["hy_hf1"][:, :])
                Cf = 64
                gin = P.sb("gin", [NS, Cf, 128], BF16); AA = P.sb("AAf", [128, NSA, 2, Cf], BF16)
                Gsb = P.sb("Gsbf", [128, NSA, 2, Cf], BF16)
                ps1s = [P.ps(f"hps1f{i}", [128, 2, 2 * NSA]) for i in range(2)]; psXs = [P.ps(f"hpsXf{i}", [128, 4, 2, Cf]) for i in range(2)]
                h2ts = [P.sb(f"h2tf{i}", [128, 8, 3, 128], BF16) for i in range(2)]
                cnt = [0]
                for gi in range(512 // Cf):
                    ld(gin, gin[:], gF[gi * Cf:(gi + 1) * Cf, :].rearrange("c (g n) -> g c n", n=128), src=dB["gF"])
                    yield from stage1(gin, hf1, AA, ps1s, cnt, Cg=Cf)

                    def evacG(psX, j0, nj, gi=gi):
                        dve(lambda e: e.tensor_tensor(out=Gsb[:, j0:j0 + nj, :, :], in0=psX[:, 0:nj, :, :],
                                                      in1=rnb[:, gi * Cf:(gi + 1) * Cf].unsqueeze(1).unsqueeze(1).broadcast_to([128, nj, 2, Cf]), op=ALU.mult),
                            [psX, rnb], [Gsb])
                    yield from stage2(AA, h2ts, psXs, evacG, spb=4)
                    for hh_ in range(2):
                        for s0_ in range(0, NSA, 32):
                            s1_ = min(NSA, s0_ + 32)
                            stq(Gd[2 * gi + hh_].rearrange("p s (r c) -> p s r c", r=2)[:, s0_:s1_], Gsb, Gsb[:, s0_:s1_, :, hh_ * 32:(hh_ + 1) * 32], dB["Gd"])
                P.pop()

            def part3():
                P.push()
                hz = P.sb("hz", [NSA, 128, 2, NT], BF16); ld(hz, hz[:], I["hy_z"][:, :, :, :])
                h1t = P.sb("hh1", [NT, 2 * NSA], BF16); ld(h1t, h1t[:], I["hy_h1"][:, :])
                i1 = P.sb("hi1", [128, 2, 256], BF16); ld(i1, i1[:], I["hy_i1"][:, :, :])
                skb = P.sb("skb", [128, 2, 256])
                for o in range(2):
                    ld(skb, skb[:, o, :], I["hy_skip"][l, o].partition_broadcast(128))
                AA = P.sb("AAd", [128, NSA, 2, Cg], BF16); Ysb = P.sb("Ysb", [128, 2, Cg, NSA], BF16); Bsb = P.sb("Bsb", [NSA, 128, 2, Cg], BF16)
                Gsb = P.sb("Gsbd", [128, NSA, 2, Cg], BF16)
                din = P.sb("din", [NT, Cg, 128], BF16); vt = P.sb("vt", [NT, Cg, 128]); x1t = P.sb("x1t", [NT, Cg, 128]); x2t = P.sb("x2t", [NT, Cg, 128])
                ob = P.sb("ob", [NT, Cg, 128], BF16)
                pw = [P.sb(f"pw{i}", [128, 8, Cg]) for i in range(4)]
                tcv = P.sb("tcv", [NT, Cg, 16])
                ps1s = [P.ps(f"hps1d{i}", [128, 2, 2 * NSA]) for i in range(2)]; psXs = [P.ps(f"hpsXd{i}", [128, 8, 2, Cg]) for i in range(2)]
                psIs = [P.ps(f"hpsI{i}", [NSA, 2, 256]) for i in range(2)]; psYs = [P.ps(f"hpsY{i}", [NT, 16, Cg]) for i in range(2)]
                h2ts = [P.sb(f"h2td{i}", [128, 8, 3, 128], BF16) for i in range(2)]
                cnt = [0]

                def evacY(psX, j0, nj):
                    Xre, Xim = psX[:, 0:nj, 0, :], psX[:, 0:nj, 1, :]
                    Gre, Gim = Gsb[:, j0:j0 + nj, 0, :], Gsb[:, j0:j0 + nj, 1, :]
                    dve(lambda e: e.tensor_tensor(out=pw[0][:, 0:nj, :], in0=Xre, in1=Gre, op=ALU.mult), [psX, Gsb], [pw[0]])
                    dve(lambda e: e.tensor_tensor(out=pw[1][:, 0:nj, :], in0=Xim, in1=Gim, op=ALU.mult), [psX, Gsb], [pw[1]])
                    dve(lambda e: e.tensor_tensor(out=Ysb[:, 0, :, j0:j0 + nj].rearrange("p c j -> p j c"), in0=pw[0][:, 0:nj, :], in1=pw[1][:, 0:nj, :], op=ALU.subtract),
                        [pw[0], pw[1]], [Ysb])
                    dve(lambda e: e.tensor_tensor(out=pw[2][:, 0:nj, :], in0=Xre, in1=Gim, op=ALU.mult), [psX, Gsb], [pw[2]])
                    dve(lambda e: e.tensor_tensor(out=pw[3][:, 0:nj, :], in0=Xim, in1=Gre, op=ALU.mult), [psX, Gsb], [pw[3]])
                    dve(lambda e: e.tensor_tensor(out=Ysb[:, 1, :, j0:j0 + nj].rearrange("p c j -> p j c"), in0=pw[2][:, 0:nj, :], in1=pw[3][:, 0:nj, :], op=ALU.add),
                        [pw[2], pw[3]], [Ysb])

                def long_conv(o, g, xg, svt):
                    ld(Gsb, Gsb[:].rearrange("p s r c -> p s (r c)"), Gd[o * (256 // Cg) + g], src=dB["Gd"])
                    drain(stage1(din, h1t, AA, ps1s, cnt))
                    drain(stage2(AA, h2ts, psXs, evacY))
                    for c in range(0, Cg, 2):
                        psI = psIs[(c // 2) % 2]
                        for cc in range(2):
                            pe(lambda e, cc=cc, psI=psI, c=c: e.matmul(psI[:, cc, :], lhsT=Ysb[:, 0, c + cc, :], rhs=i1[:, 0, :], start=True, stop=False), [Ysb, i1], [psI])
                            pe(lambda e, cc=cc, psI=psI, c=c: e.matmul(psI[:, cc, :], lhsT=Ysb[:, 1, c + cc, :], rhs=i1[:, 1, :], start=False, stop=True), [Ysb, i1], [psI])
                        for ri in range(2):
                            src_ap = psI[:, :, ri * 128:(ri + 1) * 128].rearrange("p c n -> p n c")
                            if ri == 0:
                                dve(lambda e, src_ap=src_ap, c=c: e.tensor_copy(out=Bsb[:, :, 0, c:c + 2], in_=src_ap), [psI], [Bsb])
                            else:
                                act(lambda e, src_ap=src_ap, c=c: e.activation(out=Bsb[:, :, 1, c:c + 2], in_=src_ap, func=AF.Copy), [psI], [Bsb])
                    for nb in range(8):
                        psY = psYs[nb % 2]
                        for q in range(16):
                            n2 = nb * 16 + q
                            pe(lambda e, psY=psY, q=q, n2=n2: e.matmul(psY[:, q, :], lhsT=hz[:, n2, 0, :], rhs=Bsb[:, n2, 0, :], start=True, stop=False), [hz, Bsb], [psY])
                            pe(lambda e, psY=psY, q=q, n2=n2: e.matmul(psY[:, q, :], lhsT=hz[:, n2, 1, :], rhs=Bsb[:, n2, 1, :], start=False, stop=True), [hz, Bsb], [psY])
                        sl = slice(nb * 16, (nb + 1) * 16)
                        dve(lambda e, psY=psY, sl=sl: e.tensor_tensor(out=tcv[:], in0=psY[:].rearrange("p n c -> p c n"), in1=svt[:, :, sl], op=ALU.add), [psY, svt], [tcv])
                        dve(lambda e, sl=sl: e.tensor_tensor(out=xg[:, :, sl], in0=tcv[:], in1=xg[:, :, sl], op=ALU.mult), [tcv, xg], [xg])

                uv = lambda r0: uhF[r0:r0 + Cg, :].rearrange("c (g n) -> g c n", n=128)
                for g in range(256 // Cg):
                    c0 = g * Cg
                    ld(vt, vt[:], uv(c0), src=dB["uhF"]); ld(x1t, x1t[:], uv(256 + c0), src=dB["uhF"]); ld(x2t, x2t[:], uv(512 + c0), src=dB["uhF"])
                    act(lambda e: e.activation(out=din[:], in_=vt[:], func=AF.Copy), [vt], [din])
                    dve(lambda e, c0=c0: e.tensor_tensor(out=vt[:], in0=vt[:], in1=skb[0:NT, 0, c0:c0 + Cg].unsqueeze(2).broadcast_to([NT, Cg, 128]), op=ALU.mult), [vt, skb], [vt])
                    long_conv(0, g, x1t, vt)
                    act(lambda e: e.activation(out=din[:], in_=x1t[:], func=AF.Copy), [x1t], [din])
                    dve(lambda e, c0=c0: e.tensor_tensor(out=vt[:], in0=x1t[:], in1=skb[0:NT, 1, c0:c0 + Cg].unsqueeze(2).broadcast_to([NT, Cg, 128]), op=ALU.mult), [x1t, skb], [vt])
                    long_conv(1, g, x2t, vt)
                    act(lambda e: e.activation(out=ob[:], in_=x2t[:], func=AF.Copy), [x2t], [ob])
                    stq(brF[512 + c0:512 + c0 + Cg, :].rearrange("c (g n) -> g c n", n=128), ob, ob[:], dB["brF"])
                P.pop()
            if mode == 'filtgen':
                def both():
                    yield from part1()
                    yield from part2()
                return both()
            if mode in ('all', 'filt'):
                drain(part1())
                drain(part2())
            if mode in ('all', 'conv'):
                part3()

        def phase_zero_br(l):
            P.push()
            zt = P.sb("zbr", [128, 2048], BF16)
            dve(lambda e: e.memset(zt[:], 0.0), [], [zt])
            for k in range(8):
                for t0 in range(0, T, 2048):
                    w_ = min(2048, T - t0)
                    stq(brF[k * 128:(k + 1) * 128, t0:t0 + w_], zt, zt[:, 0:w_], dB["brF"])
            P.pop()

        PHASES = dict(mod=phase_mod, norm=phase_norm, A=lambda l: drain(phase_A(l)), Afilt=lambda l: run_concurrent(phase_A(l), phase_hyena(l, 'filtgen')), C=phase_C, D=phase_D, zero=phase_zero_br, fnet=phase_fnet, ret=lambda l: phase_scan(l, True), gla=lambda l: phase_scan(l, False), hyena=phase_hyena, hyfilt=lambda l: phase_hyena(l, 'filt'), hyconv=lambda l: phase_hyena(l, 'conv'))
        nc._I = I
        return_hook(P, PHASES, locals())
    return nc


def return_hook(P, PHASES, env):
    sched = env.get('debug') or ()
    I, dB = env['I'], env['dB']
    stop = None
    for d in sched:
        if isinstance(d, str) and d.startswith("stop:"):
            stop = d[5:]
    x_in, x1d, xmid, y_out = env['x_in'], env['x1d'], env['xmid'], env['y_out']
    Am, Af = env['Am'], env['Af']
    only = [d[5:] for d in sched if isinstance(d, str) and d.startswith("only:")]
    if only:
        for nm in only:
            if nm == 'norm':
                PHASES['norm'](x_in, dB["in"], Am, 0)
            elif nm == 'C':
                PHASES['C'](0, x_in, dB["in"])
            elif nm == 'D':
                PHASES['D'](0, x1d, dB["x1d"])
            else:
                PHASES[nm](0)
        P.barrier()
        return
    for l in range(DEPTH):
        src, sbuf = (x_in, dB["in"]) if l == 0 else (x1d, dB["x1d"])
        dst, dbuf = (x1d, dB["x1d"]) if l == 0 else (y_out, dB["y"])
        if stop == "none":
            break
        PHASES['mod'](l)
        if stop == "mod":
            break
        PHASES['norm'](src, sbuf, Am, 0)
        if stop == "norm":
            break
        PHASES['Afilt' if CONC_FILT else 'A'](l)
        if stop == "A":
            break
        PHASES['zero'](l)
        for nm in ('fnet', 'ret', 'hyconv' if CONC_FILT else 'hyena', 'gla'):
            if nm in PHASES:
                PHASES[nm](l)
        if stop == "mix":
            break
        PHASES['C'](l, src, sbuf)
        PHASES['norm'](xmid, dB["xmid"], Af, 24)
        PHASES['D'](l, dst, dbuf)
        if stop == "L0":
            break
    P.barrier()


def prep_core_inputs(x, c2, W, tb):
    m = {"x": np.ascontiguousarray(x, np.float32)}
    m["cT"] = np.ascontiguousarray(c2.reshape(2, 8, 128).transpose(2, 1, 0), np.float32)
    m.update(W)
    m.update(tb)
    return m


def prep_weights(inp):
    f = lambda a: np.ascontiguousarray(a, np.float32)
    W = {}
    W["ada_w"] = f(inp["ada_w"]); W["ada_b"] = f(inp["ada_b"])
    W["ada_b_col"] = f(inp["ada_b"].reshape(DEPTH, 48, 128).transpose(0, 2, 1))
    nw = np.stack([inp["norm_pre_mix"], inp["norm_post_mix"], inp["norm_pre_ffn"], inp["norm_post_ffn"]], 1)
    W["normw_col"] = f(nw.reshape(DEPTH, 4, 8, 128).transpose(0, 3, 1, 2))
    W["norm_post_mix"] = f(inp["norm_post_mix"]); W["norm_post_ffn"] = f(inp["norm_post_ffn"])
    W["w_in"] = f(inp["w_in"])
    W["hy_cw"] = f(inp["hy_conv_w"].reshape(DEPTH, 3, 6, 128).transpose(0, 3, 2, 1))
    W["hy_cb"] = f(inp["hy_conv_b"].reshape(DEPTH, 6, 128).transpose(0, 2, 1))
    W["flt_w1"] = f(inp["flt_w1"]); W["flt_w2"] = f(inp["flt_w2"]); W["flt_w3"] = f(inp["flt_w3"]); W["flt_b3"] = f(inp["flt_b3"])
    W["flt_c1"] = f(np.stack([inp["flt_freq"], inp["flt_b1"]], -1)); W["flt_c2"] = f(np.stack([inp["flt_freq"], inp["flt_b2"]], -1))
    W["hy_skip"] = f(inp["hy_skip"])
    wd = np.zeros((DEPTH, 33, 256), np.float32)
    wd[:, 0:16, 0:128] = inp["gla_w_decay"][:, 0]; wd[:, 16:32, 128:256] = inp["gla_w_decay"][:, 1]
    wd[:, 32, 0:128] = inp["gla_b_decay"][:, 0]; wd[:, 32, 128:256] = inp["gla_b_decay"][:, 1]
    W["gla_wd"] = wd
    W["ret_gn"] = f(inp["ret_gn"]); W["gla_gn"] = f(inp["gla_gn"])
    W["w_branch"] = f(inp["w_branch"].reshape(DEPTH, 1024, D)); W["w_out"] = f(inp["w_out"])
    W["ffn_up"] = f(inp["ffn_up"])
    W["ffn_cw"] = f(inp["ffn_conv_w"].reshape(DEPTH, 3, 44, 128).transpose(0, 3, 2, 1))
    W["ffn_cb"] = f(inp["ffn_conv_b"].reshape(DEPTH, 44, 128).transpose(0, 2, 1))
    W["ffn_down"] = f(inp["ffn_down"])
    return W


_T = 8192


def kernel(**inp):
    inp = {k: np.asarray(v) for k, v in inp.items()}
    T = _T
    W = prep_weights(inp)
    tbP, tbS = make_tables(T, 'P'), make_tables(T, 'S')
    xp, xs, cp, cs = inp["x_prompt"], inp["x_sample"], inp["c_prompt"], inp["c_sample"]
    in_maps = []
    for b in range(2):
        in_maps.append(prep_core_inputs(xp[b], np.stack([cp[b], cp[b]]), W, tbP))
    for b in range(2):
        in_maps.append(prep_core_inputs(xs[2 * b:2 * b + 2].reshape(T, D), cs[2 * b:2 * b + 2], W, tbS))
    nc = build_program(T)
    res = run_bass_kernel_spmd(nc, in_maps, core_ids=list(range(4)))
    outs = [np.asarray(r["y"], np.float32) for r in res.results]
    y_prompt = np.stack([outs[0], outs[1]], 0)
    y_sample = np.concatenate([outs[2].reshape(2, T // 2, D), outs[3].reshape(2, T // 2, D)], 0)
    return (y_prompt, y_sample)
```

```python
import math
from contextlib import ExitStack
import numpy as np
import ml_dtypes
import concourse.bass as bass
import concourse.mybir as mybir
from concourse.bass_utils import run_bass_kernel_spmd

F32 = mybir.dt.float32
BF16 = mybir.dt.bfloat16
AF = mybir.ActivationFunctionType
ALU = mybir.AluOpType
NPBF = ml_dtypes.bfloat16

D = 1024
DEPTH = 2
DFF = 2816
EPS = 1e-6
MAGIC = 12582912.0
import os
PIPE_DEPTH = int(os.environ.get("PIPE_DEPTH", "2"))
PD_POS = int(os.environ.get("PD_POS", "99"))
CONC_FILT = int(os.environ.get("CONC_FILT", "1"))
USE_POOL = int(os.environ.get("USE_POOL", "0"))
PRIM_STEPS = int(os.environ.get("PRIM_STEPS", "2"))


class Stream:
    def __init__(self, P, inc):
        self.P, self.inc = P, inc
        self.sem = P.new_sem()
        self.count = 0

    def bump(self):
        if self.count + self.inc > 30000:
            self.sem = self.P.new_sem()
            self.count = 0
        self.count += self.inc
        return (self.sem, self.count)

    def cur(self):
        return (self.sem, self.count) if self.count else None


class Buf:
    def __init__(self, name, t=None):
        self.name, self.t = name, t
        self.w = None
        self.r = {}

    def __getitem__(self, idx):
        return self.t[idx]


class Prog:
    def __init__(self, nc, es):
        self.nc, self.es = nc, es
        self.nsem = 0
        self.engs = {'pe': nc.tensor, 'act': nc.scalar, 'dve': nc.vector, 'pool': nc.gpsimd, 'sp': nc.sync}
        self.streams = {k: Stream(self, 1) for k in ('pe', 'act', 'dve', 'pool')}
        self.seen = {k: {} for k in self.engs}
        self.dma_pool = {q: [Stream(self, 16) for _ in range(8)] for q in ('sp', 'pool')}
        self.dma_rr = {q: 0 for q in self.dma_pool}
        self.scopes = [es]
        self.nuniq = 0

    def new_sem(self):
        self.nsem += 1
        return self.es.enter_context(self.nc.semaphore(f"s{self.nsem}"))

    def sb(self, name, shape, dt=F32):
        self.nuniq += 1
        return Buf(name, self.scopes[-1].enter_context(self.nc.sbuf_tensor(f"{name}_{self.nuniq}", shape, dt)))

    def ps(self, name, shape, dt=F32):
        self.nuniq += 1
        return Buf(name, self.scopes[-1].enter_context(self.nc.psum_tensor(f"{name}_{self.nuniq}", shape, dt)))

    def push(self):
        st = ExitStack()
        self.scopes.append(st)
        return st

    def pop(self):
        self.barrier()
        self.scopes.pop().close()

    def _wait(self, eng, tok):
        if tok is None:
            return
        sem, val = tok
        seen = self.seen[eng]
        if seen.get(id(sem), 0) >= val:
            return
        self.engs[eng].wait_ge(sem, val)
        seen[id(sem)] = val

    def barrier(self):
        toks = [s.cur() for s in self.streams.values()]
        for pool in self.dma_pool.values():
            toks += [s.cur() for s in pool]
        for eng in self.engs:
            for t in toks:
                self._wait(eng, t)

    def _deps(self, eng, reads, writes, accum):
        for b in reads:
            self._wait(eng, b.w)
        for b in writes:
            if not accum:
                self._wait(eng, b.w)
            for t in b.r.values():
                self._wait(eng, t)

    def _commit(self, tok, reads, writes):
        for b in writes:
            b.w = tok
            b.r = {}
        for b in reads:
            b.r[id(tok[0])] = tok

    def op(self, eng, fn, reads=(), writes=(), accum=False):
        self._deps(eng, reads, writes, accum)
        inst = fn(self.engs[eng])
        tok = self.streams[eng].bump()
        inst.then_inc(tok[0], 1)
        self._commit(tok, reads, writes)
        return tok

    def dma(self, q, out, in_, reads=(), writes=()):
        pool = self.dma_pool[q]
        st = pool[self.dma_rr[q] % len(pool)]
        self.dma_rr[q] += 1
        self._wait(q, st.cur())
        self._deps(q, reads, writes, False)
        inst = self.engs[q].dma_start(out=out, in_=in_)
        tok = st.bump()
        inst.then_inc(tok[0], 16)
        self._commit(tok, reads, writes)
        return tok


def _cplx_pair(M):
    return np.concatenate([M.real, M.imag], 1), np.concatenate([-M.imag, M.real], 1)


def make_tables(T, kind):
    NT = T // 128
    NS = 2 * NT
    H = NT // 2
    isS = (kind == 'S')
    tb = {}
    tb['ident_b'] = np.eye(128).astype(NPBF)
    tb['ident_f'] = np.eye(128, dtype=np.float32)
    cc = np.arange(64)
    ang = 2 * np.pi * np.outer(cc, cc) / 64
    bdc = np.zeros((128, 128)); bds = np.zeros((128, 128))
    for g in range(2):
        bdc[g * 64:(g + 1) * 64, g * 64:(g + 1) * 64] = np.cos(ang)
        bds[g * 64:(g + 1) * 64, g * 64:(g + 1) * 64] = -np.sin(ang)
    tb['bdcs'] = np.stack([bdc, bds], 1).astype(NPBF)
    n1 = np.arange(NT)
    M1 = np.zeros((NT, NS), np.complex128)
    M2 = np.zeros((NS, 128, 128), np.complex128)
    n2 = np.arange(128)[:, None]
    k2 = np.arange(128)[None, :]
    if not isS:
        L = T
        for j in range(NT):
            M1[:, j] = np.exp(-2j * np.pi * n1 * j / NT)
            M2[j] = np.exp(-2j * np.pi * n2 * (j + NT * k2) / T)
    else:
        L = T // 2
        for s in range(2):
            for j in range(NT):
                M1[s * H:(s + 1) * H, s * NT + j] = np.exp(-2j * np.pi * np.arange(H) * j / H)
                m = np.exp(-2j * np.pi * n2 * (NT * (k2 % 64) + j) / L) * ((k2 // 64) == s)
                M2[s * NT + j] = m
    M2 = M2 / math.sqrt(L * 64)
    a, b = _cplx_pair(M1)
    tb['fn_m1'] = np.stack([a, b], 1).astype(NPBF)
    fm2 = np.zeros((NT, 128, 2, 2, 128), np.float64)
    for s in range(2):
        for j in range(NT):
            fm2[j, :, s, 0] = M2[s * NT + j].real
            fm2[j, :, s, 1] = -M2[s * NT + j].imag
    tb['fn_m2'] = fm2.astype(NPBF)
    HF1 = np.zeros((NS, NS), np.complex128)
    H2 = np.zeros((NS, 128, 128), np.complex128)
    HZ = np.zeros((128, NS, NT), np.complex128)
    if not isS:
        N = 2 * T
        for j in range(NS):
            HF1[:, j] = np.exp(-2j * np.pi * np.arange(NS) * j / NS)
            H2[j] = np.exp(-2j * np.pi * n2 * (j + NS * k2) / N)
        for q in range(128):
            HZ[q] = np.exp(2j * np.pi * np.outer(np.arange(NS), 128 * np.arange(NT) + q) / N) / N
    else:
        N = T
        for s in range(2):
            for j in range(NT):
                HF1[s * NT:(s + 1) * NT, s * NT + j] = np.exp(-2j * np.pi * np.arange(NT) * j / NT)
                H2[s * NT + j] = np.exp(-2j * np.pi * n2 * (j + NT * k2) / N)
        for q in range(128):
            for s in range(2):
                HZ[q, s * NT:(s + 1) * NT, s * H:(s + 1) * H] = \
                    np.exp(2j * np.pi * np.outer(np.arange(NT), 128 * np.arange(H) + q) / N) / N
    if not isS:
        H1 = HF1[:NT]
    else:
        H1 = np.concatenate([HF1[0:H], HF1[NT:NT + H]], 0)
    if not isS:
        act = list(range(NS // 2 + 1)) + [None]
        wts = [1.0 if j in (0, NS // 2) else 2.0 for j in range(NS // 2 + 1)] + [0.0]
    else:
        act = [s * NT + j for s in range(2) for j in range(NT // 2 + 1)]
        wts = [1.0 if j in (0, NT // 2) else 2.0 for s in range(2) for j in range(NT // 2 + 1)]
    def sel(M, axis):
        parts = []
        for a in act:
            if a is None:
                parts.append(np.zeros_like(np.take(M, [0], axis=axis)))
            else:
                parts.append(np.take(M, [a], axis=axis))
        return np.concatenate(parts, axis=axis)
    H1 = sel(H1, 1); HF1 = sel(HF1, 1); H2 = sel(H2, 0)
    HZ = sel(HZ, 1) * np.asarray(wts)[None, :, None]
    tb['hy_h1'] = np.concatenate([H1.real, H1.imag], 1).astype(NPBF)
    tb['hy_hf1'] = np.concatenate([HF1.real, HF1.imag], 1).astype(NPBF)
    tb['hy_h2'] = np.stack([H2.real, H2.imag, -H2.imag], 2).astype(NPBF)
    Fi = np.exp(2j * np.pi * np.outer(np.arange(128), np.arange(128)) / 128)
    a, b = _cplx_pair(Fi)
    tb['hy_i1'] = np.stack([a, b], 1).astype(NPBF)
    tb['hy_z'] = np.stack([HZ.real, -HZ.imag], 2).transpose(1, 0, 2, 3).astype(NPBF).copy()
    Lf = L
    mpos = np.arange(2 * T)
    mloc = mpos % (2 * Lf)
    lag = np.where(mloc < Lf, mloc, 2 * Lf - mloc)
    lag = np.where(mloc == Lf, 0, lag)
    mf = (mloc < Lf).astype(np.float32)
    mb = (mloc > Lf).astype(np.float32)
    tl = np.linspace(0.0, 1.0, Lf, dtype=np.float32)
    wl = (2.0 * np.float32(math.pi) * np.arange(Lf, dtype=np.float32) / np.float32(Lf)).astype(np.float32)
    fb = np.linspace(1e-4, 15, 16, dtype=np.float32)[None, :]
    feat = np.concatenate([tl[:, None], np.cos(fb * wl[:, None]), -np.sin(fb * wl[:, None])], -1).astype(np.float32)
    tb['flt_feat'] = np.ascontiguousarray(feat[lag].T).astype(np.float32)
    tb['flt_msk'] = np.stack([mf, mb, -tl[lag]], 0).astype(np.float32)
    deltas = np.abs(np.linspace(math.log(1e-2) / 0.3, math.log(1e-2) / 1.5, 256, dtype=np.float32))
    tb['flt_delta'] = np.ascontiguousarray(deltas.reshape(2, 128).T).astype(np.float32)
    sc = np.zeros((128, 4), np.float32)
    sc[:, 0] = 0.0 if isS else 1.0
    sc[:, 1] = 0.5 if isS else 1.0
    tb['scal'] = sc
    NB = T // 512
    hal = np.ones((NB, 2), np.float32)
    hal[0, 0] = 0.0; hal[NB - 1, 1] = 0.0
    if isS:
        hal[NB // 2, 0] = 0.0; hal[NB // 2 - 1, 1] = 0.0
    tb['hal'] = np.broadcast_to(hal.reshape(1, NB * 2), (128, NB * 2)).astype(np.float32).copy()
    pos = (np.arange(T) % L).astype(np.float32)
    inv = (10000.0 ** (-np.arange(32, dtype=np.float32) / 32)).astype(np.float32)
    angr = pos[:, None] * inv[None, :]
    tb['rot'] = np.stack([np.cos(angr), np.sin(angr)], 1).astype(np.float32)
    lg = np.log(1.0 - 2.0 ** (-5.0 - np.arange(4)))
    i = np.arange(128)
    ret_e = np.zeros((128, 2, 6, 128), np.float64)
    ret_tok = np.zeros((128, 2, 256), np.float64)
    ret_dec = np.zeros((128, 2, 2), np.float64)
    for c in range(2):
        for p in range(128):
            h = 2 * c + p // 64
            bf = (i + 1) * lg[h]; bb = (128 - i) * lg[h]
            ret_e[p, c, 0] = np.exp(bf - bf[64]) / 8.0
            ret_e[p, c, 1] = np.exp(bf[64] - bf)
            ret_e[p, c, 2] = np.exp(bb - bb[64]) / 8.0
            ret_e[p, c, 3] = np.exp(bb[64] - bb)
            ret_e[p, c, 4] = np.exp(bf) / 8.0
            ret_e[p, c, 5] = np.exp(bb) / 8.0
            ret_dec[p, c, :] = np.exp(128 * lg[h])
    for h in range(4):
        ret_tok[:, 0, h * 64:(h + 1) * 64] = np.exp((127 - i) * lg[h])[:, None]
        ret_tok[:, 1, h * 64:(h + 1) * 64] = np.exp(i * lg[h])[:, None]
    tb['ret_e'] = ret_e.astype(np.float32)
    tb['ret_tok'] = ret_tok.astype(np.float32)
    tb['ret_dec'] = ret_dec.astype(np.float32)
    mh_ret = np.zeros((128, 2), np.float32); mh_gla = np.zeros((128, 4), np.float32)
    bd_ret = np.zeros((128, 2, 256), np.float32); bd_gla = np.zeros((128, 1, 256), np.float32)
    for p in range(128):
        mh_ret[p, p // 64] = 1.0; mh_gla[p, p // 32] = 1.0
        for c in range(2):
            h = 2 * c + p // 64
            bd_ret[p, c, h * 64:(h + 1) * 64] = 1.0
        h = p // 32
        bd_gla[p, 0, h * 64:(h + 1) * 64] = 1.0
    tb['mh_ret'] = mh_ret; tb['mh_gla'] = mh_gla; tb['bd_ret'] = bd_ret; tb['bd_gla'] = bd_gla
    jj = np.arange(128)[:, None]; ii = np.arange(128)[None, :]
    tri = np.zeros((128, 6, 128), np.float32)
    tri[:, 0] = (jj <= ii)
    tri[:, 1] = (jj > ii)
    tri[:, 2] = -(jj <= ii).astype(np.float32) / 16.0
    tri[:, 3] = -(jj >= ii).astype(np.float32) / 16.0
    tri[:, 4] = -(jj > ii).astype(np.float32) / 16.0
    tri[:, 5] = -(jj < ii).astype(np.float32) / 16.0
    tb['tri'] = tri
    return tb


OFF_FN, OFF_QR, OFF_HY, OFF_QG, OFF_GATES = 0, 256, 1280, 2048, 2848
NTM = 1824


def build_program(T, debug=()):
    NT, NS, NB, H = T // 128, T // 64, T // 512, T // 256
    NSA = NT + 2
    nc = bass.Bass("TRN2", target_bir_lowering=False)
    I = {}

    def inp(name, shape, dt=F32):
        I[name] = nc.dram_tensor(name, list(shape), dt, kind="ExternalInput").ap()
        return I[name]

    def scratch(name, shape, dt=F32):
        kind = "ExternalOutput" if name in debug else "Internal"
        return nc.dram_tensor(name, list(shape), dt, kind=kind).ap()

    x_in = inp("x", [T, D]); inp("cT", [128, 8, 2])
    inp("ada_w", [DEPTH, D, 6 * D]); inp("ada_b_col", [DEPTH, 128, 48]); inp("ada_b", [DEPTH, 6 * D])
    inp("normw_col", [DEPTH, 128, 4, 8]); inp("norm_post_mix", [DEPTH, D]); inp("norm_post_ffn", [DEPTH, D])
    inp("w_in", [DEPTH, D, 6944])
    inp("hy_cw", [DEPTH, 128, 6, 3]); inp("hy_cb", [DEPTH, 128, 6])
    inp("flt_w1", [DEPTH, 33, 64]); inp("flt_c1", [DEPTH, 64, 2]); inp("flt_w2", [DEPTH, 64, 64]); inp("flt_c2", [DEPTH, 64, 2])
    inp("flt_w3", [DEPTH, 64, 1024]); inp("flt_b3", [DEPTH, 1024]); inp("hy_skip", [DEPTH, 2, 256])
    inp("gla_wd", [DEPTH, 33, 256]); inp("ret_gn", [DEPTH, 256]); inp("gla_gn", [DEPTH, 256])
    inp("w_branch", [DEPTH, 1024, D]); inp("w_out", [DEPTH, D, D])
    inp("ffn_up", [DEPTH, D, 2 * DFF]); inp("ffn_cw", [DEPTH, 128, 44, 3]); inp("ffn_cb", [DEPTH, 128, 44])
    inp("ffn_down", [DEPTH, DFF, D])
    for nm, shp, dt in (("ident_b", [128, 128], BF16), ("ident_f", [128, 128], F32), ("bdcs", [128, 2, 128], BF16),
                        ("fn_m1", [NT, 2, 2 * NS], BF16), ("fn_m2", [NT, 128, 2, 2, 128], BF16),
                        ("hy_h1", [NT, 2 * NSA], BF16), ("hy_hf1", [NS, 2 * NSA], BF16), ("hy_h2", [NSA, 128, 3, 128], BF16),
                        ("hy_i1", [128, 2, 256], BF16), ("hy_z", [NSA, 128, 2, NT], BF16),
                        ("flt_feat", [33, 2 * T], F32), ("flt_msk", [3, 2 * T], F32), ("flt_delta", [128, 2], F32),
                        ("scal", [128, 4], F32), ("hal", [128, NB * 2], F32), ("rot", [T, 2, 32], F32),
                        ("ret_e", [128, 2, 6, 128], F32), ("ret_tok", [128, 2, 256], F32), ("ret_dec", [128, 2, 2], F32),
                        ("mh_ret", [128, 2], F32), ("mh_gla", [128, 4], F32), ("bd_ret", [128, 2, 256], F32),
                        ("bd_gla", [128, 1, 256], F32), ("tri", [128, 6, 128], F32)):
        inp(nm, shp, dt)
    y_out = nc.dram_tensor("y", [T, D], F32, kind="ExternalOutput").ap()
    x1d = scratch("x1d", [T, D]); xmid = scratch("xmid", [T, D])
    projT = scratch("projT", [T, NTM]); zF = scratch("zF", [2, 256, T], BF16); uhF = scratch("uhF", [768, T])
    brF = scratch("brF", [1024, T], BF16); modrow = scratch("modrow", [2, 2048]); hF = scratch("hF", [D, T + 2], BF16); gFF = scratch("gFF", [DFF, T], BF16)
    gF = scratch("gF", [512, 2 * T], BF16); rnD = scratch("rnD", [512]); Gd = scratch("Gd", [16, 128, NSA, 64], BF16)

    es = ExitStack()
    with es:
        es.enter_context(nc.allow_non_contiguous_dma(reason="strided scratch layouts"))
        P = Prog(nc, es)
        dB = {k: Buf(k) for k in ("x1d", "xmid", "projT", "zF", "uhF", "brF", "modrow", "gF", "rnD", "Gd", "y", "in", "hF", "gFF")}
        IN = dB["in"]

        def ld(dst_buf, dst_ap, src_ap, src=IN):
            return P.dma('sp', dst_ap, src_ap, reads=[src], writes=[dst_buf])

        def stq(dst_ap, src_buf, src_ap, dst):
            return P.dma('pool', dst_ap, src_ap, reads=[src_buf], writes=[dst])

        def dve(fn, r, w):
            return P.op('dve', fn, r, w)

        def act(fn, r, w):
            return P.op('act', fn, r, w)

        def gps(fn, r, w):
            return P.op('pool' if USE_POOL else 'dve', fn, r, w)

        def pe(fn, r, w):
            return P.op('pe', fn, r, w, accum=True)

        identb = P.sb("identb", [128, 128], BF16); identf = P.sb("identf", [128, 128], F32)
        scal = P.sb("scal", [128, 4]); hal = P.sb("hal", [128, NB * 2]); epsb = P.sb("epsb", [128, 1])
        modc = P.sb("modc", [128, 48, 2]); Am = P.sb("Am", [128, 8, 2]); Af = P.sb("Af", [128, 8, 2])
        gtb = P.sb("gtb", [128, 2, 2, D])
        ld(identb, identb[:], I["ident_b"][:, :]); ld(identf, identf[:], I["ident_f"][:, :])
        ld(scal, scal[:], I["scal"][:, :]); ld(hal, hal[:], I["hal"][:, :])
        dve(lambda e: e.memset(epsb[:], EPS), [], [epsb])

        def phase_mod(l):
            P.push()
            cT = P.sb("cT", [128, 8, 2]); scT = P.sb("scT", [128, 8, 2])
            ld(cT, cT[:], I["cT"][:, :, :])
            act(lambda e: e.activation(out=scT[:], in_=cT[:], func=AF.Silu), [cT], [scT])
            psc = P.ps("psc", [128, 96]); psr = P.ps("psr", [2, 2048]); macc = P.sb("macc", [128, 96])
            wts = [P.sb(f"adaw{i}", [128, 6 * D]) for i in range(2)]
            rowcols = (2048, 2560, 5120, 5632)
            for kc in range(8):
                wt = wts[kc % 2]
                ld(wt, wt[:], I["ada_w"][l, kc * 128:(kc + 1) * 128, :])
                for q in range(48):
                    pe(lambda e, q=q: e.matmul(psc[:, 2 * q:2 * q + 2], lhsT=wt[:, q * 128:(q + 1) * 128], rhs=scT[:, kc, :],
                                               start=True, stop=True), [wt, scT], [psc])
                if kc == 0:
                    dve(lambda e: e.tensor_copy(out=macc[:], in_=psc[:]), [psc], [macc])
                else:
                    dve(lambda e: e.tensor_tensor(out=macc[:], in0=macc[:], in1=psc[:], op=ALU.add), [psc, macc], [macc])
                for bi, c0 in enumerate(rowcols):
                    pe(lambda e, bi=bi, c0=c0: e.matmul(psr[:, bi * 512:(bi + 1) * 512], lhsT=scT[:, kc, :], rhs=wt[:, c0:c0 + 512],
                                                        start=(kc == 0), stop=(kc == 7)), [wt, scT], [psr])
            abc = P.sb("abc", [128, 48]); nwc = P.sb("nwc", [128, 4, 8])
            ld(abc, abc[:], I["ada_b_col"][l]); ld(nwc, nwc[:], I["normw_col"][l])
            dve(lambda e: e.tensor_tensor(out=modc[:], in0=macc[:].rearrange("p (q s) -> p q s", s=2),
                                          in1=abc[:].unsqueeze(2).broadcast_to([128, 48, 2]), op=ALU.add), [macc, abc], [modc])
            dve(lambda e: e.scalar_tensor_tensor(out=Am[:], in0=modc[:, 8:16, :], scalar=1.0,
                                                 in1=nwc[:, 0, :].unsqueeze(2).broadcast_to([128, 8, 2]), op0=ALU.add, op1=ALU.mult),
                [modc, nwc], [Am])
            dve(lambda e: e.scalar_tensor_tensor(out=Af[:], in0=modc[:, 32:40, :], scalar=1.0,
                                                 in1=nwc[:, 2, :].unsqueeze(2).broadcast_to([128, 8, 2]), op0=ALU.add, op1=ALU.mult),
                [modc, nwc], [Af])
            abr = P.sb("abr", [2, 2048]); nwr = P.sb("nwr", [2, 2048]); gr = P.sb("gr", [2, 2048])
            ld(abr, abr[:, 0:1024], I["ada_b"][l, 2048:3072].partition_broadcast(2))
            ld(abr, abr[:, 1024:2048], I["ada_b"][l, 5120:6144].partition_broadcast(2))
            ld(nwr, nwr[:, 0:1024], I["norm_post_mix"][l].partition_broadcast(2))
            ld(nwr, nwr[:, 1024:2048], I["norm_post_ffn"][l].partition_broadcast(2))
            dve(lambda e: e.tensor_tensor(out=gr[:], in0=psr[:], in1=abr[:], op=ALU.add), [psr, abr], [gr])
            dve(lambda e: e.tensor_tensor(out=gr[:], in0=gr[:], in1=nwr[:], op=ALU.mult), [gr, nwr], [gr])
            stq(modrow[:, :], gr, gr[:], dB["modrow"])
            for sg in range(2):
                ld(gtb, gtb[:, sg, :, :].rearrange("p a d -> p (a d)"), modrow[sg].partition_broadcast(128), src=dB["modrow"])
            P.pop()

        def phase_norm(src_ap2d, src_buf, A, bq0):
            P.push()
            zt = P.sb("zt", [128, 8, 1], BF16)
            dve(lambda e: e.memset(zt[:], 0.0), [], [zt])
            hFv = hF.rearrange("(c p) t -> p c t", p=128)
            stq(hFv[:, :, 0:1], zt, zt[:], dB["hF"]); stq(hFv[:, :, T + 1:T + 2], zt, zt[:], dB["hF"])
            xts = [P.sb(f"xt{i}", [128, D]) for i in range(2)]
            sq = P.sb("sq", [128, D]); ss = P.sb("ss", [128, 1]); rs = P.sb("rs", [128, 1])
            xn = P.sb("xn", [128, D], BF16); ptr = P.ps("ptr", [128, 8, 128], BF16); tmp = P.sb("tmpT", [128, 8, 128])
            hts = [P.sb(f"ht{i}", [128, 8, 512], BF16) for i in range(2)]
            xns = [P.sb(f"xnN{i}", [128, D], BF16) for i in range(2)]

            def gen(it):
                xt, ht, xn = xts[it % 2], hts[(it // 4) % 2], xns[it % 2]
                q4 = it % 4
                seg = 0 if it < NT // 2 else 1
                ld(xt, xt[:], src_ap2d[it * 128:(it + 1) * 128, :], src=src_buf)
                act(lambda e: e.activation(out=sq[:], in_=xt[:], func=AF.Square, accum_out=ss[:, 0:1]), [xt], [sq, ss])
                act(lambda e: e.activation(out=rs[:], in_=ss[:], func=AF.Sqrt, scale=1.0 / D, bias=epsb[:, 0:1]), [ss, epsb], [rs])
                yield
                dve(lambda e: e.reciprocal(out=rs[:], in_=rs[:]), [rs], [rs])
                dve(lambda e: e.tensor_scalar(out=xn[:], in0=xt[:], scalar1=rs[:, 0:1], scalar2=None, op0=ALU.mult), [xt, rs], [xn])
                yield 'prev_done'
                for c8 in range(8):
                    pe(lambda e, c8=c8: e.transpose(ptr[:, c8, :], xn[:, c8 * 128:(c8 + 1) * 128], identb[:]), [xn, identb], [ptr])
                yield
                dve(lambda e: e.tensor_tensor(out=tmp[:], in0=ptr[:], in1=A[:, :, seg:seg + 1].broadcast_to([128, 8, 128]), op=ALU.mult),
                    [ptr, A], [tmp])
                dve(lambda e: e.tensor_tensor(out=ht[:, :, q4 * 128:(q4 + 1) * 128], in0=tmp[:], in1=modc[:, bq0:bq0 + 8, seg:seg + 1].broadcast_to([128, 8, 128]), op=ALU.add),
                    [tmp, modc], [ht])
                if q4 == 3:
                    stq(hFv[:, :, 1 + (it - 3) * 128:1 + (it + 1) * 128], ht, ht[:], dB["hF"])

            run_pipelined(gen, range(NT))
            P.pop()

        def load_w_bf16(dst, dst_ap_fn, src_ap_fn, nk, width, stg):
            for k in range(nk):
                st = stg[k % 2]
                ld(st, st[:, 0:width], src_ap_fn(k))
                if k % 2 == 0:
                    dve(lambda e, k=k, st=st: e.tensor_copy(out=dst_ap_fn(k), in_=st[:, 0:width]), [st], [dst])
                else:
                    act(lambda e, k=k, st=st: e.activation(out=dst_ap_fn(k), in_=st[:, 0:width], func=AF.Copy), [st], [dst])

        def load_window(hw, b):
            hFv = hF.rearrange("(c p) t -> p c t", p=128)
            ld(hw, hw[:], hFv[:, :, b * 512:b * 512 + 514], src=dB["hF"])
            for side, col in ((0, 0), (1, 513)):
                dve(lambda e, side=side, col=col: e.tensor_tensor(out=hw[:, :, col:col + 1], in0=hw[:, :, col:col + 1],
                                                                  in1=hal[:, 2 * b + side:2 * b + side + 1].unsqueeze(1).broadcast_to([128, 8, 1]),
                                                                  op=ALU.mult), [hw, hal], [hw])

        def conv3_fm(out_ap, pm, ph, cw, cb, ci, rbufs, wbuf):
            act(lambda e: e.activation(out=out_ap, in_=pm[:, 0:512], func=AF.Identity, scale=cw[:, ci, 1:2], bias=cb[:, ci:ci + 1]), rbufs, [wbuf])
            dve(lambda e: e.scalar_tensor_tensor(out=out_ap[:, 1:512], in0=pm[:, 0:511], scalar=cw[:, ci, 0:1], in1=out_ap[:, 1:512],
                                                 op0=ALU.mult, op1=ALU.add), rbufs + [wbuf], [wbuf])
            dve(lambda e: e.scalar_tensor_tensor(out=out_ap[:, 0:511], in0=pm[:, 1:512], scalar=cw[:, ci, 2:3], in1=out_ap[:, 0:511],
                                                 op0=ALU.mult, op1=ALU.add), rbufs + [wbuf], [wbuf])
            dve(lambda e: e.scalar_tensor_tensor(out=out_ap[:, 0:1], in0=ph[:, 0:1], scalar=cw[:, ci, 0:1], in1=out_ap[:, 0:1],
                                                 op0=ALU.mult, op1=ALU.add), rbufs + [wbuf], [wbuf])
            dve(lambda e: e.scalar_tensor_tensor(out=out_ap[:, 511:512], in0=ph[:, 1:2], scalar=cw[:, ci, 2:3], in1=out_ap[:, 511:512],
                                                 op0=ALU.mult, op1=ALU.add), rbufs + [wbuf], [wbuf])

        def phase_A(l):
            P.push()
            wA = P.sb("wA", [128, 8, OFF_GATES], BF16)
            stg = [P.sb(f"stgA{i}", [128, OFF_GATES]) for i in range(2)]
            load_w_bf16(wA, lambda k: wA[:, k, :], lambda k: I["w_in"][l, k * 128:(k + 1) * 128, 0:OFF_GATES], 8, OFF_GATES, stg)
            bdcs = P.sb("bdcs", [128, 2, 128], BF16); ld(bdcs, bdcs[:], I["bdcs"][:, :, :])
            cw = P.sb("hcw", [128, 6, 3]); cb = P.sb("hcb", [128, 6])
            ld(cw, cw[:], I["hy_cw"][l]); ld(cb, cb[:], I["hy_cb"][l])
            hws = [P.sb(f"hw{i}", [128, 8, 514], BF16) for i in range(2)]
            pms = [P.ps(f"pmA{i}", [128, 512]) for i in range(2)]
            ph = P.ps("phA", [128, 2])
            pjs = [P.sb(f"pj{i}", [128, NTM]) for i in range(2)]; uT = P.sb("uT", [128, 2, 512], BF16)
            zts = [P.sb(f"ztA{i}", [128, 512], BF16) for i in range(2)]
            cvs = [P.sb(f"cvA{i}", [128, 512]) for i in range(2)]
            tmcols = ((256, 512), (768, 512), (2048, 512), (2560, 288))
            zFv = zF
            npm = 0
            load_window(hws[0], 0)
            for b in range(NB):
                hw = hws[b % 2]
                if b + 1 < NB:
                    load_window(hws[(b + 1) % 2], b + 1)
                t0 = b * 512
                for s in range(4):
                    o = 0
                    pj = pjs[s % 2]
                    for (c0, wd) in tmcols:
                        pm = pms[npm % 2]; npm += 1
                        for kc in range(8):
                            pe(lambda e, kc=kc, pm=pm, c0=c0, wd=wd: e.matmul(pm[:, 0:wd], lhsT=hw[:, kc, 1 + s * 128:1 + (s + 1) * 128],
                                                                                rhs=wA[:, kc, c0:c0 + wd], start=(kc == 0), stop=(kc == 7)),
                               [hw, wA], [pm])
                        if (npm % 2) == 0:
                            dve(lambda e, pm=pm, o=o, wd=wd, pj=pj: e.tensor_copy(out=pj[:, o:o + wd], in_=pm[:, 0:wd]), [pm], [pj])
                        else:
                            act(lambda e, pm=pm, o=o, wd=wd, pj=pj: e.activation(out=pj[:, o:o + wd], in_=pm[:, 0:wd], func=AF.Copy), [pm], [pj])
                        o += wd
                        yield
                    stq(projT[t0 + s * 128:t0 + (s + 1) * 128, :], pj, pj[:], dB["projT"])
                for ch in range(2):
                    pm = pms[npm % 2]; npm += 1
                    for kc in range(8):
                        pe(lambda e, kc=kc, pm=pm, ch=ch: e.matmul(pm[:], lhsT=wA[:, kc, ch * 128:(ch + 1) * 128], rhs=hw[:, kc, 1:513],
                                                                     start=(kc == 0), stop=(kc == 7)), [hw, wA], [pm])
                    act(lambda e, pm=pm, ch=ch: e.activation(out=uT[:, ch, :], in_=pm[:], func=AF.Copy), [pm], [uT])
                    yield
                for ri in range(2):
                    for ch in range(2):
                        pm = pms[npm % 2]; zt = zts[npm % 2]; npm += 1
                        pe(lambda e, pm=pm, ri=ri, ch=ch: e.matmul(pm[:], lhsT=bdcs[:, ri, :], rhs=uT[:, ch, :], start=True, stop=True), [bdcs, uT], [pm])
                        act(lambda e, pm=pm, zt=zt: e.activation(out=zt[:], in_=pm[:], func=AF.Copy), [pm], [zt])
                        stq(zFv[ri, ch * 128:(ch + 1) * 128, t0:t0 + 512], zt, zt[:], dB["zF"])
                        yield
                for ch in range(6):
                    pm = pms[npm % 2]; cv = cvs[npm % 2]; npm += 1
                    c0 = OFF_HY + ch * 128
                    for kc in range(8):
                        pe(lambda e, kc=kc, pm=pm, c0=c0: e.matmul(pm[:], lhsT=wA[:, kc, c0:c0 + 128], rhs=hw[:, kc, 1:513],
                                                                     start=(kc == 0), stop=(kc == 7)), [hw, wA], [pm])
                    for kc in range(8):
                        pe(lambda e, kc=kc, c0=c0: e.matmul(ph[:], lhsT=wA[:, kc, c0:c0 + 128], rhs=hw[:, kc, 0:514:513],
                                                              start=(kc == 0), stop=(kc == 7)), [hw, wA], [ph])
                    conv3_fm(cv[:], pm, ph, cw, cb, ch, [pm, ph, cw, cb], cv)
                    stq(uhF[ch * 128:(ch + 1) * 128, t0:t0 + 512], cv, cv[:], dB["uhF"])
                    yield
            yield 'finished'
            P.pop()

        def rms_residual(py, xt, sq, ss, rs, tmpo, outt, seg, which, dst_ap, dst_buf):
            act(lambda e: e.activation(out=sq[:], in_=py[:], func=AF.Square, accum_out=ss[:, 0:1]), [py], [sq, ss])
            act(lambda e: e.activation(out=rs[:], in_=ss[:], func=AF.Sqrt, scale=1.0 / D, bias=epsb[:, 0:1]), [ss, epsb], [rs])
            dve(lambda e: e.reciprocal(out=rs[:], in_=rs[:]), [rs], [rs])
            dve(lambda e: e.scalar_tensor_tensor(out=tmpo[:], in0=py[:], scalar=rs[:, 0:1], in1=gtb[:, seg, which, :], op0=ALU.mult, op1=ALU.mult),
                [py, rs, gtb], [tmpo])
            dve(lambda e: e.tensor_tensor(out=outt[:], in0=tmpo[:], in1=xt[:], op=ALU.add), [tmpo, xt], [outt])
            stq(dst_ap, outt, outt[:], dst_buf)

        def phase_C(l, src_ap2d, src_buf):
            P.push()
            wbr = P.sb("wbr", [128, 8, D], BF16); wout = P.sb("wout", [128, 8, D], BF16); wg = P.sb("wg", [128, 8, 4 * D], BF16)
            stg = [P.sb(f"stgC{i}", [128, 4 * D]) for i in range(2)]
            load_w_bf16(wbr, lambda k: wbr[:, k, :], lambda k: I["w_branch"][l, k * 128:(k + 1) * 128, :], 8, D, stg)
            load_w_bf16(wout, lambda k: wout[:, k, :], lambda k: I["w_out"][l, k * 128:(k + 1) * 128, :], 8, D, stg)
            load_w_bf16(wg, lambda k: wg[:, k, :], lambda k: I["w_in"][l, k * 128:(k + 1) * 128, OFF_GATES:OFF_GATES + 4 * D], 8, 4 * D, stg)
            xts = [P.sb(f"xtC{i}", [128, D]) for i in range(2)]
            hts = [P.sb(f"htC{i}", [128, 8, 128], BF16) for i in range(2)]
            brs = [P.sb(f"brC{i}", [128, 8, 128], BF16) for i in range(2)]
            pgs = [P.ps(f"pg{i}", [128, 512]) for i in range(2)]; pbs = [P.ps(f"pb{i}", [128, 512]) for i in range(2)]
            py = P.ps("py", [128, D]); ptr = P.ps("ptrC", [128, 8, 128], BF16)
            sigs = [P.sb(f"sig{i}", [128, 512]) for i in range(2)]; tms = [P.sb(f"tmc{i}", [128, 512]) for i in range(2)]
            merged = P.sb("merged", [128, D]); tmpm = P.sb("tmpm", [128, D]); mb = P.sb("mb", [128, D], BF16)
            mT = P.sb("mT", [128, 8, 128], BF16); sq = P.sb("sqC", [128, D]); ss = P.sb("ssC", [128, 1]); rs = P.sb("rsC", [128, 1])
            outt = P.sb("outC", [128, D])
            hFv = hF.rearrange("(c p) t -> p c t", p=128); brv = brF.rearrange("(k p) t -> p k t", p=128)
            mergeds = [merged, P.sb("merged1", [128, D])]

            def gen(it):
                merged = mergeds[it % 2]
                xt, ht, brt = xts[it % 2], hts[it % 2], brs[it % 2]
                seg = 0 if it < NT // 2 else 1
                ld(xt, xt[:], src_ap2d[it * 128:(it + 1) * 128, :], src=src_buf)
                ld(ht, ht[:], hFv[:, :, 1 + it * 128:1 + (it + 1) * 128], src=dB["hF"])
                ld(brt, brt[:], brv[:, :, it * 128:(it + 1) * 128], src=dB["brF"])
                for br in range(4):
                    for cb in range(2):
                        u = br * 2 + cb
                        pgu, pbu, sgu, tmu = pgs[u % 2], pbs[u % 2], sigs[u % 2], tms[u % 2]
                        for kc in range(8):
                            pe(lambda e, kc=kc, cb=cb, pgu=pgu, br=br: e.matmul(pgu[:], lhsT=ht[:, kc, :],
                                                                               rhs=wg[:, kc, br * D + cb * 512:br * D + (cb + 1) * 512], start=(kc == 0), stop=(kc == 7)),
                               [ht, wg], [pgu])
                        for k2 in range(2):
                            pe(lambda e, k2=k2, cb=cb, pbu=pbu, br=br: e.matmul(pbu[:], lhsT=brt[:, br * 2 + k2, :],
                                                                               rhs=wbr[:, br * 2 + k2, cb * 512:(cb + 1) * 512], start=(k2 == 0), stop=(k2 == 1)),
                               [brt, wbr], [pbu])
                        act(lambda e, pgu=pgu, sgu=sgu: e.activation(out=sgu[:], in_=pgu[:], func=AF.Sigmoid), [pgu], [sgu])
                        mslice = merged[:, cb * 512:(cb + 1) * 512]
                        if br == 0:
                            dve(lambda e, sgu=sgu, pbu=pbu, mslice=mslice: e.tensor_tensor(out=mslice, in0=sgu[:], in1=pbu[:], op=ALU.mult), [sgu, pbu], [merged])
                        else:
                            dve(lambda e, sgu=sgu, pbu=pbu, tmu=tmu: e.tensor_tensor(out=tmu[:], in0=sgu[:], in1=pbu[:], op=ALU.mult), [sgu, pbu], [tmu])
                            dve(lambda e, tmu=tmu, mslice=mslice: e.tensor_tensor(out=mslice, in0=mslice, in1=tmu[:], op=ALU.add), [merged, tmu], [merged])
                        yield
                yield 'prev_done'
                act(lambda e: e.activation(out=mb[:], in_=merged[:], func=AF.Copy), [merged], [mb])
                yield
                for c8 in range(8):
                    pe(lambda e, c8=c8: e.transpose(ptr[:, c8, :], mb[:, c8 * 128:(c8 + 1) * 128], identb[:]), [mb, identb], [ptr])
                yield
                dve(lambda e: e.tensor_copy(out=mT[:], in_=ptr[:]), [ptr], [mT])
                yield
                for cb in range(2):
                    for kc in range(8):
                        pe(lambda e, kc=kc, cb=cb: e.matmul(py[:, cb * 512:(cb + 1) * 512], lhsT=mT[:, kc, :], rhs=wout[:, kc, cb * 512:(cb + 1) * 512],
                                                           start=(kc == 0), stop=(kc == 7)), [mT, wout], [py])
                rms_residual(py, xt, sq, ss, rs, tmpm, outt, seg, 0, xmid[it * 128:(it + 1) * 128, :], dB["xmid"])
            run_pipelined(gen, range(NT))
            P.pop()

        def phase_D(l, dst_ap2d, dst_buf):
            P.push()
            wup = P.sb("wup", [128, 8, 2 * DFF], BF16)
            stg = [P.sb(f"stgD{i}", [128, 1408]) for i in range(2)]
            for part in range(4):
                c0 = part * 1408
                load_w_bf16(wup, lambda k, c0=c0: wup[:, k, c0:c0 + 1408], lambda k, c0=c0: I["ffn_up"][l, k * 128:(k + 1) * 128, c0:c0 + 1408], 8, 1408, stg)
            cw = P.sb("fcw", [128, 44, 3]); cb_ = P.sb("fcb", [128, 44])
            ld(cw, cw[:], I["ffn_cw"][l]); ld(cb_, cb_[:], I["ffn_cb"][l])
            hws = [P.sb(f"hwD{i}", [128, 8, 514], BF16) for i in range(2)]
            pms = [P.ps(f"pmD{i}", [128, 512]) for i in range(4)]; phs = [P.ps(f"phD{i}", [128, 2]) for i in range(4)]
            cvs = [P.sb(f"cvD{i}", [128, 512]) for i in range(4)]; gls = [P.sb(f"glD{i}", [128, 512]) for i in range(2)]
            gts = [P.sb(f"gtD{i}", [128, 512], BF16) for i in range(2)]
            load_window(hws[0], 0)
            for b in range(NB):
                hw = hws[b % 2]
                if b + 1 < NB:
                    load_window(hws[(b + 1) % 2], b + 1)
                for pc in range(22):
                    for wi in range(2):
                        ci = pc + 22 * wi
                        bi = 2 * (pc % 2) + wi
                        pm, ph, cv = pms[bi], phs[bi], cvs[bi]
                        for kc in range(8):
                            pe(lambda e, kc=kc, pm=pm, ci=ci: e.matmul(pm[:], lhsT=wup[:, kc, ci * 128:(ci + 1) * 128], rhs=hw[:, kc, 1:513],
                                                                         start=(kc == 0), stop=(kc == 7)), [hw, wup], [pm])
                        for kc in range(8):
                            pe(lambda e, kc=kc, ph=ph, ci=ci: e.matmul(ph[:], lhsT=wup[:, kc, ci * 128:(ci + 1) * 128], rhs=hw[:, kc, 0:514:513],
                                                                         start=(kc == 0), stop=(kc == 7)), [hw, wup], [ph])
                        conv3_fm(cv[:], pm, ph, cw, cb_, ci, [pm, ph, cw, cb_], cv)
                    gt = gts[pc % 2]; gl = gls[pc % 2]; cg_, cv_ = cvs[2 * (pc % 2)], cvs[2 * (pc % 2) + 1]
                    act(lambda e, gl=gl, cg_=cg_: e.activation(out=gl[:], in_=cg_[:], func=AF.Gelu_apprx_tanh), [cg_], [gl])
                    P.op('pool', lambda e, gt=gt, gl=gl, cv_=cv_: e.tensor_tensor(out=gt[:], in0=gl[:], in1=cv_[:], op=ALU.mult), [gl, cv_], [gt])
                    stq(gFF[pc * 128:(pc + 1) * 128, b * 512:(b + 1) * 512], gt, gt[:], dB["gFF"])
            P.pop()
            P.push()
            wdn = P.sb("wdn", [128, 22, D], BF16)
            stg = [P.sb(f"stgE{i}", [128, D]) for i in range(2)]
            load_w_bf16(wdn, lambda k: wdn[:, k, :], lambda k: I["ffn_down"][l, k * 128:(k + 1) * 128, :], 22, D, stg)
            xts = [P.sb(f"xtE{i}", [128, D]) for i in range(2)]
            ggs = [P.sb(f"ggE{i}", [128, 22, 512], BF16) for i in range(2)]
            pys = [P.ps(f"pyE{i}", [128, D]) for i in range(2)]; sq = P.sb("sqE", [128, D])
            sss = [P.sb(f"ssE{i}", [128, 1]) for i in range(2)]; rss = [P.sb(f"rsE{i}", [128, 1]) for i in range(2)]
            tmpos = [P.sb(f"tmpE{i}", [128, D]) for i in range(2)]; outts = [P.sb(f"outE{i}", [128, D]) for i in range(2)]
            gv = gFF.rearrange("(k p) t -> p k t", p=128)
            for it in range(NT):
                xt, gg = xts[it % 2], ggs[(it // 4) % 2]
                py, ss, rs, tmpo, outt = pys[it % 2], sss[it % 2], rss[it % 2], tmpos[it % 2], outts[it % 2]
                q4 = it % 4
                seg = 0 if it < NT // 2 else 1
                ld(xt, xt[:], xmid[it * 128:(it + 1) * 128, :], src=dB["xmid"])
                if q4 == 0:
                    ld(gg, gg[:], gv[:, :, it * 128:(it + 4) * 128], src=dB["gFF"])
                for cb in range(2):
                    for pc in range(22):
                        pe(lambda e, pc=pc, cb=cb: e.matmul(py[:, cb * 512:(cb + 1) * 512], lhsT=gg[:, pc, q4 * 128:(q4 + 1) * 128], rhs=wdn[:, pc, cb * 512:(cb + 1) * 512],
                                                           start=(pc == 0), stop=(pc == 21)), [gg, wdn], [py])
                rms_residual(py, xt, sq, ss, rs, tmpo, outt, seg, 1, dst_ap2d[it * 128:(it + 1) * 128, :], dst_buf)
            P.pop()

        def phase_fnet(l):
            P.push()
            Cg = 64
            m1 = P.sb("fm1", [NT, 2, 2 * NS], BF16); ld(m1, m1[:], I["fn_m1"][:, :, :])
            zin = P.sb("zin", [NT, 2, Cg, 128], BF16); A = P.sb("fA", [128, NS, 2, Cg], BF16)
            osb = P.sb("osb", [Cg, T], BF16); osbv = osb[:].rearrange("c (k j) -> c k j", j=NT)
            ps1s = [P.ps(f"fps1{i}", [128, 2, 2 * NS]) for i in range(2)]
            ps2s = [P.ps(f"fps2{i}", [Cg, 4, 128]) for i in range(2)]
            m2s = [P.sb(f"fm2{i}", [128, 4, 2, 2, 128], BF16) for i in range(2)]
            n1 = 0
            for g in range(256 // Cg):
                c0 = g * Cg
                for ri in range(2):
                    ld(zin, zin[:, ri, :, :], zF[ri, c0:c0 + Cg, :].rearrange("c (g n) -> g c n", n=128), src=dB["zF"])
                for c in range(0, Cg, 2):
                    ps1 = ps1s[n1 % 2]; n1 += 1
                    for cc in range(2):
                        pe(lambda e, cc=cc, ps1=ps1: e.matmul(ps1[:, cc, :], lhsT=zin[:, 0, c + cc, :], rhs=m1[:, 0, :], start=True, stop=False), [zin, m1], [ps1])
                        pe(lambda e, cc=cc, ps1=ps1: e.matmul(ps1[:, cc, :], lhsT=zin[:, 1, c + cc, :], rhs=m1[:, 1, :], start=False, stop=True), [zin, m1], [ps1])
                    for ri in range(2):
                        src_ap = ps1[:, :, ri * NS:(ri + 1) * NS].rearrange("p c s -> p s c")
                        if ri == 0:
                            dve(lambda e, src_ap=src_ap: e.tensor_copy(out=A[:, :, 0, c:c + 2], in_=src_ap), [ps1], [A])
                        else:
                            act(lambda e, src_ap=src_ap: e.activation(out=A[:, :, 1, c:c + 2], in_=src_ap, func=AF.Copy), [ps1], [A])
                m2v = I["fn_m2"].rearrange("j p s r k -> p j s r k")
                for j in range(NT):
                    m2t = m2s[(j // 4) % 2]
                    if j % 4 == 0:
                        ld(m2t, m2t[:], m2v[:, j:j + 4, :, :, :])
                    ps2 = ps2s[(j // 4) % 2]
                    k = 0
                    for s_ in range(2):
                        for ri in range(2):
                            pe(lambda e, s_=s_, ri=ri, k=k, ps2=ps2, m2t=m2t: e.matmul(ps2[:, j % 4, :], lhsT=A[:, s_ * NT + j, ri, :], rhs=m2t[:, j % 4, s_, ri, :],
                                                                                      start=(k == 0), stop=(k == 3)), [A, m2t], [ps2])
                            k += 1
                    if j % 4 == 3:
                        j0 = j - 3
                        src_ap = ps2[:, :, :].rearrange("c j k -> c k j")
                        if (j // 4) % 2 == 0:
                            dve(lambda e, src_ap=src_ap, j0=j0: e.tensor_copy(out=osbv[:, :, j0:j0 + 4], in_=src_ap), [ps2], [osb])
                        else:
                            act(lambda e, src_ap=src_ap, j0=j0: e.activation(out=osbv[:, :, j0:j0 + 4], in_=src_ap, func=AF.Copy), [ps2], [osb])
                stq(brF[c0:c0 + Cg, :], osb, osb[:], dB["brF"])
            P.pop()

        def drain(g):
            for _ in g:
                pass

        def run_concurrent(primary, secondary, ratio=int(os.environ.get("CONC_RATIO", "1"))):
            p_alive, s_alive, p_fin = True, True, False
            while p_alive or s_alive:
                for _ in range(PRIM_STEPS):
                    if p_alive and not (p_fin and s_alive):
                        try:
                            if next(primary) == 'finished':
                                p_fin = True
                        except StopIteration:
                            p_alive = False
                for _ in range(ratio):
                    if s_alive:
                        try:
                            next(secondary)
                        except StopIteration:
                            s_alive = False

        def run_pipelined(make_gen, order, depth=PIPE_DEPTH):
            active = []
            order = list(order)
            pos = 0
            while pos < len(order) or active:
                if pos < len(order) and len(active) < depth and all(e[1] == 'second' for e in active):
                    active.append([make_gen(order[pos]), 'first']); pos += 1
                for ent in list(active):
                    if ent[1] == 'waiting':
                        if active[0] is ent:
                            ent[1] = 'second'
                        else:
                            continue
                    try:
                        v = next(ent[0])
                        if v == 'prev_done' and ent[1] == 'first':
                            ent[1] = 'second' if active[0] is ent else 'waiting'
                    except StopIteration:
                        active.remove(ent)

        def phase_scan(l, ret):
            P.push()
            nkc, hpc = (2, 2) if ret else (1, 4)
            KW = nkc * 128
            col0, width = (0, 1024) if ret else (1024, 800)
            qo, ko, vo, go = (0, 256, 512, 768) if ret else (0, 128, 256, 512)
            lro = 768
            kbr = 1 if ret else 3
            tri = P.sb("tri", [128, 6, 128]); ld(tri, tri[:], I["tri"][:, :, :])
            mh = P.sb("mh", [128, hpc]); ld(mh, mh[:], I["mh_ret" if ret else "mh_gla"][:, :])
            bd = P.sb("bd", [128, nkc, 256]); ld(bd, bd[:], I["bd_ret" if ret else "bd_gla"][:, :, :])
            gn = P.sb("gn", [128, 256]); ld(gn, gn[:], I["ret_gn" if ret else "gla_gn"][l].partition_broadcast(128))
            lns = math.log(32.0 ** -0.5)
            if ret:
                Ec = P.sb("E", [128, nkc, 6, 128]); Epc = P.sb("Epad", [128, nkc, 2, hpc, 128])
                dtokc = P.sb("dtok", [128, 2, KW]); decc = P.sb("dec", [128, nkc, 2])
                ld(Ec, Ec[:], I["ret_e"][:, :, :, :]); ld(dtokc, dtokc[:], I["ret_tok"][:, :, :]); ld(decc, decc[:], I["ret_dec"][:, :, :])
                for c in range(nkc):
                    for d_ in range(2):
                        dve(lambda e, c=c, d_=d_: e.tensor_tensor(out=Epc[:, c, d_, :, :], in0=Ec[:, c, 1 + 2 * d_, :].unsqueeze(1).broadcast_to([128, hpc, 128]),
                                                                  in1=mh[:].unsqueeze(2).broadcast_to([128, hpc, 128]), op=ALU.mult), [Ec, mh], [Epc])
            else:
                wd = P.sb("wd", [33, 256]); ld(wd, wd[:], I["gla_wd"][l])
                lnsb = P.sb("lnsb", [128, 1])
                dve(lambda e: e.memset(lnsb[:], lns), [], [lnsb])
                psZ = P.ps("psZ", [128, 512]); psB = P.ps("psB", [128, 4, 128])

            class TS:
                pass

            def mk_set(i):
                S = TS()
                S.pt = P.sb(f"pt{i}", [128, width])
                S.Qt = P.sb(f"Qt{i}", [128, nkc, 4, 128], BF16); S.Kp = P.sb(f"Kp{i}", [128, nkc, 2, hpc, 128], BF16)
                S.khat = P.sb(f"khat{i}", [128, 2, KW], BF16); S.Vb = P.sb(f"Vb{i}", [128, 256], BF16)
                S.st1 = P.sb(f"st1{i}", [128, 4, 128]); S.st2 = P.sb(f"st2{i}", [128, 4, 128]); S.PT = P.sb(f"PT{i}", [128, 4, 128], BF16)
                S.hn1 = P.sb(f"hn1{i}", [128, 4]); S.hn2 = P.sb(f"hn2{i}", [128, 4]); S.oc = P.sb(f"oc{i}", [128, 4, 64]); S.osq = P.sb(f"osq{i}", [128, 4, 64])
                S.sg = P.sb(f"sg{i}", [128, 256]); S.resb = P.sb(f"resb{i}", [128, 256], BF16); S.resT = P.sb(f"resT{i}", [128, 2, 128], BF16)
                S.tU = P.sb(f"tU{i}", [128, nkc, 256])
                if ret:
                    S.rot = P.sb(f"rot{i}", [128, 2, 32]); S.qkr = P.sb(f"qkr{i}", [128, 8, 64])
                    S.rt1 = P.sb(f"rt1{i}", [128, 8, 32]); S.rt2 = P.sb(f"rt2{i}", [128, 8, 32])
                    S.E, S.Epad, S.dtok, S.dec = Ec, Epc, dtokc, decc
                else:
                    S.lrT = P.sb(f"lrT{i}", [33, 128]); dve(lambda e: e.memset(S.lrT[:], 1.0), [], [S.lrT])
                    S.et = P.sb(f"et{i}", [128, 256]); S.lt = P.sb(f"lt{i}", [128, 256]); S.bsb = P.sb(f"bsb{i}", [128, 4, 128])
                    S.mids = P.sb(f"mids{i}", [128, 4])
                    S.E = P.sb(f"E{i}", [128, nkc, 6, 128]); S.Epad = P.sb(f"Epad{i}", [128, nkc, 2, hpc, 128])
                    S.dtok = P.sb(f"dtok{i}", [128, 2, KW]); S.dec = P.sb(f"dec{i}", [128, nkc, 2])
                return S

            sets = [mk_set(0), mk_set(1)]
            psT = P.ps("psT", [128, 2 * nkc, 128])
            psS = [P.ps(f"psS{i}", [128, 4, 128]) for i in range(2)]
            psO = P.ps("psO", [128, 256]); psU = P.ps("psU", [128, nkc, 256]); psR = P.ps("psR", [128, 2, 128], BF16)
            Sm = P.sb("Sm", [128, nkc, 256]); Sbf = P.sb("Sbf", [128, nkc, 256], BF16); Sball = P.sb("Sball", [128, NT, nkc, 256], BF16)
            brv = brF.rearrange("(k p) t -> p k t", p=128)

            def prep(n, S, full=True):
                pt = S.pt
                ld(pt, pt[:], projT[n * 128:(n + 1) * 128, col0:col0 + width], src=dB["projT"])
                if ret:
                    rot, qkr, rt1, rt2 = S.rot, S.qkr, S.rt1, S.rt2
                    ld(rot, rot[:], I["rot"][n * 128:(n + 1) * 128, :, :])
                    h0 = 0 if full else 4
                    nh_ = 8 - h0
                    src = pt[:, 0:512].rearrange("p (h d) -> p h d", d=64)[:, h0:8, :]
                    cosb = rot[:, 0, :].unsqueeze(1).broadcast_to([128, nh_, 32]); sinb = rot[:, 1, :].unsqueeze(1).broadcast_to([128, nh_, 32])
                    qkr_full = qkr
                    qkr = qkr[:, h0:8, :]; rt1 = rt1[:, h0:8, :]; rt2 = rt2[:, h0:8, :]
                    gps(lambda e: e.tensor_tensor(out=rt1, in0=src[:, :, 0:32], in1=cosb, op=ALU.mult), [pt, rot], [S.rt1])
                    gps(lambda e: e.tensor_tensor(out=rt2, in0=src[:, :, 32:64], in1=sinb, op=ALU.mult), [pt, rot], [S.rt2])
                    gps(lambda e: e.tensor_tensor(out=qkr[:, :, 0:32], in0=rt1, in1=rt2, op=ALU.subtract), [S.rt1, S.rt2], [S.qkr])
                    gps(lambda e: e.tensor_tensor(out=rt1, in0=src[:, :, 0:32], in1=sinb, op=ALU.mult), [pt, rot, S.qkr], [S.rt1])
                    gps(lambda e: e.tensor_tensor(out=rt2, in0=src[:, :, 32:64], in1=cosb, op=ALU.mult), [pt, rot, S.qkr], [S.rt2])
                    gps(lambda e: e.tensor_tensor(out=qkr[:, :, 32:64], in0=rt1, in1=rt2, op=ALU.add), [S.rt1, S.rt2], [S.qkr])
                    qk = qkr_full[:].rearrange("p h d -> p (h d)")
                    S.q_tok, S.k_tok, S.qkb = qk[:, 0:256], qk[:, 256:512], qkr_full
                else:
                    lrT, et, lt, bsb, mids, E, Epad, dtok, dec = S.lrT, S.et, S.lt, S.bsb, S.mids, S.E, S.Epad, S.dtok, S.dec
                    S.q_tok, S.k_tok, S.qkb = pt[:, qo:qo + 128], pt[:, ko:ko + 128], pt
                    pe(lambda e: e.transpose(psZ[0:32, 256:384], pt[:, lro:lro + 32], identf[:]), [pt, identf], [psZ])
                    yield
                    dve(lambda e: e.tensor_copy(out=lrT[0:32, :], in_=psZ[0:32, 256:384]), [psZ], [lrT])
                    yield
                    pe(lambda e: e.matmul(psZ[:, 0:256], lhsT=lrT[:], rhs=wd[:], start=True, stop=True), [lrT, wd], [psZ])
                    yield
                    act(lambda e: e.activation(out=et[:], in_=psZ[:, 0:256], func=AF.Exp, scale=-1.0), [psZ], [et])
                    act(lambda e: e.activation(out=lt[:], in_=et[:], func=AF.Ln, bias=1.0), [et], [lt])
                    yield
                    pe(lambda e: e.matmul(psB[:, 0, :], lhsT=lt[:, 0:128], rhs=tri[:, 2, :], start=True, stop=True), [lt, tri], [psB])
                    pe(lambda e: e.matmul(psB[:, 1, :], lhsT=lt[:, 128:256], rhs=tri[:, 3, :], start=True, stop=True), [lt, tri], [psB])
                    pe(lambda e: e.matmul(psB[:, 2, :], lhsT=tri[:, 4, :], rhs=lt[:, 0:128], start=True, stop=True), [lt, tri], [psB])
                    pe(lambda e: e.matmul(psB[:, 3, :], lhsT=tri[:, 5, :], rhs=lt[:, 128:256], start=True, stop=True), [lt, tri], [psB])
                    yield
                    dve(lambda e: e.tensor_copy(out=bsb[:], in_=psB[:]), [psB], [bsb])
                    dve(lambda e: e.tensor_scalar(out=mids[:, 0:2], in0=bsb[:, 0:2, 64], scalar1=-1.0, scalar2=lns, op0=ALU.mult, op1=ALU.add), [bsb], [mids])
                    dve(lambda e: e.tensor_copy(out=mids[:, 2:4], in_=bsb[:, 0:2, 64]), [bsb], [mids])
                    yield
                    for d_ in (range(2) if full else (1,)):
                        if full:
                            act(lambda e, d_=d_: e.activation(out=E[:, 0, 2 * d_, :], in_=bsb[:, d_, :], func=AF.Exp, bias=mids[:, d_:d_ + 1]), [bsb, mids], [E])
                            act(lambda e, d_=d_: e.activation(out=E[:, 0, 2 * d_ + 1, :], in_=bsb[:, d_, :], func=AF.Exp, scale=-1.0, bias=mids[:, 2 + d_:3 + d_]), [bsb, mids], [E])
                            act(lambda e, d_=d_: e.activation(out=E[:, 0, 4 + d_, :], in_=bsb[:, d_, :], func=AF.Exp, bias=lnsb[:, 0:1]), [bsb, lnsb], [E])
                        act(lambda e, d_=d_: e.activation(out=dtok[:, d_, :], in_=bsb[:, 2 + d_, :], func=AF.Exp), [bsb], [dtok])
                    if full:
                        act(lambda e: e.activation(out=dec[:, 0, 0:1], in_=bsb[:, 0, 127:128], func=AF.Exp), [bsb], [dec])
                    act(lambda e: e.activation(out=dec[:, 0, 1:2], in_=bsb[:, 1, 0:1], func=AF.Exp), [bsb], [dec])
                    yield
                    if full:
                        for d_ in range(2):
                            dve(lambda e, d_=d_: e.tensor_tensor(out=Epad[:, 0, d_, :, :], in0=E[:, 0, 1 + 2 * d_, :].unsqueeze(1).broadcast_to([128, hpc, 128]),
                                                                 in1=mh[:].unsqueeze(2).broadcast_to([128, hpc, 128]), op=ALU.mult), [E, mh], [Epad])
                act(lambda e: e.activation(out=S.Vb[:], in_=pt[:, vo:vo + 256], func=AF.Copy), [pt], [S.Vb])
                for d_ in (range(2) if full else (1,)):
                    gps(lambda e, d_=d_: e.tensor_tensor(out=S.khat[:, d_, :], in0=S.k_tok, in1=S.dtok[:, d_, :], op=ALU.mult), [S.qkb, S.dtok], [S.khat])
                yield

            def state_update(d_, S):
                for c in range(nkc):
                    pe(lambda e, c=c: e.matmul(psU[:, c, :], lhsT=S.khat[:, d_, c * 128:(c + 1) * 128], rhs=S.Vb[:], start=True, stop=True), [S.khat, S.Vb], [psU])
                yield
                dve(lambda e: e.tensor_tensor(out=S.tU[:], in0=psU[:], in1=bd[:], op=ALU.mult), [psU, bd], [S.tU])
                for c in range(nkc):
                    dve(lambda e, c=c: e.scalar_tensor_tensor(out=Sm[:, c, :], in0=Sm[:, c, :], scalar=S.dec[:, c, d_:d_ + 1], in1=S.tU[:, c, :],
                                                              op0=ALU.mult, op1=ALU.add), [Sm, S.dec, S.tU], [Sm])

            def keep_mul():
                dve(lambda e: e.tensor_scalar(out=Sm[:], in0=Sm[:], scalar1=scal[:, 0:1], scalar2=None, op0=ALU.mult), [Sm, scal], [Sm])

            def gen1(n):
                S = sets[n % 2]
                yield from prep(n, S, full=False)
                yield 'prev_done'
                act(lambda e: e.activation(out=Sball[:, n, :, :], in_=Sm[:], func=AF.Copy), [Sm], [Sball])
                yield from state_update(1, S)
                if n == NT // 2:
                    keep_mul()

            dve(lambda e: e.memset(Sm[:], 0.0), [], [Sm])
            run_pipelined(gen1, reversed(range(NT)), depth=int(os.environ.get('PIPE1', '2')))

            def gen2(n):
                S = sets[n % 2]
                if PD_POS == 0:
                    yield 'prev_done'
                yield from prep(n, S)
                if PD_POS == 1:
                    yield 'prev_done'
                Qt, Kp, PT, Vb, E, Epad = S.Qt, S.Kp, S.PT, S.Vb, S.E, S.Epad
                for c in range(nkc):
                    pe(lambda e, c=c: e.transpose(psT[:, c, :], S.q_tok[:, c * 128:(c + 1) * 128], identf[:]), [S.qkb, identf], [psT])
                    pe(lambda e, c=c: e.transpose(psT[:, nkc + c, :], S.k_tok[:, c * 128:(c + 1) * 128], identf[:]), [S.qkb, identf], [psT])
                yield
                if PD_POS == 2:
                    yield 'prev_done'
                for c in range(nkc):
                    for vi, ei in enumerate((0, 2, 4, 5)):
                        dve(lambda e, c=c, vi=vi, ei=ei: e.tensor_tensor(out=Qt[:, c, vi, :], in0=psT[:, c, :], in1=E[:, c, ei, :], op=ALU.mult), [psT, E], [Qt])
                    for d_ in range(2):
                        dve(lambda e, c=c, d_=d_: e.tensor_tensor(out=Kp[:, c, d_, :, :], in0=psT[:, nkc + c, :].unsqueeze(1).broadcast_to([128, hpc, 128]),
                                                                  in1=Epad[:, c, d_, :, :], op=ALU.mult), [psT, Epad], [Kp])
                yield
                if PD_POS == 3:
                    yield 'prev_done'
                for d_ in range(2):
                    for c in range(nkc):
                        for hh in range(hpc):
                            pe(lambda e, d_=d_, c=c, hh=hh: e.matmul(psS[d_][:, c * hpc + hh, :], lhsT=Kp[:, c, d_, hh, :], rhs=Qt[:, c, d_, :], start=True, stop=True),
                               [Kp, Qt], [psS[d_]])
                yield
                if PD_POS == 4:
                    yield 'prev_done'
                dve(lambda e: e.tensor_tensor(out=S.st1[:], in0=psS[0][:], in1=tri[:, 0, :].unsqueeze(1).broadcast_to([128, 4, 128]), op=ALU.mult), [psS[0], tri], [S.st1])
                dve(lambda e: e.tensor_tensor(out=S.st2[:], in0=psS[1][:], in1=tri[:, 1, :].unsqueeze(1).broadcast_to([128, 4, 128]), op=ALU.mult), [psS[1], tri], [S.st2])
                gps(lambda e: e.tensor_tensor(out=PT[:], in0=S.st1[:], in1=S.st2[:], op=ALU.add), [S.st1, S.st2], [PT])
                yield 'prev_done'
                if n == NT // 2:
                    keep_mul()
                act(lambda e: e.activation(out=Sbf[:], in_=Sm[:], func=AF.Copy), [Sm], [Sbf])
                yield
                for h_ in range(4):
                    c = h_ // hpc
                    hs = slice(h_ * 64, (h_ + 1) * 64)
                    pe(lambda e, c=c, hs=hs: e.matmul(psO[:, hs], lhsT=Qt[:, c, 2, :], rhs=Sbf[:, c, hs], start=True, stop=False), [Qt, Sbf], [psO])
                    pe(lambda e, c=c, hs=hs: e.matmul(psO[:, hs], lhsT=Qt[:, c, 3, :], rhs=Sball[:, n, c, hs], start=False, stop=False), [Qt, Sball], [psO])
                    pe(lambda e, h_=h_, hs=hs: e.matmul(psO[:, hs], lhsT=PT[:, h_, :], rhs=Vb[:, hs], start=False, stop=True), [PT, Vb], [psO])
                yield
                hn1, hn2, oc, osq, sg, resb, resT, pt = S.hn1, S.hn2, S.oc, S.osq, S.sg, S.resb, S.resT, S.pt
                O3 = psO[:].rearrange("p (h d) -> p h d", d=64)
                if ret:
                    dve(lambda e: e.tensor_reduce(out=hn1[:], in_=O3, axis=mybir.AxisListType.X, op=ALU.add), [psO], [hn1])
                    dve(lambda e: e.tensor_scalar(out=hn1[:], in0=hn1[:], scalar1=-1.0 / 64, scalar2=None, op0=ALU.mult), [hn1], [hn1])
                    dve(lambda e: e.tensor_tensor(out=oc[:], in0=O3, in1=hn1[:].unsqueeze(2).broadcast_to([128, 4, 64]), op=ALU.add), [psO, hn1], [oc])
                else:
                    dve(lambda e: e.tensor_copy(out=oc[:], in_=O3), [psO], [oc])
                gps(lambda e: e.tensor_tensor(out=osq[:], in0=oc[:], in1=oc[:], op=ALU.mult), [oc], [osq])
                dve(lambda e: e.tensor_reduce(out=hn2[:], in_=osq[:], axis=mybir.AxisListType.X, op=ALU.add), [osq], [hn2])
                act(lambda e: e.activation(out=sg[:], in_=pt[:, go:go + 256], func=AF.Silu), [pt], [sg])
                act(lambda e: e.activation(out=hn2[:], in_=hn2[:], func=AF.Sqrt, scale=1.0 / 64, bias=epsb[:, 0:1]), [hn2, epsb], [hn2])
                yield
                dve(lambda e: e.reciprocal(out=hn2[:], in_=hn2[:]), [hn2], [hn2])
                gps(lambda e: e.tensor_tensor(out=oc[:], in0=oc[:], in1=hn2[:].unsqueeze(2).broadcast_to([128, 4, 64]), op=ALU.mult), [oc, hn2], [oc])
                gps(lambda e: e.tensor_tensor(out=sg[:], in0=sg[:], in1=gn[:], op=ALU.mult), [sg, gn], [sg])
                gps(lambda e: e.tensor_tensor(out=resb[:], in0=oc[:].rearrange("p h d -> p (h d)"), in1=sg[:], op=ALU.mult), [oc, sg], [resb])
                yield
                for c2_ in range(2):
                    pe(lambda e, c2_=c2_: e.transpose(psR[:, c2_, :], resb[:, c2_ * 128:(c2_ + 1) * 128], identb[:]), [resb, identb], [psR])
                yield from state_update(0, S)
                dve(lambda e: e.tensor_copy(out=resT[:], in_=psR[:]), [psR], [resT])
                stq(brv[:, 2 * kbr:2 * kbr + 2, n * 128:(n + 1) * 128], resT, resT[:], dB["brF"])

            dve(lambda e: e.memset(Sm[:], 0.0), [], [Sm])
            run_pipelined(gen2, range(NT), depth=int(os.environ.get('PIPE2', '2')))
            P.pop()

        def phase_hyena(l, mode='all'):
            NBLK = 2 * T // 512
            Cg = 32
            TWO_PI = 2.0 * math.pi
            def part1():
                P.push()
                w1 = P.sb("w1", [33, 64]); w2 = P.sb("w2", [64, 64]); w3a = P.sb("w3a", [65, 1024])
                c1 = P.sb("c1", [64, 2]); c2_ = P.sb("c2", [64, 2]); fb = P.sb("fb", [64, 2]); delta = P.sb("delta", [128, 2])
                ld(w1, w1[:], I["flt_w1"][l]); ld(w2, w2[:], I["flt_w2"][l]); ld(w3a, w3a[0:64, :], I["flt_w3"][l])
                ld(w3a, w3a[64:65, :], I["flt_b3"][l:l + 1, :])
                ld(c1, c1[:], I["flt_c1"][l]); ld(c2_, c2_[:], I["flt_c2"][l]); ld(delta, delta[:], I["flt_delta"][:, :])
                dve(lambda e: e.tensor_tensor(out=fb[:, 0:1], in0=c1[:, 0:1], in1=c1[:, 1:2], op=ALU.mult), [c1], [fb])
                dve(lambda e: e.tensor_tensor(out=fb[:, 1:2], in0=c2_[:, 0:1], in1=c2_[:, 1:2], op=ALU.mult), [c2_, fb], [fb])
                h2a = P.sb("h2a", [65, 512], BF16); dve(lambda e: e.memset(h2a[:], 1.0), [], [h2a])
                w3b = P.sb("w3b", [65, 1024], BF16); dve(lambda e: e.tensor_copy(out=w3b[:], in_=w3a[:]), [w3a], [w3b])
                h1 = P.sb("h1", [64, 512]); a1 = P.sb("a1", [64, 512]); kk = P.sb("kk", [64, 512])
                nrm = P.sb("nrm", [128, 4, NBLK]); rn = P.sb("rn", [128, 4])
                fts = [P.sb(f"ft{i}", [33, 512]) for i in range(2)]; msks = [P.sb(f"msk{i}", [128, 3, 512]) for i in range(2)]
                win = P.sb("win", [128, 2, 512]); t1 = P.sb("ft1", [128, 512]); t2 = P.sb("ft2", [128, 512]); ab = P.sb("fab", [128, 512])
                gbs = [P.sb(f"gb{i}", [128, 512], BF16) for i in range(2)]
                psh = P.ps("psh", [64, 512]); psf = [P.ps(f"psf{i}", [128, 512]) for i in range(2)]

                def sin_layer(cc, col, dst):
                    dve(lambda e: e.tensor_scalar(out=a1[:], in0=psh[:], scalar1=cc[:, 0:1], scalar2=fb[:, col:col + 1], op0=ALU.mult, op1=ALU.add), [psh, cc, fb], [a1])
                    dve(lambda e: e.tensor_scalar(out=kk[:], in0=a1[:], scalar1=1.0 / TWO_PI, scalar2=MAGIC, op0=ALU.mult, op1=ALU.add), [a1], [kk])
                    dve(lambda e: e.tensor_scalar(out=kk[:], in0=kk[:], scalar1=-MAGIC, scalar2=None, op0=ALU.add), [kk], [kk])
                    dve(lambda e: e.scalar_tensor_tensor(out=a1[:], in0=kk[:], scalar=-TWO_PI, in1=a1[:], op0=ALU.mult, op1=ALU.add), [kk, a1], [a1])
                    act(lambda e: e.activation(out=dst, in_=a1[:], func=AF.Sin), [a1], [h1 if dst is not None and cc is c1 else h2a])

                ng = 0
                for blk in range(NBLK):
                    m0 = blk * 512
                    ft, msk = fts[blk % 2], msks[blk % 2]
                    ld(ft, ft[:], I["flt_feat"][:, m0:m0 + 512])
                    for r_ in range(3):
                        ld(msk, msk[:, r_, :], I["flt_msk"][r_, m0:m0 + 512].partition_broadcast(128))
                    pe(lambda e: e.matmul(psh[:], lhsT=w1[:], rhs=ft[:], start=True, stop=True), [w1, ft], [psh])
                    sin_layer(c1, 0, h1[:])
                    yield
                    pe(lambda e: e.matmul(psh[:], lhsT=w2[:], rhs=h1[:], start=True, stop=True), [w2, h1], [psh])
                    sin_layer(c2_, 1, h2a[0:64, :])
                    yield
                    for ch in range(2):
                        act(lambda e, ch=ch: e.activation(out=win[:, ch, :], in_=msk[:, 2, :], func=AF.Exp, scale=delta[:, ch:ch + 1]), [msk, delta], [win])
                    for o in range(2):
                        for ch in range(2):
                            for dr in range(2):
                                q = o * 4 + dr * 2 + ch
                                pe(lambda e, dr=dr, q=q: e.matmul(psf[dr][:], lhsT=w3b[:, q * 128:(q + 1) * 128], rhs=h2a[:], start=True, stop=True), [w3b, h2a], [psf[dr]])
                            dve(lambda e: e.tensor_tensor(out=t1[:], in0=psf[0][:], in1=msk[:, 0, :], op=ALU.mult), [psf[0], msk], [t1])
                            dve(lambda e: e.tensor_tensor(out=t2[:], in0=psf[1][:], in1=msk[:, 1, :], op=ALU.mult), [psf[1], msk], [t2])
                            dve(lambda e: e.tensor_tensor(out=t1[:], in0=t1[:], in1=t2[:], op=ALU.add), [t1, t2], [t1])
                            dve(lambda e, ch=ch: e.tensor_tensor(out=t1[:], in0=t1[:], in1=win[:, ch, :], op=ALU.mult), [t1, win], [t1])
                            gb = gbs[ng % 2]; ng += 1
                            idx = o * 2 + ch
                            act(lambda e, gb=gb: e.activation(out=gb[:], in_=t1[:], func=AF.Copy), [t1], [gb])
                            act(lambda e, idx=idx, blk=blk: e.activation(out=ab[:], in_=t1[:], func=AF.Abs, accum_out=nrm[:, idx, blk:blk + 1]), [t1], [ab, nrm])
                            stq(gF[idx * 128:(idx + 1) * 128, m0:m0 + 512], gb, gb[:], dB["gF"])
                            yield
                dve(lambda e: e.tensor_reduce(out=rn[:], in_=nrm[:], axis=mybir.AxisListType.X, op=ALU.add), [nrm], [rn])
                dve(lambda e: e.tensor_scalar(out=rn[:], in0=rn[:], scalar1=scal[:, 1:2], scalar2=EPS, op0=ALU.mult, op1=ALU.add), [rn, scal], [rn])
                dve(lambda e: e.reciprocal(out=rn[:], in_=rn[:]), [rn], [rn])
                stq(rnD.rearrange("(q p) -> p q", p=128), rn, rn[:], dB["rnD"])
                P.pop()

            def stage1(din, m1, AA, ps1s, cnt, Cg=Cg):
                for c in range(0, Cg, 2):
                    ps1 = ps1s[cnt[0] % 2]; cnt[0] += 1
                    for cc in range(2):
                        pe(lambda e, cc=cc, ps1=ps1, c=c: e.matmul(ps1[:, cc, :], lhsT=din[:, c + cc, :], rhs=m1[:], start=True, stop=True), [din, m1], [ps1])
                    for ri in range(2):
                        src_ap = ps1[:, :, ri * NSA:(ri + 1) * NSA].rearrange("p c s -> p s c")
                        if ri == 0:
                            dve(lambda e, src_ap=src_ap, c=c: e.tensor_copy(out=AA[:, :, 0, c:c + 2], in_=src_ap), [ps1], [AA])
                        else:
                            act(lambda e, src_ap=src_ap, c=c: e.activation(out=AA[:, :, 1, c:c + 2], in_=src_ap, func=AF.Copy), [ps1], [AA])
                    yield

            def stage2(AA, h2ts, psXs, evac, spb=8):
                h2v = I["hy_h2"].rearrange("s p a k -> p s a k")
                for j in range(NSA):
                    h2t = h2ts[(j // 8) % 2]; jj = j % spb
                    if j % 8 == 0:
                        nj_ = min(8, NSA - j)
                        ld(h2t, h2t[:, 0:nj_, :, :], h2v[:, j:j + nj_, :, :])
                    psX = psXs[(j // spb) % 2]
                    j8 = j % 8
                    pe(lambda e, psX=psX, jj=jj, h2t=h2t, j=j, j8=j8: e.matmul(psX[:, jj, 0, :], lhsT=h2t[:, j8, 0, :], rhs=AA[:, j, 0, :], start=True, stop=False), [h2t, AA], [psX])
                    pe(lambda e, psX=psX, jj=jj, h2t=h2t, j=j, j8=j8: e.matmul(psX[:, jj, 0, :], lhsT=h2t[:, j8, 2, :], rhs=AA[:, j, 1, :], start=False, stop=True), [h2t, AA], [psX])
                    pe(lambda e, psX=psX, jj=jj, h2t=h2t, j=j, j8=j8: e.matmul(psX[:, jj, 1, :], lhsT=h2t[:, j8, 0, :], rhs=AA[:, j, 1, :], start=True, stop=False), [h2t, AA], [psX])
                    pe(lambda e, psX=psX, jj=jj, h2t=h2t, j=j, j8=j8: e.matmul(psX[:, jj, 1, :], lhsT=h2t[:, j8, 1, :], rhs=AA[:, j, 0, :], start=False, stop=True), [h2t, AA], [psX])
                    if jj == spb - 1 or j == NSA - 1:
                        evac(psX, j - jj, jj + 1)
                        yield

            def part2():
                P.push()
                rnb = P.sb("rnb", [128, 512]); ld(rnb, rnb[:], rnD.partition_broadcast(128), src=dB["rnD"])
                hf1 = P.sb("hf1", [NS, 2 * NSA], BF16); ld(hf1, hf1[:], I["hy_hf1"][:, :])
                Cf = 64
                gin = P.sb("gin", [NS, Cf, 128], BF16); AA = P.sb("AAf", [128, NSA, 2, Cf], BF16)
                Gsb = P.sb("Gsbf", [128, NSA, 2, Cf], BF16)
                ps1s = [P.ps(f"hps1f{i}", [128, 2, 2 * NSA]) for i in range(2)]; psXs = [P.ps(f"hpsXf{i}", [128, 4, 2, Cf]) for i in range(2)]
                h2ts = [P.sb(f"h2tf{i}", [128, 8, 3, 128], BF16) for i in range(2)]
                cnt = [0]
                for gi in range(512 // Cf):
                    ld(gin, gin[:], gF[gi * Cf:(gi + 1) * Cf, :].rearrange("c (g n) -> g c n", n=128), src=dB["gF"])
                    yield from stage1(gin, hf1, AA, ps1s, cnt, Cg=Cf)

                    def evacG(psX, j0, nj, gi=gi):
                        dve(lambda e: e.tensor_tensor(out=Gsb[:, j0:j0 + nj, :, :], in0=psX[:, 0:nj, :, :],
                                                      in1=rnb[:, gi * Cf:(gi + 1) * Cf].unsqueeze(1).unsqueeze(1).broadcast_to([128, nj, 2, Cf]), op=ALU.mult),
                            [psX, rnb], [Gsb])
                    yield from stage2(AA, h2ts, psXs, evacG, spb=4)
                    for hh_ in range(2):
                        for s0_ in range(0, NSA, 32):
                            s1_ = min(NSA, s0_ + 32)
                            stq(Gd[2 * gi + hh_].rearrange("p s (r c) -> p s r c", r=2)[:, s0_:s1_], Gsb, Gsb[:, s0_:s1_, :, hh_ * 32:(hh_ + 1) * 32], dB["Gd"])
                P.pop()

            def part3():
                P.push()
                hz = P.sb("hz", [NSA, 128, 2, NT], BF16); ld(hz, hz[:], I["hy_z"][:, :, :, :])
                h1t = P.sb("hh1", [NT, 2 * NSA], BF16); ld(h1t, h1t[:], I["hy_h1"][:, :])
                i1 = P.sb("hi1", [128, 2, 256], BF16); ld(i1, i1[:], I["hy_i1"][:, :, :])
                skb = P.sb("skb", [128, 2, 256])
                for o in range(2):
                    ld(skb, skb[:, o, :], I["hy_skip"][l, o].partition_broadcast(128))
                AA = P.sb("AAd", [128, NSA, 2, Cg], BF16); Ysb = P.sb("Ysb", [128, 2, Cg, NSA], BF16); Bsb = P.sb("Bsb", [NSA, 128, 2, Cg], BF16)
                Gsb = P.sb("Gsbd", [128, NSA, 2, Cg], BF16)
                din = P.sb("din", [NT, Cg, 128], BF16); vt = P.sb("vt", [NT, Cg, 128]); x1t = P.sb("x1t", [NT, Cg, 128]); x2t = P.sb("x2t", [NT, Cg, 128])
                ob = P.sb("ob", [NT, Cg, 128], BF16)
                pw = [P.sb(f"pw{i}", [128, 8, Cg]) for i in range(4)]
                tcv = P.sb("tcv", [NT, Cg, 16])
                ps1s = [P.ps(f"hps1d{i}", [128, 2, 2 * NSA]) for i in range(2)]; psXs = [P.ps(f"hpsXd{i}", [128, 8, 2, Cg]) for i in range(2)]
                psIs = [P.ps(f"hpsI{i}", [NSA, 2, 256]) for i in range(2)]; psYs = [P.ps(f"hpsY{i}", [NT, 16, Cg]) for i in range(2)]
                h2ts = [P.sb(f"h2td{i}", [128, 8, 3, 128], BF16) for i in range(2)]
                cnt = [0]

                def evacY(psX, j0, nj):
                    Xre, Xim = psX[:, 0:nj, 0, :], psX[:, 0:nj, 1, :]
                    Gre, Gim = Gsb[:, j0:j0 + nj, 0, :], Gsb[:, j0:j0 + nj, 1, :]
                    dve(lambda e: e.tensor_tensor(out=pw[0][:, 0:nj, :], in0=Xre, in1=Gre, op=ALU.mult), [psX, Gsb], [pw[0]])
                    dve(lambda e: e.tensor_tensor(out=pw[1][:, 0:nj, :], in0=Xim, in1=Gim, op=ALU.mult), [psX, Gsb], [pw[1]])
                    dve(lambda e: e.tensor_tensor(out=Ysb[:, 0, :, j0:j0 + nj].rearrange("p c j -> p j c"), in0=pw[0][:, 0:nj, :], in1=pw[1][:, 0:nj, :], op=ALU.subtract),
                        [pw[0], pw[1]], [Ysb])
                    dve(lambda e: e.tensor_tensor(out=pw[2][:, 0:nj, :], in0=Xre, in1=Gim, op=ALU.mult), [psX, Gsb], [pw[2]])
                    dve(lambda e: e.tensor_tensor(out=pw[3][:, 0:nj, :], in0=Xim, in1=Gre, op=ALU.mult), [psX, Gsb], [pw[3]])
                    dve(lambda e: e.tensor_tensor(out=Ysb[:, 1, :, j0:j0 + nj].rearrange("p c j -> p j c"), in0=pw[2][:, 0:nj, :], in1=pw[3][:, 0:nj, :], op=ALU.add),
                        [pw[2], pw[3]], [Ysb])

                def long_conv(o, g, xg, svt):
                    ld(Gsb, Gsb[:].rearrange("p s r c -> p s (r c)"), Gd[o * (256 // Cg) + g], src=dB["Gd"])
                    drain(stage1(din, h1t, AA, ps1s, cnt))
                    drain(stage2(AA, h2ts, psXs, evacY))
                    for c in range(0, Cg, 2):
                        psI = psIs[(c // 2) % 2]
                        for cc in range(2):
                            pe(lambda e, cc=cc, psI=psI, c=c: e.matmul(psI[:, cc, :], lhsT=Ysb[:, 0, c + cc, :], rhs=i1[:, 0, :], start=True, stop=False), [Ysb, i1], [psI])
                            pe(lambda e, cc=cc, psI=psI, c=c: e.matmul(psI[:, cc, :], lhsT=Ysb[:, 1, c + cc, :], rhs=i1[:, 1, :], start=False, stop=True), [Ysb, i1], [psI])
                        for ri in range(2):
                            src_ap = psI[:, :, ri * 128:(ri + 1) * 128].rearrange("p c n -> p n c")
                            if ri == 0:
                                dve(lambda e, src_ap=src_ap, c=c: e.tensor_copy(out=Bsb[:, :, 0, c:c + 2], in_=src_ap), [psI], [Bsb])
                            else:
                                act(lambda e, src_ap=src_ap, c=c: e.activation(out=Bsb[:, :, 1, c:c + 2], in_=src_ap, func=AF.Copy), [psI], [Bsb])
                    for nb in range(8):
                        psY = psYs[nb % 2]
                        for q in range(16):
                            n2 = nb * 16 + q
                            pe(lambda e, psY=psY, q=q, n2=n2: e.matmul(psY[:, q, :], lhsT=hz[:, n2, 0, :], rhs=Bsb[:, n2, 0, :], start=True, stop=False), [hz, Bsb], [psY])
                            pe(lambda e, psY=psY, q=q, n2=n2: e.matmul(psY[:, q, :], lhsT=hz[:, n2, 1, :], rhs=Bsb[:, n2, 1, :], start=False, stop=True), [hz, Bsb], [psY])
                        sl = slice(nb * 16, (nb + 1) * 16)
                        dve(lambda e, psY=psY, sl=sl: e.tensor_tensor(out=tcv[:], in0=psY[:].rearrange("p n c -> p c n"), in1=svt[:, :, sl], op=ALU.add), [psY, svt], [tcv])
                        dve(lambda e, sl=sl: e.tensor_tensor(out=xg[:, :, sl], in0=tcv[:], in1=xg[:, :, sl], op=ALU.mult), [tcv, xg], [xg])

                uv = lambda r0: uhF[r0:r0 + Cg, :].rearrange("c (g n) -> g c n", n=128)
                for g in range(256 // Cg):
                    c0 = g * Cg
                    ld(vt, vt[:], uv(c0), src=dB["uhF"]); ld(x1t, x1t[:], uv(256 + c0), src=dB["uhF"]); ld(x2t, x2t[:], uv(512 + c0), src=dB["uhF"])
                    act(lambda e: e.activation(out=din[:], in_=vt[:], func=AF.Copy), [vt], [din])
                    dve(lambda e, c0=c0: e.tensor_tensor(out=vt[:], in0=vt[:], in1=skb[0:NT, 0, c0:c0 + Cg].unsqueeze(2).broadcast_to([NT, Cg, 128]), op=ALU.mult), [vt, skb], [vt])
                    long_conv(0, g, x1t, vt)
                    act(lambda e: e.activation(out=din[:], in_=x1t[:], func=AF.Copy), [x1t], [din])
                    dve(lambda e, c0=c0: e.tensor_tensor(out=vt[:], in0=x1t[:], in1=skb[0:NT, 1, c0:c0 + Cg].unsqueeze(2).broadcast_to([NT, Cg, 128]), op=ALU.mult), [x1t, skb], [vt])
                    long_conv(1, g, x2t, vt)
                    act(lambda e: e.activation(out=ob[:], in_=x2t[:], func=AF.Copy), [x2t], [ob])
                    stq(brF[512 + c0:512 + c0 + Cg, :].rearrange("c (g n) -> g c n", n=128), ob, ob[:], dB["brF"])
                P.pop()
            if mode == 'filtgen':
                def both():
                    yield from part1()
                    yield from part2()
                return both()
            if mode in ('all', 'filt'):
                drain(part1())
                drain(part2())
            if mode in ('all', 'conv'):
                part3()

        def phase_zero_br(l):
            P.push()
            zt = P.sb("zbr", [128, 2048], BF16)
            dve(lambda e: e.memset(zt[:], 0.0), [], [zt])
            for k in range(8):
                for t0 in range(0, T, 2048):
                    w_ = min(2048, T - t0)
                    stq(brF[k * 128:(k + 1) * 128, t0:t0 + w_], zt, zt[:, 0:w_], dB["brF"])
            P.pop()

        PHASES = dict(mod=phase_mod, norm=phase_norm, A=lambda l: drain(phase_A(l)), Afilt=lambda l: run_concurrent(phase_A(l), phase_hyena(l, 'filtgen')), C=phase_C, D=phase_D, zero=phase_zero_br, fnet=phase_fnet, ret=lambda l: phase_scan(l, True), gla=lambda l: phase_scan(l, False), hyena=phase_hyena, hyfilt=lambda l: phase_hyena(l, 'filt'), hyconv=lambda l: phase_hyena(l, 'conv'))
        nc._I = I
        return_hook(P, PHASES, locals())
    return nc


def return_hook(P, PHASES, env):
    sched = env.get('debug') or ()
    I, dB = env['I'], env['dB']
    stop = None
    for d in sched:
        if isinstance(d, str) and d.startswith("stop:"):
            stop = d[5:]
    x_in, x1d, xmid, y_out = env['x_in'], env['x1d'], env['xmid'], env['y_out']
    Am, Af = env['Am'], env['Af']
    only = [d[5:] for d in sched if isinstance(d, str) and d.startswith("only:")]
    if only:
        for nm in only:
            if nm == 'norm':
                PHASES['norm'](x_in, dB["in"], Am, 0)
            elif nm == 'C':
                PHASES['C'](0, x_in, dB["in"])
            elif nm == 'D':
                PHASES['D'](0, x1d, dB["x1d"])
            else:
                PHASES[nm](0)
        P.barrier()
        return
    for l in range(DEPTH):
        src, sbuf = (x_in, dB["in"]) if l == 0 else (x1d, dB["x1d"])
        dst, dbuf = (x1d, dB["x1d"]) if l == 0 else (y_out, dB["y"])
        if stop == "none":
            break
        PHASES['mod'](l)
        if stop == "mod":
            break
        PHASES['norm'](src, sbuf, Am, 0)
        if stop == "norm":
            break
        PHASES['Afilt' if CONC_FILT else 'A'](l)
        if stop == "A":
            break
        PHASES['zero'](l)
        for nm in ('fnet', 'ret', 'hyconv' if CONC_FILT else 'hyena', 'gla'):
            if nm in PHASES:
                PHASES[nm](l)
        if stop == "mix":
            break
        PHASES['C'](l, src, sbuf)
        PHASES['norm'](xmid, dB["xmid"], Af, 24)
        PHASES['D'](l, dst, dbuf)
        if stop == "L0":
            break
    P.barrier()


def prep_core_inputs(x, c2, W, tb):
    m = {"x": np.ascontiguousarray(x, np.float32)}
    m["cT"] = np.ascontiguousarray(c2.reshape(2, 8, 128).transpose(2, 1, 0), np.float32)
    m.update(W)
    m.update(tb)
    return m


def prep_weights(inp):
    f = lambda a: np.ascontiguousarray(a, np.float32)
    W = {}
    W["ada_w"] = f(inp["ada_w"]); W["ada_b"] = f(inp["ada_b"])
    W["ada_b_col"] = f(inp["ada_b"].reshape(DEPTH, 48, 128).transpose(0, 2, 1))
    nw = np.stack([inp["norm_pre_mix"], inp["norm_post_mix"], inp["norm_pre_ffn"], inp["norm_post_ffn"]], 1)
    W["normw_col"] = f(nw.reshape(DEPTH, 4, 8, 128).transpose(0, 3, 1, 2))
    W["norm_post_mix"] = f(inp["norm_post_mix"]); W["norm_post_ffn"] = f(inp["norm_post_ffn"])
    W["w_in"] = f(inp["w_in"])
    W["hy_cw"] = f(inp["hy_conv_w"].reshape(DEPTH, 3, 6, 128).transpose(0, 3, 2, 1))
    W["hy_cb"] = f(inp["hy_conv_b"].reshape(DEPTH, 6, 128).transpose(0, 2, 1))
    W["flt_w1"] = f(inp["flt_w1"]); W["flt_w2"] = f(inp["flt_w2"]); W["flt_w3"] = f(inp["flt_w3"]); W["flt_b3"] = f(inp["flt_b3"])
    W["flt_c1"] = f(np.stack([inp["flt_freq"], inp["flt_b1"]], -1)); W["flt_c2"] = f(np.stack([inp["flt_freq"], inp["flt_b2"]], -1))
    W["hy_skip"] = f(inp["hy_skip"])
    wd = np.zeros((DEPTH, 33, 256), np.float32)
    wd[:, 0:16, 0:128] = inp["gla_w_decay"][:, 0]; wd[:, 16:32, 128:256] = inp["gla_w_decay"][:, 1]
    wd[:, 32, 0:128] = inp["gla_b_decay"][:, 0]; wd[:, 32, 128:256] = inp["gla_b_decay"][:, 1]
    W["gla_wd"] = wd
    W["ret_gn"] = f(inp["ret_gn"]); W["gla_gn"] = f(inp["gla_gn"])
    W["w_branch"] = f(inp["w_branch"].reshape(DEPTH, 1024, D)); W["w_out"] = f(inp["w_out"])
    W["ffn_up"] = f(inp["ffn_up"])
    W["ffn_cw"] = f(inp["ffn_conv_w"].reshape(DEPTH, 3, 44, 128).transpose(0, 3, 2, 1))
    W["ffn_cb"] = f(inp["ffn_conv_b"].reshape(DEPTH, 44, 128).transpose(0, 2, 1))
    W["ffn_down"] = f(inp["ffn_down"])
    return W


_T = 8192


def kernel(**inp):
    inp = {k: np.asarray(v) for k, v in inp.items()}
    T = _T
    W = prep_weights(inp)
    tbP, tbS = make_tables(T, 'P'), make_tables(T, 'S')
    xp, xs, cp, cs = inp["x_prompt"], inp["x_sample"], inp["c_prompt"], inp["c_sample"]
    in_maps = []
    for b in range(2):
        in_maps.append(prep_core_inputs(xp[b], np.stack([cp[b], cp[b]]), W, tbP))
    for b in range(2):
        in_maps.append(prep_core_inputs(xs[2 * b:2 * b + 2].reshape(T, D), cs[2 * b:2 * b + 2], W, tbS))
    nc = build_program(T)
    res = run_bass_kernel_spmd(nc, in_maps, core_ids=list(range(4)))
    outs = [np.asarray(r["y"], np.float32) for r in res.results]
    y_prompt = np.stack([outs[0], outs[1]], 0)
    y_sample = np.concatenate([outs[2].reshape(2, T // 2, D), outs[3].reshape(2, T // 2, D)], 0)
    return (y_prompt, y_sample)
```
